# Optimizing a Trainium2 kernel written in Bass

```python
import math
import jax, jax.numpy as jnp
from jax import lax
import numpy as np

D_MODEL = 1024
BATCH = 8
SEQ = 2048
DEPTH = 4
DEC_BATCH = 128
DEC_SEQ = 1
PAST_LEN = 16384
PAGE_SIZE = 128

N_MIXERS = 3
N_S5 = (DEPTH + 2) // N_MIXERS
N_RWKV = (DEPTH + 1) // N_MIXERS
N_RGLRU = DEPTH // N_MIXERS
RMS_EPS = 1e-6

S5_GROUP = 16
S5_GROUPS = D_MODEL // S5_GROUP
S5_STATE = 64
S5_DT_MIN = 1e-3
S5_DT_MAX = 1e-1

RWKV_HEAD = 64
RWKV_HEADS = D_MODEL // RWKV_HEAD
RWKV_LORA_W = 64
RWKV_LORA_A = 64
RWKV_LORA_G = 128
RWKV_GN_EPS = 64e-5

D_RNN = D_MODEL
LRU_BLOCKS = 4
LRU_BLOCK = D_RNN // LRU_BLOCKS
LRU_C = 8.0
LRU_CONV = 4

D_FF = 2816
FFN_CONV = 3

kernel_name = 'hybrid_s5_rwkv7_rglru_convffn_step'


def rms_norm(x, g):
    xf = x.astype(jnp.float32)
    xf = xf * lax.rsqrt(jnp.mean(xf * xf, axis=-1, keepdims=True) + RMS_EPS)
    return (xf * g.astype(jnp.float32)).astype(x.dtype)


def causal_dwconv(u, buf, w, b):
    width = w.shape[0]
    length = u.shape[1]
    ext = jnp.concatenate([buf.astype(u.dtype), u], axis=1)
    out = b + ext[:, 0:length] * w[0]
    for k in range(1, width):
        out = out + ext[:, k:k + length] * w[k]
    return out, ext[:, ext.shape[1] - (width - 1):]


def _lin_combine(e1, e2):
    a1, b1 = e1
    a2, b2 = e2
    return a1 * a2, a2 * b1 + b2


def _cplx_combine(e1, e2):
    a1r, a1i, b1r, b1i = e1
    a2r, a2i, b2r, b2i = e2
    return (a2r * a1r - a2i * a1i, a2r * a1i + a2i * a1r,
            a2r * b1r - a2i * b1i + b2r, a2r * b1i + a2i * b1r + b2i)


def s5_mixer(u, h0_re, h0_im, p):
    f32 = jnp.float32
    bsz, length, _ = u.shape
    ug = u.astype(f32).reshape(bsz, length, S5_GROUPS, S5_GROUP)
    lam_re = jnp.minimum(p['s5_a_re'].astype(f32), -1e-4)
    lam_im = p['s5_a_im'].astype(f32)
    dt = jnp.exp(p['s5_log_dt'].astype(f32))[:, None]
    mag = jnp.exp(lam_re * dt)
    ab_re = mag * jnp.cos(lam_im * dt)
    ab_im = mag * jnp.sin(lam_im * dt)
    den = lam_re * lam_re + lam_im * lam_im
    coef_re = ((ab_re - 1.0) * lam_re + ab_im * lam_im) / den
    coef_im = (ab_im * lam_re - (ab_re - 1.0) * lam_im) / den
    bu_re = jnp.einsum('blgc,gpc->blgp', ug, p['s5_b_re'].astype(f32))
    bu_im = jnp.einsum('blgc,gpc->blgp', ug, p['s5_b_im'].astype(f32))
    x_re = coef_re * bu_re - coef_im * bu_im
    x_im = coef_re * bu_im + coef_im * bu_re
    h0r = h0_re.astype(f32)
    h0i = h0_im.astype(f32)
    x_re = x_re.at[:, 0].add(ab_re * h0r - ab_im * h0i)
    x_im = x_im.at[:, 0].add(ab_re * h0i + ab_im * h0r)
    shp = x_re.shape
    _, _, h_re, h_im = lax.associative_scan(
        _cplx_combine,
        (jnp.broadcast_to(ab_re, shp), jnp.broadcast_to(ab_im, shp), x_re, x_im), axis=1)
    y = (jnp.einsum('blgp,gcp->blgc', h_re, p['s5_c_re'].astype(f32))
         - jnp.einsum('blgp,gcp->blgc', h_im, p['s5_c_im'].astype(f32)))
    y = y.reshape(bsz, length, D_MODEL) + p['s5_d'].astype(f32) * u.astype(f32)
    z = jnp.einsum('bld,de->ble', jax.nn.gelu(y).astype(u.dtype), p['s5_w_glu'])
    val, gate = jnp.split(z, 2, axis=-1)
    return val * jax.nn.sigmoid(gate), h_re[:, -1], h_im[:, -1]


def rwkv7_mixer(x, shift0, s0, p):
    f32 = jnp.float32
    bsz, length, _ = x.shape
    prev = jnp.concatenate([shift0[:, None].astype(x.dtype), x[:, :-1]], axis=1)
    xx = prev - x
    xs = x[None] + xx[None] * p['rw_mu'][:, None, None, :]
    rkv = jnp.einsum('nbld,nde->nble', xs[:3], p['rw_w_rkv']).astype(f32)
    r, k, v = rkv[0], rkv[1], rkv[2]
    xw, xa, xg = xs[3], xs[4], xs[5]
    wlog = -jax.nn.softplus(-(p['rw_w0'] + jnp.tanh(xw @ p['rw_w1']) @ p['rw_w2']).astype(f32)) - 0.5
    decay = jnp.exp(-jnp.exp(wlog))
    a = jax.nn.sigmoid((p['rw_a0'] + (xa @ p['rw_a1']) @ p['rw_a2']).astype(f32))
    g = (jax.nn.sigmoid(xg @ p['rw_g1']) @ p['rw_g2']).astype(f32)
    heads = lambda t: t.reshape(bsz, length, RWKV_HEADS, RWKV_HEAD)
    kk = heads(k * p['rw_k_k'].astype(f32))
    kk = kk / jnp.maximum(jnp.sqrt(jnp.sum(kk * kk, axis=-1, keepdims=True)), 1e-12)
    k = k * (1.0 + (a - 1.0) * p['rw_k_a'].astype(f32))
    rh, dh, kh, vh, ah = heads(r), heads(decay), heads(k), heads(v), heads(a)

    def step(S, inp):
        r_t, w_t, k_t, v_t, kk_t, a_t = inp
        sa = jnp.einsum('bhij,bhj->bhi', S, -kk_t)
        S = (S * w_t[:, :, None, :] + sa[..., :, None] * (kk_t * a_t)[..., None, :]
             + v_t[..., :, None] * k_t[..., None, :])
        return S, jnp.einsum('bhij,bhj->bhi', S, r_t)

    seq = tuple(jnp.moveaxis(t, 1, 0) for t in (rh, dh, kh, vh, kk, ah))
    s_fin, o = lax.scan(step, s0.astype(f32), seq)
    o = jnp.moveaxis(o, 0, 1)
    mean = jnp.mean(o, axis=-1, keepdims=True)
    var = jnp.mean(jnp.square(o - mean), axis=-1, keepdims=True)
    o = ((o - mean) * lax.rsqrt(var + RWKV_GN_EPS)).reshape(bsz, length, D_MODEL)
    o = o * p['rw_ln_w'].astype(f32) + p['rw_ln_b'].astype(f32)
    bonus = jnp.sum(rh * kh * p['rw_r_k'].astype(f32), axis=-1, keepdims=True) * vh
    o = (o + bonus.reshape(bsz, length, D_MODEL)) * g
    out = jnp.einsum('bld,de->ble', o.astype(x.dtype), p['rw_w_o'])
    return out, x[:, -1], s_fin


def rglru_mixer(x, conv0, h0, p):
    f32 = jnp.float32
    bsz, length, _ = x.shape
    gy = jnp.einsum('bld,de->ble', x, p['lru_w_in'])
    gate_br, u = jnp.split(gy, 2, axis=-1)
    gate_br = jax.nn.gelu(gate_br.astype(f32))
    u, conv_new = causal_dwconv(u, conv0, p['lru_conv_w'], p['lru_conv_b'])
    ub = u.reshape(bsz, length, LRU_BLOCKS, LRU_BLOCK)
    rg = jax.nn.sigmoid((jnp.einsum('blnc,ncd->blnd', ub, p['lru_w_rg']).reshape(bsz, length, D_RNN)
                         + p['lru_b_rg']).astype(f32))
    ig = jax.nn.sigmoid((jnp.einsum('blnc,ncd->blnd', ub, p['lru_w_ig']).reshape(bsz, length, D_RNN)
                         + p['lru_b_ig']).astype(f32))
    log_a = LRU_C * rg * jax.nn.log_sigmoid(p['lru_lambda'].astype(f32))
    a = jnp.exp(log_a)
    mult = jnp.sqrt(-jnp.expm1(2.0 * log_a))
    bx = mult * ig * u.astype(f32)
    bx = bx.at[:, 0].add(a[:, 0] * h0.astype(f32))
    _, h = lax.associative_scan(_lin_combine, (a, bx), axis=1)
    out = jnp.einsum('ble,ed->bld', (h * gate_br).astype(x.dtype), p['lru_w_out'])
    return out, conv_new, h[:, -1]


def conv_ffn(x, conv0, w_in, conv_w, conv_b, w_out):
    hu = jnp.einsum('bld,df->blf', x, w_in)
    gate, up = jnp.split(hu, 2, axis=-1)
    gate, conv_new = causal_dwconv(gate, conv0, conv_w, conv_b)
    out = jnp.einsum('blf,fd->bld', jax.nn.silu(gate) * up, w_out)
    return out, conv_new


def _layer_slice(w, prefix, j):
    return {name: arr[j] for name, arr in w.items() if name.startswith(prefix)}


def trunk(x, st, w):
    dt = x.dtype
    out = {'s5_re': [], 's5_im': [], 'rw_wkv': [], 'rw_shift': [], 'lru_h': [], 'lru_conv': [], 'ffn_conv': []}
    for i in range(DEPTH):
        kind, j = i % N_MIXERS, i // N_MIXERS
        h = rms_norm(x, w['norm_mix'][i])
        if kind == 0:
            y, hr, hi = s5_mixer(h, st['s5_re'][j], st['s5_im'][j], _layer_slice(w, 's5_', j))
            out['s5_re'].append(hr)
            out['s5_im'].append(hi)
        elif kind == 1:
            y, sh, s = rwkv7_mixer(h, st['rw_shift'][j], st['rw_wkv'][j], _layer_slice(w, 'rw_', j))
            out['rw_shift'].append(sh)
            out['rw_wkv'].append(s)
        else:
            y, cb, hl = rglru_mixer(h, st['lru_conv'][j], st['lru_h'][j], _layer_slice(w, 'lru_', j))
            out['lru_conv'].append(cb)
            out['lru_h'].append(hl)
        x = x + y.astype(dt)
        h = rms_norm(x, w['norm_ffn'][i])
        y, cb = conv_ffn(h, st['ffn_conv'][i], w['ffn_w_in'][i], w['ffn_conv_w'][i],
                         w['ffn_conv_b'][i], w['ffn_w_out'][i])
        out['ffn_conv'].append(cb)
        x = x + y.astype(dt)
    y = rms_norm(x, w['norm_final'])
    new = {name: jnp.stack(v).astype(st[name].dtype) for name, v in out.items()}
    return y, new


def setup_inputs(seed: int = 0) -> dict:
    key = jax.random.key(seed)
    ks = iter(jax.random.split(key, 64))
    f32 = jnp.float32
    nrm = lambda shape, scale: scale * jax.random.normal(next(ks), shape, f32)
    unif = lambda shape, lo, hi: jax.random.uniform(next(ks), shape, f32, lo, hi)
    D = D_MODEL
    inp = {}
    inp['x_prompt'] = nrm((BATCH, SEQ, D), 1.0)
    inp['x_sample'] = nrm((DEC_BATCH, DEC_SEQ, D), 1.0)
    inp['state_s5_re'] = nrm((N_S5, DEC_BATCH, S5_GROUPS, S5_STATE), 0.1)
    inp['state_s5_im'] = nrm((N_S5, DEC_BATCH, S5_GROUPS, S5_STATE), 0.1)
    inp['state_rwkv_wkv'] = nrm((N_RWKV, DEC_BATCH, RWKV_HEADS, RWKV_HEAD, RWKV_HEAD), 0.1)
    inp['state_rwkv_shift'] = nrm((N_RWKV, DEC_BATCH, D), 1.0)
    inp['state_lru_h'] = nrm((N_RGLRU, DEC_BATCH, D_RNN), 0.5)
    inp['state_lru_conv'] = nrm((N_RGLRU, DEC_BATCH, LRU_CONV - 1, D_RNN), 1.0)
    inp['state_ffn_conv'] = nrm((DEPTH, DEC_BATCH, FFN_CONV - 1, D_FF), 1.0)
    inp['norm_mix'] = 1.0 + nrm((DEPTH, D), 0.02)
    inp['norm_ffn'] = 1.0 + nrm((DEPTH, D), 0.02)
    inp['norm_final'] = 1.0 + nrm((D,), 0.02)
    inp['s5_a_re'] = -0.5 + nrm((N_S5, S5_GROUPS, S5_STATE), 0.01)
    inp['s5_a_im'] = math.pi * jnp.arange(S5_STATE, dtype=f32) + nrm((N_S5, S5_GROUPS, S5_STATE), 0.01)
    inp['s5_log_dt'] = unif((N_S5, S5_GROUPS), math.log(S5_DT_MIN), math.log(S5_DT_MAX))
    inp['s5_b_re'] = nrm((N_S5, S5_GROUPS, S5_STATE, S5_GROUP), (2 * S5_GROUP) ** -0.5)
    inp['s5_b_im'] = nrm((N_S5, S5_GROUPS, S5_STATE, S5_GROUP), (2 * S5_GROUP) ** -0.5)
    inp['s5_c_re'] = nrm((N_S5, S5_GROUPS, S5_GROUP, S5_STATE), S5_STATE ** -0.5)
    inp['s5_c_im'] = nrm((N_S5, S5_GROUPS, S5_GROUP, S5_STATE), S5_STATE ** -0.5)
    inp['s5_d'] = nrm((N_S5, D), 1.0)
    inp['s5_w_glu'] = nrm((N_S5, D, 2 * D), D ** -0.5)
    inp['rw_mu'] = unif((N_RWKV, 6, D), 0.0, 1.0)
    inp['rw_w_rkv'] = nrm((N_RWKV, 3, D, D), D ** -0.5)
    inp['rw_w0'] = unif((N_RWKV, D), -6.0, 1.0)
    inp['rw_w1'] = nrm((N_RWKV, D, RWKV_LORA_W), D ** -0.5)
    inp['rw_w2'] = nrm((N_RWKV, RWKV_LORA_W, D), 0.5 * RWKV_LORA_W ** -0.5)
    inp['rw_a0'] = nrm((N_RWKV, D), 0.5)
    inp['rw_a1'] = nrm((N_RWKV, D, RWKV_LORA_A), D ** -0.5)
    inp['rw_a2'] = nrm((N_RWKV, RWKV_LORA_A, D), 0.5 * RWKV_LORA_A ** -0.5)
    inp['rw_g1'] = nrm((N_RWKV, D, RWKV_LORA_G), D ** -0.5)
    inp['rw_g2'] = nrm((N_RWKV, RWKV_LORA_G, D), RWKV_LORA_G ** -0.5)
    inp['rw_k_k'] = 0.85 + nrm((N_RWKV, D), 0.05)
    inp['rw_k_a'] = 1.0 + nrm((N_RWKV, D), 0.05)
    inp['rw_r_k'] = nrm((N_RWKV, RWKV_HEADS, RWKV_HEAD), 0.1)
    inp['rw_ln_w'] = 1.0 + nrm((N_RWKV, D), 0.02)
    inp['rw_ln_b'] = nrm((N_RWKV, D), 0.02)
    inp['rw_w_o'] = nrm((N_RWKV, D, D), D ** -0.5)
    inp['lru_w_in'] = nrm((N_RGLRU, D, 2 * D_RNN), D ** -0.5)
    inp['lru_conv_w'] = nrm((N_RGLRU, LRU_CONV, D_RNN), LRU_CONV ** -0.5)
    inp['lru_conv_b'] = nrm((N_RGLRU, D_RNN), 0.02)
    inp['lru_w_rg'] = nrm((N_RGLRU, LRU_BLOCKS, LRU_BLOCK, LRU_BLOCK), LRU_BLOCK ** -0.5)
    inp['lru_b_rg'] = nrm((N_RGLRU, D_RNN), 0.02)
    inp['lru_w_ig'] = nrm((N_RGLRU, LRU_BLOCKS, LRU_BLOCK, LRU_BLOCK), LRU_BLOCK ** -0.5)
    inp['lru_b_ig'] = nrm((N_RGLRU, D_RNN), 0.02)
    s = unif((N_RGLRU, D_RNN), 0.9, 0.999) ** (1.0 / LRU_C)
    inp['lru_lambda'] = jnp.log(s) - jnp.log1p(-s)
    inp['lru_w_out'] = nrm((N_RGLRU, D_RNN, D), D_RNN ** -0.5)
    inp['ffn_w_in'] = nrm((DEPTH, D, 2 * D_FF), D ** -0.5)
    inp['ffn_conv_w'] = nrm((DEPTH, FFN_CONV, D_FF), FFN_CONV ** -0.5)
    inp['ffn_conv_b'] = nrm((DEPTH, D_FF), 0.02)
    inp['ffn_w_out'] = nrm((DEPTH, D_FF, D), D_FF ** -0.5)
    return inp


def reference(x_prompt, x_sample, state_s5_re, state_s5_im, state_rwkv_wkv, state_rwkv_shift,
              state_lru_h, state_lru_conv, state_ffn_conv, norm_mix, norm_ffn, norm_final,
              s5_a_re, s5_a_im, s5_log_dt, s5_b_re, s5_b_im, s5_c_re, s5_c_im, s5_d, s5_w_glu,
              rw_mu, rw_w_rkv, rw_w0, rw_w1, rw_w2, rw_a0, rw_a1, rw_a2, rw_g1, rw_g2, rw_k_k,
              rw_k_a, rw_r_k, rw_ln_w, rw_ln_b, rw_w_o, lru_w_in, lru_conv_w, lru_conv_b,
              lru_w_rg, lru_b_rg, lru_w_ig, lru_b_ig, lru_lambda, lru_w_out, ffn_w_in,
              ffn_conv_w, ffn_conv_b, ffn_w_out):
    w = {'norm_mix': norm_mix, 'norm_ffn': norm_ffn, 'norm_final': norm_final,
         's5_a_re': s5_a_re, 's5_a_im': s5_a_im, 's5_log_dt': s5_log_dt, 's5_b_re': s5_b_re,
         's5_b_im': s5_b_im, 's5_c_re': s5_c_re, 's5_c_im': s5_c_im, 's5_d': s5_d, 's5_w_glu': s5_w_glu,
         'rw_mu': rw_mu, 'rw_w_rkv': rw_w_rkv, 'rw_w0': rw_w0, 'rw_w1': rw_w1, 'rw_w2': rw_w2,
         'rw_a0': rw_a0, 'rw_a1': rw_a1, 'rw_a2': rw_a2, 'rw_g1': rw_g1, 'rw_g2': rw_g2,
         'rw_k_k': rw_k_k, 'rw_k_a': rw_k_a, 'rw_r_k': rw_r_k, 'rw_ln_w': rw_ln_w, 'rw_ln_b': rw_ln_b,
         'rw_w_o': rw_w_o, 'lru_w_in': lru_w_in, 'lru_conv_w': lru_conv_w, 'lru_conv_b': lru_conv_b,
         'lru_w_rg': lru_w_rg, 'lru_b_rg': lru_b_rg, 'lru_w_ig': lru_w_ig, 'lru_b_ig': lru_b_ig,
         'lru_lambda': lru_lambda, 'lru_w_out': lru_w_out, 'ffn_w_in': ffn_w_in,
         'ffn_conv_w': ffn_conv_w, 'ffn_conv_b': ffn_conv_b, 'ffn_w_out': ffn_w_out}
    bsz, dt = x_prompt.shape[0], x_prompt.dtype
    st_prompt = {'s5_re': jnp.zeros((N_S5, bsz, S5_GROUPS, S5_STATE), dt),
                 's5_im': jnp.zeros((N_S5, bsz, S5_GROUPS, S5_STATE), dt),
                 'rw_wkv': jnp.zeros((N_RWKV, bsz, RWKV_HEADS, RWKV_HEAD, RWKV_HEAD), dt),
                 'rw_shift': jnp.zeros((N_RWKV, bsz, D_MODEL), dt),
                 'lru_h': jnp.zeros((N_RGLRU, bsz, D_RNN), dt),
                 'lru_conv': jnp.zeros((N_RGLRU, bsz, LRU_CONV - 1, D_RNN), dt),
                 'ffn_conv': jnp.zeros((DEPTH, bsz, FFN_CONV - 1, D_FF), dt)}
    st_sample = {'s5_re': state_s5_re, 's5_im': state_s5_im, 'rw_wkv': state_rwkv_wkv,
                 'rw_shift': state_rwkv_shift, 'lru_h': state_lru_h, 'lru_conv': state_lru_conv,
                 'ffn_conv': state_ffn_conv}
    y_prompt, new_p = trunk(x_prompt, st_prompt, w)
    y_sample, new_s = trunk(x_sample, st_sample, w)
    return (y_prompt, y_sample, new_p['s5_re'], new_s['s5_re'], new_p['s5_im'], new_s['s5_im'],
            new_p['rw_wkv'], new_s['rw_wkv'], new_p['rw_shift'], new_s['rw_shift'],
            new_p['lru_h'], new_s['lru_h'], new_p['lru_conv'], new_s['lru_conv'],
            new_p['ffn_conv'], new_s['ffn_conv'])
```

```python
import contextlib
import math
import numpy as np
import concourse.bass as bass
import concourse.mybir as mybir
from concourse.bass_utils import run_bass_kernel_spmd

F32 = mybir.dt.float32
BF16 = mybir.dt.bfloat16
I32 = mybir.dt.int32
AF = mybir.ActivationFunctionType
ALU = mybir.AluOpType
AX = mybir.AxisListType

ENGS = ['pe', 'act', 'dve', 'pool', 'sp']
N_DMA_SEMS = 32

NP = 2048
NS = 16
TT = NP + NS
D = 1024
DFF = 2816
NJ = 22
DEPTH = 4
EPS = 1e-6
S5_STAGE = 9


class Cell:
    __slots__ = ('w', 'r')

    def __init__(self):
        self.w = None
        self.r = {}


def cells(*shape):
    if len(shape) == 1:
        return [Cell() for _ in range(shape[0])]
    return [cells(*shape[1:]) for _ in range(shape[0])]


def flat(x):
    if isinstance(x, Cell):
        return [x]
    out = []
    for y in x:
        out.extend(flat(y))
    return out


class Sched:
    def __init__(self, nc):
        self.nc = nc
        self.ops = {e: [] for e in ENGS}
        self.cnt = {e: 0 for e in ENGS}
        self.seen = {e: {} for e in ENGS}
        self.dma_rr = 0
        self.dma_rr2 = 0
        for i in range(N_DMA_SEMS):
            self.cnt[('d', i)] = 0

    def _deps(self, reads, writes):
        deps = {}
        for c in reads:
            if c.w is not None:
                k, v = c.w
                if deps.get(k, 0) < v:
                    deps[k] = v
        for c in writes:
            if c.w is not None:
                k, v = c.w
                if deps.get(k, 0) < v:
                    deps[k] = v
            for k, v in c.r.items():
                if deps.get(k, 0) < v:
                    deps[k] = v
        return deps

    def _waits(self, eng, deps):
        waits = []
        seen = self.seen[eng]
        for k, v in deps.items():
            if k == eng and eng in ('pe', 'sp'):
                continue
            if seen.get(k, 0) < v:
                seen[k] = v
                waits.append((k, v))
        return waits

    def op(self, eng, fn, reads=(), writes=()):
        reads = flat(reads)
        writes = flat(writes)
        waits = self._waits(eng, self._deps(reads, writes))
        self.cnt[eng] += 1
        seq = self.cnt[eng]
        self.ops[eng].append((waits, fn, (eng, 1)))
        for c in reads:
            if c.r.get(eng, 0) < seq:
                c.r[eng] = seq
        for c in writes:
            c.w = (eng, seq)
            c.r = {}

    def dma(self, eng, out, in_, reads=(), writes=(), **kw):
        reads = flat(reads)
        writes = flat(writes)
        half = N_DMA_SEMS // 2
        if eng == 'pool':
            key = ('d', half + self.dma_rr2)
            self.dma_rr2 = (self.dma_rr2 + 1) % half
        else:
            key = ('d', self.dma_rr)
            self.dma_rr = (self.dma_rr + 1) % half
        deps = self._deps(reads, writes)
        prev = self.cnt[key]
        if prev > 0:
            deps[key] = max(deps.get(key, 0), prev)
        waits = self._waits(eng, deps)
        self.cnt[key] += 16
        val = self.cnt[key]

        def fn(e, out=out, in_=in_, kw=kw):
            return e.dma_start(out=out, in_=in_, **kw)
        self.ops[eng].append((waits, fn, (key, 16)))
        for c in reads:
            c.r[key] = val
        for c in writes:
            c.w = (key, val)
            c.r = {}

    def barrier(self, engines=('pe', 'act', 'dve', 'pool', 'sp')):
        snap = dict(self.cnt)
        for e in engines:
            deps = {k: v for k, v in snap.items() if v > 0 and k != e}
            waits = self._waits(e, deps)
            if waits:
                self.ops[e].append((waits, None, None))

    def finish(self):
        snap = dict(self.cnt)
        for e in ('sp',):
            deps = {k: v for k, v in snap.items() if v > 0 and k != e}
            waits = self._waits(e, deps)
            self.ops[e].append((waits, None, None))

    def emit(self):
        nc = self.nc
        sems = {}
        with contextlib.ExitStack() as st:
            for k in self.cnt:
                name = k if isinstance(k, str) else 'dma%d' % k[1]
                sems[k] = st.enter_context(nc.semaphore('s_' + name))
            block = st.enter_context(nc.Block())

            def run(engname):
                def body(e):
                    for waits, fn, inc in self.ops[engname]:
                        for k, v in waits:
                            e.wait_ge(sems[k], v)
                        if fn is not None:
                            fn(e).then_inc(sems[inc[0]], inc[1])
                return body
            block.tensor(run('pe'))
            block.scalar(run('act'))
            block.vector(run('dve'))
            block.gpsimd(run('pool'))
            block.sync(run('sp'))


VROWS = {}


def _vrow(name, n=1):
    VROWS[name] = (len_vrows[0], n)
    len_vrows[0] += n


len_vrows = [0]
_vrow('norm_mix', 4)
_vrow('norm_ffn', 4)
_vrow('norm_final', 1)
_vrow('s5_d', 2)
_vrow('rw_mu', 6)
_vrow('rw_w0')
_vrow('rw_a0')
_vrow('rw_k_k')
_vrow('rw_k_a')
_vrow('rw_r_k')
_vrow('rw_ln_w')
_vrow('rw_ln_b')
_vrow('lru_conv_w', 4)
_vrow('lru_conv_b')
_vrow('lru_b_rg')
_vrow('lru_b_ig')
_vrow('lru_lambda')
NV = len_vrows[0]

CST_COLS = {}


def make_consts():
    cols = []
    off = 0

    def add(name, arr):
        nonlocal off
        arr = np.asarray(arr, np.float32)
        CST_COLS[name] = (off, arr.shape[1])
        off += arr.shape[1]
        cols.append(arr)
    p = np.arange(128)
    add('ident', np.eye(128))
    add('ones', np.ones((128, 128)))
    add('blk64', (p[:, None] // 64 == p[None, :] // 64).astype(np.float32))
    add('mask_hi', (p[:, None] // 64 == np.arange(2)[None, :]).astype(np.float32))
    add('mask_c16', (((p[:, None] % 32) // 16) == np.arange(2)[None, :]).astype(np.float32))
    add('maskq', (p[:, None] // 32 == np.arange(4)[None, :]).astype(np.float32))
    add('i2', (p[:, None] % 64 == np.arange(64)[None, :]).astype(np.float32))
    s64 = np.arange(64)
    us = (s64[:, None] < s64[None, :]).astype(np.float32)
    ui = (s64[:, None] <= s64[None, :]).astype(np.float32)
    ls = (s64[:, None] > s64[None, :]).astype(np.float32)
    m5 = np.concatenate([us, us, ls, ui, -ui], 1)
    add('mask5', np.concatenate([m5, m5], 0))
    add('imp', np.concatenate([np.eye(64), np.eye(64)], 0))
    s = np.arange(64)
    up_strict = (s[:, None] < s[None, :]).astype(np.float32)
    up_incl = (s[:, None] <= s[None, :]).astype(np.float32)
    add('tri_s', np.concatenate([up_strict, up_strict], 0))
    add('tri_i', np.concatenate([up_incl, up_incl], 0))
    return np.concatenate(cols, 1)


CONSTS = make_consts()
CMASK = np.tile((np.arange(NP) % 64 != 0).astype(np.float32)[None, :], (128, 1))
NCST = CONSTS.shape[1]


def build(mixers=(True, True, True, True), ffn=True, dbg=False):
    nc = bass.Bass("TRN2", target_bir_lowering=False)
    S = Sched(nc)
    I = {}
    O = {}

    def din(name, shape, dt=F32):
        I[name] = nc.dram_tensor(name, list(shape), dt, kind="ExternalInput").ap()

    def dout(name, shape, dt=F32):
        O[name] = nc.dram_tensor(name, list(shape), dt, kind="ExternalOutput").ap()

    din('x_prompt', [NP, D])
    din('x_sample', [NS, D])
    din('state_s5_re', [2, NS, 4096])
    din('state_s5_im', [2, NS, 4096])
    din('state_rwkv_wkv', [NS, 16, 64, 64])
    din('state_rwkv_shift', [NS, D])
    din('state_lru_h', [NS, D])
    din('state_lru_conv', [NS, 3, D])
    din('state_ffn_conv', [4, NS, 2, DFF])
    din('norm_mix', [4, D]); din('norm_ffn', [4, D]); din('norm_final', [1, D])
    din('s5_a_re', [2, 64, 64]); din('s5_a_im', [2, 64, 64]); din('s5_log_dt', [2, 64])
    din('s5_b_re', [2, 64, 64, 16]); din('s5_b_im', [2, 64, 64, 16])
    din('s5_c_re', [2, 64, 16, 64]); din('s5_c_im', [2, 64, 16, 64])
    din('s5_d', [2, D]); din('s5_w_glu', [2, D, 2 * D])
    din('rw_mu', [6, D]); din('rw_w_rkv', [3, D, D]); din('rw_w0', [1, D]); din('rw_w1', [D, 64])
    din('rw_w2', [64, D]); din('rw_a0', [1, D]); din('rw_a1', [D, 64]); din('rw_a2', [64, D])
    din('rw_g1', [D, 128]); din('rw_g2', [128, D]); din('rw_k_k', [1, D]); din('rw_k_a', [1, D])
    din('rw_r_k', [1, D]); din('rw_ln_w', [1, D]); din('rw_ln_b', [1, D]); din('rw_w_o', [D, D])
    din('lru_w_in', [D, 2 * D]); din('lru_conv_w', [4, D]); din('lru_conv_b', [1, D])
    din('lru_w_rg', [4, 256, 256]); din('lru_b_rg', [1, D]); din('lru_w_ig', [4, 256, 256])
    din('lru_b_ig', [1, D]); din('lru_lambda', [1, D]); din('lru_w_out', [D, D])
    din('ffn_w_in', [4, D, 2 * DFF]); din('ffn_conv_w', [4, 3, DFF]); din('ffn_conv_b', [4, 1, DFF])
    din('ffn_w_out', [4, DFF, D])
    din('consts', [128, NCST])
    din('cmask', [128, NP])

    dout('y_prompt', [NP, D]); dout('y_sample', [NS, D])
    dout('s5_re_p', [2, 4096]); dout('s5_re_s', [2, NS, 4096])
    dout('s5_im_p', [2, 4096]); dout('s5_im_s', [2, NS, 4096])
    dout('wkv_p', [16, 64, 64]); dout('wkv_s', [NS, 16, 64, 64])
    dout('shift_p', [1, D]); dout('shift_s', [NS, D])
    dout('lru_h_p', [1, D]); dout('lru_h_s', [NS, D])
    dout('lru_conv_p', [3, D]); dout('lru_conv_s', [NS, 3, D])
    dout('ffn_conv_p', [4, 2, DFF]); dout('ffn_conv_s', [4, NS, 2, DFF])
    if dbg:
        dout('dbg', [9, D, TT])

    with contextlib.ExitStack() as st:
        def sb(name, shape, dt=F32):
            return st.enter_context(nc.sbuf_tensor(name, list(shape), dt))

        xres = sb('xres', [128, 8, TT]); c_x = cells(8, 5)
        hb = sb('hb', [128, 8, TT], BF16); c_h = cells(8, 5)
        cst = sb('cst', [128, NCST]); c_cst = Cell()
        identb = sb('identb', [128, 128], BF16)
        onesb = sb('onesb', [128, 128], BF16)
        blk64b = sb('blk64b', [128, 128], BF16)
        c_cb = Cell()
        vec = sb('vec', [128, 8, NV]); c_vec = Cell()
        epsc = sb('epsc', [128, 1]); c_eps = Cell()
        ARENA_W = 26800
        arena = sb('arena', [128, ARENA_W])
        psum = st.enter_context(nc.psum_tensor('psum', [128, 8, 512], F32))
        c_ps = cells(8)

        def PS(b):
            return psum[:, b, :]

        def cc(name):
            o, n = CST_COLS[name]
            return cst[:, o:o + n]

        class Arena:
            def __init__(self):
                self.off = 0

            def f32(self, n):
                a = arena[:, self.off:self.off + n]
                self.off += n
                assert self.off <= ARENA_W, self.off
                return a

            def bf16(self, n):
                n2 = (n + 1) // 2
                a = arena[:, self.off:self.off + n2].bitcast(BF16)
                self.off += n2
                assert self.off <= ARENA_W, self.off
                return a[:, 0:n]

        BLKS = [(0, 512), (512, 512), (1024, 512), (1536, 512), (NP, NS)]

        def vcol(name, ct, k=0):
            r0, _ = VROWS[name]
            return vec[:, ct, r0 + k:r0 + k + 1]

        S.dma('sp', cst[:], I['consts'][:, :], writes=[c_cst])
        o_id = CST_COLS['ident'][0]
        S.op('dve', lambda e: e.tensor_copy(out=identb[:], in_=cc('ident')), reads=[c_cst], writes=[c_cb])
        S.op('dve', lambda e: e.tensor_copy(out=onesb[:], in_=cc('ones')), reads=[c_cst], writes=[c_cb])
        S.op('dve', lambda e: e.tensor_copy(out=blk64b[:], in_=cc('blk64')), reads=[c_cst], writes=[c_cb])
        S.op('dve', lambda e: e.memset(epsc[:], EPS), writes=[c_eps])
        ident = cc('ident')

        A = Arena()
        vst = A.f32(D)
        c_vst = Cell()
        for name, (r0, n) in VROWS.items():
            S.dma('sp', vst[r0:r0 + n, :], I[name][:, :], writes=[c_vst])
        for ct in range(8):
            S.op('pe', lambda e, ct=ct: e.transpose(PS(0)[:, ct * NV:(ct + 1) * NV], vst[0:NV, ct * 128:(ct + 1) * 128], ident[0:NV, 0:NV]),
                 reads=[c_vst, c_cst], writes=[c_ps[0]])
        S.op('dve', lambda e: e.tensor_copy(out=vec[:].rearrange("p a b -> p (a b)"), in_=PS(0)[:, 0:8 * NV]), reads=[c_ps[0]], writes=[c_vec])

        xin = [A.f32(4 * D), A.f32(4 * D)]
        c_xin = cells(2)
        k = 0
        for tb in range(4):
            S.dma('sp', xin[tb % 2].rearrange("p (a c) -> p a c", a=4),
                  I['x_prompt'][tb * 512:(tb + 1) * 512, :].rearrange("(a p) c -> p a c", p=128), writes=[c_xin[tb % 2]])
            for ct in range(8):
                b = 1 + (k % 2)
                for a in range(4):
                    S.op('pe', lambda e, tb=tb, ct=ct, a=a, b=b: e.transpose(
                        PS(b)[:, a * 128:(a + 1) * 128], xin[tb % 2][:, a * D + ct * 128: a * D + (ct + 1) * 128], ident),
                        reads=[c_xin[tb % 2], c_cst], writes=[c_ps[b]])
                eng = 'act' if k % 2 == 0 else 'dve'
                if eng == 'act':
                    S.op('act', lambda e, tb=tb, ct=ct, b=b: e.activation(out=xres[:, ct, tb * 512:(tb + 1) * 512], in_=PS(b), func=AF.Copy),
                         reads=[c_ps[b]], writes=[c_x[ct][tb]])
                else:
                    S.op('dve', lambda e, tb=tb, ct=ct, b=b: e.tensor_copy(out=xres[:, ct, tb * 512:(tb + 1) * 512], in_=PS(b)),
                         reads=[c_ps[b]], writes=[c_x[ct][tb]])
                k += 1
        xsin = A.f32(D)
        c_xsin = Cell()
        S.dma('sp', xsin[0:NS, :], I['x_sample'][:, :], writes=[c_xsin])
        for ct in range(8):
            S.op('pe', lambda e, ct=ct: e.transpose(PS(3)[:, ct * NS:(ct + 1) * NS], xsin[0:NS, ct * 128:(ct + 1) * 128], ident[0:NS, 0:NS]),
                 reads=[c_xsin, c_cst], writes=[c_ps[3]])
        S.op('dve', lambda e: e.tensor_copy(out=xres[:, :, NP:TT], in_=PS(3)[:, 0:8 * NS].rearrange("p (a b) -> p a b", a=8)),
             reads=[c_ps[3]], writes=[c_x[ct][4] for ct in range(8)])
        S.barrier()

        def rmsnorm(gname, gk, A, block_cb=None):
            sq = [A.bf16(512), A.bf16(512)]
            c_sq = cells(2)
            rstd = [A.f32(512), A.f32(512)]
            c_rs = cells(2)
            k = 0
            for bi, (c0, n) in enumerate(BLKS):
                pb = 6 + (bi % 2)
                for ct in range(8):
                    q = k % 2
                    S.op('pool', lambda e, ct=ct, c0=c0, n=n, q=q: e.tensor_tensor(out=sq[q][:, 0:n], in0=xres[:, ct, c0:c0 + n], in1=xres[:, ct, c0:c0 + n], op=ALU.mult),
                         reads=[c_x[ct][bi]], writes=[c_sq[q]])
                    S.op('pe', lambda e, ct=ct, n=n, q=q, pb=pb: e.matmul(PS(pb)[:, 0:n], lhsT=onesb[:], rhs=sq[q][:, 0:n], start=(ct == 0), stop=(ct == 7)),
                         reads=[c_sq[q], c_cb], writes=[c_ps[pb]])
                    k += 1
                r = bi % 2
                S.op('act', lambda e, n=n, r=r, pb=pb: e.activation(out=rstd[r][:, 0:n], in_=PS(pb)[:, 0:n], func=AF.Ln, bias=epsc[:, 0:1], scale=1.0 / D),
                     reads=[c_ps[pb], c_eps], writes=[c_rs[r]])
                S.op('act', lambda e, n=n, r=r: e.activation(out=rstd[r][:, 0:n], in_=rstd[r][:, 0:n], func=AF.Exp, scale=-0.5),
                     reads=[c_rs[r]], writes=[c_rs[r]])
                if block_cb is not None:
                    block_cb(bi, c0, n, rstd[r], c_rs[r])
                    continue
                for ct in range(8):
                    S.op('dve', lambda e, ct=ct, c0=c0, n=n, r=r: e.scalar_tensor_tensor(
                        out=hb[:, ct, c0:c0 + n], in0=xres[:, ct, c0:c0 + n], scalar=vcol(gname, ct, gk), in1=rstd[r][:, 0:n], op0=ALU.mult, op1=ALU.mult),
                        reads=[c_x[ct][bi], c_rs[r], c_vec], writes=[c_h[ct][bi]])

        def wload(dst, src_rows_ap, kt, c_dst):
            S.dma('pool', dst, src_rows_ap.rearrange("(k p) c -> p k c", p=128), writes=[c_dst])

        def ffn_layer(li):
            A = Arena()
            rmsnorm('norm_ffn', li, A)
            act = A.bf16(NJ * 528).rearrange("p (j n) -> p j n", j=NJ)
            c_act = cells(NJ)
            NWB = 4
            win = [A.bf16(8 * 256).rearrange("p (k c) -> p k c", k=8) for _ in range(NWB)]
            c_win = cells(NWB, 2)
            wo = [A.bf16(NJ * 128).rearrange("p (j c) -> p j c", j=NJ) for _ in range(2)]
            c_wo = cells(2)
            NQ = 3
            G = [A.f32(514) for _ in range(NQ)]
            c_G = cells(NQ)
            acc = [A.f32(512) for _ in range(NQ)]
            c_acc = cells(NQ)
            halo = A.f32(2 * NJ).rearrange("p (s j) -> p s j", s=2)
            c_halo = cells(NJ)
            cw = A.f32(NJ * 4).rearrange("p (j k) -> p j k", j=NJ)
            c_cw = Cell()
            stS = A.f32(NJ * 32).rearrange("p (j b s) -> p j b s", j=NJ, b=NS)
            c_stS = Cell()
            gnew = A.f32(NJ * NS).rearrange("p (j b) -> p j b", j=NJ)
            c_gnew = cells(NJ)
            accs = [A.f32(NS) for _ in range(3)]
            c_accs = cells(3)
            stg = A.f32(DFF)
            c_stg = Cell()
            S.dma('sp', stg[0:3, :], I['ffn_conv_w'][li, :, :], writes=[c_stg])
            S.dma('sp', stg[3:4, :], I['ffn_conv_b'][li, :, :], writes=[c_stg])
            for j in range(NJ):
                S.op('pe', lambda e, j=j: e.transpose(PS(5)[:, j * 4:(j + 1) * 4], stg[0:4, j * 128:(j + 1) * 128], ident[0:4, 0:4]),
                     reads=[c_stg, c_cst], writes=[c_ps[5]])
            S.op('dve', lambda e: e.tensor_copy(out=cw[:].rearrange("p j k -> p (j k)"), in_=PS(5)[:, 0:NJ * 4]), reads=[c_ps[5]], writes=[c_cw])
            S.dma('sp', stg[0:32, :], I['state_ffn_conv'][li, :, :, :].rearrange("b s f -> (b s) f"), reads=[c_ps[5]], writes=[c_stg])
            for j0 in range(0, NJ, 16):
                j1 = min(NJ, j0 + 16)
                for j in range(j0, j1):
                    S.op('pe', lambda e, j=j, j0=j0: e.transpose(PS(5)[:, (j - j0) * 32:(j - j0 + 1) * 32], stg[0:32, j * 128:(j + 1) * 128], ident[0:32, 0:32]),
                         reads=[c_stg, c_cst], writes=[c_ps[5]])
                S.op('dve', lambda e, j0=j0, j1=j1: e.tensor_copy(out=stS[:, j0:j1, :, :].rearrange("p j b s -> p (j b s)"), in_=PS(5)[:, 0:(j1 - j0) * 32]),
                     reads=[c_ps[5]], writes=[c_stS])
            S.dma('sp', O['ffn_conv_s'][li, :, 0, :], I['state_ffn_conv'][li, :, 1, :])

            kw = 0
            ko = 0
            for bi in range(4):
                c0 = bi * 512
                last = (bi == 3)
                for j in range(NJ):
                    wb = kw % NWB
                    wload(win[wb][:, :, 0:128], I['ffn_w_in'][li, :, j * 128:(j + 1) * 128], 8, c_win[wb][0])
                    wload(win[wb][:, :, 128:256], I['ffn_w_in'][li, :, DFF + j * 128:DFF + (j + 1) * 128], 8, c_win[wb][1])
                    gb = kw % 3
                    ub = 3 + (kw % 3)
                    q = kw % NQ
                    kw += 1
                    for kt in range(8):
                        S.op('pe', lambda e, kt=kt, wb=wb, gb=gb, c0=c0: e.matmul(PS(gb), lhsT=win[wb][:, kt, 0:128], rhs=hb[:, kt, c0:c0 + 512], start=(kt == 0), stop=(kt == 7)),
                             reads=[c_win[wb][0], c_h[kt][bi]], writes=[c_ps[gb]])
                    for kt in range(8):
                        S.op('pe', lambda e, kt=kt, wb=wb, ub=ub, c0=c0: e.matmul(PS(ub), lhsT=win[wb][:, kt, 128:256], rhs=hb[:, kt, c0:c0 + 512], start=(kt == 0), stop=(kt == 7)),
                             reads=[c_win[wb][1], c_h[kt][bi]], writes=[c_ps[ub]])
                    if last:
                        for kt in range(8):
                            S.op('pe', lambda e, kt=kt, wb=wb: e.matmul(PS(7)[:, 0:NS], lhsT=win[wb][:, kt, 0:128], rhs=hb[:, kt, NP:TT], start=(kt == 0), stop=(kt == 7)),
                                 reads=[c_win[wb][0], c_h[kt][4]], writes=[c_ps[7]])
                        for kt in range(8):
                            S.op('pe', lambda e, kt=kt, wb=wb: e.matmul(PS(7)[:, 32:32 + NS], lhsT=win[wb][:, kt, 128:256], rhs=hb[:, kt, NP:TT], start=(kt == 0), stop=(kt == 7)),
                                 reads=[c_win[wb][1], c_h[kt][4]], writes=[c_ps[7]])
                    S.op('act', lambda e, q=q, gb=gb: e.activation(out=G[q][:, 2:514], in_=PS(gb), func=AF.Copy), reads=[c_ps[gb]], writes=[c_G[q]])
                    if bi == 0:
                        S.op('dve', lambda e, q=q: e.memset(G[q][:, 0:2], 0.0), writes=[c_G[q]])
                    else:
                        S.op('dve', lambda e, q=q, j=j: e.tensor_copy(out=G[q][:, 0:2], in_=halo[:, :, j]), reads=[c_halo[j]], writes=[c_G[q]])
                    S.op('act', lambda e, q=q, gb=gb, j=j: e.activation(out=acc[q][:], in_=PS(gb), func=AF.Identity, bias=cw[:, j, 3:4], scale=cw[:, j, 2:3]),
                         reads=[c_ps[gb], c_cw], writes=[c_acc[q]])
                    S.op('dve', lambda e, q=q, j=j: e.scalar_tensor_tensor(out=acc[q][:], in0=G[q][:, 1:513], scalar=cw[:, j, 1:2], in1=acc[q][:], op0=ALU.mult, op1=ALU.add),
                         reads=[c_G[q], c_acc[q], c_cw], writes=[c_acc[q]])
                    S.op('dve', lambda e, q=q, j=j: e.scalar_tensor_tensor(out=acc[q][:], in0=G[q][:, 0:512], scalar=cw[:, j, 0:1], in1=acc[q][:], op0=ALU.mult, op1=ALU.add),
                         reads=[c_G[q], c_acc[q], c_cw], writes=[c_acc[q]])
                    S.op('dve', lambda e, q=q, j=j: e.tensor_copy(out=halo[:, :, j], in_=G[q][:, 512:514]), reads=[c_G[q]], writes=[c_halo[j]])
                    S.op('act', lambda e, q=q: e.activation(out=acc[q][:], in_=acc[q][:], func=AF.Silu), reads=[c_acc[q]], writes=[c_acc[q]])
                    S.op('dve', lambda e, q=q, ub=ub, j=j: e.tensor_tensor(out=act[:, j, 0:512], in0=acc[q][:], in1=PS(ub), op=ALU.mult),
                         reads=[c_acc[q], c_ps[ub]], writes=[c_act[j]])
                    if last:
                        S.op('dve', lambda e, q=q, j=j: e.tensor_scalar(out=accs[q][:], in0=PS(7)[:, 0:NS], scalar1=cw[:, j, 2:3], scalar2=cw[:, j, 3:4], op0=ALU.mult, op1=ALU.add),
                             reads=[c_ps[7], c_cw], writes=[c_accs[q]])
                        S.op('dve', lambda e, q=q, j=j: e.scalar_tensor_tensor(out=accs[q][:], in0=stS[:, j, :, 1], scalar=cw[:, j, 1:2], in1=accs[q][:], op0=ALU.mult, op1=ALU.add),
                             reads=[c_stS, c_accs[q], c_cw], writes=[c_accs[q]])
                        S.op('dve', lambda e, q=q, j=j: e.scalar_tensor_tensor(out=accs[q][:], in0=stS[:, j, :, 0], scalar=cw[:, j, 0:1], in1=accs[q][:], op0=ALU.mult, op1=ALU.add),
                             reads=[c_stS, c_accs[q], c_cw], writes=[c_accs[q]])
                        S.op('act', lambda e, q=q: e.activation(out=accs[q][:], in_=accs[q][:], func=AF.Silu), reads=[c_accs[q]], writes=[c_accs[q]])
                        S.op('act', lambda e, j=j: e.activation(out=gnew[:, j, :], in_=PS(7)[:, 0:NS], func=AF.Copy), reads=[c_ps[7]], writes=[c_gnew[j]])
                        S.op('dve', lambda e, q=q, j=j: e.tensor_tensor(out=act[:, j, 512:528], in0=accs[q][:], in1=PS(7)[:, 32:32 + NS], op=ALU.mult),
                             reads=[c_accs[q], c_ps[7]], writes=[c_act[j]])
                for mt in range(8):
                    wq = ko % 2
                    ob = 6 + (ko % 2)
                    ko += 1
                    wload(wo[wq][:], I['ffn_w_out'][li, :, mt * 128:(mt + 1) * 128], NJ, c_wo[wq])
                    for j in range(NJ):
                        S.op('pe', lambda e, j=j, wq=wq, ob=ob: e.matmul(PS(ob), lhsT=wo[wq][:, j, :], rhs=act[:, j, 0:512], start=(j == 0), stop=(j == NJ - 1)),
                             reads=[c_wo[wq], c_act[j]], writes=[c_ps[ob]])
                    S.op('dve', lambda e, mt=mt, ob=ob, c0=c0: e.tensor_tensor(out=xres[:, mt, c0:c0 + 512], in0=xres[:, mt, c0:c0 + 512], in1=PS(ob), op=ALU.add),
                         reads=[c_ps[ob], c_x[mt][bi]], writes=[c_x[mt][bi]])
                    if last:
                        for j in range(NJ):
                            S.op('pe', lambda e, j=j, wq=wq: e.matmul(PS(5)[:, 0:NS], lhsT=wo[wq][:, j, :], rhs=act[:, j, 512:528], start=(j == 0), stop=(j == NJ - 1)),
                                 reads=[c_wo[wq], c_act[j]], writes=[c_ps[5]])
                        S.op('dve', lambda e, mt=mt: e.tensor_tensor(out=xres[:, mt, NP:TT], in0=xres[:, mt, NP:TT], in1=PS(5)[:, 0:NS], op=ALU.add),
                             reads=[c_ps[5], c_x[mt][4]], writes=[c_x[mt][4]])
            S.op('pe', lambda e: e.transpose(PS(0)[0:2 * NJ, 0:128], halo[:].rearrange("p s j -> p (s j)"), ident), reads=[c_halo, c_cst], writes=[c_ps[0]])
            S.op('dve', lambda e: e.tensor_copy(out=stg[0:2 * NJ, 0:128], in_=PS(0)[0:2 * NJ, 0:128]), reads=[c_ps[0]], writes=[c_stg])
            S.dma('sp', O['ffn_conv_p'][li, :, :].rearrange("s (j p) -> (s j) p", p=128), stg[0:2 * NJ, 0:128], reads=[c_stg])
            stg2 = stg
            c_stg2 = c_stg
            for j0 in range(0, NJ, 4):
                j1 = min(NJ, j0 + 4)
                for j in range(j0, j1):
                    S.op('pe', lambda e, j=j, j0=j0: e.transpose(PS(1)[0:NS, (j - j0) * 128:(j - j0 + 1) * 128], gnew[:, j, :], ident), reads=[c_gnew[j], c_cst], writes=[c_ps[1]])
                S.op('dve', lambda e, j0=j0, j1=j1: e.tensor_copy(out=stg2[0:NS, j0 * 128:j1 * 128], in_=PS(1)[0:NS, 0:(j1 - j0) * 128]), reads=[c_ps[1]], writes=[c_stg2])
            S.dma('sp', O['ffn_conv_s'][li, :, 1, :], stg2[0:NS, :], reads=[c_stg2])
            S.barrier()

        def final_out():
            A = Arena()
            yf = [A.f32(8 * 512).rearrange("p (a t) -> p a t", a=8) for _ in range(2)]
            c_y = cells(2)
            ost = [A.f32(D), A.f32(D)]
            c_ost = cells(2)
            kk = [0]

            def cb(bi, c0, n, rs, c_r):
                yq = bi % 2
                for ct in range(8):
                    S.op('dve', lambda e, ct=ct, c0=c0, n=n, yq=yq: e.scalar_tensor_tensor(
                        out=yf[yq][:, ct, 0:n], in0=xres[:, ct, c0:c0 + n], scalar=vcol('norm_final', ct, 0), in1=rs[:, 0:n], op0=ALU.mult, op1=ALU.mult),
                        reads=[c_x[ct][bi], c_r, c_vec], writes=[c_y[yq]])
                for t4 in range((n + 127) // 128):
                    rows = min(128, n - t4 * 128)
                    q = kk[0] % 2
                    for half in range(2):
                        b = kk[0] % 2
                        kk[0] += 1
                        for a in range(4):
                            ct = half * 4 + a
                            S.op('pe', lambda e, t4=t4, ct=ct, a=a, b=b, yq=yq, rows=rows: e.transpose(PS(b)[0:rows, a * 128:(a + 1) * 128], yf[yq][:, ct, t4 * 128:t4 * 128 + rows], ident),
                                 reads=[c_y[yq], c_cst], writes=[c_ps[b]])
                        if half == 0:
                            S.op('act', lambda e, q=q, b=b, rows=rows: e.activation(out=ost[q][0:rows, 0:512], in_=PS(b)[0:rows, :], func=AF.Copy), reads=[c_ps[b]], writes=[c_ost[q]])
                        else:
                            S.op('dve', lambda e, q=q, b=b, rows=rows: e.tensor_copy(out=ost[q][0:rows, 512:1024], in_=PS(b)[0:rows, :]), reads=[c_ps[b]], writes=[c_ost[q]])
                    if bi < 4:
                        t0 = c0 + t4 * 128
                        S.dma('sp', O['y_prompt'][t0:t0 + 128, :], ost[q][:], reads=[c_ost[q]])
                    else:
                        S.dma('sp', O['y_sample'][:, :], ost[q][0:NS, :], reads=[c_ost[q]])
            rmsnorm('norm_final', 0, A, block_cb=cb)

        def dump(k):
            if dbg:
                for ct in range(8):
                    S.dma('sp', O['dbg'][k, ct * 128:(ct + 1) * 128, :], xres[:, ct, :], reads=[c_x[ct]])


        def lru_layer(li):
            A = Arena()
            rmsnorm('norm_mix', li, A)
            hg = A.bf16(8 * TT).rearrange("p (a t) -> p a t", a=8)
            c_hg = cells(8, 5)
            wt = [A.bf16(8 * 256).rearrange("p (k c) -> p k c", k=8) for _ in range(2)]
            c_wt = cells(2, 2)
            wg2 = [A.bf16(2 * 2 * 256).rearrange("p (g k c) -> p g k c", g=2, k=2) for _ in range(2)]
            c_wg2 = cells(2)
            Ub = [A.f32(3 + 512) for _ in range(2)]
            c_Ub = cells(2)
            uc = [A.f32(528) for _ in range(2)]
            c_uc = cells(2)
            ucb = [A.bf16(528) for _ in range(2)]
            c_ucb = cells(2)
            NT = 6
            tb = [A.f32(528) for _ in range(NT)]
            c_tb = cells(NT)
            uh = A.f32(24).rearrange("p (k a) -> p k a", k=3)
            c_uh = cells(8)
            hprev = A.f32(8)
            c_hp = cells(8)
            lc = A.f32(16).rearrange("p (a k) -> p a k", a=8)
            c_lc = Cell()
            cs = A.f32(8 * 48).rearrange("p (a b k) -> p a b k", a=8, b=NS)
            c_cs = Cell()
            h0 = A.f32(8 * NS).rearrange("p (a b) -> p a b", a=8)
            c_h0 = Cell()
            hs = A.f32(8 * NS).rearrange("p (a b) -> p a b", a=8)
            c_hs = cells(8)
            us = A.f32(8 * NS).rearrange("p (a b) -> p a b", a=8)
            c_us = cells(8)
            stg = A.f32(D)
            c_stg = Cell()
            onec = A.f32(1)
            c_one = Cell()
            S.op('dve', lambda e: e.memset(onec[:], 1.0), writes=[c_one])
            for ct in range(8):
                S.op('act', lambda e, ct=ct: e.activation(out=lc[:, ct, 0:1], in_=vcol('lru_lambda', ct), func=AF.Exp, scale=-1.0), reads=[c_vec], writes=[c_lc])
            S.op('act', lambda e: e.activation(out=lc[:, :, 0:1], in_=lc[:, :, 0:1], func=AF.Ln, bias=onec[:, 0:1], scale=1.0), reads=[c_lc, c_one], writes=[c_lc])
            S.op('dve', lambda e: e.tensor_scalar(out=lc[:, :, 1:2], in0=lc[:, :, 0:1], scalar1=-16.0, scalar2=None, op0=ALU.mult), reads=[c_lc], writes=[c_lc])
            S.op('dve', lambda e: e.tensor_scalar(out=lc[:, :, 0:1], in0=lc[:, :, 0:1], scalar1=-8.0, scalar2=None, op0=ALU.mult), reads=[c_lc], writes=[c_lc])
            S.dma('sp', stg[0:48, :], I['state_lru_conv'][:, :, :].rearrange("b k c -> (b k) c"), writes=[c_stg])
            for ct in range(8):
                S.op('pe', lambda e, ct=ct: e.transpose(PS(7)[:, ct * 48:(ct + 1) * 48], stg[0:48, ct * 128:(ct + 1) * 128], ident[0:48, 0:48]), reads=[c_stg, c_cst], writes=[c_ps[7]])
            S.op('dve', lambda e: e.tensor_copy(out=cs[:].rearrange("p a b k -> p (a b k)"), in_=PS(7)[:, 0:384]), reads=[c_ps[7]], writes=[c_cs])
            S.dma('sp', stg[0:NS, :], I['state_lru_h'][:, :], reads=[c_ps[7]], writes=[c_stg])
            for ct in range(8):
                S.op('pe', lambda e, ct=ct: e.transpose(PS(7)[:, ct * NS:(ct + 1) * NS], stg[0:NS, ct * 128:(ct + 1) * 128], ident[0:NS, 0:NS]), reads=[c_stg, c_cst], writes=[c_ps[7]])
            S.op('dve', lambda e: e.tensor_copy(out=h0[:].rearrange("p a b -> p (a b)"), in_=PS(7)[:, 0:8 * NS]), reads=[c_ps[7]], writes=[c_h0])
            S.dma('sp', O['lru_conv_s'][:, 0:2, :], I['state_lru_conv'][:, 1:3, :])

            kq = 0
            kt_ = 0
            for n in range(4):
                g2 = n % 2
                wload(wg2[g2][:, 0, :, :], I['lru_w_rg'][n, :, :], 2, c_wg2[g2])
                wload(wg2[g2][:, 1, :, :], I['lru_w_ig'][n, :, :], 2, c_wg2[g2])
                for bi in range(4):
                    c0 = bi * 512
                    last = (bi == 3)
                    nn = 528 if last else 512
                    for c2 in range(2):
                        ct = 2 * n + c2
                        wq = kq % 2
                        kq += 1
                        wload(wt[wq][:, :, 0:128], I['lru_w_in'][:, D + ct * 128:D + (ct + 1) * 128], 8, c_wt[wq][0])
                        wload(wt[wq][:, :, 128:256], I['lru_w_in'][:, ct * 128:(ct + 1) * 128], 8, c_wt[wq][1])
                        ub = c2
                        gb = 2 + c2
                        for kt in range(8):
                            S.op('pe', lambda e, kt=kt, wq=wq, ub=ub, c0=c0: e.matmul(PS(ub), lhsT=wt[wq][:, kt, 0:128], rhs=hb[:, kt, c0:c0 + 512], start=(kt == 0), stop=(kt == 7)),
                                 reads=[c_wt[wq][0], c_h[kt][bi]], writes=[c_ps[ub]])
                        for kt in range(8):
                            S.op('pe', lambda e, kt=kt, wq=wq, gb=gb, c0=c0: e.matmul(PS(gb), lhsT=wt[wq][:, kt, 128:256], rhs=hb[:, kt, c0:c0 + 512], start=(kt == 0), stop=(kt == 7)),
                                 reads=[c_wt[wq][1], c_h[kt][bi]], writes=[c_ps[gb]])
                        if last:
                            sb_ = 4
                            for kt in range(8):
                                S.op('pe', lambda e, kt=kt, wq=wq, c2=c2: e.matmul(PS(4)[:, c2 * 64:c2 * 64 + NS], lhsT=wt[wq][:, kt, 0:128], rhs=hb[:, kt, NP:TT], start=(kt == 0), stop=(kt == 7)),
                                     reads=[c_wt[wq][0], c_h[kt][4]], writes=[c_ps[4]])
                            for kt in range(8):
                                S.op('pe', lambda e, kt=kt, wq=wq, c2=c2: e.matmul(PS(4)[:, c2 * 64 + 32:c2 * 64 + 32 + NS], lhsT=wt[wq][:, kt, 128:256], rhs=hb[:, kt, NP:TT], start=(kt == 0), stop=(kt == 7)),
                                     reads=[c_wt[wq][1], c_h[kt][4]], writes=[c_ps[4]])
                        S.op('act', lambda e, c2=c2, ub=ub: e.activation(out=Ub[c2][:, 3:515], in_=PS(ub), func=AF.Copy), reads=[c_ps[ub]], writes=[c_Ub[c2]])
                        if bi == 0:
                            S.op('dve', lambda e, c2=c2: e.memset(Ub[c2][:, 0:3], 0.0), writes=[c_Ub[c2]])
                        else:
                            S.op('dve', lambda e, c2=c2, ct=ct: e.tensor_copy(out=Ub[c2][:, 0:3], in_=uh[:, :, ct]), reads=[c_uh[ct]], writes=[c_Ub[c2]])
                        S.op('dve', lambda e, c2=c2, ub=ub, ct=ct: e.tensor_scalar(out=uc[c2][:, 0:512], in0=PS(ub), scalar1=vcol('lru_conv_w', ct, 3), scalar2=vcol('lru_conv_b', ct), op0=ALU.mult, op1=ALU.add),
                             reads=[c_ps[ub], c_vec], writes=[c_uc[c2]])
                        for k in range(3):
                            S.op('dve', lambda e, c2=c2, ct=ct, k=k: e.scalar_tensor_tensor(out=uc[c2][:, 0:512], in0=Ub[c2][:, k:k + 512], scalar=vcol('lru_conv_w', ct, k), in1=uc[c2][:, 0:512], op0=ALU.mult, op1=ALU.add),
                                 reads=[c_Ub[c2], c_uc[c2], c_vec], writes=[c_uc[c2]])
                        S.op('dve', lambda e, c2=c2, ct=ct: e.tensor_copy(out=uh[:, :, ct], in_=Ub[c2][:, 512:515]), reads=[c_Ub[c2]], writes=[c_uh[ct]])
                        if last:
                            o4 = c2 * 64
                            S.op('dve', lambda e, c2=c2, ct=ct, o4=o4: e.tensor_scalar(out=uc[c2][:, 512:528], in0=PS(4)[:, o4:o4 + NS], scalar1=vcol('lru_conv_w', ct, 3), scalar2=vcol('lru_conv_b', ct), op0=ALU.mult, op1=ALU.add),
                                 reads=[c_ps[4], c_vec], writes=[c_uc[c2]])
                            for k in range(3):
                                S.op('dve', lambda e, c2=c2, ct=ct, k=k: e.scalar_tensor_tensor(out=uc[c2][:, 512:528], in0=cs[:, ct, :, k], scalar=vcol('lru_conv_w', ct, k), in1=uc[c2][:, 512:528], op0=ALU.mult, op1=ALU.add),
                                     reads=[c_cs, c_uc[c2], c_vec], writes=[c_uc[c2]])
                            S.op('act', lambda e, ct=ct, o4=o4: e.activation(out=us[:, ct, :], in_=PS(4)[:, o4:o4 + NS], func=AF.Copy), reads=[c_ps[4]], writes=[c_us[ct]])
                        S.op('act', lambda e, c2=c2, nn=nn: e.activation(out=ucb[c2][:, 0:nn], in_=uc[c2][:, 0:nn], func=AF.Copy), reads=[c_uc[c2]], writes=[c_ucb[c2]])
                    for c2 in range(2):
                        ct = 2 * n + c2
                        rb = 5
                        ib = 6
                        T = [tb[(kt_ + i) % NT] for i in range(3)]
                        cT = [c_tb[(kt_ + i) % NT] for i in range(3)]
                        kt_ += 3
                        segs = [(0, 512)] + ([(512, NS)] if last else [])
                        for (s0, sn) in segs:
                            for k2 in range(2):
                                S.op('pe', lambda e, k2=k2, c2=c2, g2=g2, s0=s0, sn=sn: e.matmul(PS(5)[:, 0:sn], lhsT=wg2[g2][:, 0, k2, c2 * 128:(c2 + 1) * 128], rhs=ucb[k2][:, s0:s0 + sn], start=(k2 == 0), stop=(k2 == 1)),
                                     reads=[c_wg2[g2], c_ucb[k2]], writes=[c_ps[5]])
                            for k2 in range(2):
                                S.op('pe', lambda e, k2=k2, c2=c2, g2=g2, s0=s0, sn=sn: e.matmul(PS(6)[:, 0:sn], lhsT=wg2[g2][:, 1, k2, c2 * 128:(c2 + 1) * 128], rhs=ucb[k2][:, s0:s0 + sn], start=(k2 == 0), stop=(k2 == 1)),
                                     reads=[c_wg2[g2], c_ucb[k2]], writes=[c_ps[6]])
                            S.op('act', lambda e, ct=ct, s0=s0, sn=sn, T=T: e.activation(out=T[0][:, s0:s0 + sn], in_=PS(5)[:, 0:sn], func=AF.Sigmoid, bias=vcol('lru_b_rg', ct), scale=1.0), reads=[c_ps[5], c_vec], writes=[cT[0]])
                            S.op('act', lambda e, ct=ct, s0=s0, sn=sn, T=T: e.activation(out=T[1][:, s0:s0 + sn], in_=PS(6)[:, 0:sn], func=AF.Sigmoid, bias=vcol('lru_b_ig', ct), scale=1.0), reads=[c_ps[6], c_vec], writes=[cT[1]])
                        S.op('act', lambda e, ct=ct, nn=nn, T=T: e.activation(out=T[2][:, 0:nn], in_=T[0][:, 0:nn], func=AF.Exp, scale=lc[:, ct, 1:2]), reads=[cT[0], c_lc], writes=[cT[2]])
                        S.op('act', lambda e, ct=ct, nn=nn, T=T: e.activation(out=T[0][:, 0:nn], in_=T[0][:, 0:nn], func=AF.Exp, scale=lc[:, ct, 0:1]), reads=[cT[0], c_lc], writes=[cT[0]])
                        S.op('act', lambda e, nn=nn, T=T: e.activation(out=T[2][:, 0:nn], in_=T[2][:, 0:nn], func=AF.Sqrt, bias=onec[:, 0:1], scale=-1.0), reads=[cT[2], c_one], writes=[cT[2]])
                        S.op('dve', lambda e, nn=nn, T=T: e.tensor_tensor(out=T[1][:, 0:nn], in0=T[1][:, 0:nn], in1=T[2][:, 0:nn], op=ALU.mult), reads=[cT[1], cT[2]], writes=[cT[1]])
                        S.op('dve', lambda e, nn=nn, T=T, c2=c2: e.tensor_tensor(out=T[1][:, 0:nn], in0=T[1][:, 0:nn], in1=uc[c2][:, 0:nn], op=ALU.mult), reads=[cT[1], c_uc[c2]], writes=[cT[1]])
                        if bi == 0:
                            S.op('dve', lambda e, T=T: e.tensor_tensor_scan(out=T[2][:, 0:512], data0=T[0][:, 0:512], data1=T[1][:, 0:512], initial=0.0, op0=ALU.mult, op1=ALU.add),
                                 reads=[cT[0], cT[1]], writes=[cT[2]])
                        else:
                            S.op('dve', lambda e, T=T, ct=ct: e.tensor_tensor_scan(out=T[2][:, 0:512], data0=T[0][:, 0:512], data1=T[1][:, 0:512], initial=hprev[:, ct:ct + 1], op0=ALU.mult, op1=ALU.add),
                                 reads=[cT[0], cT[1], c_hp[ct]], writes=[cT[2]])
                        S.op('dve', lambda e, T=T, ct=ct: e.tensor_copy(out=hprev[:, ct:ct + 1], in_=T[2][:, 511:512]), reads=[cT[2]], writes=[c_hp[ct]])
                        if last:
                            S.op('dve', lambda e, T=T, ct=ct: e.tensor_tensor(out=T[2][:, 512:528], in0=T[0][:, 512:528], in1=h0[:, ct, :], op=ALU.mult), reads=[cT[0], c_h0], writes=[cT[2]])
                            S.op('dve', lambda e, T=T: e.tensor_tensor(out=T[2][:, 512:528], in0=T[2][:, 512:528], in1=T[1][:, 512:528], op=ALU.add), reads=[cT[1], cT[2]], writes=[cT[2]])
                            S.op('act', lambda e, T=T, ct=ct: e.activation(out=hs[:, ct, :], in_=T[2][:, 512:528], func=AF.Copy), reads=[cT[2]], writes=[c_hs[ct]])
                        gb = 2 + c2
                        S.op('act', lambda e, T=T, gb=gb: e.activation(out=T[0][:, 0:512], in_=PS(gb), func=AF.Gelu_apprx_tanh), reads=[c_ps[gb]], writes=[cT[0]])
                        if last:
                            o4 = c2 * 64 + 32
                            S.op('act', lambda e, T=T, o4=o4: e.activation(out=T[0][:, 512:528], in_=PS(4)[:, o4:o4 + NS], func=AF.Gelu_apprx_tanh), reads=[c_ps[4]], writes=[cT[0]])
                        S.op('dve', lambda e, T=T, ct=ct, c0=c0: e.tensor_tensor(out=hg[:, ct, c0:c0 + 512], in0=T[2][:, 0:512], in1=T[0][:, 0:512], op=ALU.mult), reads=[cT[0], cT[2]], writes=[c_hg[ct][bi]])
                        if last:
                            S.op('dve', lambda e, T=T, ct=ct: e.tensor_tensor(out=hg[:, ct, NP:TT], in0=T[2][:, 512:528], in1=T[0][:, 512:528], op=ALU.mult), reads=[cT[0], cT[2]], writes=[c_hg[ct][4]])
            proj_residual(hg, c_hg, I['lru_w_out'], wt, c_wt)
            S.op('pe', lambda e: e.transpose(PS(0)[0:8, 0:128], hprev[:, 0:8], ident), reads=[c_hp, c_cst], writes=[c_ps[0]])
            S.op('dve', lambda e: e.tensor_copy(out=stg[0:8, 0:128], in_=PS(0)[0:8, 0:128]), reads=[c_ps[0]], writes=[c_stg])
            S.dma('sp', O['lru_h_p'][:, :].rearrange("o (a p) -> (o a) p", p=128), stg[0:8, 0:128], reads=[c_stg])
            S.op('pe', lambda e: e.transpose(PS(1)[0:24, 0:128], uh[:].rearrange("p k a -> p (k a)"), ident), reads=[c_uh, c_cst], writes=[c_ps[1]])
            S.op('dve', lambda e: e.tensor_copy(out=stg[32:56, 0:128], in_=PS(1)[0:24, 0:128]), reads=[c_ps[1]], writes=[c_stg])
            S.dma('sp', O['lru_conv_p'][:, :].rearrange("k (a p) -> (k a) p", p=128), stg[32:56, 0:128], reads=[c_stg])
            fm_to_rows(hs, c_hs, O['lru_h_s'][:, :], stg, c_stg, 64)
            fm_to_rows(us, c_us, O['lru_conv_s'][:, 2, :], stg, c_stg, 96)
            S.barrier()

        def fm_to_rows(src, c_src, dst_ap, stg, c_stg, prow):
            for half in range(2):
                pb = 2 + half
                for a in range(4):
                    ct = half * 4 + a
                    S.op('pe', lambda e, ct=ct, a=a, pb=pb: e.transpose(PS(pb)[0:NS, a * 128:(a + 1) * 128], src[:, ct, :], ident), reads=[c_src[ct], c_cst], writes=[c_ps[pb]])
                S.op('dve', lambda e, half=half, pb=pb: e.tensor_copy(out=stg[prow:prow + NS, half * 512:(half + 1) * 512], in_=PS(pb)[0:NS, :]), reads=[c_ps[pb]], writes=[c_stg])
            S.dma('sp', dst_ap, stg[prow:prow + NS, :], reads=[c_stg])

        def proj_residual(src, c_src, w_ap, wbuf, c_wbuf):
            ko = 0
            for bi in range(4):
                c0 = bi * 512
                last = (bi == 3)
                for mt in range(8):
                    wq = ko % 2
                    ob = 5 + (ko % 2)
                    ko += 1
                    wload(wbuf[wq][:, :, 0:128], w_ap[:, mt * 128:(mt + 1) * 128], 8, c_wbuf[wq])
                    for kt in range(8):
                        S.op('pe', lambda e, kt=kt, wq=wq, ob=ob, c0=c0: e.matmul(PS(ob), lhsT=wbuf[wq][:, kt, 0:128], rhs=src[:, kt, c0:c0 + 512], start=(kt == 0), stop=(kt == 7)),
                             reads=[c_wbuf[wq], c_src[kt][bi]], writes=[c_ps[ob]])
                    S.op('dve', lambda e, mt=mt, ob=ob, c0=c0: e.tensor_tensor(out=xres[:, mt, c0:c0 + 512], in0=xres[:, mt, c0:c0 + 512], in1=PS(ob), op=ALU.add),
                         reads=[c_ps[ob], c_x[mt][bi]], writes=[c_x[mt][bi]])
                    if last:
                        for kt in range(8):
                            S.op('pe', lambda e, kt=kt, wq=wq: e.matmul(PS(7)[:, 0:NS], lhsT=wbuf[wq][:, kt, 0:128], rhs=src[:, kt, NP:TT], start=(kt == 0), stop=(kt == 7)),
                                 reads=[c_wbuf[wq], c_src[kt][4]], writes=[c_ps[7]])
                        S.op('dve', lambda e, mt=mt: e.tensor_tensor(out=xres[:, mt, NP:TT], in0=xres[:, mt, NP:TT], in1=PS(7)[:, 0:NS], op=ALU.add),
                             reads=[c_ps[7], c_x[mt][4]], writes=[c_x[mt][4]])


        def s5_layer(li, j):
            A = Arena()
            R = A.f32(4096)
            c_R = cells(4)
            Rr = [R[:, i * 1024:(i + 1) * 1024] for i in range(4)]
            A0 = Arena()
            A0.off = 0
            rmsnorm('norm_mix', li, A0)
            S.barrier()
            yg = A.bf16(8 * TT).rearrange("p (a t) -> p a t", a=8)
            c_yg = cells(8, 5)
            stg = A.f32(512)
            c_stg = Cell()
            prm = A.f32(96).rearrange("p (k s) -> p k s", k=3)
            c_prm = Cell()
            NSM = 22
            sm = A.f32(NSM * 32).rearrange("p (k s) -> p k s", k=NSM)
            c_sm = Cell()
            Apw = A.f32(9 * 2 * 32).rearrange("p (d k s) -> p d k s", d=9, k=2)
            cks = A.f32(8 * 2 * 32).rearrange("p (l k s) -> p l k s", l=8, k=2)
            Apn = A.f32(9 * 32).rearrange("p (d s) -> p d s", d=9)
            Bst = R[:, 2048:3072].rearrange("p (k s c) -> p k s c", k=2, s=32)
            c_Bst = c_R[2]
            c_Bbar = Cell()
            Bbar = A.f32(2 * 512).rearrange("p (k s c) -> p k s c", k=2, s=32)
            Cn = [A.f32(2 * 64).rearrange("p (k q) -> p k q", k=2) for _ in range(2)]
            c_Cn = cells(2)
            SmT = A.f32(2 * 2 * 32 * 16).rearrange("p (t k s n) -> p t k s n", t=2, k=2, s=32)
            c_SmT = Cell()
            Hfin = A.f32(64).rearrange("p (k s) -> p k s", k=2)
            c_Hfin = Cell()
            CCe = A.f32(256).rearrange("p (k c) -> p k c", k=2)
            c_CCe = Cell()
            CCt = A.f32(256).rearrange("p (k q c) -> p k q c", k=2, q=4)
            c_CCt = Cell()
            CA = A.bf16(4 * 9 * 2 * 32).rearrange("p (q d k c) -> p q d k c", q=4, d=9, k=2)
            c_CA = Cell()
            Wz = A.f32(2 * 128).rearrange("p (k q g c) -> p k q g c", k=2, q=4, g=2)
            c_Wz = cells(2)
            W1 = A.bf16(8 * 2 * 128).rearrange("p (s k c) -> p s k c", s=8, k=2)
            c_W1 = Cell()
            ZBb = A.bf16(2 * 128).rearrange("p (k c) -> p k c", k=2)
            c_ZBb = Cell()
            KTf = A.bf16(8 * 128).rearrange("p (d q c) -> p d q c", d=8, q=4)
            c_KTf = Cell()
            E = A.f32(2 * 1024).rearrange("p (k q n) -> p k q n", k=2, q=4)
            c_E = Cell()
            Et = R[:, 2048:4096].rearrange("p (i q n) -> p i q n", i=4, q=4)
            c_Et = [c_R[2], c_R[3]]
            Hb = A.bf16(2 * 4 * 258).rearrange("p (k q n) -> p k q n", k=2, q=4)
            c_Hb = Cell()
            h0 = A.f32(2 * 64).rearrange("p (k q b) -> p k q b", k=2, q=4)
            c_h0 = Cell()
            hn = A.f32(2 * 64).rearrange("p (k q b) -> p k q b", k=2, q=4)
            c_hn = Cell()
            hnb = A.bf16(2 * 64).rearrange("p (k q b) -> p k q b", k=2, q=4)
            c_hnb = Cell()
            ts_ = A.f32(4 * 64).rearrange("p (i q b) -> p i q b", i=4, q=4)
            c_ts = Cell()
            ysv = A.f32(NS)
            c_ysv = Cell()
            st2 = A.f32(2 * 512).rearrange("p (k c) -> p k c", k=2)
            c_st2 = Cell()
            wgl = [R[:, i * 1024:(i + 1) * 1024].bitcast(BF16).rearrange("p (k c) -> p k c", k=8) for i in range(2)]
            c_wgl = [c_R[0], c_R[1]]
            sg = [R[:, 2048:2576], R[:, 3072:3600]]
            c_sg = [c_R[2], c_R[3]]

            SMI = [0]

            def smn():
                SMI[0] += 1
                assert SMI[0] <= NSM
                return sm[:, SMI[0] - 1, :]

            def dv(fn, eng='dve'):
                S.op(eng, fn, reads=[c_sm, c_prm], writes=[c_sm])

            S.dma('sp', stg[0:32, 0:128], I['s5_a_re'][j, :, :].rearrange("(s g) p -> s (g p)", g=2), writes=[c_stg])
            S.dma('sp', stg[0:32, 128:256], I['s5_a_im'][j, :, :].rearrange("(s g) p -> s (g p)", g=2), writes=[c_stg])
            S.dma('sp', stg[0:32, 256:258], I['s5_log_dt'][j:j + 1, :].rearrange("o (s g) -> (o s) g", g=2), writes=[c_stg])
            S.op('dve', lambda e: e.tensor_copy(out=stg[0:32, 384:512].rearrange("p (g n) -> p g n", g=2), in_=stg[0:32, 256:258].unsqueeze(2).to_broadcast([32, 2, 64])), reads=[c_stg], writes=[c_stg])
            for k, c0 in enumerate((0, 128, 384)):
                S.op('pe', lambda e, k=k, c0=c0: e.transpose(PS(7)[:, k * 32:(k + 1) * 32], stg[0:32, c0:c0 + 128], ident[0:32, 0:32]), reads=[c_stg, c_cst], writes=[c_ps[7]])
            S.op('dve', lambda e: e.tensor_copy(out=prm[:].rearrange("p k s -> p (k s)"), in_=PS(7)[:, 0:96]), reads=[c_ps[7]], writes=[c_prm])
            are, aim, ldt = prm[:, 0, :], prm[:, 1, :], prm[:, 2, :]
            dtb, lre, lrd, ang, mag, y_, nf, f_, m_, sinv, cosv, abr, abi, den, t1_, t2_, cre, cim, m8, rm8 = [smn() for _ in range(20)]
            ni = A.f32(32).bitcast(I32)
            TS = lambda e, out, in0, s1, op0, s2=None, op1=None: e.tensor_scalar(out=out, in0=in0, scalar1=s1, scalar2=s2, op0=op0, **({'op1': op1} if op1 is not None else {}))
            TTn = lambda e, out, a, b, op: e.tensor_tensor(out=out, in0=a, in1=b, op=op)
            dv(lambda e: e.activation(out=dtb, in_=ldt, func=AF.Exp), 'act')
            dv(lambda e: TS(e, lre, are, -1e-4, ALU.min))
            dv(lambda e: TTn(e, lrd, lre, dtb, ALU.mult))
            dv(lambda e: TTn(e, ang, aim, dtb, ALU.mult))
            dv(lambda e: e.activation(out=mag, in_=lrd, func=AF.Exp), 'act')
            dv(lambda e: e.activation(out=m8, in_=lrd, func=AF.Exp, scale=8.0), 'act')
            dv(lambda e: TS(e, y_, ang, 1.0 / (2 * math.pi), ALU.mult))
            S.op('dve', lambda e: e.tensor_copy(out=ni, in_=y_), reads=[c_sm], writes=[c_sm])
            S.op('dve', lambda e: e.tensor_copy(out=nf, in_=ni), reads=[c_sm], writes=[c_sm])
            dv(lambda e: TTn(e, f_, y_, nf, ALU.subtract))

            def wrap(x):
                dv(lambda e: TS(e, m_, x, 0.5, ALU.is_gt))
                dv(lambda e: TTn(e, x, x, m_, ALU.subtract))
                dv(lambda e: TS(e, m_, x, -0.5, ALU.is_lt))
                dv(lambda e: TTn(e, x, x, m_, ALU.add))
            wrap(f_)
            dv(lambda e: e.activation(out=sinv, in_=f_, func=AF.Sin, scale=2 * math.pi), 'act')
            dv(lambda e: TS(e, f_, f_, 0.25, ALU.add))
            wrap(f_)
            dv(lambda e: e.activation(out=cosv, in_=f_, func=AF.Sin, scale=2 * math.pi), 'act')
            dv(lambda e: TTn(e, abr, mag, cosv, ALU.mult))
            dv(lambda e: TTn(e, abi, mag, sinv, ALU.mult))
            dv(lambda e: TTn(e, den, lre, lre, ALU.mult))
            dv(lambda e: TTn(e, t1_, aim, aim, ALU.mult))
            dv(lambda e: TTn(e, den, den, t1_, ALU.add))
            dv(lambda e: e.reciprocal(out=den, in_=den))
            dv(lambda e: TS(e, t1_, abr, -1.0, ALU.add))
            dv(lambda e: TTn(e, cre, t1_, lre, ALU.mult))
            dv(lambda e: TTn(e, t2_, abi, aim, ALU.mult))
            dv(lambda e: TTn(e, cre, cre, t2_, ALU.add))
            dv(lambda e: TTn(e, cre, cre, den, ALU.mult))
            dv(lambda e: TTn(e, cim, abi, lre, ALU.mult))
            dv(lambda e: TTn(e, t2_, t1_, aim, ALU.mult))
            dv(lambda e: TTn(e, cim, cim, t2_, ALU.subtract))
            dv(lambda e: TTn(e, cim, cim, den, ALU.mult))
            dv(lambda e: e.memset(Apw[:, 0, 0, :], 1.0))
            dv(lambda e: e.memset(Apw[:, 0, 1, :], 0.0))
            for d in range(8):
                pr, pi = Apw[:, d, 0, :], Apw[:, d, 1, :]
                qr, qi = Apw[:, d + 1, 0, :], Apw[:, d + 1, 1, :]
                dv(lambda e, pr=pr, qr=qr: TTn(e, qr, pr, abr, ALU.mult))
                dv(lambda e, pi=pi: TTn(e, t2_, pi, abi, ALU.mult))
                dv(lambda e, qr=qr: TTn(e, qr, qr, t2_, ALU.subtract))
                dv(lambda e, pr=pr, qi=qi: TTn(e, qi, pr, abi, ALU.mult))
                dv(lambda e, pi=pi: TTn(e, t2_, pi, abr, ALU.mult))
                dv(lambda e, qi=qi: TTn(e, qi, qi, t2_, ALU.add))
            dv(lambda e: TS(e, Apn[:], Apw[:, :, 1, :], -1.0, ALU.mult))
            dv(lambda e: e.reciprocal(out=rm8, in_=m8))
            dv(lambda e: TTn(e, cks[:, 0, 0, :], Apw[:, 8, 0, :], rm8, ALU.mult))
            dv(lambda e: TTn(e, cks[:, 0, 1, :], Apw[:, 8, 1, :], rm8, ALU.mult))
            dv(lambda e: TS(e, cks[:, 0, 1, :], cks[:, 0, 1, :], -1.0, ALU.mult))
            for l in range(7):
                c0_, s0_ = cks[:, l, 0, :], cks[:, l, 1, :]
                c1_, s1_ = cks[:, l + 1, 0, :], cks[:, l + 1, 1, :]
                dv(lambda e, c0_=c0_, c1_=c1_: TTn(e, c1_, c0_, c0_, ALU.mult))
                dv(lambda e, s0_=s0_: TTn(e, t2_, s0_, s0_, ALU.mult))
                dv(lambda e, c1_=c1_: TTn(e, c1_, c1_, t2_, ALU.subtract))
                dv(lambda e, c0_=c0_, s0_=s0_, s1_=s1_: TTn(e, s1_, c0_, s0_, ALU.mult))
                dv(lambda e, s1_=s1_: TS(e, s1_, s1_, 2.0, ALU.mult))
            for k, nm in enumerate(('s5_b_re', 's5_b_im')):
                S.dma('sp', Bst[:, k, :, :], I[nm][j, :, :, :].rearrange("(s g) p c -> (g p) s c", g=2), writes=[c_Bst])
            bc = lambda x: x.unsqueeze(2).to_broadcast([128, 32, 16])
            Bt0 = R[:, 0:512].rearrange("p (s c) -> p s c", s=32)
            Bt1 = R[:, 512:1024].rearrange("p (s c) -> p s c", s=32)
            S.op('dve', lambda e: TTn(e, Bt0, Bst[:, 0, :, :], bc(cre), ALU.mult), reads=[c_Bst, c_sm], writes=[c_R[0]])
            S.op('dve', lambda e: TTn(e, Bt1, Bst[:, 1, :, :], bc(cim), ALU.mult), reads=[c_Bst, c_sm], writes=[c_R[0]])
            S.op('dve', lambda e: TTn(e, Bbar[:, 0, :, :], Bt0, Bt1, ALU.subtract), reads=[c_R[0]], writes=[c_Bbar])
            S.op('dve', lambda e: TTn(e, Bt0, Bst[:, 1, :, :], bc(cre), ALU.mult), reads=[c_Bst, c_sm], writes=[c_R[0]])
            S.op('dve', lambda e: TTn(e, Bt1, Bst[:, 0, :, :], bc(cim), ALU.mult), reads=[c_Bst, c_sm], writes=[c_R[0]])
            S.op('dve', lambda e: TTn(e, Bbar[:, 1, :, :], Bt0, Bt1, ALU.add), reads=[c_R[0]], writes=[c_Bbar])
            def load_C(ct):
                for k, nm in enumerate(('s5_c_re', 's5_c_im')):
                    S.dma('sp', Cn[ct % 2][:, k, :], I[nm][j, 8 * ct:8 * ct + 8, :, :].rearrange("g c p -> (g c) p"), writes=[c_Cn[ct % 2]])
            load_C(0)
            S.op('pool', lambda e: e.memset(SmT[:, :, 0, :, 0:1], 1.0), writes=[c_SmT])
            S.op('pool', lambda e: e.memset(SmT[:, :, 1, :, 0:1], 0.0), writes=[c_SmT])
            stt = R[:, 2048:4096].rearrange("p (i s n) -> p i s n", i=4, s=32)
            for t in range(2):
                for l in range(4):
                    n_ = 1 << l
                    ckl = cks[:, 4 * t + l, 0, :].unsqueeze(2).to_broadcast([128, 32, n_])
                    skl = cks[:, 4 * t + l, 1, :].unsqueeze(2).to_broadcast([128, 32, n_])
                    lo = slice(0, n_)
                    hi = slice(n_, 2 * n_)
                    S.op('pool', lambda e, t=t, ckl=ckl, lo=lo, n_=n_: TTn(e, stt[:, 0, :, 0:n_], SmT[:, t, 0, :, lo], ckl, ALU.mult), reads=[c_SmT, c_sm], writes=[c_R[2], c_R[3]])
                    S.op('pool', lambda e, t=t, skl=skl, lo=lo, n_=n_: TTn(e, stt[:, 1, :, 0:n_], SmT[:, t, 1, :, lo], skl, ALU.mult), reads=[c_SmT, c_sm], writes=[c_R[2], c_R[3]])
                    S.op('pool', lambda e, t=t, skl=skl, lo=lo, n_=n_: TTn(e, stt[:, 2, :, 0:n_], SmT[:, t, 0, :, lo], skl, ALU.mult), reads=[c_SmT, c_sm], writes=[c_R[2], c_R[3]])
                    S.op('pool', lambda e, t=t, ckl=ckl, lo=lo, n_=n_: TTn(e, stt[:, 3, :, 0:n_], SmT[:, t, 1, :, lo], ckl, ALU.mult), reads=[c_SmT, c_sm], writes=[c_R[2], c_R[3]])
                    S.op('pool', lambda e, t=t, hi=hi, n_=n_: TTn(e, SmT[:, t, 0, :, hi], stt[:, 0, :, 0:n_], stt[:, 1, :, 0:n_], ALU.subtract), reads=[c_R[2], c_R[3]], writes=[c_SmT])
                    S.op('pool', lambda e, t=t, hi=hi, n_=n_: TTn(e, SmT[:, t, 1, :, hi], stt[:, 2, :, 0:n_], stt[:, 3, :, 0:n_], ALU.add), reads=[c_R[2], c_R[3]], writes=[c_SmT])
            mask_hi = cc('mask_hi')
            mask_c16 = cc('mask_c16')
            maskq = cc('maskq')
            Dname = 's5_d'

            def do_ct(ct):
                if S5_STAGE < 2:
                    return
                Asl = lambda d, k: Apw[:, d, k, 4 * ct:4 * ct + 4]
                for k in range(2):
                    S.op('pool', lambda e, k=k: e.tensor_tensor(out=CCe[:, k, :].rearrange("p (g n) -> p g n", g=2), in0=Cn[ct % 2][:, k, :].unsqueeze(1).to_broadcast([128, 2, 64]),
                                                                in1=mask_c16.unsqueeze(2).to_broadcast([128, 2, 64]), op=ALU.mult), reads=[c_Cn[ct % 2], c_cst], writes=[c_CCe])
                    S.op('pe', lambda e, k=k: e.transpose(PS(6)[:, k * 128:(k + 1) * 128], CCe[:, k, :], ident), reads=[c_CCe, c_cst], writes=[c_ps[6]])
                S.op('act', lambda e: e.activation(out=CCt[:].rearrange("p k q c -> p (k q c)"), in_=PS(6)[:, 0:256], func=AF.Copy), reads=[c_ps[6]], writes=[c_CCt])
                if ct < 7:
                    load_C(ct + 1)
                for (d0, d1) in ((0, 5), (5, 9)):
                    nd = d1 - d0
                    t0 = R[:, 2048:2048 + 128 * nd].rearrange("p (q d c) -> p q d c", q=4, d=nd)
                    t1 = R[:, 3072:3072 + 128 * nd].rearrange("p (q d c) -> p q d c", q=4, d=nd)
                    Cb = lambda k, nd=nd: CCt[:, k, :, :].unsqueeze(2).to_broadcast([128, 4, nd, 32])
                    Ab = lambda k, nd=nd, d0=d0, d1=d1: Apw[:, d0:d1, k, 4 * ct:4 * ct + 4].rearrange("p d q -> p q d").unsqueeze(3).to_broadcast([128, 4, nd, 32])
                    S.op('pool', lambda e, t0=t0, Cb=Cb, Ab=Ab: TTn(e, t0, Cb(0), Ab(0), ALU.mult), reads=[c_CCt, c_sm], writes=[c_R[2]])
                    S.op('pool', lambda e, t1=t1, Cb=Cb, Ab=Ab: TTn(e, t1, Cb(1), Ab(1), ALU.mult), reads=[c_CCt, c_sm], writes=[c_R[3]])
                    S.op('dve', lambda e, t0=t0, t1=t1, d0=d0, d1=d1: TTn(e, CA[:, :, d0:d1, 0, :], t0, t1, ALU.subtract), reads=[c_R[2], c_R[3]], writes=[c_CA])
                    An = lambda nd=nd, d0=d0, d1=d1: Apn[:, d0:d1, 4 * ct:4 * ct + 4].rearrange("p d q -> p q d").unsqueeze(3).to_broadcast([128, 4, nd, 32])
                    S.op('pool', lambda e, t0=t0, Cb=Cb, An=An: TTn(e, t0, Cb(0), An(), ALU.mult), reads=[c_CCt, c_sm], writes=[c_R[2]])
                    S.op('pool', lambda e, t1=t1, Cb=Cb, Ab=Ab: TTn(e, t1, Cb(1), Ab(0), ALU.mult), reads=[c_CCt, c_sm], writes=[c_R[3]])
                    S.op('dve', lambda e, t0=t0, t1=t1, d0=d0, d1=d1: TTn(e, CA[:, :, d0:d1, 1, :], t0, t1, ALU.subtract), reads=[c_R[2], c_R[3]], writes=[c_CA])
                u_ = [R[:, 2048 + 512 * i:2048 + 512 * (i + 1)].rearrange("p (d q c) -> p d q c", d=8, q=4) for i in range(4)]
                Bb_ = lambda k: Bbar[:, k, 4 * ct:4 * ct + 4, :].unsqueeze(1).to_broadcast([128, 8, 4, 16])
                Ad_ = lambda k: Apw[:, 0:8, k, 4 * ct:4 * ct + 4].unsqueeze(3).to_broadcast([128, 8, 4, 16])
                S.op('pool', lambda e: TTn(e, u_[0], Bb_(0), Ad_(0), ALU.mult), reads=[c_Bbar, c_sm], writes=[c_R[2]])
                S.op('pool', lambda e: TTn(e, u_[1], Bb_(1), Ad_(1), ALU.mult), reads=[c_Bbar, c_sm], writes=[c_R[2]])
                S.op('pool', lambda e: TTn(e, u_[2], Bb_(0), Ad_(1), ALU.mult), reads=[c_Bbar, c_sm], writes=[c_R[3]])
                S.op('pool', lambda e: TTn(e, u_[3], Bb_(1), Ad_(0), ALU.mult), reads=[c_Bbar, c_sm], writes=[c_R[3]])
                S.op('dve', lambda e: TTn(e, u_[0], u_[0], u_[1], ALU.subtract), reads=[c_R[2]], writes=[c_R[2]])
                S.op('dve', lambda e: TTn(e, u_[2], u_[2], u_[3], ALU.add), reads=[c_R[3]], writes=[c_R[3]])
                for s_ in range(8):
                    d = 7 - s_
                    for k in range(2):
                        S.op('dve', lambda e, k=k, d=d: e.tensor_tensor(out=Wz[:, k, :, :, :], in0=u_[2 * k][:, d, :, :].unsqueeze(2).to_broadcast([128, 4, 2, 16]),
                                                                  in1=mask_hi.unsqueeze(1).unsqueeze(3).to_broadcast([128, 4, 2, 16]), op=ALU.mult),
                             reads=[c_R[2 + k], c_cst], writes=[c_Wz[k]])
                        S.op('pe', lambda e, k=k: e.transpose(PS(6)[:, 256 + k * 128:256 + (k + 1) * 128], Wz[:, k, :, :, :].rearrange("p q g c -> p (q g c)"), ident),
                             reads=[c_Wz[k], c_cst], writes=[c_ps[6]])
                        if s_ == 7:
                            S.op('act', lambda e, k=k: e.activation(out=ZBb[:, k, :], in_=Wz[:, k, :, :, :].rearrange("p q g c -> p (q g c)"), func=AF.Copy), reads=[c_Wz[k]], writes=[c_ZBb])
                    S.op('act', lambda e, s_=s_: e.activation(out=W1[:, s_, :, :].rearrange("p k c -> p (k c)"), in_=PS(6)[:, 256:512], func=AF.Copy), reads=[c_ps[6]], writes=[c_W1])
                for q in range(4):
                    for k in range(2):
                        S.op('pe', lambda e, q=q, k=k: e.matmul(PS(7)[32 * q:32 * q + 32, 0:288].rearrange("p (d c) -> p d c", d=9), lhsT=ZBb[:, k, 32 * q:32 * q + 32], rhs=CA[:, q, :, k, :], start=(k == 0), stop=(k == 1), tile_position=(0, 32 * q)),
                             reads=[c_ZBb, c_CA], writes=[c_ps[7]])
                S.op('dve', lambda e: e.tensor_tensor(out=KTf[:], in0=PS(7)[:, 0:256].rearrange("p (d c) -> p d c", d=8).unsqueeze(2).to_broadcast([128, 8, 4, 32]),
                                                      in1=maskq.unsqueeze(1).unsqueeze(3).to_broadcast([128, 8, 4, 32]), op=ALU.mult), reads=[c_ps[7], c_cst], writes=[c_KTf])
                if S5_STAGE < 3:
                    return
                for q in range(4):
                    for k in range(2):
                        bnk = 2 * k + q // 2
                        o_ = (q % 2) * 256
                        for s_ in range(8):
                            S.op('pe', lambda e, q=q, k=k, s_=s_, bnk=bnk, o_=o_: e.matmul(PS(bnk)[:, o_:o_ + 256], lhsT=W1[32 * q:32 * q + 32, s_, k, :], rhs=hb[32 * q:32 * q + 32, ct, s_:NP:8], start=(s_ == 0), stop=(s_ == 7), tile_position=(32 * q, 0)),
                                 reads=[c_W1, c_h[ct][0:4]], writes=[c_ps[bnk]])
                Xre = psum[:, 0:2, :].rearrange("p a (h n) -> p (a h) n", h=2)
                Xim = psum[:, 2:4, :].rearrange("p a (h n) -> p (a h) n", h=2)
                Ea = lambda k: SmT[:, 1, k, 4 * ct:4 * ct + 4, :].unsqueeze(3).to_broadcast([128, 4, 16, 16])
                Eb = lambda k: SmT[:, 0, k, 4 * ct:4 * ct + 4, :].unsqueeze(2).to_broadcast([128, 4, 16, 16])
                e0 = R[:, 2048:3072].rearrange("p (q a b) -> p q a b", q=4, a=16)
                e1 = R[:, 3072:4096].rearrange("p (q a b) -> p q a b", q=4, a=16)
                Ev = lambda k: E[:, k, :, :].rearrange("p q (a b) -> p q a b", a=16)
                S.op('pool', lambda e: TTn(e, e0, Ea(0), Eb(0), ALU.mult), reads=[c_SmT], writes=[c_R[2]])
                S.op('pool', lambda e: TTn(e, e1, Ea(1), Eb(1), ALU.mult), reads=[c_SmT], writes=[c_R[3]])
                S.op('dve', lambda e: TTn(e, Ev(0), e0, e1, ALU.subtract), reads=[c_R[2], c_R[3]], writes=[c_E])
                S.op('pool', lambda e: TTn(e, e0, Ea(0), Eb(1), ALU.mult), reads=[c_SmT], writes=[c_R[2]])
                S.op('pool', lambda e: TTn(e, e1, Ea(1), Eb(0), ALU.mult), reads=[c_SmT], writes=[c_R[3]])
                S.op('dve', lambda e: TTn(e, Ev(1), e0, e1, ALU.add), reads=[c_R[2], c_R[3]], writes=[c_E])
                Er = E[:, 0, :, :]
                Ei = E[:, 1, :, :]
                R4 = [r.rearrange("p (q n) -> p q n", q=4) for r in Rr]
                S.op('dve', lambda e: TTn(e, R4[0], Xre, Er, ALU.mult), reads=[c_ps[0], c_ps[1], c_E], writes=[c_R[0]])
                S.op('dve', lambda e: TTn(e, R4[1], Xim, Ei, ALU.mult), reads=[c_ps[2], c_ps[3], c_E], writes=[c_R[1]])
                S.op('dve', lambda e: TTn(e, R4[0], R4[0], R4[1], ALU.subtract), reads=[c_R[0], c_R[1]], writes=[c_R[0]])
                S.op('dve', lambda e: TTn(e, R4[1], Xre, Ei, ALU.mult), reads=[c_ps[0], c_ps[1], c_E], writes=[c_R[1]])
                S.op('dve', lambda e: TTn(e, R4[2], Xim, Er, ALU.mult), reads=[c_ps[2], c_ps[3], c_E], writes=[c_R[2]])
                S.op('dve', lambda e: TTn(e, R4[1], R4[1], R4[2], ALU.add), reads=[c_R[1], c_R[2]], writes=[c_R[1]])
                for q in range(4):
                    st_i = 4 * ct + q
                    S.op('dve', lambda e, q=q, st_i=st_i: e.tensor_tensor_scan(out=R4[2][:, q, :], data0=m8[:, st_i:st_i + 1].to_broadcast([128, 256]), data1=R4[0][:, q, :], initial=0.0, op0=ALU.mult, op1=ALU.add),
                         reads=[c_R[0], c_sm], writes=[c_R[2]])
                    S.op('dve', lambda e, q=q, st_i=st_i: e.tensor_tensor_scan(out=R4[3][:, q, :], data0=m8[:, st_i:st_i + 1].to_broadcast([128, 256]), data1=R4[1][:, q, :], initial=0.0, op0=ALU.mult, op1=ALU.add),
                         reads=[c_R[1], c_sm], writes=[c_R[3]])
                S.op('pool', lambda e: TTn(e, R4[0], R4[2], Er, ALU.mult), reads=[c_R[2], c_E], writes=[c_R[0]])
                S.op('pool', lambda e: TTn(e, R4[1], R4[3], Ei, ALU.mult), reads=[c_R[3], c_E], writes=[c_R[1]])
                S.op('dve', lambda e: TTn(e, R4[0], R4[0], R4[1], ALU.add), reads=[c_R[0], c_R[1]], writes=[c_R[0]])
                S.op('pool', lambda e: TTn(e, R4[1], R4[3], Er, ALU.mult), reads=[c_R[3], c_E], writes=[c_R[1]])
                S.op('dve', lambda e: TTn(e, R4[2], R4[2], Ei, ALU.mult), reads=[c_R[2], c_E], writes=[c_R[2]])
                S.op('dve', lambda e: TTn(e, R4[1], R4[1], R4[2], ALU.subtract), reads=[c_R[1], c_R[2]], writes=[c_R[1]])
                S.op('act', lambda e: e.activation(out=Hb[:, 0, :, 1:257], in_=R4[0], func=AF.Copy), reads=[c_R[0]], writes=[c_Hb])
                S.op('act', lambda e: e.activation(out=Hb[:, 1, :, 1:257], in_=R4[1], func=AF.Copy), reads=[c_R[1]], writes=[c_Hb])
                S.op('pool', lambda e: e.memset(Hb[:, :, :, 0:1], 0.0), writes=[c_Hb])
                S.op('act', lambda e: e.activation(out=Hfin[:, 0, 4 * ct:4 * ct + 4], in_=R4[0][:, :, 255], func=AF.Copy), reads=[c_R[0]], writes=[c_Hfin])
                S.op('act', lambda e: e.activation(out=Hfin[:, 1, 4 * ct:4 * ct + 4], in_=R4[1][:, :, 255], func=AF.Copy), reads=[c_R[1]], writes=[c_Hfin])
                if S5_STAGE < 4:
                    return
                yv = R[:, 0:2048]
                for tau in range(8):
                    yb = 4 + (tau % 2)
                    for s_ in range(tau + 1):
                        S.op('pe', lambda e, tau=tau, s_=s_, yb=yb: e.matmul(PS(yb)[:, 0:256], lhsT=KTf[:, tau - s_, :, :].rearrange("p q c -> p (q c)"), rhs=hb[:, ct, s_:NP:8], start=(s_ == 0), stop=False),
                             reads=[c_KTf, c_h[ct][0:4]], writes=[c_ps[yb]])
                    for q in range(4):
                        for k in range(2):
                            S.op('pe', lambda e, tau=tau, q=q, k=k, yb=yb: e.matmul(PS(yb)[32 * q:32 * q + 32, 0:256], lhsT=CA[:, q, tau + 1, k, :], rhs=Hb[:, k, q, 0:256], start=False, stop=(k == 1), tile_position=(0, 32 * q)),
                                 reads=[c_CA, c_Hb], writes=[c_ps[yb]])
                    S.op('dve', lambda e, tau=tau, yb=yb: e.scalar_tensor_tensor(out=yv[:, tau:NP:8], in0=hb[:, ct, tau:NP:8], scalar=vcol(Dname, ct, j), in1=PS(yb)[:, 0:256], op0=ALU.mult, op1=ALU.add),
                         reads=[c_ps[yb], c_h[ct][0:4], c_vec, c_R[2], c_R[3]], writes=[c_R[0], c_R[1]])
                for bi in range(4):
                    S.op('act', lambda e, bi=bi: e.activation(out=yg[:, ct, bi * 512:(bi + 1) * 512], in_=yv[:, bi * 512:(bi + 1) * 512], func=AF.Gelu_apprx_tanh), reads=[c_R[0], c_R[1]], writes=[c_yg[ct][bi]])
                if S5_STAGE < 5:
                    return
                for k, nm in enumerate(('state_s5_re', 'state_s5_im')):
                    S.dma('sp', st2[0:NS, k, :], I[nm][j, :, ct * 512:(ct + 1) * 512], writes=[c_st2])
                for k in range(2):
                    for q in range(4):
                        S.op('pe', lambda e, k=k, q=q: e.transpose(PS(6)[:, (k * 4 + q) * NS:(k * 4 + q + 1) * NS], st2[0:NS, k, q * 128:(q + 1) * 128], ident[0:NS, 0:NS]), reads=[c_st2, c_cst], writes=[c_ps[6]])
                S.op('act', lambda e: e.activation(out=h0[:].rearrange("p k q b -> p (k q b)"), in_=PS(6)[:, 0:128], func=AF.Copy), reads=[c_ps[6]], writes=[c_h0])
                for q in range(4):
                    for k in range(2):
                        S.op('pe', lambda e, q=q, k=k: e.matmul(PS(4 + q)[:, 480 + k * NS:480 + (k + 1) * NS], lhsT=W1[32 * q:32 * q + 32, 7, k, :], rhs=hb[32 * q:32 * q + 32, ct, NP:TT], start=True, stop=True, tile_position=(32 * q, 0)),
                             reads=[c_W1, c_h[ct][4]], writes=[c_ps[4 + q]])
                xs_ = lambda k: psum[:, 4:8, 480 + k * NS:480 + (k + 1) * NS]
                bb = lambda x: x.unsqueeze(2).to_broadcast([128, 4, NS])
                S.op('dve', lambda e: TTn(e, ts_[:, 0, :, :], h0[:, 0, :, :], bb(Asl(1, 0)), ALU.mult), reads=[c_h0, c_sm], writes=[c_ts])
                S.op('dve', lambda e: TTn(e, ts_[:, 1, :, :], h0[:, 1, :, :], bb(Asl(1, 1)), ALU.mult), reads=[c_h0, c_sm], writes=[c_ts])
                S.op('dve', lambda e: TTn(e, ts_[:, 0, :, :], ts_[:, 0, :, :], ts_[:, 1, :, :], ALU.subtract), reads=[c_ts], writes=[c_ts])
                S.op('dve', lambda e: TTn(e, hn[:, 0, :, :], ts_[:, 0, :, :], xs_(0), ALU.add), reads=[c_ts, c_ps[4:8]], writes=[c_hn])
                S.op('dve', lambda e: TTn(e, ts_[:, 2, :, :], h0[:, 1, :, :], bb(Asl(1, 0)), ALU.mult), reads=[c_h0, c_sm], writes=[c_ts])
                S.op('dve', lambda e: TTn(e, ts_[:, 3, :, :], h0[:, 0, :, :], bb(Asl(1, 1)), ALU.mult), reads=[c_h0, c_sm], writes=[c_ts])
                S.op('dve', lambda e: TTn(e, ts_[:, 2, :, :], ts_[:, 2, :, :], ts_[:, 3, :, :], ALU.add), reads=[c_ts], writes=[c_ts])
                S.op('dve', lambda e: TTn(e, hn[:, 1, :, :], ts_[:, 2, :, :], xs_(1), ALU.add), reads=[c_ts, c_ps[4:8]], writes=[c_hn])
                S.op('act', lambda e: e.activation(out=hnb[:].rearrange("p k q b -> p (k q b)"), in_=hn[:].rearrange("p k q b -> p (k q b)"), func=AF.Copy), reads=[c_hn], writes=[c_hnb])
                for q in range(4):
                    for k in range(2):
                        S.op('pe', lambda e, q=q, k=k: e.matmul(PS(7)[32 * q:32 * q + 32, 448:448 + NS], lhsT=CA[:, q, 0, k, :], rhs=hnb[:, k, q, :], start=(k == 0), stop=(k == 1), tile_position=(0, 32 * q)),
                             reads=[c_CA, c_hnb], writes=[c_ps[7]])
                S.op('dve', lambda e: e.scalar_tensor_tensor(out=ysv[:], in0=hb[:, ct, NP:TT], scalar=vcol(Dname, ct, j), in1=PS(7)[:, 448:448 + NS], op0=ALU.mult, op1=ALU.add),
                     reads=[c_ps[7], c_h[ct][4], c_vec], writes=[c_ysv])
                S.op('act', lambda e: e.activation(out=yg[:, ct, NP:TT], in_=ysv[:], func=AF.Gelu_apprx_tanh), reads=[c_ysv], writes=[c_yg[ct][4]])
                for k in range(2):
                    for q in range(4):
                        S.op('pe', lambda e, k=k, q=q: e.transpose(PS(6)[0:NS, q * 128:(q + 1) * 128], hn[:, k, q, :], ident), reads=[c_hn, c_cst], writes=[c_ps[6]])
                    S.op('dve', lambda e, k=k: e.tensor_copy(out=st2[0:NS, k, :], in_=PS(6)[0:NS, :]), reads=[c_ps[6]], writes=[c_st2])
                S.dma('sp', O['s5_re_s'][j, :, ct * 512:(ct + 1) * 512], st2[0:NS, 0, :], reads=[c_st2])
                S.dma('sp', O['s5_im_s'][j, :, ct * 512:(ct + 1) * 512], st2[0:NS, 1, :], reads=[c_st2])
            for ct in range(8):
                do_ct(ct)
            for k, nm in enumerate(('s5_re_p', 's5_im_p')):
                S.op('pe', lambda e, k=k: e.transpose(PS(6)[0:32, k * 128:(k + 1) * 128], Hfin[:, k, :], ident), reads=[c_Hfin, c_cst], writes=[c_ps[6]])
            S.op('dve', lambda e: e.tensor_copy(out=stg[0:32, 0:256], in_=PS(6)[0:32, 0:256]), reads=[c_ps[6]], writes=[c_stg])
            S.dma('sp', O['s5_re_p'][j:j + 1, :].rearrange("o (s n) -> (o s) n", n=128), stg[0:32, 0:128], reads=[c_stg])
            S.dma('sp', O['s5_im_p'][j:j + 1, :].rearrange("o (s n) -> (o s) n", n=128), stg[0:32, 128:256], reads=[c_stg])
            if S5_STAGE < 6:
                S.barrier()
                return
            kq = 0
            for bi in range(4):
                c0 = bi * 512
                last = (bi == 3)
                for mt in range(8):
                    wq = kq % 2
                    vb = kq % 2
                    gb = 2 + (kq % 2)
                    kq += 1
                    wload(wgl[wq][:, :, 0:128], I['s5_w_glu'][j, :, mt * 128:(mt + 1) * 128], 8, c_wgl[wq])
                    wload(wgl[wq][:, :, 128:256], I['s5_w_glu'][j, :, D + mt * 128:D + (mt + 1) * 128], 8, c_wgl[wq])
                    segs = [(c0, 512, bi, vb, gb, 0)] + ([(NP, NS, 4, 4, 4, 32)] if last else [])
                    for (s0, sn, cb, vbb, gbb, go) in segs:
                        for kt in range(8):
                            S.op('pe', lambda e, kt=kt, wq=wq, s0=s0, sn=sn, vbb=vbb: e.matmul(PS(vbb)[:, 0:sn], lhsT=wgl[wq][:, kt, 0:128], rhs=yg[:, kt, s0:s0 + sn], start=(kt == 0), stop=(kt == 7)),
                                 reads=[c_wgl[wq], c_yg[kt][cb]], writes=[c_ps[vbb]])
                        for kt in range(8):
                            S.op('pe', lambda e, kt=kt, wq=wq, s0=s0, sn=sn, gbb=gbb, go=go: e.matmul(PS(gbb)[:, go:go + sn], lhsT=wgl[wq][:, kt, 128:256], rhs=yg[:, kt, s0:s0 + sn], start=(kt == 0), stop=(kt == 7)),
                                 reads=[c_wgl[wq], c_yg[kt][cb]], writes=[c_ps[gbb]])
                        S.op('act', lambda e, wq=wq, sn=sn, gbb=gbb, go=go: e.activation(out=sg[wq][:, 0:sn], in_=PS(gbb)[:, go:go + sn], func=AF.Sigmoid), reads=[c_ps[gbb]], writes=[c_sg[wq]])
                        S.op('dve', lambda e, wq=wq, sn=sn, vbb=vbb: e.tensor_tensor(out=sg[wq][:, 0:sn], in0=sg[wq][:, 0:sn], in1=PS(vbb)[:, 0:sn], op=ALU.mult), reads=[c_sg[wq], c_ps[vbb]], writes=[c_sg[wq]])
                        S.op('dve', lambda e, wq=wq, sn=sn, s0=s0, mt=mt: e.tensor_tensor(out=xres[:, mt, s0:s0 + sn], in0=xres[:, mt, s0:s0 + sn], in1=sg[wq][:, 0:sn], op=ALU.add),
                             reads=[c_sg[wq], c_x[mt][cb]], writes=[c_x[mt][cb]])
            S.barrier()


        TT_OFF = [0]

        def rwkv_layer(li):
            A0 = Arena()
            rmsnorm('norm_mix', li, A0)
            S.barrier()
            A = Arena()
            NB = 8
            BW = 256
            NW = BW + NS
            TTn = lambda e, out, a, b, op: e.tensor_tensor(out=out, in0=a, in1=b, op=op)
            mask_hi = cc('mask_hi')
            blk64 = cc('blk64')
            w1 = A.bf16(8 * 64).rearrange("p (k c) -> p k c", k=8); c_w1 = Cell()
            a1 = A.bf16(8 * 64).rearrange("p (k c) -> p k c", k=8); c_a1 = Cell()
            g1 = A.bf16(8 * 128).rearrange("p (k c) -> p k c", k=8); c_g1 = Cell()
            w2 = A.bf16(D); c_w2 = Cell()
            a2 = A.bf16(D); c_a2 = Cell()
            g2 = A.bf16(D); c_g2 = Cell()
            wload(w1[:], I['rw_w1'][:, :], 8, c_w1)
            wload(a1[:], I['rw_a1'][:, :], 8, c_a1)
            wload(g1[:], I['rw_g1'][:, :], 8, c_g1)
            S.dma('pool', w2[0:64, :], I['rw_w2'][:, :], writes=[c_w2])
            S.dma('pool', a2[0:64, :], I['rw_a2'][:, :], writes=[c_a2])
            S.dma('pool', g2[:, :], I['rw_g2'][:, :], writes=[c_g2])
            cmk = A.f32(BW); c_cmk = Cell()
            S.dma('sp', cmk[:], I['cmask'][:, 0:BW], writes=[c_cmk])
            sh0 = A.f32(8 * NS).rearrange("p (a b) -> p a b", a=8); c_sh0 = Cell()
            stg = arena[:, ARENA_W - D:ARENA_W]; c_stg = Cell()
            S.dma('sp', stg[0:NS, :], I['state_rwkv_shift'][:, :], writes=[c_stg])
            for ct in range(8):
                S.op('pe', lambda e, ct=ct: e.transpose(PS(7)[:, ct * NS:(ct + 1) * NS], stg[0:NS, ct * 128:(ct + 1) * 128], ident[0:NS, 0:NS]), reads=[c_stg, c_cst], writes=[c_ps[7]])
            S.op('dve', lambda e: e.tensor_copy(out=sh0[:].rearrange("p a b -> p (a b)"), in_=PS(7)[:, 0:8 * NS]), reads=[c_ps[7]], writes=[c_sh0])
            Mf = A.f32(8 * 2 * 64).rearrange("p (a h i) -> p a h i", a=8, h=2); c_Mf = cells(8)
            Mb = A.bf16(8 * 2 * 64).rearrange("p (a h i) -> p a h i", a=8, h=2); c_Mb = cells(8)
            S.op('pool', lambda e: e.memset(Mf[:], 0.0), writes=[c_Mf])
            S.op('pool', lambda e: e.memset(Mb[:], 0.0), writes=[c_Mb])
            SV = A.f32(8 * 7 * NS).rearrange("p (a v b) -> p a v b", a=8, v=7); c_SV = cells(8)
            onec = A.f32(1); c_one = Cell()
            gnc = A.f32(1)
            S.op('dve', lambda e: e.memset(onec[:], 1.0), writes=[c_one])
            S.op('dve', lambda e: e.memset(gnc[:], 64e-5), writes=[c_one])
            xx = A.bf16(8 * NW).rearrange("p (a n) -> p a n", a=8); c_xx = Cell()
            xs = [A.bf16(8 * NW).rearrange("p (a n) -> p a n", a=8) for _ in range(4)]; c_xs = cells(4)
            og = A.bf16(8 * NW).rearrange("p (a n) -> p a n", a=8); c_og = cells(8)
            l1 = [A.bf16(NW) for _ in range(3)]; c_l1 = cells(3)
            wr = [A.bf16(8 * 3 * 128).rearrange("p (k n c) -> p k n c", k=8, n=3) for _ in range(2)]; c_wr = cells(2, 3)
            wo = [A.bf16(8 * 128).rearrange("p (k c) -> p k c", k=8) for _ in range(2)]; c_wo = cells(2)
            TT_OFF[0] = A.off
            NT_ = 13
            Tt = [A.f32(NW) for _ in range(NT_)]; c_T = cells(NT_)
            KKm = [A.bf16(NW) for _ in range(2)]; c_KKm = cells(2)
            Rm = [A.bf16(NW) for _ in range(2)]; c_Rm = cells(2)
            Kb = A.bf16(NW); c_Kb = Cell()
            Bb = A.bf16(NW); c_Bb = Cell()
            KBe = A.f32(3 * BW).rearrange("p (v n) -> p v n", v=3); c_KBe = Cell()
            AT = A.bf16(4 * 2 * 5 * 64).rearrange("p (c h m t) -> p c h m t", c=4, h=2, m=5); c_AT = Cell()
            PQ = [A.bf16(4 * 2 * 2 * 64).rearrange("p (c h m t) -> p c h m t", c=4, h=2, m=2) for _ in range(2)]; c_PQ = cells(2)
            Y = A.bf16(4 * 2 * 64).rearrange("p (c h t) -> p c h t", c=4, h=2); c_Y = Cell()
            Tok = A.bf16(4 * 3 * 128).rearrange("p (c v n) -> p c v n", c=4, v=3); c_Tok = Cell()
            Wp = A.bf16(2 * 64).rearrange("p (h i) -> p h i", h=2); c_Wp = Cell()
            Us = A.bf16(2 * 64).rearrange("p (h i) -> p h i", h=2); c_Us = Cell()
            ot = A.f32(NW); c_ot = Cell()
            imp = cc('imp')
            mask5 = cc('mask5')

            def gn_gate(ct, ncol, o_ap, c_o, r_ap, k2_ap, v_ap, g_ap, c_in, out_ap, c_out, tA, tB, c_tA, c_tB):
                S.op('pe', lambda e: e.matmul(PS(6)[:, 0:ncol], lhsT=blk64, rhs=o_ap, start=True, stop=True), reads=[c_o, c_cst], writes=[c_ps[6]])
                S.op('dve', lambda e: e.scalar_tensor_tensor(out=tA, in0=PS(6)[:, 0:ncol], scalar=-1.0 / 64, in1=o_ap, op0=ALU.mult, op1=ALU.add), reads=[c_ps[6], c_o], writes=[c_tA])
                S.op('pool', lambda e: TTn(e, tB, tA, tA, ALU.mult), reads=[c_tA], writes=[c_tB])
                S.op('pe', lambda e: e.matmul(PS(6)[:, 0:ncol], lhsT=blk64, rhs=tB, start=True, stop=True), reads=[c_tB, c_cst], writes=[c_ps[6]])
                S.op('act', lambda e: e.activation(out=tB, in_=PS(6)[:, 0:ncol], func=AF.Ln, bias=gnc[:, 0:1], scale=1.0 / 64), reads=[c_ps[6], c_one], writes=[c_tB])
                S.op('act', lambda e: e.activation(out=tB, in_=tB, func=AF.Exp, scale=-0.5), reads=[c_tB], writes=[c_tB])
                S.op('dve', lambda e: TTn(e, tA, tA, tB, ALU.mult), reads=[c_tA, c_tB], writes=[c_tA])
                S.op('dve', lambda e: e.tensor_scalar(out=tA, in0=tA, scalar1=vcol('rw_ln_w', ct), scalar2=vcol('rw_ln_b', ct), op0=ALU.mult, op1=ALU.add), reads=[c_tA, c_vec], writes=[c_tA])
                S.op('dve', lambda e: e.scalar_tensor_tensor(out=tB, in0=r_ap, scalar=vcol('rw_r_k', ct), in1=k2_ap, op0=ALU.mult, op1=ALU.mult), reads=[c_in, c_vec, c_tB], writes=[c_tB])
                S.op('pe', lambda e: e.matmul(PS(6)[:, 0:ncol], lhsT=blk64, rhs=tB, start=True, stop=True), reads=[c_tB, c_cst], writes=[c_ps[6]])
                S.op('dve', lambda e: TTn(e, tB, v_ap, PS(6)[:, 0:ncol], ALU.mult), reads=[c_ps[6], c_in], writes=[c_tB])
                S.op('dve', lambda e: TTn(e, tA, tA, tB, ALU.add), reads=[c_tA, c_tB], writes=[c_tA])
                S.op('dve', lambda e: TTn(e, out_ap, tA, g_ap, ALU.mult), reads=[c_tA, c_in], writes=[c_out])

            kwr = [0]
            kwo = [0]

            prologue_done = set()

            def block_prologue(bi):
                prologue_done.add(bi)
                c0 = bi * BW
                last = (bi == NB - 1)
                nw = NW if last else BW
                if bi == 0:
                    S.op('dve', lambda e: TTn(e, xx[:, :, 1:BW], hb[:, :, 0:BW - 1], hb[:, :, 1:BW], ALU.subtract), reads=[c_h], writes=[c_xx])
                    S.op('dve', lambda e: e.tensor_scalar(out=xx[:, :, 0:1], in0=hb[:, :, 0:1], scalar1=-1.0, scalar2=None, op0=ALU.mult), reads=[c_h], writes=[c_xx])
                else:
                    S.op('dve', lambda e: TTn(e, xx[:, :, 0:BW], hb[:, :, c0 - 1:c0 + BW - 1], hb[:, :, c0:c0 + BW], ALU.subtract), reads=[c_h], writes=[c_xx])
                if last:
                    S.op('dve', lambda e: TTn(e, xx[:, :, BW:NW], sh0[:], hb[:, :, NP:TT], ALU.subtract), reads=[c_h, c_sh0], writes=[c_xx])

                def mk_xs(n, q):
                    for kt in range(8):
                        S.op('dve', lambda e, kt=kt: e.scalar_tensor_tensor(out=xs[q][:, kt, 0:nw], in0=xx[:, kt, 0:nw], scalar=vcol('rw_mu', kt, n), in1=hb[:, kt, c0:c0 + nw], op0=ALU.mult, op1=ALU.add),
                             reads=[c_xx, c_h, c_vec], writes=[c_xs[q]])
                for n, (wt_, c_wt_, m_, func, li_) in enumerate([(w1, c_w1, 64, AF.Tanh, 0), (a1, c_a1, 64, AF.Copy, 1), (g1, c_g1, 128, AF.Sigmoid, 2)]):
                    mk_xs(3 + n, 3)
                    for kt in range(8):
                        S.op('pe', lambda e, kt=kt, wt_=wt_, m_=m_: e.matmul(PS(7)[0:m_, 0:nw], lhsT=wt_[:, kt, :], rhs=xs[3][:, kt, 0:nw], start=(kt == 0), stop=(kt == 7)),
                             reads=[c_wt_, c_xs[3]], writes=[c_ps[7]])
                    S.op('act', lambda e, m_=m_, func=func, li_=li_: e.activation(out=l1[li_][0:m_, 0:nw], in_=PS(7)[0:m_, 0:nw], func=func), reads=[c_ps[7]], writes=[c_l1[li_]])
                for n in range(3):
                    mk_xs(n, n)

            def do_block(bi):
                c0 = bi * BW
                last = (bi == NB - 1)
                nw = NW if last else BW
                if bi not in prologue_done:
                    block_prologue(bi)
                for ct in range(8):
                    do_ct(bi, ct, c0, last, nw)
                for mt in range(8):
                    wq = kwo[0] % 2
                    kwo[0] += 1
                    wload(wo[wq][:], I['rw_w_o'][:, mt * 128:(mt + 1) * 128], 8, c_wo[wq])
                    segs = [(0, BW, c0, [c_x[mt][c0 // 512]])]
                    for (s0, sn_, x0, cx) in segs:
                        for kt in range(8):
                            S.op('pe', lambda e, kt=kt, wq=wq, s0=s0, sn_=sn_: e.matmul(PS(7)[:, 0:sn_], lhsT=wo[wq][:, kt, :], rhs=og[:, kt, s0:s0 + sn_], start=(kt == 0), stop=(kt == 7)),
                                 reads=[c_wo[wq], c_og[kt]], writes=[c_ps[7]])
                        S.op('dve', lambda e, mt=mt, sn_=sn_, x0=x0: TTn(e, xres[:, mt, x0:x0 + sn_], xres[:, mt, x0:x0 + sn_], PS(7)[:, 0:sn_], ALU.add), reads=[c_ps[7]] + cx, writes=cx)

            RS = [[Tt[0], Tt[2], Tt[5], Tt[7]], [A.f32(NW), A.f32(NW), A.f32(NW), A.f32(NW)]]
            c_RS = [[c_T[0], c_T[2], c_T[5], c_T[7]], cells(4)]
            proj_done = set()

            def emit_proj(bi, ct, nw):
                proj_done.add((bi, ct))
                s_ = ct % 2
                r_, v_, g_ = RS[s_][0], RS[s_][1], RS[s_][2]
                cr, cv, cg = c_RS[s_][0], c_RS[s_][1], c_RS[s_][2]
                k_, lw, a_ = Tt[1], Tt[3], Tt[4]
                ck, clw, ca = c_T[1], c_T[3], c_T[4]
                wq = kwr[0] % 2
                if kwr[0] == 0:
                    for n in range(3):
                        wload(wr[0][:, :, n, :], I['rw_w_rkv'][n, :, 0:128], 8, c_wr[0][n])
                kwr[0] += 1
                for n in range(3):
                    for kt in range(8):
                        S.op('pe', lambda e, n=n, kt=kt: e.matmul(PS(n)[:, 0:nw], lhsT=wr[wq][:, kt, n, :], rhs=xs[n][:, kt, 0:nw], start=(kt == 0), stop=(kt == 7)),
                             reads=[c_wr[wq][n], c_xs[n]], writes=[c_ps[n]])
                if not (bi == NB - 1 and ct == 7):
                    nct = (ct + 1) % 8
                    for n in range(3):
                        wload(wr[1 - wq][:, :, n, :], I['rw_w_rkv'][n, :, nct * 128:(nct + 1) * 128], 8, c_wr[1 - wq][n])
                S.op('pe', lambda e: e.matmul(PS(3)[:, 0:nw], lhsT=w2[0:64, ct * 128:(ct + 1) * 128], rhs=l1[0][0:64, 0:nw], start=True, stop=True), reads=[c_w2, c_l1[0]], writes=[c_ps[3]])
                S.op('pe', lambda e: e.matmul(PS(4)[:, 0:nw], lhsT=a2[0:64, ct * 128:(ct + 1) * 128], rhs=l1[1][0:64, 0:nw], start=True, stop=True), reads=[c_a2, c_l1[1]], writes=[c_ps[4]])
                S.op('pe', lambda e: e.matmul(PS(5)[:, 0:nw], lhsT=g2[:, ct * 128:(ct + 1) * 128], rhs=l1[2][:, 0:nw], start=True, stop=True), reads=[c_g2, c_l1[2]], writes=[c_ps[5]])
                W = slice(0, nw)
                S.op('act', lambda e: e.activation(out=r_[:, W], in_=PS(0)[:, W], func=AF.Copy), reads=[c_ps[0]], writes=[cr])
                S.op('act', lambda e: e.activation(out=k_[:, W], in_=PS(1)[:, W], func=AF.Copy), reads=[c_ps[1]], writes=[ck])
                S.op('act', lambda e: e.activation(out=v_[:, W], in_=PS(2)[:, W], func=AF.Copy), reads=[c_ps[2]], writes=[cv])
                S.op('act', lambda e: e.activation(out=g_[:, W], in_=PS(5)[:, W], func=AF.Copy), reads=[c_ps[5]], writes=[cg])
                S.op('act', lambda e: e.activation(out=lw[:, W], in_=PS(3)[:, W], func=AF.Sigmoid, bias=vcol('rw_w0', ct), scale=1.0), reads=[c_ps[3], c_vec], writes=[clw])
                S.op('act', lambda e: e.activation(out=a_[:, W], in_=PS(4)[:, W], func=AF.Sigmoid, bias=vcol('rw_a0', ct), scale=1.0), reads=[c_ps[4], c_vec], writes=[ca])

            def do_ct(bi, ct, c0, last, nw):
                s_ = ct % 2
                r_, v_, g_, k2 = RS[s_]
                cr, cv, cg, ck2 = c_RS[s_]
                _, k_, _, lw, a_, _, kk, _, b_, cum, Wi, Wn, We = Tt
                _, ck, _, clw, ca, _, ckk, _, cb, ccum, cWi, cWn, cWe = c_T
                if (bi, ct) not in proj_done:
                    emit_proj(bi, ct, nw)
                W = slice(0, nw)
                CD = math.exp(-0.5)
                S.op('dve', lambda e: e.tensor_scalar(out=kk[:, W], in0=k_[:, W], scalar1=vcol('rw_k_k', ct), scalar2=None, op0=ALU.mult), reads=[ck, c_vec], writes=[ckk])
                S.op('pool', lambda e: TTn(e, cum[:, W], kk[:, W], kk[:, W], ALU.mult), reads=[ckk], writes=[ccum])
                S.op('pe', lambda e: e.matmul(PS(6)[:, W], lhsT=blk64, rhs=cum[:, W], start=True, stop=True), reads=[ccum, c_cst], writes=[c_ps[6]])
                S.op('dve', lambda e: e.tensor_scalar(out=cum[:, W], in0=PS(6)[:, W], scalar1=1e-24, scalar2=None, op0=ALU.max), reads=[c_ps[6]], writes=[ccum])
                S.op('act', lambda e: e.activation(out=cum[:, W], in_=cum[:, W], func=AF.Ln), reads=[ccum], writes=[ccum])
                S.op('act', lambda e: e.activation(out=cum[:, W], in_=cum[:, W], func=AF.Exp, scale=-0.5), reads=[ccum], writes=[ccum])
                S.op('dve', lambda e: TTn(e, kk[:, W], kk[:, W], cum[:, W], ALU.mult), reads=[ckk, ccum], writes=[ckk])
                S.op('dve', lambda e: e.tensor_scalar(out=k2[:, W], in0=a_[:, W], scalar1=-1.0, scalar2=vcol('rw_k_a', ct), op0=ALU.add, op1=ALU.mult), reads=[ca, c_vec], writes=[ck2])
                S.op('dve', lambda e: e.scalar_tensor_tensor(out=k2[:, W], in0=k2[:, W], scalar=1.0, in1=k_[:, W], op0=ALU.add, op1=ALU.mult), reads=[ck2, ck], writes=[ck2])
                S.op('pool', lambda e: TTn(e, b_[:, W], kk[:, W], a_[:, W], ALU.mult), reads=[ckk, ca], writes=[cb])
                if last:
                    for vi, (src, csrc) in enumerate([(kk, ckk), (lw, clw), (b_, cb), (k2, ck2), (r_, cr), (v_, cv), (g_, cg)]):
                        if vi == 1:
                            S.op('act', lambda e, src=src, vi=vi: e.activation(out=SV[:, ct, vi, :], in_=src[:, BW:NW], func=AF.Exp, scale=-CD), reads=[csrc], writes=[c_SV[ct]])
                        else:
                            S.op('pool', lambda e, src=src, vi=vi: e.tensor_copy(out=SV[:, ct, vi, :], in_=src[:, BW:NW]), reads=[csrc], writes=[c_SV[ct]])
                P_ = slice(0, BW)
                S.op('dve', lambda e: e.tensor_tensor_scan(out=cum[:, P_], data0=cmk[:], data1=lw[:, P_], initial=0.0, op0=ALU.mult, op1=ALU.add), reads=[c_cmk, clw, ccum], writes=[ccum])
                S.op('act', lambda e: e.activation(out=Wi[:, P_], in_=cum[:, P_], func=AF.Exp, scale=-CD), reads=[ccum], writes=[cWi])
                S.op('act', lambda e: e.activation(out=Wn[:, P_], in_=cum[:, P_], func=AF.Exp, scale=CD), reads=[ccum], writes=[cWn])
                S.op('pool', lambda e: TTn(e, We[:, P_], cum[:, P_], lw[:, P_], ALU.subtract), reads=[ccum, clw], writes=[cWe])
                S.op('act', lambda e: e.activation(out=We[:, P_], in_=We[:, P_], func=AF.Exp, scale=-CD), reads=[cWe], writes=[cWe])
                for h in range(2):
                    S.op('dve', lambda e, h=h: e.scalar_tensor_tensor(out=KKm[h][:, P_], in0=kk[:, P_], scalar=mask_hi[:, h:h + 1], in1=We[:, P_], op0=ALU.mult, op1=ALU.mult), reads=[ckk, cWe, c_cst], writes=[c_KKm[h]])
                    S.op('dve', lambda e, h=h: e.scalar_tensor_tensor(out=Rm[h][:, P_], in0=r_[:, P_], scalar=mask_hi[:, h:h + 1], in1=Wi[:, P_], op0=ALU.mult, op1=ALU.mult), reads=[cr, cWi, c_cst], writes=[c_Rm[h]])
                S.op('pool', lambda e: TTn(e, cum[:, P_], k2[:, P_], Wn[:, P_], ALU.mult), reads=[ck2, cWn, ccum], writes=[ccum])
                S.op('pool', lambda e: TTn(e, We[:, P_], b_[:, P_], Wn[:, P_], ALU.mult), reads=[cb, cWn, cWe, c_KKm], writes=[cWe])
                S.op('act', lambda e: e.activation(out=Kb[:, P_], in_=cum[:, P_], func=AF.Copy), reads=[ccum], writes=[c_Kb])
                S.op('act', lambda e: e.activation(out=Bb[:, P_], in_=We[:, P_], func=AF.Copy), reads=[cWe], writes=[c_Bb])
                wend = Wi[:, 63:BW:64]
                wend_b = wend.unsqueeze(2).to_broadcast([128, 4, 64])
                S.op('dve', lambda e: TTn(e, KBe[:, 1, :].rearrange("p (c n) -> p c n", c=4), cum[:, P_].rearrange("p (c n) -> p c n", c=4), wend_b, ALU.mult), reads=[ccum, cWi], writes=[c_KBe])
                S.op('dve', lambda e: e.scalar_tensor_tensor(out=KBe[:, 2, :].rearrange("p (c n) -> p c n", c=4), in0=We[:, P_].rearrange("p (c n) -> p c n", c=4), scalar=-1.0, in1=wend_b, op0=ALU.mult, op1=ALU.mult),
                     reads=[cWe, cWi], writes=[c_KBe])
                S.op('pool', lambda e: e.tensor_copy(out=KBe[:, 0, :], in_=v_[:, P_]), reads=[cv], writes=[c_KBe])
                for c in range(4):
                    for vi in range(3):
                        bnk = (c * 3 + vi) // 4
                        o_ = ((c * 3 + vi) % 4) * 128
                        S.op('pe', lambda e, c=c, vi=vi, bnk=bnk, o_=o_: e.transpose(PS(bnk)[0:64, o_:o_ + 128], KBe[:, vi, c * 64:(c + 1) * 64], ident), reads=[c_KBe, c_cst], writes=[c_ps[bnk]])
                S.op('act', lambda e: e.activation(out=Tok[0:64].rearrange("p c v n -> p (c v n)"), in_=psum[0:64, 0:3, :].rearrange("p a n -> p (a n)"), func=AF.Copy), reads=[c_ps[0:3]], writes=[c_Tok])
                for c in range(4):
                    cs_ = slice(c * 64, (c + 1) * 64)
                    for h in range(2):
                        for m, (lh, rh, cl, crr) in enumerate([(Kb, KKm[h], c_Kb, c_KKm[h]), (Bb, KKm[h], c_Bb, c_KKm[h]), (KKm[h], Bb, c_KKm[h], c_Bb), (Kb, Rm[h], c_Kb, c_Rm[h]), (Bb, Rm[h], c_Bb, c_Rm[h])]):
                            idx = (c * 2 + h) * 5 + m
                            bnk = 3 + idx // 8
                            o_ = (idx % 8) * 64
                            S.op('pe', lambda e, lh=lh, rh=rh, cs_=cs_, bnk=bnk, o_=o_: e.matmul(PS(bnk)[0:64, o_:o_ + 64], lhsT=lh[:, cs_], rhs=rh[:, cs_], start=True, stop=True),
                                 reads=[cl, crr], writes=[c_ps[bnk]])
                S.op('dve', lambda e: TTn(e, AT[0:64].rearrange("p c h m t -> p (c h) (m t)"), psum[0:64, 3:8, :].rearrange("p a (u n) -> p (a u) n", u=8).rearrange("p (ch m) n -> p ch (m n)", m=5),
                                          mask5[0:64, :].unsqueeze(1).to_broadcast([64, 8, 320]), ALU.mult), reads=[c_ps[3:8], c_cst], writes=[c_AT])
                S.op('dve', lambda e: TTn(e, Y[0:64].rearrange("p c h t -> p (c h) t"), imp[0:64, :].unsqueeze(1).to_broadcast([64, 8, 64]), AT[0:64, :, :, 1, :].rearrange("p c h t -> p (c h) t"), ALU.subtract), reads=[c_AT, c_cst], writes=[c_Y])
                prevP = lambda c, h: AT[0:64, c, h, 1, :]
                prevQ = lambda c, h: AT[0:64, c, h, 2, :]
                c_prev = c_AT
                for lvl in range(1, 7):
                    pq = PQ[lvl % 2]
                    c_pq = c_PQ[lvl % 2]
                    for c in range(4):
                        for h in range(2):
                            o_ = (c * 2 + h) * 64
                            if lvl <= 4:
                                S.op('pe', lambda e, c=c, h=h, o_=o_, prevP=prevP, prevQ=prevQ: e.matmul(PS(3)[0:64, o_:o_ + 64], lhsT=prevQ(c, h), rhs=prevP(c, h), start=True, stop=True), reads=[c_prev], writes=[c_ps[3]])
                            if lvl <= 5:
                                S.op('pe', lambda e, c=c, h=h, o_=o_, prevP=prevP, prevQ=prevQ: e.matmul(PS(4)[0:64, o_:o_ + 64], lhsT=prevP(c, h), rhs=prevQ(c, h), start=True, stop=True), reads=[c_prev], writes=[c_ps[4]])
                            if lvl >= 2:
                                S.op('pe', lambda e, c=c, h=h, o_=o_, prevQ=prevQ: e.matmul(PS(5)[0:64, o_:o_ + 64], lhsT=prevQ(c, h), rhs=Y[0:64, c, h, :], start=True, stop=True), reads=[c_prev, c_Y], writes=[c_ps[5]])
                    if lvl <= 4:
                        S.op('act', lambda e, pq=pq: e.activation(out=pq[0:64, :, :, 0, :].rearrange("p c h t -> p (c h) t"), in_=PS(3)[0:64, :].rearrange("p (u t) -> p u t", u=8), func=AF.Copy), reads=[c_ps[3]], writes=[c_pq])
                    if lvl <= 5:
                        S.op('act', lambda e, pq=pq: e.activation(out=pq[0:64, :, :, 1, :].rearrange("p c h t -> p (c h) t"), in_=PS(4)[0:64, :].rearrange("p (u t) -> p u t", u=8), func=AF.Copy), reads=[c_ps[4]], writes=[c_pq])
                    if lvl >= 2:
                        S.op('dve', lambda e: TTn(e, Y[0:64].rearrange("p c h t -> p (c h t)"), Y[0:64].rearrange("p c h t -> p (c h t)"), PS(5)[0:64, :], ALU.add), reads=[c_ps[5], c_Y], writes=[c_Y])
                    prevP = (lambda pq: (lambda c, h: pq[0:64, c, h, 0, :]))(pq)
                    prevQ = (lambda pq: (lambda c, h: pq[0:64, c, h, 1, :]))(pq)
                    c_prev = c_pq
                for c in range(4):
                    cs_ = slice(c * 64, (c + 1) * 64)
                    for h in range(2):
                        S.op('pe', lambda e, h=h, cs_=cs_: e.matmul(PS(6)[0:64, h * 64:(h + 1) * 64], lhsT=KKm[h][:, cs_], rhs=Mb[:, ct, h, :], start=True, stop=False), reads=[c_KKm[h], c_Mb[ct]], writes=[c_ps[6]])
                        S.op('pe', lambda e, h=h, c=c: e.matmul(PS(6)[0:64, h * 64:(h + 1) * 64], lhsT=AT[0:64, c, h, 0, :], rhs=Tok[0:64, c, 0, h * 64:(h + 1) * 64], start=False, stop=True), reads=[c_AT, c_Tok], writes=[c_ps[6]])
                    S.op('act', lambda e: e.activation(out=Wp[0:64].rearrange("p h i -> p (h i)"), in_=PS(6)[0:64, 0:128], func=AF.Copy), reads=[c_ps[6]], writes=[c_Wp])
                    for h in range(2):
                        S.op('pe', lambda e, h=h, c=c: e.matmul(PS(7)[0:64, h * 64:(h + 1) * 64], lhsT=Y[0:64, c, h, :], rhs=Wp[0:64, h, :], start=True, stop=True), reads=[c_Y, c_Wp], writes=[c_ps[7]])
                    S.op('act', lambda e: e.activation(out=Us[0:64].rearrange("p h i -> p (h i)"), in_=PS(7)[0:64, 0:128], func=AF.Copy), reads=[c_ps[7]], writes=[c_Us])
                    for h in range(2):
                        p0 = 64 * h
                        S.op('pe', lambda e, h=h, cs_=cs_, p0=p0: e.matmul(PS(0)[p0:p0 + 64, cs_], lhsT=Mb[:, ct, h, :], rhs=Rm[h][:, cs_], start=True, stop=False, tile_position=(0, p0)), reads=[c_Mb[ct], c_Rm[h]], writes=[c_ps[0]])
                        S.op('pe', lambda e, h=h, c=c, cs_=cs_, p0=p0: e.matmul(PS(0)[p0:p0 + 64, cs_], lhsT=Tok[0:64, c, 0, h * 64:(h + 1) * 64], rhs=AT[0:64, c, h, 3, :], start=False, stop=False, tile_position=(0, p0)), reads=[c_Tok, c_AT], writes=[c_ps[0]])
                        S.op('pe', lambda e, h=h, c=c, cs_=cs_, p0=p0: e.matmul(PS(0)[p0:p0 + 64, cs_], lhsT=Us[0:64, h, :], rhs=AT[0:64, c, h, 4, :], start=False, stop=True, tile_position=(0, p0)), reads=[c_Us, c_AT], writes=[c_ps[0]])
                        S.op('pe', lambda e, h=h, c=c, p0=p0: e.matmul(PS(1)[p0:p0 + 64, 0:64], lhsT=Tok[0:64, c, 1, h * 64:(h + 1) * 64], rhs=Tok[0:64, c, 0, h * 64:(h + 1) * 64], start=True, stop=False, tile_position=(0, p0)), reads=[c_Tok], writes=[c_ps[1]])
                        S.op('pe', lambda e, h=h, c=c, p0=p0: e.matmul(PS(1)[p0:p0 + 64, 0:64], lhsT=Tok[0:64, c, 2, h * 64:(h + 1) * 64], rhs=Us[0:64, h, :], start=False, stop=True, tile_position=(0, p0)), reads=[c_Tok, c_Us], writes=[c_ps[1]])
                    for h in range(2):
                        p0 = 64 * h
                        S.op('dve', lambda e, h=h, c=c, p0=p0: e.scalar_tensor_tensor(out=Mf[p0:p0 + 64, ct, h, :], in0=Mf[p0:p0 + 64, ct, h, :], scalar=Wi[p0:p0 + 64, c * 64 + 63:c * 64 + 64], in1=PS(1)[p0:p0 + 64, 0:64], op0=ALU.mult, op1=ALU.add),
                             reads=[c_ps[1], c_Mf[ct], cWi], writes=[c_Mf[ct]])
                        S.op('act', lambda e, h=h, p0=p0: e.activation(out=Mb[p0:p0 + 64, ct, h, :], in_=Mf[p0:p0 + 64, ct, h, :], func=AF.Copy), reads=[c_Mf[ct]], writes=[c_Mb[ct]])
                S.op('act', lambda e: e.activation(out=ot[:, P_], in_=PS(0)[:, P_], func=AF.Copy), reads=[c_ps[0]], writes=[c_ot])
                if bi == 0:
                    S.op('dve', lambda e: TTn(e, cum[:, 0:2], r_[:, 0:2], k2[:, 0:2], ALU.mult), reads=[cr, ck2, ccum], writes=[ccum])
                    S.op('pe', lambda e: e.matmul(PS(6)[:, 0:2], lhsT=blk64, rhs=cum[:, 0:2], start=True, stop=True), reads=[ccum, c_cst], writes=[c_ps[6]])
                    S.op('dve', lambda e: TTn(e, ot[:, 0:1], v_[:, 0:1], PS(6)[:, 0:1], ALU.mult), reads=[c_ps[6], cv, c_ot], writes=[c_ot])
                if ct < 7:
                    emit_proj(bi, ct + 1, nw)
                if ct == 6 and bi + 1 < NB:
                    block_prologue(bi + 1)
                gn_gate(ct, BW, ot[:, P_], c_ot, r_[:, P_], k2[:, P_], v_[:, P_], g_[:, P_], [cr, ck2, cv, cg], og[:, ct, 0:BW], c_og[ct], cum[:, P_], We[:, P_], ccum, cWe)

            for bi in range(NB):
                do_block(bi)
            S.barrier()
            A2 = Arena()
            A2.off = TT_OFF[0]
            S0 = A2.f32(NS * 64).rearrange("p (b j) -> p b j", b=NS); c_S0 = Cell()
            S1 = A2.f32(NS * 64).rearrange("p (b j) -> p b j", b=NS); c_S1 = Cell()
            t1 = A2.f32(NS * 64).rearrange("p (b j) -> p b j", b=NS); c_t1 = Cell()
            rx = A2.f32(NS * 64).rearrange("p (b j) -> p b j", b=NS); c_rx = Cell()
            sa = A2.f32(NS); c_sa = Cell()
            os_ = A2.f32(NS); c_os = Cell()
            ogs = A2.f32(8 * NS).rearrange("p (a b) -> p a b", a=8); c_ogs = cells(8)
            ogsb = A2.bf16(8 * NS).rearrange("p (a b) -> p a b", a=8); c_ogsb = cells(8)
            tA = A2.f32(NS); c_tA = Cell()
            tB = A2.f32(NS); c_tB = Cell()
            i2 = cc('i2')

            def bvec(ct, vi, pb):
                S.op('dve', lambda e: TTn(e, rx[:], SV[:, ct, vi, :].unsqueeze(2).to_broadcast([128, NS, 64]), i2.unsqueeze(1).to_broadcast([128, NS, 64]), ALU.mult), reads=[c_SV[ct], c_cst], writes=[c_rx])
                for half in range(2):
                    S.op('pe', lambda e, half=half: e.matmul(PS(pb + half), lhsT=blk64, rhs=rx[:, half * 8:(half + 1) * 8, :].rearrange("p b j -> p (b j)"), start=True, stop=True), reads=[c_rx, c_cst], writes=[c_ps[pb + half]])
                return psum[:, pb:pb + 2, :].rearrange("p a (b j) -> p (a b) j", j=64), [c_ps[pb], c_ps[pb + 1]]

            for ct in range(8):
                S.dma('sp', S0[:], I['state_rwkv_wkv'][:, 2 * ct:2 * ct + 2, :, :].rearrange("b h i j -> (h i) b j"), writes=[c_S0])
                KKb, cK = bvec(ct, 0, 0)
                S.op('dve', lambda e, KKb=KKb: TTn(e, t1[:], S0[:], KKb, ALU.mult), reads=[c_S0] + cK, writes=[c_t1])
                S.op('dve', lambda e: e.tensor_reduce(out=sa[:], in_=t1[:], axis=AX.X, op=ALU.add), reads=[c_t1], writes=[c_sa])
                Wb, cW = bvec(ct, 1, 2)
                S.op('dve', lambda e, Wb=Wb: TTn(e, S1[:], S0[:], Wb, ALU.mult), reads=[c_S0] + cW, writes=[c_S1])
                Bq, cB = bvec(ct, 2, 0)
                S.op('dve', lambda e, Bq=Bq: TTn(e, t1[:], Bq, sa[:].unsqueeze(2).to_broadcast([128, NS, 64]), ALU.mult), reads=[c_sa] + cB, writes=[c_t1])
                S.op('dve', lambda e: TTn(e, S1[:], S1[:], t1[:], ALU.subtract), reads=[c_S1, c_t1], writes=[c_S1])
                K2q, cK2 = bvec(ct, 3, 2)
                S.op('dve', lambda e, K2q=K2q, ct=ct: TTn(e, t1[:], K2q, SV[:, ct, 5, :].unsqueeze(2).to_broadcast([128, NS, 64]), ALU.mult), reads=[c_SV[ct]] + cK2, writes=[c_t1])
                S.op('dve', lambda e: TTn(e, S1[:], S1[:], t1[:], ALU.add), reads=[c_S1, c_t1], writes=[c_S1])
                Rq, cRq = bvec(ct, 4, 0)
                S.op('dve', lambda e, Rq=Rq: TTn(e, t1[:], S1[:], Rq, ALU.mult), reads=[c_S1] + cRq, writes=[c_t1])
                S.op('dve', lambda e: e.tensor_reduce(out=os_[:], in_=t1[:], axis=AX.X, op=ALU.add), reads=[c_t1], writes=[c_os])
                S.dma('sp', O['wkv_s'][:, 2 * ct:2 * ct + 2, :, :].rearrange("b h i j -> (h i) b j"), S1[:], reads=[c_S1])
                gn_gate(ct, NS, os_[:], c_os, SV[:, ct, 4, :], SV[:, ct, 3, :], SV[:, ct, 5, :], SV[:, ct, 6, :], [c_SV[ct]], ogs[:, ct, :], c_ogs[ct], tA[:], tB[:], c_tA, c_tB)
                S.op('act', lambda e, ct=ct: e.activation(out=ogsb[:, ct, :], in_=ogs[:, ct, :], func=AF.Copy), reads=[c_ogs[ct]], writes=[c_ogsb[ct]])
            if dbg:
                for ct in range(8):
                    S.dma('sp', O['dbg'][8, ct * 128:(ct + 1) * 128, NP:TT], ogs[:, ct, :], reads=[c_ogs[ct]])
                    S.dma('sp', O['dbg'][8, ct * 128:(ct + 1) * 128, 0:NS], SV[:, ct, 4, :], reads=[c_SV[ct]])
                    S.dma('sp', O['dbg'][8, ct * 128:(ct + 1) * 128, NS:2 * NS], SV[:, ct, 6, :], reads=[c_SV[ct]])
                    S.dma('sp', O['dbg'][8, ct * 128:(ct + 1) * 128, 2 * NS:3 * NS], SV[:, ct, 3, :], reads=[c_SV[ct]])
                    S.dma('sp', O['dbg'][8, ct * 128:(ct + 1) * 128, 3 * NS:4 * NS], SV[:, ct, 5, :], reads=[c_SV[ct]])
            for mt in range(8):
                wq = kwo[0] % 2
                kwo[0] += 1
                wload(wo[wq][:], I['rw_w_o'][:, mt * 128:(mt + 1) * 128], 8, c_wo[wq])
                for kt in range(8):
                    S.op('pe', lambda e, kt=kt, wq=wq: e.matmul(PS(7)[:, 0:NS], lhsT=wo[wq][:, kt, :], rhs=ogsb[:, kt, :], start=(kt == 0), stop=(kt == 7)), reads=[c_wo[wq], c_ogsb[kt]], writes=[c_ps[7]])
                S.op('dve', lambda e, mt=mt: TTn(e, xres[:, mt, NP:TT], xres[:, mt, NP:TT], PS(7)[:, 0:NS], ALU.add), reads=[c_ps[7], c_x[mt][4]], writes=[c_x[mt][4]])
            shf = A2.f32(8 * (NS + 1)).rearrange("p (a b) -> p a b", a=8); c_shf = cells(8)
            S.op('dve', lambda e: e.tensor_copy(out=shf[:, :, 0:NS], in_=hb[:, :, NP:TT]), reads=[c_h], writes=[c_shf])
            S.op('dve', lambda e: e.tensor_copy(out=shf[:, :, NS:NS + 1], in_=hb[:, :, NP - 1:NP]), reads=[c_h], writes=[c_shf])
            shs = A2.f32(8 * NS).rearrange("p (a b) -> p a b", a=8); c_shs = cells(8)
            S.op('dve', lambda e: e.tensor_copy(out=shs[:], in_=shf[:, :, 0:NS]), reads=[c_shf], writes=[c_shs])
            fm_to_rows(shs, c_shs, O['shift_s'][:, :], stg, c_stg, 0)
            shp = A2.f32(8); c_shp = Cell()
            S.op('dve', lambda e: e.tensor_copy(out=shp[:], in_=shf[:, :, NS]), reads=[c_shf], writes=[c_shp])
            S.op('pe', lambda e: e.transpose(PS(0)[0:8, 0:128], shp[:, 0:8], ident), reads=[c_shp, c_cst], writes=[c_ps[0]])
            S.op('dve', lambda e: e.tensor_copy(out=stg[32:40, 0:128], in_=PS(0)[0:8, 0:128]), reads=[c_ps[0]], writes=[c_stg])
            S.dma('sp', O['shift_p'][:, :].rearrange("o (a p) -> (o a) p", p=128), stg[32:40, 0:128], reads=[c_stg])
            wst = A2.f32(16 * 64).rearrange("p (h j) -> p h j", h=16); c_wst = Cell()
            for ct in range(8):
                for h in range(2):
                    S.op('pe', lambda e, ct=ct, h=h: e.transpose(PS(1 + h)[0:64, (ct % 4) * 128:(ct % 4 + 1) * 128], Mf[:, ct, h, :], ident), reads=[c_Mf[ct], c_cst], writes=[c_ps[1 + h]])
                    S.op('dve', lambda e, ct=ct, h=h: e.tensor_copy(out=wst[0:64, 2 * ct + h, :], in_=PS(1 + h)[0:64, (ct % 4) * 128 + 64 * h:(ct % 4) * 128 + 64 * h + 64]), reads=[c_ps[1 + h]], writes=[c_wst])
            S.dma('sp', O['wkv_p'][:, :, :].rearrange("h i j -> i h j"), wst[0:64, :, :], reads=[c_wst])
            S.barrier()

        env = dict(locals())
        for li in range(DEPTH):
            kind = li % 3
            if mixers[li]:
                if kind == 0:
                    s5_layer(li, li // 3)
                elif kind == 1:
                    rwkv_layer(li)
                else:
                    lru_layer(li)
            dump(2 * li)
            if ffn:
                ffn_layer(li)
            dump(2 * li + 1)
        final_out()
        S.finish()
        S.emit()
    return nc


def s5_layer(env, li, j):
    raise NotImplementedError


def rwkv_layer(env, li):
    raise NotImplementedError


def lru_layer(env, li):
    raise NotImplementedError


OUT_ORDER = ['y_prompt', 'y_sample', 's5_re_p', 's5_re_s', 's5_im_p', 's5_im_s', 'wkv_p', 'wkv_s',
             'shift_p', 'shift_s', 'lru_h_p', 'lru_h_s', 'lru_conv_p', 'lru_conv_s', 'ffn_conv_p', 'ffn_conv_s']


def make_in_maps(inputs):
    f = lambda a: np.ascontiguousarray(np.asarray(a, dtype=np.float32))
    w = {}
    for name in ['norm_mix', 'norm_ffn', 's5_a_re', 's5_a_im', 's5_log_dt', 's5_b_re', 's5_b_im', 's5_c_re', 's5_c_im',
                 's5_d', 's5_w_glu', 'ffn_w_in', 'ffn_conv_w', 'ffn_w_out']:
        w[name] = f(inputs[name])
    w['norm_final'] = f(inputs['norm_final']).reshape(1, D)
    w['ffn_conv_b'] = f(inputs['ffn_conv_b']).reshape(4, 1, DFF)
    for name in ['rw_mu', 'rw_w_rkv', 'rw_w1', 'rw_w2', 'rw_a1', 'rw_a2', 'rw_g1', 'rw_g2', 'rw_w_o',
                 'lru_w_in', 'lru_conv_w', 'lru_w_rg', 'lru_w_ig', 'lru_w_out']:
        w[name] = f(inputs[name])[0]
    for name in ['rw_w0', 'rw_a0', 'rw_k_k', 'rw_k_a', 'rw_ln_w', 'rw_ln_b', 'lru_conv_b', 'lru_b_rg', 'lru_b_ig', 'lru_lambda']:
        w[name] = f(inputs[name]).reshape(1, D)
    w['rw_r_k'] = f(inputs['rw_r_k']).reshape(1, D)
    w['consts'] = CONSTS
    w['cmask'] = CMASK
    xp = f(inputs['x_prompt'])
    xs = f(inputs['x_sample']).reshape(128, D)
    maps = []
    for c in range(8):
        sl = slice(c * NS, (c + 1) * NS)
        m = dict(w)
        m['x_prompt'] = xp[c]
        m['x_sample'] = xs[sl]
        m['state_s5_re'] = f(inputs['state_s5_re'])[:, sl].reshape(2, NS, 4096)
        m['state_s5_im'] = f(inputs['state_s5_im'])[:, sl].reshape(2, NS, 4096)
        m['state_rwkv_wkv'] = f(inputs['state_rwkv_wkv'])[0, sl]
        m['state_rwkv_shift'] = f(inputs['state_rwkv_shift'])[0, sl]
        m['state_lru_h'] = f(inputs['state_lru_h'])[0, sl]
        m['state_lru_conv'] = f(inputs['state_lru_conv'])[0, sl]
        m['state_ffn_conv'] = f(inputs['state_ffn_conv'])[:, sl]
        maps.append({k: np.ascontiguousarray(v) for k, v in m.items()})
    return maps


def gather(results):
    cat = lambda name, axis: np.concatenate([r[name] for r in results], axis=axis)
    stk = lambda name, axis: np.stack([r[name] for r in results], axis=axis)
    out = {}
    out['y_prompt'] = stk('y_prompt', 0)
    out['y_sample'] = cat('y_sample', 0).reshape(128, 1, D)
    out['s5_re_p'] = stk('s5_re_p', 1).reshape(2, 8, 64, 64)
    out['s5_im_p'] = stk('s5_im_p', 1).reshape(2, 8, 64, 64)
    out['s5_re_s'] = cat('s5_re_s', 1).reshape(2, 128, 64, 64)
    out['s5_im_s'] = cat('s5_im_s', 1).reshape(2, 128, 64, 64)
    out['wkv_p'] = stk('wkv_p', 0)[None]
    out['wkv_s'] = cat('wkv_s', 0)[None]
    out['shift_p'] = cat('shift_p', 0)[None]
    out['shift_s'] = cat('shift_s', 0)[None]
    out['lru_h_p'] = cat('lru_h_p', 0)[None]
    out['lru_h_s'] = cat('lru_h_s', 0)[None]
    out['lru_conv_p'] = stk('lru_conv_p', 0)[None]
    out['lru_conv_s'] = cat('lru_conv_s', 0)[None]
    out['ffn_conv_p'] = stk('ffn_conv_p', 1)
    out['ffn_conv_s'] = cat('ffn_conv_s', 1)
    return tuple(np.ascontiguousarray(out[k], dtype=np.float32) for k in OUT_ORDER)


_NC_CACHE = {}


def kernel(**inputs):
    if 'nc' not in _NC_CACHE:
        _NC_CACHE['nc'] = build()
    nc = _NC_CACHE['nc']
    res = run_bass_kernel_spmd(nc, make_in_maps(inputs), core_ids=list(range(8)))
    return gather(res.results)
```

```python
import contextlib
import math
import numpy as np
import concourse.bass as bass
import concourse.mybir as mybir
from concourse.bass_utils import run_bass_kernel_spmd

F32 = mybir.dt.float32
BF16 = mybir.dt.bfloat16
I32 = mybir.dt.int32
AF = mybir.ActivationFunctionType
ALU = mybir.AluOpType
AX = mybir.AxisListType

ENGS = ['pe', 'act', 'dve', 'pool', 'sp']
N_DMA_SEMS = 32

NP = 2048
NS = 16
TT = NP + NS
D = 1024
DFF = 2816
NJ = 22
DEPTH = 4
EPS = 1e-6
S5_STAGE = 9


class Cell:
    __slots__ = ('w', 'r')

    def __init__(self):
        self.w = None
        self.r = {}


def cells(*shape):
    if len(shape) == 1:
        return [Cell() for _ in range(shape[0])]
    return [cells(*shape[1:]) for _ in range(shape[0])]


def flat(x):
    if isinstance(x, Cell):
        return [x]
    out = []
    for y in x:
        out.extend(flat(y))
    return out


class Sched:
    def __init__(self, nc):
        self.nc = nc
        self.ops = {e: [] for e in ENGS}
        self.cnt = {e: 0 for e in ENGS}
        self.seen = {e: {} for e in ENGS}
        self.dma_rr = 0
        self.dma_rr2 = 0
        for i in range(N_DMA_SEMS):
            self.cnt[('d', i)] = 0

    def _deps(self, reads, writes):
        deps = {}
        for c in reads:
            if c.w is not None:
                k, v = c.w
                if deps.get(k, 0) < v:
                    deps[k] = v
        for c in writes:
            if c.w is not None:
                k, v = c.w
                if deps.get(k, 0) < v:
                    deps[k] = v
            for k, v in c.r.items():
                if deps.get(k, 0) < v:
                    deps[k] = v
        return deps

    def _waits(self, eng, deps):
        waits = []
        seen = self.seen[eng]
        for k, v in deps.items():
            if k == eng and eng in ('pe', 'sp'):
                continue
            if seen.get(k, 0) < v:
                seen[k] = v
                waits.append((k, v))
        return waits

    def op(self, eng, fn, reads=(), writes=()):
        reads = flat(reads)
        writes = flat(writes)
        waits = self._waits(eng, self._deps(reads, writes))
        self.cnt[eng] += 1
        seq = self.cnt[eng]
        self.ops[eng].append((waits, fn, (eng, 1)))
        for c in reads:
            if c.r.get(eng, 0) < seq:
                c.r[eng] = seq
        for c in writes:
            c.w = (eng, seq)
            c.r = {}

    def dma(self, eng, out, in_, reads=(), writes=(), **kw):
        reads = flat(reads)
        writes = flat(writes)
        half = N_DMA_SEMS // 2
        if eng == 'pool':
            key = ('d', half + self.dma_rr2)
            self.dma_rr2 = (self.dma_rr2 + 1) % half
        else:
            key = ('d', self.dma_rr)
            self.dma_rr = (self.dma_rr + 1) % half
        deps = self._deps(reads, writes)
        prev = self.cnt[key]
        if prev > 0:
            deps[key] = max(deps.get(key, 0), prev)
        waits = self._waits(eng, deps)
        self.cnt[key] += 16
        val = self.cnt[key]

        def fn(e, out=out, in_=in_, kw=kw):
            return e.dma_start(out=out, in_=in_, **kw)
        self.ops[eng].append((waits, fn, (key, 16)))
        for c in reads:
            c.r[key] = val
        for c in writes:
            c.w = (key, val)
            c.r = {}

    def barrier(self, engines=('pe', 'act', 'dve', 'pool', 'sp')):
        snap = dict(self.cnt)
        for e in engines:
            deps = {k: v for k, v in snap.items() if v > 0 and k != e}
            waits = self._waits(e, deps)
            if waits:
                self.ops[e].append((waits, None, None))

    def finish(self):
        snap = dict(self.cnt)
        for e in ('sp',):
            deps = {k: v for k, v in snap.items() if v > 0 and k != e}
            waits = self._waits(e, deps)
            self.ops[e].append((waits, None, None))

    def emit(self):
        nc = self.nc
        sems = {}
        with contextlib.ExitStack() as st:
            for k in self.cnt:
                name = k if isinstance(k, str) else 'dma%d' % k[1]
                sems[k] = st.enter_context(nc.semaphore('s_' + name))
            block = st.enter_context(nc.Block())

            def run(engname):
                def body(e):
                    for waits, fn, inc in self.ops[engname]:
                        for k, v in waits:
                            e.wait_ge(sems[k], v)
                        if fn is not None:
                            fn(e).then_inc(sems[inc[0]], inc[1])
                return body
            block.tensor(run('pe'))
            block.scalar(run('act'))
            block.vector(run('dve'))
            block.gpsimd(run('pool'))
            block.sync(run('sp'))


VROWS = {}


def _vrow(name, n=1):
    VROWS[name] = (len_vrows[0], n)
    len_vrows[0] += n


len_vrows = [0]
_vrow('norm_mix', 4)
_vrow('norm_ffn', 4)
_vrow('norm_final', 1)
_vrow('s5_d', 2)
_vrow('rw_mu', 6)
_vrow('rw_w0')
_vrow('rw_a0')
_vrow('rw_k_k')
_vrow('rw_k_a')
_vrow('rw_r_k')
_vrow('rw_ln_w')
_vrow('rw_ln_b')
_vrow('lru_conv_w', 4)
_vrow('lru_conv_b')
_vrow('lru_b_rg')
_vrow('lru_b_ig')
_vrow('lru_lambda')
NV = len_vrows[0]

CST_COLS = {}


def make_consts():
    cols = []
    off = 0

    def add(name, arr):
        nonlocal off
        arr = np.asarray(arr, np.float32)
        CST_COLS[name] = (off, arr.shape[1])
        off += arr.shape[1]
        cols.append(arr)
    p = np.arange(128)
    add('ident', np.eye(128))
    add('ones', np.ones((128, 128)))
    add('blk64', (p[:, None] // 64 == p[None, :] // 64).astype(np.float32))
    add('mask_hi', (p[:, None] // 64 == np.arange(2)[None, :]).astype(np.float32))
    add('mask_c16', (((p[:, None] % 32) // 16) == np.arange(2)[None, :]).astype(np.float32))
    add('maskq', (p[:, None] // 32 == np.arange(4)[None, :]).astype(np.float32))
    add('i2', (p[:, None] % 64 == np.arange(64)[None, :]).astype(np.float32))
    s64 = np.arange(64)
    us = (s64[:, None] < s64[None, :]).astype(np.float32)
    ui = (s64[:, None] <= s64[None, :]).astype(np.float32)
    ls = (s64[:, None] > s64[None, :]).astype(np.float32)
    m5 = np.concatenate([us, us, ls, ui, -ui], 1)
    add('mask5', np.concatenate([m5, m5], 0))
    add('imp', np.concatenate([np.eye(64), np.eye(64)], 0))
    s = np.arange(64)
    up_strict = (s[:, None] < s[None, :]).astype(np.float32)
    up_incl = (s[:, None] <= s[None, :]).astype(np.float32)
    add('tri_s', np.concatenate([up_strict, up_strict], 0))
    add('tri_i', np.concatenate([up_incl, up_incl], 0))
    return np.concatenate(cols, 1)


CONSTS = make_consts()
CMASK = np.tile((np.arange(NP) % 64 != 0).astype(np.float32)[None, :], (128, 1))
NCST = CONSTS.shape[1]


def build(mixers=(True, True, True, True), ffn=True, dbg=False):
    nc = bass.Bass("TRN2", target_bir_lowering=False)
    S = Sched(nc)
    I = {}
    O = {}

    def din(name, shape, dt=F32):
        I[name] = nc.dram_tensor(name, list(shape), dt, kind="ExternalInput").ap()

    def dout(name, shape, dt=F32):
        O[name] = nc.dram_tensor(name, list(shape), dt, kind="ExternalOutput").ap()

    din('x_prompt', [NP, D])
    din('x_sample', [NS, D])
    din('state_s5_re', [2, NS, 4096])
    din('state_s5_im', [2, NS, 4096])
    din('state_rwkv_wkv', [NS, 16, 64, 64])
    din('state_rwkv_shift', [NS, D])
    din('state_lru_h', [NS, D])
    din('state_lru_conv', [NS, 3, D])
    din('state_ffn_conv', [4, NS, 2, DFF])
    din('norm_mix', [4, D]); din('norm_ffn', [4, D]); din('norm_final', [1, D])
    din('s5_a_re', [2, 64, 64]); din('s5_a_im', [2, 64, 64]); din('s5_log_dt', [2, 64])
    din('s5_b_re', [2, 64, 64, 16]); din('s5_b_im', [2, 64, 64, 16])
    din('s5_c_re', [2, 64, 16, 64]); din('s5_c_im', [2, 64, 16, 64])
    din('s5_d', [2, D]); din('s5_w_glu', [2, D, 2 * D])
    din('rw_mu', [6, D]); din('rw_w_rkv', [3, D, D]); din('rw_w0', [1, D]); din('rw_w1', [D, 64])
    din('rw_w2', [64, D]); din('rw_a0', [1, D]); din('rw_a1', [D, 64]); din('rw_a2', [64, D])
    din('rw_g1', [D, 128]); din('rw_g2', [128, D]); din('rw_k_k', [1, D]); din('rw_k_a', [1, D])
    din('rw_r_k', [1, D]); din('rw_ln_w', [1, D]); din('rw_ln_b', [1, D]); din('rw_w_o', [D, D])
    din('lru_w_in', [D, 2 * D]); din('lru_conv_w', [4, D]); din('lru_conv_b', [1, D])
    din('lru_w_rg', [4, 256, 256]); din('lru_b_rg', [1, D]); din('lru_w_ig', [4, 256, 256])
    din('lru_b_ig', [1, D]); din('lru_lambda', [1, D]); din('lru_w_out', [D, D])
    din('ffn_w_in', [4, D, 2 * DFF]); din('ffn_conv_w', [4, 3, DFF]); din('ffn_conv_b', [4, 1, DFF])
    din('ffn_w_out', [4, DFF, D])
    din('consts', [128, NCST])
    din('cmask', [128, NP])

    dout('y_prompt', [NP, D]); dout('y_sample', [NS, D])
    dout('s5_re_p', [2, 4096]); dout('s5_re_s', [2, NS, 4096])
    dout('s5_im_p', [2, 4096]); dout('s5_im_s', [2, NS, 4096])
    dout('wkv_p', [16, 64, 64]); dout('wkv_s', [NS, 16, 64, 64])
    dout('shift_p', [1, D]); dout('shift_s', [NS, D])
    dout('lru_h_p', [1, D]); dout('lru_h_s', [NS, D])
    dout('lru_conv_p', [3, D]); dout('lru_conv_s', [NS, 3, D])
    dout('ffn_conv_p', [4, 2, DFF]); dout('ffn_conv_s', [4, NS, 2, DFF])
    if dbg:
        dout('dbg', [9, D, TT])

    with contextlib.ExitStack() as st:
        def sb(name, shape, dt=F32):
            return st.enter_context(nc.sbuf_tensor(name, list(shape), dt))

        xres = sb('xres', [128, 8, TT]); c_x = cells(8, 5)
        hb = sb('hb', [128, 8, TT], BF16); c_h = cells(8, 5)
        cst = sb('cst', [128, NCST]); c_cst = Cell()
        identb = sb('identb', [128, 128], BF16)
        onesb = sb('onesb', [128, 128], BF16)
        blk64b = sb('blk64b', [128, 128], BF16)
        c_cb = Cell()
        vec = sb('vec', [128, 8, NV]); c_vec = Cell()
        epsc = sb('epsc', [128, 1]); c_eps = Cell()
        ARENA_W = 26800
        arena = sb('arena', [128, ARENA_W])
        psum = st.enter_context(nc.psum_tensor('psum', [128, 8, 512], F32))
        c_ps = cells(8)

        def PS(b):
            return psum[:, b, :]

        def cc(name):
            o, n = CST_COLS[name]
            return cst[:, o:o + n]

        class Arena:
            def __init__(self):
                self.off = 0

            def f32(self, n):
                a = arena[:, self.off:self.off + n]
                self.off += n
                assert self.off <= ARENA_W, self.off
                return a

            def bf16(self, n):
                n2 = (n + 1) // 2
                a = arena[:, self.off:self.off + n2].bitcast(BF16)
                self.off += n2
                assert self.off <= ARENA_W, self.off
                return a[:, 0:n]

        BLKS = [(0, 512), (512, 512), (1024, 512), (1536, 512), (NP, NS)]

        def vcol(name, ct, k=0):
            r0, _ = VROWS[name]
            return vec[:, ct, r0 + k:r0 + k + 1]

        S.dma('sp', cst[:], I['consts'][:, :], writes=[c_cst])
        o_id = CST_COLS['ident'][0]
        S.op('dve', lambda e: e.tensor_copy(out=identb[:], in_=cc('ident')), reads=[c_cst], writes=[c_cb])
        S.op('dve', lambda e: e.tensor_copy(out=onesb[:], in_=cc('ones')), reads=[c_cst], writes=[c_cb])
        S.op('dve', lambda e: e.tensor_copy(out=blk64b[:], in_=cc('blk64')), reads=[c_cst], writes=[c_cb])
        S.op('dve', lambda e: e.memset(epsc[:], EPS), writes=[c_eps])
        ident = cc('ident')

        A = Arena()
        vst = A.f32(D)
        c_vst = Cell()
        for name, (r0, n) in VROWS.items():
            S.dma('sp', vst[r0:r0 + n, :], I[name][:, :], writes=[c_vst])
        for ct in range(8):
            S.op('pe', lambda e, ct=ct: e.transpose(PS(0)[:, ct * NV:(ct + 1) * NV], vst[0:NV, ct * 128:(ct + 1) * 128], ident[0:NV, 0:NV]),
                 reads=[c_vst, c_cst], writes=[c_ps[0]])
        S.op('dve', lambda e: e.tensor_copy(out=vec[:].rearrange("p a b -> p (a b)"), in_=PS(0)[:, 0:8 * NV]), reads=[c_ps[0]], writes=[c_vec])

        xin = [A.f32(4 * D), A.f32(4 * D)]
        c_xin = cells(2)
        k = 0
        for tb in range(4):
            S.dma('sp', xin[tb % 2].rearrange("p (a c) -> p a c", a=4),
                  I['x_prompt'][tb * 512:(tb + 1) * 512, :].rearrange("(a p) c -> p a c", p=128), writes=[c_xin[tb % 2]])
            for ct in range(8):
                b = 1 + (k % 2)
                for a in range(4):
                    S.op('pe', lambda e, tb=tb, ct=ct, a=a, b=b: e.transpose(
                        PS(b)[:, a * 128:(a + 1) * 128], xin[tb % 2][:, a * D + ct * 128: a * D + (ct + 1) * 128], ident),
                        reads=[c_xin[tb % 2], c_cst], writes=[c_ps[b]])
                eng = 'act' if k % 2 == 0 else 'dve'
                if eng == 'act':
                    S.op('act', lambda e, tb=tb, ct=ct, b=b: e.activation(out=xres[:, ct, tb * 512:(tb + 1) * 512], in_=PS(b), func=AF.Copy),
                         reads=[c_ps[b]], writes=[c_x[ct][tb]])
                else:
                    S.op('dve', lambda e, tb=tb, ct=ct, b=b: e.tensor_copy(out=xres[:, ct, tb * 512:(tb + 1) * 512], in_=PS(b)),
                         reads=[c_ps[b]], writes=[c_x[ct][tb]])
                k += 1
        xsin = A.f32(D)
        c_xsin = Cell()
        S.dma('sp', xsin[0:NS, :], I['x_sample'][:, :], writes=[c_xsin])
        for ct in range(8):
            S.op('pe', lambda e, ct=ct: e.transpose(PS(3)[:, ct * NS:(ct + 1) * NS], xsin[0:NS, ct * 128:(ct + 1) * 128], ident[0:NS, 0:NS]),
                 reads=[c_xsin, c_cst], writes=[c_ps[3]])
        S.op('dve', lambda e: e.tensor_copy(out=xres[:, :, NP:TT], in_=PS(3)[:, 0:8 * NS].rearrange("p (a b) -> p a b", a=8)),
             reads=[c_ps[3]], writes=[c_x[ct][4] for ct in range(8)])
        S.barrier()

        def rmsnorm(gname, gk, A, block_cb=None):
            sq = [A.bf16(512), A.bf16(512)]
            c_sq = cells(2)
            rstd = [A.f32(512), A.f32(512)]
            c_rs = cells(2)
            k = 0
            for bi, (c0, n) in enumerate(BLKS):
                pb = 6 + (bi % 2)
                for ct in range(8):
                    q = k % 2
                    S.op('pool', lambda e, ct=ct, c0=c0, n=n, q=q: e.tensor_tensor(out=sq[q][:, 0:n], in0=xres[:, ct, c0:c0 + n], in1=xres[:, ct, c0:c0 + n], op=ALU.mult),
                         reads=[c_x[ct][bi]], writes=[c_sq[q]])
                    S.op('pe', lambda e, ct=ct, n=n, q=q, pb=pb: e.matmul(PS(pb)[:, 0:n], lhsT=onesb[:], rhs=sq[q][:, 0:n], start=(ct == 0), stop=(ct == 7)),
                         reads=[c_sq[q], c_cb], writes=[c_ps[pb]])
                    k += 1
                r = bi % 2
                S.op('act', lambda e, n=n, r=r, pb=pb: e.activation(out=rstd[r][:, 0:n], in_=PS(pb)[:, 0:n], func=AF.Ln, bias=epsc[:, 0:1], scale=1.0 / D),
                     reads=[c_ps[pb], c_eps], writes=[c_rs[r]])
                S.op('act', lambda e, n=n, r=r: e.activation(out=rstd[r][:, 0:n], in_=rstd[r][:, 0:n], func=AF.Exp, scale=-0.5),
                     reads=[c_rs[r]], writes=[c_rs[r]])
                if block_cb is not None:
                    block_cb(bi, c0, n, rstd[r], c_rs[r])
                    continue
                for ct in range(8):
                    S.op('dve', lambda e, ct=ct, c0=c0, n=n, r=r: e.scalar_tensor_tensor(
                        out=hb[:, ct, c0:c0 + n], in0=xres[:, ct, c0:c0 + n], scalar=vcol(gname, ct, gk), in1=rstd[r][:, 0:n], op0=ALU.mult, op1=ALU.mult),
                        reads=[c_x[ct][bi], c_rs[r], c_vec], writes=[c_h[ct][bi]])

        def wload(dst, src_rows_ap, kt, c_dst):
            S.dma('pool', dst, src_rows_ap.rearrange("(k p) c -> p k c", p=128), writes=[c_dst])

        def ffn_layer(li):
            A = Arena()
            rmsnorm('norm_ffn', li, A)
            act = A.bf16(NJ * 528).rearrange("p (j n) -> p j n", j=NJ)
            c_act = cells(NJ)
            NWB = 4
            win = [A.bf16(8 * 256).rearrange("p (k c) -> p k c", k=8) for _ in range(NWB)]
            c_win = cells(NWB, 2)
            wo = [A.bf16(NJ * 128).rearrange("p (j c) -> p j c", j=NJ) for _ in range(2)]
            c_wo = cells(2)
            NQ = 3
            G = [A.f32(514) for _ in range(NQ)]
            c_G = cells(NQ)
            acc = [A.f32(512) for _ in range(NQ)]
            c_acc = cells(NQ)
            halo = A.f32(2 * NJ).rearrange("p (s j) -> p s j", s=2)
            c_halo = cells(NJ)
            cw = A.f32(NJ * 4).rearrange("p (j k) -> p j k", j=NJ)
            c_cw = Cell()
            stS = A.f32(NJ * 32).rearrange("p (j b s) -> p j b s", j=NJ, b=NS)
            c_stS = Cell()
            gnew = A.f32(NJ * NS).rearrange("p (j b) -> p j b", j=NJ)
            c_gnew = cells(NJ)
            accs = [A.f32(NS) for _ in range(3)]
            c_accs = cells(3)
            stg = A.f32(DFF)
            c_stg = Cell()
            S.dma('sp', stg[0:3, :], I['ffn_conv_w'][li, :, :], writes=[c_stg])
            S.dma('sp', stg[3:4, :], I['ffn_conv_b'][li, :, :], writes=[c_stg])
            for j in range(NJ):
                S.op('pe', lambda e, j=j: e.transpose(PS(5)[:, j * 4:(j + 1) * 4], stg[0:4, j * 128:(j + 1) * 128], ident[0:4, 0:4]),
                     reads=[c_stg, c_cst], writes=[c_ps[5]])
            S.op('dve', lambda e: e.tensor_copy(out=cw[:].rearrange("p j k -> p (j k)"), in_=PS(5)[:, 0:NJ * 4]), reads=[c_ps[5]], writes=[c_cw])
            S.dma('sp', stg[0:32, :], I['state_ffn_conv'][li, :, :, :].rearrange("b s f -> (b s) f"), reads=[c_ps[5]], writes=[c_stg])
            for j0 in range(0, NJ, 16):
                j1 = min(NJ, j0 + 16)
                for j in range(j0, j1):
                    S.op('pe', lambda e, j=j, j0=j0: e.transpose(PS(5)[:, (j - j0) * 32:(j - j0 + 1) * 32], stg[0:32, j * 128:(j + 1) * 128], ident[0:32, 0:32]),
                         reads=[c_stg, c_cst], writes=[c_ps[5]])
                S.op('dve', lambda e, j0=j0, j1=j1: e.tensor_copy(out=stS[:, j0:j1, :, :].rearrange("p j b s -> p (j b s)"), in_=PS(5)[:, 0:(j1 - j0) * 32]),
                     reads=[c_ps[5]], writes=[c_stS])
            S.dma('sp', O['ffn_conv_s'][li, :, 0, :], I['state_ffn_conv'][li, :, 1, :])

            kw = 0
            ko = 0
            for bi in range(4):
                c0 = bi * 512
                last = (bi == 3)
                for j in range(NJ):
                    wb = kw % NWB
                    wload(win[wb][:, :, 0:128], I['ffn_w_in'][li, :, j * 128:(j + 1) * 128], 8, c_win[wb][0])
                    wload(win[wb][:, :, 128:256], I['ffn_w_in'][li, :, DFF + j * 128:DFF + (j + 1) * 128], 8, c_win[wb][1])
                    gb = kw % 3
                    ub = 3 + (kw % 3)
                    q = kw % NQ
                    kw += 1
                    for kt in range(8):
                        S.op('pe', lambda e, kt=kt, wb=wb, gb=gb, c0=c0: e.matmul(PS(gb), lhsT=win[wb][:, kt, 0:128], rhs=hb[:, kt, c0:c0 + 512], start=(kt == 0), stop=(kt == 7)),
                             reads=[c_win[wb][0], c_h[kt][bi]], writes=[c_ps[gb]])
                    for kt in range(8):
                        S.op('pe', lambda e, kt=kt, wb=wb, ub=ub, c0=c0: e.matmul(PS(ub), lhsT=win[wb][:, kt, 128:256], rhs=hb[:, kt, c0:c0 + 512], start=(kt == 0), stop=(kt == 7)),
                             reads=[c_win[wb][1], c_h[kt][bi]], writes=[c_ps[ub]])
                    if last:
                        for kt in range(8):
                            S.op('pe', lambda e, kt=kt, wb=wb: e.matmul(PS(7)[:, 0:NS], lhsT=win[wb][:, kt, 0:128], rhs=hb[:, kt, NP:TT], start=(kt == 0), stop=(kt == 7)),
                                 reads=[c_win[wb][0], c_h[kt][4]], writes=[c_ps[7]])
                        for kt in range(8):
                            S.op('pe', lambda e, kt=kt, wb=wb: e.matmul(PS(7)[:, 32:32 + NS], lhsT=win[wb][:, kt, 128:256], rhs=hb[:, kt, NP:TT], start=(kt == 0), stop=(kt == 7)),
                                 reads=[c_win[wb][1], c_h[kt][4]], writes=[c_ps[7]])
                    S.op('act', lambda e, q=q, gb=gb: e.activation(out=G[q][:, 2:514], in_=PS(gb), func=AF.Copy), reads=[c_ps[gb]], writes=[c_G[q]])
                    if bi == 0:
                        S.op('dve', lambda e, q=q: e.memset(G[q][:, 0:2], 0.0), writes=[c_G[q]])
                    else:
                        S.op('dve', lambda e, q=q, j=j: e.tensor_copy(out=G[q][:, 0:2], in_=halo[:, :, j]), reads=[c_halo[j]], writes=[c_G[q]])
                    S.op('act', lambda e, q=q, gb=gb, j=j: e.activation(out=acc[q][:], in_=PS(gb), func=AF.Identity, bias=cw[:, j, 3:4], scale=cw[:, j, 2:3]),
                         reads=[c_ps[gb], c_cw], writes=[c_acc[q]])
                    S.op('dve', lambda e, q=q, j=j: e.scalar_tensor_tensor(out=acc[q][:], in0=G[q][:, 1:513], scalar=cw[:, j, 1:2], in1=acc[q][:], op0=ALU.mult, op1=ALU.add),
                         reads=[c_G[q], c_acc[q], c_cw], writes=[c_acc[q]])
                    S.op('dve', lambda e, q=q, j=j: e.scalar_tensor_tensor(out=acc[q][:], in0=G[q][:, 0:512], scalar=cw[:, j, 0:1], in1=acc[q][:], op0=ALU.mult, op1=ALU.add),
                         reads=[c_G[q], c_acc[q], c_cw], writes=[c_acc[q]])
                    S.op('dve', lambda e, q=q, j=j: e.tensor_copy(out=halo[:, :, j], in_=G[q][:, 512:514]), reads=[c_G[q]], writes=[c_halo[j]])
                    S.op('act', lambda e, q=q: e.activation(out=acc[q][:], in_=acc[q][:], func=AF.Silu), reads=[c_acc[q]], writes=[c_acc[q]])
                    S.op('dve', lambda e, q=q, ub=ub, j=j: e.tensor_tensor(out=act[:, j, 0:512], in0=acc[q][:], in1=PS(ub), op=ALU.mult),
                         reads=[c_acc[q], c_ps[ub]], writes=[c_act[j]])
                    if last:
                        S.op('dve', lambda e, q=q, j=j: e.tensor_scalar(out=accs[q][:], in0=PS(7)[:, 0:NS], scalar1=cw[:, j, 2:3], scalar2=cw[:, j, 3:4], op0=ALU.mult, op1=ALU.add),
                             reads=[c_ps[7], c_cw], writes=[c_accs[q]])
                        S.op('dve', lambda e, q=q, j=j: e.scalar_tensor_tensor(out=accs[q][:], in0=stS[:, j, :, 1], scalar=cw[:, j, 1:2], in1=accs[q][:], op0=ALU.mult, op1=ALU.add),
                             reads=[c_stS, c_accs[q], c_cw], writes=[c_accs[q]])
                        S.op('dve', lambda e, q=q, j=j: e.scalar_tensor_tensor(out=accs[q][:], in0=stS[:, j, :, 0], scalar=cw[:, j, 0:1], in1=accs[q][:], op0=ALU.mult, op1=ALU.add),
                             reads=[c_stS, c_accs[q], c_cw], writes=[c_accs[q]])
                        S.op('act', lambda e, q=q: e.activation(out=accs[q][:], in_=accs[q][:], func=AF.Silu), reads=[c_accs[q]], writes=[c_accs[q]])
                        S.op('act', lambda e, j=j: e.activation(out=gnew[:, j, :], in_=PS(7)[:, 0:NS], func=AF.Copy), reads=[c_ps[7]], writes=[c_gnew[j]])
                        S.op('dve', lambda e, q=q, j=j: e.tensor_tensor(out=act[:, j, 512:528], in0=accs[q][:], in1=PS(7)[:, 32:32 + NS], op=ALU.mult),
                             reads=[c_accs[q], c_ps[7]], writes=[c_act[j]])
                for mt in range(8):
                    wq = ko % 2
                    ob = 6 + (ko % 2)
                    ko += 1
                    wload(wo[wq][:], I['ffn_w_out'][li, :, mt * 128:(mt + 1) * 128], NJ, c_wo[wq])
                    for j in range(NJ):
                        S.op('pe', lambda e, j=j, wq=wq, ob=ob: e.matmul(PS(ob), lhsT=wo[wq][:, j, :], rhs=act[:, j, 0:512], start=(j == 0), stop=(j == NJ - 1)),
                             reads=[c_wo[wq], c_act[j]], writes=[c_ps[ob]])
                    S.op('dve', lambda e, mt=mt, ob=ob, c0=c0: e.tensor_tensor(out=xres[:, mt, c0:c0 + 512], in0=xres[:, mt, c0:c0 + 512], in1=PS(ob), op=ALU.add),
                         reads=[c_ps[ob], c_x[mt][bi]], writes=[c_x[mt][bi]])
                    if last:
                        for j in range(NJ):
                            S.op('pe', lambda e, j=j, wq=wq: e.matmul(PS(5)[:, 0:NS], lhsT=wo[wq][:, j, :], rhs=act[:, j, 512:528], start=(j == 0), stop=(j == NJ - 1)),
                                 reads=[c_wo[wq], c_act[j]], writes=[c_ps[5]])
                        S.op('dve', lambda e, mt=mt: e.tensor_tensor(out=xres[:, mt, NP:TT], in0=xres[:, mt, NP:TT], in1=PS(5)[:, 0:NS], op=ALU.add),
                             reads=[c_ps[5], c_x[mt][4]], writes=[c_x[mt][4]])
            S.op('pe', lambda e: e.transpose(PS(0)[0:2 * NJ, 0:128], halo[:].rearrange("p s j -> p (s j)"), ident), reads=[c_halo, c_cst], writes=[c_ps[0]])
            S.op('dve', lambda e: e.tensor_copy(out=stg[0:2 * NJ, 0:128], in_=PS(0)[0:2 * NJ, 0:128]), reads=[c_ps[0]], writes=[c_stg])
            S.dma('sp', O['ffn_conv_p'][li, :, :].rearrange("s (j p) -> (s j) p", p=128), stg[0:2 * NJ, 0:128], reads=[c_stg])
            stg2 = stg
            c_stg2 = c_stg
            for j0 in range(0, NJ, 4):
                j1 = min(NJ, j0 + 4)
                for j in range(j0, j1):
                    S.op('pe', lambda e, j=j, j0=j0: e.transpose(PS(1)[0:NS, (j - j0) * 128:(j - j0 + 1) * 128], gnew[:, j, :], ident), reads=[c_gnew[j], c_cst], writes=[c_ps[1]])
                S.op('dve', lambda e, j0=j0, j1=j1: e.tensor_copy(out=stg2[0:NS, j0 * 128:j1 * 128], in_=PS(1)[0:NS, 0:(j1 - j0) * 128]), reads=[c_ps[1]], writes=[c_stg2])
            S.dma('sp', O['ffn_conv_s'][li, :, 1, :], stg2[0:NS, :], reads=[c_stg2])
            S.barrier()

        def final_out():
            A = Arena()
            yf = [A.f32(8 * 512).rearrange("p (a t) -> p a t", a=8) for _ in range(2)]
            c_y = cells(2)
            ost = [A.f32(D), A.f32(D)]
            c_ost = cells(2)
            kk = [0]

            def cb(bi, c0, n, rs, c_r):
                yq = bi % 2
                for ct in range(8):
                    S.op('dve', lambda e, ct=ct, c0=c0, n=n, yq=yq: e.scalar_tensor_tensor(
                        out=yf[yq][:, ct, 0:n], in0=xres[:, ct, c0:c0 + n], scalar=vcol('norm_final', ct, 0), in1=rs[:, 0:n], op0=ALU.mult, op1=ALU.mult),
                        reads=[c_x[ct][bi], c_r, c_vec], writes=[c_y[yq]])
                for t4 in range((n + 127) // 128):
                    rows = min(128, n - t4 * 128)
                    q = kk[0] % 2
                    for half in range(2):
                        b = kk[0] % 2
                        kk[0] += 1
                        for a in range(4):
                            ct = half * 4 + a
                            S.op('pe', lambda e, t4=t4, ct=ct, a=a, b=b, yq=yq, rows=rows: e.transpose(PS(b)[0:rows, a * 128:(a + 1) * 128], yf[yq][:, ct, t4 * 128:t4 * 128 + rows], ident),
                                 reads=[c_y[yq], c_cst], writes=[c_ps[b]])
                        if half == 0:
                            S.op('act', lambda e, q=q, b=b, rows=rows: e.activation(out=ost[q][0:rows, 0:512], in_=PS(b)[0:rows, :], func=AF.Copy), reads=[c_ps[b]], writes=[c_ost[q]])
                        else:
                            S.op('dve', lambda e, q=q, b=b, rows=rows: e.tensor_copy(out=ost[q][0:rows, 512:1024], in_=PS(b)[0:rows, :]), reads=[c_ps[b]], writes=[c_ost[q]])
                    if bi < 4:
                        t0 = c0 + t4 * 128
                        S.dma('sp', O['y_prompt'][t0:t0 + 128, :], ost[q][:], reads=[c_ost[q]])
                    else:
                        S.dma('sp', O['y_sample'][:, :], ost[q][0:NS, :], reads=[c_ost[q]])
            rmsnorm('norm_final', 0, A, block_cb=cb)

        def dump(k):
            if dbg:
                for ct in range(8):
                    S.dma('sp', O['dbg'][k, ct * 128:(ct + 1) * 128, :], xres[:, ct, :], reads=[c_x[ct]])


        def lru_layer(li):
            A = Arena()
            rmsnorm('norm_mix', li, A)
            hg = A.bf16(8 * TT).rearrange("p (a t) -> p a t", a=8)
            c_hg = cells(8, 5)
            wt = [A.bf16(8 * 256).rearrange("p (k c) -> p k c", k=8) for _ in range(2)]
            c_wt = cells(2, 2)
            wg2 = [A.bf16(2 * 2 * 256).rearrange("p (g k c) -> p g k c", g=2, k=2) for _ in range(2)]
            c_wg2 = cells(2)
            Ub = [A.f32(3 + 512) for _ in range(2)]
            c_Ub = cells(2)
            uc = [A.f32(528) for _ in range(2)]
            c_uc = cells(2)
            ucb = [A.bf16(528) for _ in range(2)]
            c_ucb = cells(2)
            NT = 6
            tb = [A.f32(528) for _ in range(NT)]
            c_tb = cells(NT)
            uh = A.f32(24).rearrange("p (k a) -> p k a", k=3)
            c_uh = cells(8)
            hprev = A.f32(8)
            c_hp = cells(8)
            lc = A.f32(16).rearrange("p (a k) -> p a k", a=8)
            c_lc = Cell()
            cs = A.f32(8 * 48).rearrange("p (a b k) -> p a b k", a=8, b=NS)
            c_cs = Cell()
            h0 = A.f32(8 * NS).rearrange("p (a b) -> p a b", a=8)
            c_h0 = Cell()
            hs = A.f32(8 * NS).rearrange("p (a b) -> p a b", a=8)
            c_hs = cells(8)
            us = A.f32(8 * NS).rearrange("p (a b) -> p a b", a=8)
            c_us = cells(8)
            stg = A.f32(D)
            c_stg = Cell()
            onec = A.f32(1)
            c_one = Cell()
            S.op('dve', lambda e: e.memset(onec[:], 1.0), writes=[c_one])
            for ct in range(8):
                S.op('act', lambda e, ct=ct: e.activation(out=lc[:, ct, 0:1], in_=vcol('lru_lambda', ct), func=AF.Exp, scale=-1.0), reads=[c_vec], writes=[c_lc])
            S.op('act', lambda e: e.activation(out=lc[:, :, 0:1], in_=lc[:, :, 0:1], func=AF.Ln, bias=onec[:, 0:1], scale=1.0), reads=[c_lc, c_one], writes=[c_lc])
            S.op('dve', lambda e: e.tensor_scalar(out=lc[:, :, 1:2], in0=lc[:, :, 0:1], scalar1=-16.0, scalar2=None, op0=ALU.mult), reads=[c_lc], writes=[c_lc])
            S.op('dve', lambda e: e.tensor_scalar(out=lc[:, :, 0:1], in0=lc[:, :, 0:1], scalar1=-8.0, scalar2=None, op0=ALU.mult), reads=[c_lc], writes=[c_lc])
            S.dma('sp', stg[0:48, :], I['state_lru_conv'][:, :, :].rearrange("b k c -> (b k) c"), writes=[c_stg])
            for ct in range(8):
                S.op('pe', lambda e, ct=ct: e.transpose(PS(7)[:, ct * 48:(ct + 1) * 48], stg[0:48, ct * 128:(ct + 1) * 128], ident[0:48, 0:48]), reads=[c_stg, c_cst], writes=[c_ps[7]])
            S.op('dve', lambda e: e.tensor_copy(out=cs[:].rearrange("p a b k -> p (a b k)"), in_=PS(7)[:, 0:384]), reads=[c_ps[7]], writes=[c_cs])
            S.dma('sp', stg[0:NS, :], I['state_lru_h'][:, :], reads=[c_ps[7]], writes=[c_stg])
            for ct in range(8):
                S.op('pe', lambda e, ct=ct: e.transpose(PS(7)[:, ct * NS:(ct + 1) * NS], stg[0:NS, ct * 128:(ct + 1) * 128], ident[0:NS, 0:NS]), reads=[c_stg, c_cst], writes=[c_ps[7]])
            S.op('dve', lambda e: e.tensor_copy(out=h0[:].rearrange("p a b -> p (a b)"), in_=PS(7)[:, 0:8 * NS]), reads=[c_ps[7]], writes=[c_h0])
            S.dma('sp', O['lru_conv_s'][:, 0:2, :], I['state_lru_conv'][:, 1:3, :])

            kq = 0
            kt_ = 0
            for n in range(4):
                g2 = n % 2
                wload(wg2[g2][:, 0, :, :], I['lru_w_rg'][n, :, :], 2, c_wg2[g2])
                wload(wg2[g2][:, 1, :, :], I['lru_w_ig'][n, :, :], 2, c_wg2[g2])
                for bi in range(4):
                    c0 = bi * 512
                    last = (bi == 3)
                    nn = 528 if last else 512
                    for c2 in range(2):
                        ct = 2 * n + c2
                        wq = kq % 2
                        kq += 1
                        wload(wt[wq][:, :, 0:128], I['lru_w_in'][:, D + ct * 128:D + (ct + 1) * 128], 8, c_wt[wq][0])
                        wload(wt[wq][:, :, 128:256], I['lru_w_in'][:, ct * 128:(ct + 1) * 128], 8, c_wt[wq][1])
                        ub = c2
                        gb = 2 + c2
                        for kt in range(8):
                            S.op('pe', lambda e, kt=kt, wq=wq, ub=ub, c0=c0: e.matmul(PS(ub), lhsT=wt[wq][:, kt, 0:128], rhs=hb[:, kt, c0:c0 + 512], start=(kt == 0), stop=(kt == 7)),
                                 reads=[c_wt[wq][0], c_h[kt][bi]], writes=[c_ps[ub]])
                        for kt in range(8):
                            S.op('pe', lambda e, kt=kt, wq=wq, gb=gb, c0=c0: e.matmul(PS(gb), lhsT=wt[wq][:, kt, 128:256], rhs=hb[:, kt, c0:c0 + 512], start=(kt == 0), stop=(kt == 7)),
                                 reads=[c_wt[wq][1], c_h[kt][bi]], writes=[c_ps[gb]])
                        if last:
                            sb_ = 4
                            for kt in range(8):
                                S.op('pe', lambda e, kt=kt, wq=wq, c2=c2: e.matmul(PS(4)[:, c2 * 64:c2 * 64 + NS], lhsT=wt[wq][:, kt, 0:128], rhs=hb[:, kt, NP:TT], start=(kt == 0), stop=(kt == 7)),
                                     reads=[c_wt[wq][0], c_h[kt][4]], writes=[c_ps[4]])
                            for kt in range(8):
                                S.op('pe', lambda e, kt=kt, wq=wq, c2=c2: e.matmul(PS(4)[:, c2 * 64 + 32:c2 * 64 + 32 + NS], lhsT=wt[wq][:, kt, 128:256], rhs=hb[:, kt, NP:TT], start=(kt == 0), stop=(kt == 7)),
                                     reads=[c_wt[wq][1], c_h[kt][4]], writes=[c_ps[4]])
                        S.op('act', lambda e, c2=c2, ub=ub: e.activation(out=Ub[c2][:, 3:515], in_=PS(ub), func=AF.Copy), reads=[c_ps[ub]], writes=[c_Ub[c2]])
                        if bi == 0:
                            S.op('dve', lambda e, c2=c2: e.memset(Ub[c2][:, 0:3], 0.0), writes=[c_Ub[c2]])
                        else:
                            S.op('dve', lambda e, c2=c2, ct=ct: e.tensor_copy(out=Ub[c2][:, 0:3], in_=uh[:, :, ct]), reads=[c_uh[ct]], writes=[c_Ub[c2]])
                        S.op('dve', lambda e, c2=c2, ub=ub, ct=ct: e.tensor_scalar(out=uc[c2][:, 0:512], in0=PS(ub), scalar1=vcol('lru_conv_w', ct, 3), scalar2=vcol('lru_conv_b', ct), op0=ALU.mult, op1=ALU.add),
                             reads=[c_ps[ub], c_vec], writes=[c_uc[c2]])
                        for k in range(3):
                            S.op('dve', lambda e, c2=c2, ct=ct, k=k: e.scalar_tensor_tensor(out=uc[c2][:, 0:512], in0=Ub[c2][:, k:k + 512], scalar=vcol('lru_conv_w', ct, k), in1=uc[c2][:, 0:512], op0=ALU.mult, op1=ALU.add),
                                 reads=[c_Ub[c2], c_uc[c2], c_vec], writes=[c_uc[c2]])
                        S.op('dve', lambda e, c2=c2, ct=ct: e.tensor_copy(out=uh[:, :, ct], in_=Ub[c2][:, 512:515]), reads=[c_Ub[c2]], writes=[c_uh[ct]])
                        if last:
                            o4 = c2 * 64
                            S.op('dve', lambda e, c2=c2, ct=ct, o4=o4: e.tensor_scalar(out=uc[c2][:, 512:528], in0=PS(4)[:, o4:o4 + NS], scalar1=vcol('lru_conv_w', ct, 3), scalar2=vcol('lru_conv_b', ct), op0=ALU.mult, op1=ALU.add),
                                 reads=[c_ps[4], c_vec], writes=[c_uc[c2]])
                            for k in range(3):
                                S.op('dve', lambda e, c2=c2, ct=ct, k=k: e.scalar_tensor_tensor(out=uc[c2][:, 512:528], in0=cs[:, ct, :, k], scalar=vcol('lru_conv_w', ct, k), in1=uc[c2][:, 512:528], op0=ALU.mult, op1=ALU.add),
                                     reads=[c_cs, c_uc[c2], c_vec], writes=[c_uc[c2]])
                            S.op('act', lambda e, ct=ct, o4=o4: e.activation(out=us[:, ct, :], in_=PS(4)[:, o4:o4 + NS], func=AF.Copy), reads=[c_ps[4]], writes=[c_us[ct]])
                        S.op('act', lambda e, c2=c2, nn=nn: e.activation(out=ucb[c2][:, 0:nn], in_=uc[c2][:, 0:nn], func=AF.Copy), reads=[c_uc[c2]], writes=[c_ucb[c2]])
                    for c2 in range(2):
                        ct = 2 * n + c2
                        rb = 5
                        ib = 6
                        T = [tb[(kt_ + i) % NT] for i in range(3)]
                        cT = [c_tb[(kt_ + i) % NT] for i in range(3)]
                        kt_ += 3
                        segs = [(0, 512)] + ([(512, NS)] if last else [])
                        for (s0, sn) in segs:
                            for k2 in range(2):
                                S.op('pe', lambda e, k2=k2, c2=c2, g2=g2, s0=s0, sn=sn: e.matmul(PS(5)[:, 0:sn], lhsT=wg2[g2][:, 0, k2, c2 * 128:(c2 + 1) * 128], rhs=ucb[k2][:, s0:s0 + sn], start=(k2 == 0), stop=(k2 == 1)),
                                     reads=[c_wg2[g2], c_ucb[k2]], writes=[c_ps[5]])
                            for k2 in range(2):
                                S.op('pe', lambda e, k2=k2, c2=c2, g2=g2, s0=s0, sn=sn: e.matmul(PS(6)[:, 0:sn], lhsT=wg2[g2][:, 1, k2, c2 * 128:(c2 + 1) * 128], rhs=ucb[k2][:, s0:s0 + sn], start=(k2 == 0), stop=(k2 == 1)),
                                     reads=[c_wg2[g2], c_ucb[k2]], writes=[c_ps[6]])
                            S.op('act', lambda e, ct=ct, s0=s0, sn=sn, T=T: e.activation(out=T[0][:, s0:s0 + sn], in_=PS(5)[:, 0:sn], func=AF.Sigmoid, bias=vcol('lru_b_rg', ct), scale=1.0), reads=[c_ps[5], c_vec], writes=[cT[0]])
                            S.op('act', lambda e, ct=ct, s0=s0, sn=sn, T=T: e.activation(out=T[1][:, s0:s0 + sn], in_=PS(6)[:, 0:sn], func=AF.Sigmoid, bias=vcol('lru_b_ig', ct), scale=1.0), reads=[c_ps[6], c_vec], writes=[cT[1]])
                        S.op('act', lambda e, ct=ct, nn=nn, T=T: e.activation(out=T[2][:, 0:nn], in_=T[0][:, 0:nn], func=AF.Exp, scale=lc[:, ct, 1:2]), reads=[cT[0], c_lc], writes=[cT[2]])
                        S.op('act', lambda e, ct=ct, nn=nn, T=T: e.activation(out=T[0][:, 0:nn], in_=T[0][:, 0:nn], func=AF.Exp, scale=lc[:, ct, 0:1]), reads=[cT[0], c_lc], writes=[cT[0]])
                        S.op('act', lambda e, nn=nn, T=T: e.activation(out=T[2][:, 0:nn], in_=T[2][:, 0:nn], func=AF.Sqrt, bias=onec[:, 0:1], scale=-1.0), reads=[cT[2], c_one], writes=[cT[2]])
                        S.op('dve', lambda e, nn=nn, T=T: e.tensor_tensor(out=T[1][:, 0:nn], in0=T[1][:, 0:nn], in1=T[2][:, 0:nn], op=ALU.mult), reads=[cT[1], cT[2]], writes=[cT[1]])
                        S.op('dve', lambda e, nn=nn, T=T, c2=c2: e.tensor_tensor(out=T[1][:, 0:nn], in0=T[1][:, 0:nn], in1=uc[c2][:, 0:nn], op=ALU.mult), reads=[cT[1], c_uc[c2]], writes=[cT[1]])
                        if bi == 0:
                            S.op('dve', lambda e, T=T: e.tensor_tensor_scan(out=T[2][:, 0:512], data0=T[0][:, 0:512], data1=T[1][:, 0:512], initial=0.0, op0=ALU.mult, op1=ALU.add),
                                 reads=[cT[0], cT[1]], writes=[cT[2]])
                        else:
                            S.op('dve', lambda e, T=T, ct=ct: e.tensor_tensor_scan(out=T[2][:, 0:512], data0=T[0][:, 0:512], data1=T[1][:, 0:512], initial=hprev[:, ct:ct + 1], op0=ALU.mult, op1=ALU.add),
                                 reads=[cT[0], cT[1], c_hp[ct]], writes=[cT[2]])
                        S.op('dve', lambda e, T=T, ct=ct: e.tensor_copy(out=hprev[:, ct:ct + 1], in_=T[2][:, 511:512]), reads=[cT[2]], writes=[c_hp[ct]])
                        if last:
                            S.op('dve', lambda e, T=T, ct=ct: e.tensor_tensor(out=T[2][:, 512:528], in0=T[0][:, 512:528], in1=h0[:, ct, :], op=ALU.mult), reads=[cT[0], c_h0], writes=[cT[2]])
                            S.op('dve', lambda e, T=T: e.tensor_tensor(out=T[2][:, 512:528], in0=T[2][:, 512:528], in1=T[1][:, 512:528], op=ALU.add), reads=[cT[1], cT[2]], writes=[cT[2]])
                            S.op('act', lambda e, T=T, ct=ct: e.activation(out=hs[:, ct, :], in_=T[2][:, 512:528], func=AF.Copy), reads=[cT[2]], writes=[c_hs[ct]])
                        gb = 2 + c2
                        S.op('act', lambda e, T=T, gb=gb: e.activation(out=T[0][:, 0:512], in_=PS(gb), func=AF.Gelu_apprx_tanh), reads=[c_ps[gb]], writes=[cT[0]])
                        if last:
                            o4 = c2 * 64 + 32
                            S.op('act', lambda e, T=T, o4=o4: e.activation(out=T[0][:, 512:528], in_=PS(4)[:, o4:o4 + NS], func=AF.Gelu_apprx_tanh), reads=[c_ps[4]], writes=[cT[0]])
                        S.op('dve', lambda e, T=T, ct=ct, c0=c0: e.tensor_tensor(out=hg[:, ct, c0:c0 + 512], in0=T[2][:, 0:512], in1=T[0][:, 0:512], op=ALU.mult), reads=[cT[0], cT[2]], writes=[c_hg[ct][bi]])
                        if last:
                            S.op('dve', lambda e, T=T, ct=ct: e.tensor_tensor(out=hg[:, ct, NP:TT], in0=T[2][:, 512:528], in1=T[0][:, 512:528], op=ALU.mult), reads=[cT[0], cT[2]], writes=[c_hg[ct][4]])
            proj_residual(hg, c_hg, I['lru_w_out'], wt, c_wt)
            S.op('pe', lambda e: e.transpose(PS(0)[0:8, 0:128], hprev[:, 0:8], ident), reads=[c_hp, c_cst], writes=[c_ps[0]])
            S.op('dve', lambda e: e.tensor_copy(out=stg[0:8, 0:128], in_=PS(0)[0:8, 0:128]), reads=[c_ps[0]], writes=[c_stg])
            S.dma('sp', O['lru_h_p'][:, :].rearrange("o (a p) -> (o a) p", p=128), stg[0:8, 0:128], reads=[c_stg])
            S.op('pe', lambda e: e.transpose(PS(1)[0:24, 0:128], uh[:].rearrange("p k a -> p (k a)"), ident), reads=[c_uh, c_cst], writes=[c_ps[1]])
            S.op('dve', lambda e: e.tensor_copy(out=stg[32:56, 0:128], in_=PS(1)[0:24, 0:128]), reads=[c_ps[1]], writes=[c_stg])
            S.dma('sp', O['lru_conv_p'][:, :].rearrange("k (a p) -> (k a) p", p=128), stg[32:56, 0:128], reads=[c_stg])
            fm_to_rows(hs, c_hs, O['lru_h_s'][:, :], stg, c_stg, 64)
            fm_to_rows(us, c_us, O['lru_conv_s'][:, 2, :], stg, c_stg, 96)
            S.barrier()

        def fm_to_rows(src, c_src, dst_ap, stg, c_stg, prow):
            for half in range(2):
                pb = 2 + half
                for a in range(4):
                    ct = half * 4 + a
                    S.op('pe', lambda e, ct=ct, a=a, pb=pb: e.transpose(PS(pb)[0:NS, a * 128:(a + 1) * 128], src[:, ct, :], ident), reads=[c_src[ct], c_cst], writes=[c_ps[pb]])
                S.op('dve', lambda e, half=half, pb=pb: e.tensor_copy(out=stg[prow:prow + NS, half * 512:(half + 1) * 512], in_=PS(pb)[0:NS, :]), reads=[c_ps[pb]], writes=[c_stg])
            S.dma('sp', dst_ap, stg[prow:prow + NS, :], reads=[c_stg])

        def proj_residual(src, c_src, w_ap, wbuf, c_wbuf):
            ko = 0
            for bi in range(4):
                c0 = bi * 512
                last = (bi == 3)
                for mt in range(8):
                    wq = ko % 2
                    ob = 5 + (ko % 2)
                    ko += 1
                    wload(wbuf[wq][:, :, 0:128], w_ap[:, mt * 128:(mt + 1) * 128], 8, c_wbuf[wq])
                    for kt in range(8):
                        S.op('pe', lambda e, kt=kt, wq=wq, ob=ob, c0=c0: e.matmul(PS(ob), lhsT=wbuf[wq][:, kt, 0:128], rhs=src[:, kt, c0:c0 + 512], start=(kt == 0), stop=(kt == 7)),
                             reads=[c_wbuf[wq], c_src[kt][bi]], writes=[c_ps[ob]])
                    S.op('dve', lambda e, mt=mt, ob=ob, c0=c0: e.tensor_tensor(out=xres[:, mt, c0:c0 + 512], in0=xres[:, mt, c0:c0 + 512], in1=PS(ob), op=ALU.add),
                         reads=[c_ps[ob], c_x[mt][bi]], writes=[c_x[mt][bi]])
                    if last:
                        for kt in range(8):
                            S.op('pe', lambda e, kt=kt, wq=wq: e.matmul(PS(7)[:, 0:NS], lhsT=wbuf[wq][:, kt, 0:128], rhs=src[:, kt, NP:TT], start=(kt == 0), stop=(kt == 7)),
                                 reads=[c_wbuf[wq], c_src[kt][4]], writes=[c_ps[7]])
                        S.op('dve', lambda e, mt=mt: e.tensor_tensor(out=xres[:, mt, NP:TT], in0=xres[:, mt, NP:TT], in1=PS(7)[:, 0:NS], op=ALU.add),
                             reads=[c_ps[7], c_x[mt][4]], writes=[c_x[mt][4]])


        def s5_layer(li, j):
            A = Arena()
            R = A.f32(4096)
            c_R = cells(4)
            Rr = [R[:, i * 1024:(i + 1) * 1024] for i in range(4)]
            A0 = Arena()
            A0.off = 0
            rmsnorm('norm_mix', li, A0)
            S.barrier()
            yg = A.bf16(8 * TT).rearrange("p (a t) -> p a t", a=8)
            c_yg = cells(8, 5)
            stg = A.f32(512)
            c_stg = Cell()
            prm = A.f32(96).rearrange("p (k s) -> p k s", k=3)
            c_prm = Cell()
            NSM = 22
            sm = A.f32(NSM * 32).rearrange("p (k s) -> p k s", k=NSM)
            c_sm = Cell()
            Apw = A.f32(9 * 2 * 32).rearrange("p (d k s) -> p d k s", d=9, k=2)
            cks = A.f32(8 * 2 * 32).rearrange("p (l k s) -> p l k s", l=8, k=2)
            Apn = A.f32(9 * 32).rearrange("p (d s) -> p d s", d=9)
            Bst = R[:, 2048:3072].rearrange("p (k s c) -> p k s c", k=2, s=32)
            c_Bst = c_R[2]
            c_Bbar = Cell()
            Bbar = A.f32(2 * 512).rearrange("p (k s c) -> p k s c", k=2, s=32)
            Cn = [A.f32(2 * 64).rearrange("p (k q) -> p k q", k=2) for _ in range(2)]
            c_Cn = cells(2)
            SmT = A.f32(2 * 2 * 32 * 16).rearrange("p (t k s n) -> p t k s n", t=2, k=2, s=32)
            c_SmT = Cell()
            Hfin = A.f32(64).rearrange("p (k s) -> p k s", k=2)
            c_Hfin = Cell()
            CCe = A.f32(256).rearrange("p (k c) -> p k c", k=2)
            c_CCe = Cell()
            CCt = A.f32(256).rearrange("p (k q c) -> p k q c", k=2, q=4)
            c_CCt = Cell()
            CA = A.bf16(4 * 9 * 2 * 32).rearrange("p (q d k c) -> p q d k c", q=4, d=9, k=2)
            c_CA = Cell()
            Wz = A.f32(2 * 128).rearrange("p (k q g c) -> p k q g c", k=2, q=4, g=2)
            c_Wz = cells(2)
            W1 = A.bf16(8 * 2 * 128).rearrange("p (s k c) -> p s k c", s=8, k=2)
            c_W1 = Cell()
            ZBb = A.bf16(2 * 128).rearrange("p (k c) -> p k c", k=2)
            c_ZBb = Cell()
            KTf = A.bf16(8 * 128).rearrange("p (d q c) -> p d q c", d=8, q=4)
            c_KTf = Cell()
            E = A.f32(2 * 1024).rearrange("p (k q n) -> p k q n", k=2, q=4)
            c_E = Cell()
            Et = R[:, 2048:4096].rearrange("p (i q n) -> p i q n", i=4, q=4)
            c_Et = [c_R[2], c_R[3]]
            Hb = A.bf16(2 * 4 * 258).rearrange("p (k q n) -> p k q n", k=2, q=4)
            c_Hb = Cell()
            h0 = A.f32(2 * 64).rearrange("p (k q b) -> p k q b", k=2, q=4)
            c_h0 = Cell()
            hn = A.f32(2 * 64).rearrange("p (k q b) -> p k q b", k=2, q=4)
            c_hn = Cell()
            hnb = A.bf16(2 * 64).rearrange("p (k q b) -> p k q b", k=2, q=4)
            c_hnb = Cell()
            ts_ = A.f32(4 * 64).rearrange("p (i q b) -> p i q b", i=4, q=4)
            c_ts = Cell()
            ysv = A.f32(NS)
            c_ysv = Cell()
            st2 = A.f32(2 * 512).rearrange("p (k c) -> p k c", k=2)
            c_st2 = Cell()
            wgl = [R[:, i * 1024:(i + 1) * 1024].bitcast(BF16).rearrange("p (k c) -> p k c", k=8) for i in range(2)]
            c_wgl = [c_R[0], c_R[1]]
            sg = [R[:, 2048:2576], R[:, 3072:3600]]
            c_sg = [c_R[2], c_R[3]]

            SMI = [0]

            def smn():
                SMI[0] += 1
                assert SMI[0] <= NSM
                return sm[:, SMI[0] - 1, :]

            def dv(fn, eng='dve'):
                S.op(eng, fn, reads=[c_sm, c_prm], writes=[c_sm])

            S.dma('sp', stg[0:32, 0:128], I['s5_a_re'][j, :, :].rearrange("(s g) p -> s (g p)", g=2), writes=[c_stg])
            S.dma('sp', stg[0:32, 128:256], I['s5_a_im'][j, :, :].rearrange("(s g) p -> s (g p)", g=2), writes=[c_stg])
            S.dma('sp', stg[0:32, 256:258], I['s5_log_dt'][j:j + 1, :].rearrange("o (s g) -> (o s) g", g=2), writes=[c_stg])
            S.op('dve', lambda e: e.tensor_copy(out=stg[0:32, 384:512].rearrange("p (g n) -> p g n", g=2), in_=stg[0:32, 256:258].unsqueeze(2).to_broadcast([32, 2, 64])), reads=[c_stg], writes=[c_stg])
            for k, c0 in enumerate((0, 128, 384)):
                S.op('pe', lambda e, k=k, c0=c0: e.transpose(PS(7)[:, k * 32:(k + 1) * 32], stg[0:32, c0:c0 + 128], ident[0:32, 0:32]), reads=[c_stg, c_cst], writes=[c_ps[7]])
            S.op('dve', lambda e: e.tensor_copy(out=prm[:].rearrange("p k s -> p (k s)"), in_=PS(7)[:, 0:96]), reads=[c_ps[7]], writes=[c_prm])
            are, aim, ldt = prm[:, 0, :], prm[:, 1, :], prm[:, 2, :]
            dtb, lre, lrd, ang, mag, y_, nf, f_, m_, sinv, cosv, abr, abi, den, t1_, t2_, cre, cim, m8, rm8 = [smn() for _ in range(20)]
            ni = A.f32(32).bitcast(I32)
            TS = lambda e, out, in0, s1, op0, s2=None, op1=None: e.tensor_scalar(out=out, in0=in0, scalar1=s1, scalar2=s2, op0=op0, **({'op1': op1} if op1 is not None else {}))
            TTn = lambda e, out, a, b, op: e.tensor_tensor(out=out, in0=a, in1=b, op=op)
            dv(lambda e: e.activation(out=dtb, in_=ldt, func=AF.Exp), 'act')
            dv(lambda e: TS(e, lre, are, -1e-4, ALU.min))
            dv(lambda e: TTn(e, lrd, lre, dtb, ALU.mult))
            dv(lambda e: TTn(e, ang, aim, dtb, ALU.mult))
            dv(lambda e: e.activation(out=mag, in_=lrd, func=AF.Exp), 'act')
            dv(lambda e: e.activation(out=m8, in_=lrd, func=AF.Exp, scale=8.0), 'act')
            dv(lambda e: TS(e, y_, ang, 1.0 / (2 * math.pi), ALU.mult))
            S.op('dve', lambda e: e.tensor_copy(out=ni, in_=y_), reads=[c_sm], writes=[c_sm])
            S.op('dve', lambda e: e.tensor_copy(out=nf, in_=ni), reads=[c_sm], writes=[c_sm])
            dv(lambda e: TTn(e, f_, y_, nf, ALU.subtract))

            def wrap(x):
                dv(lambda e: TS(e, m_, x, 0.5, ALU.is_gt))
                dv(lambda e: TTn(e, x, x, m_, ALU.subtract))
                dv(lambda e: TS(e, m_, x, -0.5, ALU.is_lt))
                dv(lambda e: TTn(e, x, x, m_, ALU.add))
            wrap(f_)
            dv(lambda e: e.activation(out=sinv, in_=f_, func=AF.Sin, scale=2 * math.pi), 'act')
            dv(lambda e: TS(e, f_, f_, 0.25, ALU.add))
            wrap(f_)
            dv(lambda e: e.activation(out=cosv, in_=f_, func=AF.Sin, scale=2 * math.pi), 'act')
            dv(lambda e: TTn(e, abr, mag, cosv, ALU.mult))
            dv(lambda e: TTn(e, abi, mag, sinv, ALU.mult))
            dv(lambda e: TTn(e, den, lre, lre, ALU.mult))
            dv(lambda e: TTn(e, t1_, aim, aim, ALU.mult))
            dv(lambda e: TTn(e, den, den, t1_, ALU.add))
            dv(lambda e: e.reciprocal(out=den, in_=den))
            dv(lambda e: TS(e, t1_, abr, -1.0, ALU.add))
            dv(lambda e: TTn(e, cre, t1_, lre, ALU.mult))
            dv(lambda e: TTn(e, t2_, abi, aim, ALU.mult))
            dv(lambda e: TTn(e, cre, cre, t2_, ALU.add))
            dv(lambda e: TTn(e, cre, cre, den, ALU.mult))
            dv(lambda e: TTn(e, cim, abi, lre, ALU.mult))
            dv(lambda e: TTn(e, t2_, t1_, aim, ALU.mult))
            dv(lambda e: TTn(e, cim, cim, t2_, ALU.subtract))
            dv(lambda e: TTn(e, cim, cim, den, ALU.mult))
            dv(lambda e: e.memset(Apw[:, 0, 0, :], 1.0))
            dv(lambda e: e.memset(Apw[:, 0, 1, :], 0.0))
            for d in range(8):
                pr, pi = Apw[:, d, 0, :], Apw[:, d, 1, :]
                qr, qi = Apw[:, d + 1, 0, :], Apw[:, d + 1, 1, :]
                dv(lambda e, pr=pr, qr=qr: TTn(e, qr, pr, abr, ALU.mult))
                dv(lambda e, pi=pi: TTn(e, t2_, pi, abi, ALU.mult))
                dv(lambda e, qr=qr: TTn(e, qr, qr, t2_, ALU.subtract))
                dv(lambda e, pr=pr, qi=qi: TTn(e, qi, pr, abi, ALU.mult))
                dv(lambda e, pi=pi: TTn(e, t2_, pi, abr, ALU.mult))
                dv(lambda e, qi=qi: TTn(e, qi, qi, t2_, ALU.add))
            dv(lambda e: TS(e, Apn[:], Apw[:, :, 1, :], -1.0, ALU.mult))
            dv(lambda e: e.reciprocal(out=rm8, in_=m8))
            dv(lambda e: TTn(e, cks[:, 0, 0, :], Apw[:, 8, 0, :], rm8, ALU.mult))
            dv(lambda e: TTn(e, cks[:, 0, 1, :], Apw[:, 8, 1, :], rm8, ALU.mult))
            dv(lambda e: TS(e, cks[:, 0, 1, :], cks[:, 0, 1, :], -1.0, ALU.mult))
            for l in range(7):
                c0_, s0_ = cks[:, l, 0, :], cks[:, l, 1, :]
                c1_, s1_ = cks[:, l + 1, 0, :], cks[:, l + 1, 1, :]
                dv(lambda e, c0_=c0_, c1_=c1_: TTn(e, c1_, c0_, c0_, ALU.mult))
                dv(lambda e, s0_=s0_: TTn(e, t2_, s0_, s0_, ALU.mult))
                dv(lambda e, c1_=c1_: TTn(e, c1_, c1_, t2_, ALU.subtract))
                dv(lambda e, c0_=c0_, s0_=s0_, s1_=s1_: TTn(e, s1_, c0_, s0_, ALU.mult))
                dv(lambda e, s1_=s1_: TS(e, s1_, s1_, 2.0, ALU.mult))
            for k, nm in enumerate(('s5_b_re', 's5_b_im')):
                S.dma('sp', Bst[:, k, :, :], I[nm][j, :, :, :].rearrange("(s g) p c -> (g p) s c", g=2), writes=[c_Bst])
            bc = lambda x: x.unsqueeze(2).to_broadcast([128, 32, 16])
            Bt0 = R[:, 0:512].rearrange("p (s c) -> p s c", s=32)
            Bt1 = R[:, 512:1024].rearrange("p (s c) -> p s c", s=32)
            S.op('dve', lambda e: TTn(e, Bt0, Bst[:, 0, :, :], bc(cre), ALU.mult), reads=[c_Bst, c_sm], writes=[c_R[0]])
            S.op('dve', lambda e: TTn(e, Bt1, Bst[:, 1, :, :], bc(cim), ALU.mult), reads=[c_Bst, c_sm], writes=[c_R[0]])
            S.op('dve', lambda e: TTn(e, Bbar[:, 0, :, :], Bt0, Bt1, ALU.subtract), reads=[c_R[0]], writes=[c_Bbar])
            S.op('dve', lambda e: TTn(e, Bt0, Bst[:, 1, :, :], bc(cre), ALU.mult), reads=[c_Bst, c_sm], writes=[c_R[0]])
            S.op('dve', lambda e: TTn(e, Bt1, Bst[:, 0, :, :], bc(cim), ALU.mult), reads=[c_Bst, c_sm], writes=[c_R[0]])
            S.op('dve', lambda e: TTn(e, Bbar[:, 1, :, :], Bt0, Bt1, ALU.add), reads=[c_R[0]], writes=[c_Bbar])
            def load_C(ct):
                for k, nm in enumerate(('s5_c_re', 's5_c_im')):
                    S.dma('sp', Cn[ct % 2][:, k, :], I[nm][j, 8 * ct:8 * ct + 8, :, :].rearrange("g c p -> (g c) p"), writes=[c_Cn[ct % 2]])
            load_C(0)
            S.op('pool', lambda e: e.memset(SmT[:, :, 0, :, 0:1], 1.0), writes=[c_SmT])
            S.op('pool', lambda e: e.memset(SmT[:, :, 1, :, 0:1], 0.0), writes=[c_SmT])
            stt = R[:, 2048:4096].rearrange("p (i s n) -> p i s n", i=4, s=32)
            for t in range(2):
                for l in range(4):
                    n_ = 1 << l
                    ckl = cks[:, 4 * t + l, 0, :].unsqueeze(2).to_broadcast([128, 32, n_])
                    skl = cks[:, 4 * t + l, 1, :].unsqueeze(2).to_broadcast([128, 32, n_])
                    lo = slice(0, n_)
                    hi = slice(n_, 2 * n_)
                    S.op('pool', lambda e, t=t, ckl=ckl, lo=lo, n_=n_: TTn(e, stt[:, 0, :, 0:n_], SmT[:, t, 0, :, lo], ckl, ALU.mult), reads=[c_SmT, c_sm], writes=[c_R[2], c_R[3]])
                    S.op('pool', lambda e, t=t, skl=skl, lo=lo, n_=n_: TTn(e, stt[:, 1, :, 0:n_], SmT[:, t, 1, :, lo], skl, ALU.mult), reads=[c_SmT, c_sm], writes=[c_R[2], c_R[3]])
                    S.op('pool', lambda e, t=t, skl=skl, lo=lo, n_=n_: TTn(e, stt[:, 2, :, 0:n_], SmT[:, t, 0, :, lo], skl, ALU.mult), reads=[c_SmT, c_sm], writes=[c_R[2], c_R[3]])
                    S.op('pool', lambda e, t=t, ckl=ckl, lo=lo, n_=n_: TTn(e, stt[:, 3, :, 0:n_], SmT[:, t, 1, :, lo], ckl, ALU.mult), reads=[c_SmT, c_sm], writes=[c_R[2], c_R[3]])
                    S.op('pool', lambda e, t=t, hi=hi, n_=n_: TTn(e, SmT[:, t, 0, :, hi], stt[:, 0, :, 0:n_], stt[:, 1, :, 0:n_], ALU.subtract), reads=[c_R[2], c_R[3]], writes=[c_SmT])
                    S.op('pool', lambda e, t=t, hi=hi, n_=n_: TTn(e, SmT[:, t, 1, :, hi], stt[:, 2, :, 0:n_], stt[:, 3, :, 0:n_], ALU.add), reads=[c_R[2], c_R[3]], writes=[c_SmT])
            mask_hi = cc('mask_hi')
            mask_c16 = cc('mask_c16')
            maskq = cc('maskq')
            Dname = 's5_d'

            def do_ct(ct):
                if S5_STAGE < 2:
                    return
                Asl = lambda d, k: Apw[:, d, k, 4 * ct:4 * ct + 4]
                for k in range(2):
                    S.op('pool', lambda e, k=k: e.tensor_tensor(out=CCe[:, k, :].rearrange("p (g n) -> p g n", g=2), in0=Cn[ct % 2][:, k, :].unsqueeze(1).to_broadcast([128, 2, 64]),
                                                                in1=mask_c16.unsqueeze(2).to_broadcast([128, 2, 64]), op=ALU.mult), reads=[c_Cn[ct % 2], c_cst], writes=[c_CCe])
                    S.op('pe', lambda e, k=k: e.transpose(PS(6)[:, k * 128:(k + 1) * 128], CCe[:, k, :], ident), reads=[c_CCe, c_cst], writes=[c_ps[6]])
                S.op('act', lambda e: e.activation(out=CCt[:].rearrange("p k q c -> p (k q c)"), in_=PS(6)[:, 0:256], func=AF.Copy), reads=[c_ps[6]], writes=[c_CCt])
                if ct < 7:
                    load_C(ct + 1)
                for (d0, d1) in ((0, 5), (5, 9)):
                    nd = d1 - d0
                    t0 = R[:, 2048:2048 + 128 * nd].rearrange("p (q d c) -> p q d c", q=4, d=nd)
                    t1 = R[:, 3072:3072 + 128 * nd].rearrange("p (q d c) -> p q d c", q=4, d=nd)
                    Cb = lambda k, nd=nd: CCt[:, k, :, :].unsqueeze(2).to_broadcast([128, 4, nd, 32])
                    Ab = lambda k, nd=nd, d0=d0, d1=d1: Apw[:, d0:d1, k, 4 * ct:4 * ct + 4].rearrange("p d q -> p q d").unsqueeze(3).to_broadcast([128, 4, nd, 32])
                    S.op('pool', lambda e, t0=t0, Cb=Cb, Ab=Ab: TTn(e, t0, Cb(0), Ab(0), ALU.mult), reads=[c_CCt, c_sm], writes=[c_R[2]])
                    S.op('pool', lambda e, t1=t1, Cb=Cb, Ab=Ab: TTn(e, t1, Cb(1), Ab(1), ALU.mult), reads=[c_CCt, c_sm], writes=[c_R[3]])
                    S.op('dve', lambda e, t0=t0, t1=t1, d0=d0, d1=d1: TTn(e, CA[:, :, d0:d1, 0, :], t0, t1, ALU.subtract), reads=[c_R[2], c_R[3]], writes=[c_CA])
                    An = lambda nd=nd, d0=d0, d1=d1: Apn[:, d0:d1, 4 * ct:4 * ct + 4].rearrange("p d q -> p q d").unsqueeze(3).to_broadcast([128, 4, nd, 32])
                    S.op('pool', lambda e, t0=t0, Cb=Cb, An=An: TTn(e, t0, Cb(0), An(), ALU.mult), reads=[c_CCt, c_sm], writes=[c_R[2]])
                    S.op('pool', lambda e, t1=t1, Cb=Cb, Ab=Ab: TTn(e, t1, Cb(1), Ab(0), ALU.mult), reads=[c_CCt, c_sm], writes=[c_R[3]])
                    S.op('dve', lambda e, t0=t0, t1=t1, d0=d0, d1=d1: TTn(e, CA[:, :, d0:d1, 1, :], t0, t1, ALU.subtract), reads=[c_R[2], c_R[3]], writes=[c_CA])
                u_ = [R[:, 2048 + 512 * i:2048 + 512 * (i + 1)].rearrange("p (d q c) -> p d q c", d=8, q=4) for i in range(4)]
                Bb_ = lambda k: Bbar[:, k, 4 * ct:4 * ct + 4, :].unsqueeze(1).to_broadcast([128, 8, 4, 16])
                Ad_ = lambda k: Apw[:, 0:8, k, 4 * ct:4 * ct + 4].unsqueeze(3).to_broadcast([128, 8, 4, 16])
                S.op('pool', lambda e: TTn(e, u_[0], Bb_(0), Ad_(0), ALU.mult), reads=[c_Bbar, c_sm], writes=[c_R[2]])
                S.op('pool', lambda e: TTn(e, u_[1], Bb_(1), Ad_(1), ALU.mult), reads=[c_Bbar, c_sm], writes=[c_R[2]])
                S.op('pool', lambda e: TTn(e, u_[2], Bb_(0), Ad_(1), ALU.mult), reads=[c_Bbar, c_sm], writes=[c_R[3]])
                S.op('pool', lambda e: TTn(e, u_[3], Bb_(1), Ad_(0), ALU.mult), reads=[c_Bbar, c_sm], writes=[c_R[3]])
                S.op('dve', lambda e: TTn(e, u_[0], u_[0], u_[1], ALU.subtract), reads=[c_R[2]], writes=[c_R[2]])
                S.op('dve', lambda e: TTn(e, u_[2], u_[2], u_[3], ALU.add), reads=[c_R[3]], writes=[c_R[3]])
                for s_ in range(8):
                    d = 7 - s_
                    for k in range(2):
                        S.op('dve', lambda e, k=k, d=d: e.tensor_tensor(out=Wz[:, k, :, :, :], in0=u_[2 * k][:, d, :, :].unsqueeze(2).to_broadcast([128, 4, 2, 16]),
                                                                  in1=mask_hi.unsqueeze(1).unsqueeze(3).to_broadcast([128, 4, 2, 16]), op=ALU.mult),
                             reads=[c_R[2 + k], c_cst], writes=[c_Wz[k]])
                        S.op('pe', lambda e, k=k: e.transpose(PS(6)[:, 256 + k * 128:256 + (k + 1) * 128], Wz[:, k, :, :, :].rearrange("p q g c -> p (q g c)"), ident),
                             reads=[c_Wz[k], c_cst], writes=[c_ps[6]])
                        if s_ == 7:
                            S.op('act', lambda e, k=k: e.activation(out=ZBb[:, k, :], in_=Wz[:, k, :, :, :].rearrange("p q g c -> p (q g c)"), func=AF.Copy), reads=[c_Wz[k]], writes=[c_ZBb])
                    S.op('act', lambda e, s_=s_: e.activation(out=W1[:, s_, :, :].rearrange("p k c -> p (k c)"), in_=PS(6)[:, 256:512], func=AF.Copy), reads=[c_ps[6]], writes=[c_W1])
                for q in range(4):
                    for k in range(2):
                        S.op('pe', lambda e, q=q, k=k: e.matmul(PS(7)[32 * q:32 * q + 32, 0:288].rearrange("p (d c) -> p d c", d=9), lhsT=ZBb[:, k, 32 * q:32 * q + 32], rhs=CA[:, q, :, k, :], start=(k == 0), stop=(k == 1), tile_position=(0, 32 * q)),
                             reads=[c_ZBb, c_CA], writes=[c_ps[7]])
                S.op('dve', lambda e: e.tensor_tensor(out=KTf[:], in0=PS(7)[:, 0:256].rearrange("p (d c) -> p d c", d=8).unsqueeze(2).to_broadcast([128, 8, 4, 32]),
                                                      in1=maskq.unsqueeze(1).unsqueeze(3).to_broadcast([128, 8, 4, 32]), op=ALU.mult), reads=[c_ps[7], c_cst], writes=[c_KTf])
                if S5_STAGE < 3:
                    return
                for q in range(4):
                    for k in range(2):
                        bnk = 2 * k + q // 2
                        o_ = (q % 2) * 256
                        for s_ in range(8):
                            S.op('pe', lambda e, q=q, k=k, s_=s_, bnk=bnk, o_=o_: e.matmul(PS(bnk)[:, o_:o_ + 256], lhsT=W1[32 * q:32 * q + 32, s_, k, :], rhs=hb[32 * q:32 * q + 32, ct, s_:NP:8], start=(s_ == 0), stop=(s_ == 7), tile_position=(32 * q, 0)),
                                 reads=[c_W1, c_h[ct][0:4]], writes=[c_ps[bnk]])
                Xre = psum[:, 0:2, :].rearrange("p a (h n) -> p (a h) n", h=2)
                Xim = psum[:, 2:4, :].rearrange("p a (h n) -> p (a h) n", h=2)
                Ea = lambda k: SmT[:, 1, k, 4 * ct:4 * ct + 4, :].unsqueeze(3).to_broadcast([128, 4, 16, 16])
                Eb = lambda k: SmT[:, 0, k, 4 * ct:4 * ct + 4, :].unsqueeze(2).to_broadcast([128, 4, 16, 16])
                e0 = R[:, 2048:3072].rearrange("p (q a b) -> p q a b", q=4, a=16)
                e1 = R[:, 3072:4096].rearrange("p (q a b) -> p q a b", q=4, a=16)
                Ev = lambda k: E[:, k, :, :].rearrange("p q (a b) -> p q a b", a=16)
                S.op('pool', lambda e: TTn(e, e0, Ea(0), Eb(0), ALU.mult), reads=[c_SmT], writes=[c_R[2]])
                S.op('pool', lambda e: TTn(e, e1, Ea(1), Eb(1), ALU.mult), reads=[c_SmT], writes=[c_R[3]])
                S.op('dve', lambda e: TTn(e, Ev(0), e0, e1, ALU.subtract), reads=[c_R[2], c_R[3]], writes=[c_E])
                S.op('pool', lambda e: TTn(e, e0, Ea(0), Eb(1), ALU.mult), reads=[c_SmT], writes=[c_R[2]])
                S.op('pool', lambda e: TTn(e, e1, Ea(1), Eb(0), ALU.mult), reads=[c_SmT], writes=[c_R[3]])
                S.op('dve', lambda e: TTn(e, Ev(1), e0, e1, ALU.add), reads=[c_R[2], c_R[3]], writes=[c_E])
                Er = E[:, 0, :, :]
                Ei = E[:, 1, :, :]
                R4 = [r.rearrange("p (q n) -> p q n", q=4) for r in Rr]
                S.op('dve', lambda e: TTn(e, R4[0], Xre, Er, ALU.mult), reads=[c_ps[0], c_ps[1], c_E], writes=[c_R[0]])
                S.op('dve', lambda e: TTn(e, R4[1], Xim, Ei, ALU.mult), reads=[c_ps[2], c_ps[3], c_E], writes=[c_R[1]])
                S.op('dve', lambda e: TTn(e, R4[0], R4[0], R4[1], ALU.subtract), reads=[c_R[0], c_R[1]], writes=[c_R[0]])
                S.op('dve', lambda e: TTn(e, R4[1], Xre, Ei, ALU.mult), reads=[c_ps[0], c_ps[1], c_E], writes=[c_R[1]])
                S.op('dve', lambda e: TTn(e, R4[2], Xim, Er, ALU.mult), reads=[c_ps[2], c_ps[3], c_E], writes=[c_R[2]])
                S.op('dve', lambda e: TTn(e, R4[1], R4[1], R4[2], ALU.add), reads=[c_R[1], c_R[2]], writes=[c_R[1]])
                for q in range(4):
                    st_i = 4 * ct + q
                    S.op('dve', lambda e, q=q, st_i=st_i: e.tensor_tensor_scan(out=R4[2][:, q, :], data0=m8[:, st_i:st_i + 1].to_broadcast([128, 256]), data1=R4[0][:, q, :], initial=0.0, op0=ALU.mult, op1=ALU.add),
                         reads=[c_R[0], c_sm], writes=[c_R[2]])
                    S.op('dve', lambda e, q=q, st_i=st_i: e.tensor_tensor_scan(out=R4[3][:, q, :], data0=m8[:, st_i:st_i + 1].to_broadcast([128, 256]), data1=R4[1][:, q, :], initial=0.0, op0=ALU.mult, op1=ALU.add),
                         reads=[c_R[1], c_sm], writes=[c_R[3]])
                S.op('pool', lambda e: TTn(e, R4[0], R4[2], Er, ALU.mult), reads=[c_R[2], c_E], writes=[c_R[0]])
                S.op('pool', lambda e: TTn(e, R4[1], R4[3], Ei, ALU.mult), reads=[c_R[3], c_E], writes=[c_R[1]])
                S.op('dve', lambda e: TTn(e, R4[0], R4[0], R4[1], ALU.add), reads=[c_R[0], c_R[1]], writes=[c_R[0]])
                S.op('pool', lambda e: TTn(e, R4[1], R4[3], Er, ALU.mult), reads=[c_R[3], c_E], writes=[c_R[1]])
                S.op('dve', lambda e: TTn(e, R4[2], R4[2], Ei, ALU.mult), reads=[c_R[2], c_E], writes=[c_R[2]])
                S.op('dve', lambda e: TTn(e, R4[1], R4[1], R4[2], ALU.subtract), reads=[c_R[1], c_R[2]], writes=[c_R[1]])
                S.op('act', lambda e: e.activation(out=Hb[:, 0, :, 1:257], in_=R4[0], func=AF.Copy), reads=[c_R[0]], writes=[c_Hb])
                S.op('act', lambda e: e.activation(out=Hb[:, 1, :, 1:257], in_=R4[1], func=AF.Copy), reads=[c_R[1]], writes=[c_Hb])
                S.op('pool', lambda e: e.memset(Hb[:, :, :, 0:1], 0.0), writes=[c_Hb])
                S.op('act', lambda e: e.activation(out=Hfin[:, 0, 4 * ct:4 * ct + 4], in_=R4[0][:, :, 255], func=AF.Copy), reads=[c_R[0]], writes=[c_Hfin])
                S.op('act', lambda e: e.activation(out=Hfin[:, 1, 4 * ct:4 * ct + 4], in_=R4[1][:, :, 255], func=AF.Copy), reads=[c_R[1]], writes=[c_Hfin])
                if S5_STAGE < 4:
                    return
                yv = R[:, 0:2048]
                for tau in range(8):
                    yb = 4 + (tau % 2)
                    for s_ in range(tau + 1):
                        S.op('pe', lambda e, tau=tau, s_=s_, yb=yb: e.matmul(PS(yb)[:, 0:256], lhsT=KTf[:, tau - s_, :, :].rearrange("p q c -> p (q c)"), rhs=hb[:, ct, s_:NP:8], start=(s_ == 0), stop=False),
                             reads=[c_KTf, c_h[ct][0:4]], writes=[c_ps[yb]])
                    for q in range(4):
                        for k in range(2):
                            S.op('pe', lambda e, tau=tau, q=q, k=k, yb=yb: e.matmul(PS(yb)[32 * q:32 * q + 32, 0:256], lhsT=CA[:, q, tau + 1, k, :], rhs=Hb[:, k, q, 0:256], start=False, stop=(k == 1), tile_position=(0, 32 * q)),
                                 reads=[c_CA, c_Hb], writes=[c_ps[yb]])
                    S.op('dve', lambda e, tau=tau, yb=yb: e.scalar_tensor_tensor(out=yv[:, tau:NP:8], in0=hb[:, ct, tau:NP:8], scalar=vcol(Dname, ct, j), in1=PS(yb)[:, 0:256], op0=ALU.mult, op1=ALU.add),
                         reads=[c_ps[yb], c_h[ct][0:4], c_vec, c_R[2], c_R[3]], writes=[c_R[0], c_R[1]])
                for bi in range(4):
                    S.op('act', lambda e, bi=bi: e.activation(out=yg[:, ct, bi * 512:(bi + 1) * 512], in_=yv[:, bi * 512:(bi + 1) * 512], func=AF.Gelu_apprx_tanh), reads=[c_R[0], c_R[1]], writes=[c_yg[ct][bi]])
                if S5_STAGE < 5:
                    return
                for k, nm in enumerate(('state_s5_re', 'state_s5_im')):
                    S.dma('sp', st2[0:NS, k, :], I[nm][j, :, ct * 512:(ct + 1) * 512], writes=[c_st2])
                for k in range(2):
                    for q in range(4):
                        S.op('pe', lambda e, k=k, q=q: e.transpose(PS(6)[:, (k * 4 + q) * NS:(k * 4 + q + 1) * NS], st2[0:NS, k, q * 128:(q + 1) * 128], ident[0:NS, 0:NS]), reads=[c_st2, c_cst], writes=[c_ps[6]])
                S.op('act', lambda e: e.activation(out=h0[:].rearrange("p k q b -> p (k q b)"), in_=PS(6)[:, 0:128], func=AF.Copy), reads=[c_ps[6]], writes=[c_h0])
                for q in range(4):
                    for k in range(2):
                        S.op('pe', lambda e, q=q, k=k: e.matmul(PS(4 + q)[:, 480 + k * NS:480 + (k + 1) * NS], lhsT=W1[32 * q:32 * q + 32, 7, k, :], rhs=hb[32 * q:32 * q + 32, ct, NP:TT], start=True, stop=True, tile_position=(32 * q, 0)),
                             reads=[c_W1, c_h[ct][4]], writes=[c_ps[4 + q]])
                xs_ = lambda k: psum[:, 4:8, 480 + k * NS:480 + (k + 1) * NS]
                bb = lambda x: x.unsqueeze(2).to_broadcast([128, 4, NS])
                S.op('dve', lambda e: TTn(e, ts_[:, 0, :, :], h0[:, 0, :, :], bb(Asl(1, 0)), ALU.mult), reads=[c_h0, c_sm], writes=[c_ts])
                S.op('dve', lambda e: TTn(e, ts_[:, 1, :, :], h0[:, 1, :, :], bb(Asl(1, 1)), ALU.mult), reads=[c_h0, c_sm], writes=[c_ts])
                S.op('dve', lambda e: TTn(e, ts_[:, 0, :, :], ts_[:, 0, :, :], ts_[:, 1, :, :], ALU.subtract), reads=[c_ts], writes=[c_ts])
                S.op('dve', lambda e: TTn(e, hn[:, 0, :, :], ts_[:, 0, :, :], xs_(0), ALU.add), reads=[c_ts, c_ps[4:8]], writes=[c_hn])
                S.op('dve', lambda e: TTn(e, ts_[:, 2, :, :], h0[:, 1, :, :], bb(Asl(1, 0)), ALU.mult), reads=[c_h0, c_sm], writes=[c_ts])
                S.op('dve', lambda e: TTn(e, ts_[:, 3, :, :], h0[:, 0, :, :], bb(Asl(1, 1)), ALU.mult), reads=[c_h0, c_sm], writes=[c_ts])
                S.op('dve', lambda e: TTn(e, ts_[:, 2, :, :], ts_[:, 2, :, :], ts_[:, 3, :, :], ALU.add), reads=[c_ts], writes=[c_ts])
                S.op('dve', lambda e: TTn(e, hn[:, 1, :, :], ts_[:, 2, :, :], xs_(1), ALU.add), reads=[c_ts, c_ps[4:8]], writes=[c_hn])
                S.op('act', lambda e: e.activation(out=hnb[:].rearrange("p k q b -> p (k q b)"), in_=hn[:].rearrange("p k q b -> p (k q b)"), func=AF.Copy), reads=[c_hn], writes=[c_hnb])
                for q in range(4):
                    for k in range(2):
                        S.op('pe', lambda e, q=q, k=k: e.matmul(PS(7)[32 * q:32 * q + 32, 448:448 + NS], lhsT=CA[:, q, 0, k, :], rhs=hnb[:, k, q, :], start=(k == 0), stop=(k == 1), tile_position=(0, 32 * q)),
                             reads=[c_CA, c_hnb], writes=[c_ps[7]])
                S.op('dve', lambda e: e.scalar_tensor_tensor(out=ysv[:], in0=hb[:, ct, NP:TT], scalar=vcol(Dname, ct, j), in1=PS(7)[:, 448:448 + NS], op0=ALU.mult, op1=ALU.add),
                     reads=[c_ps[7], c_h[ct][4], c_vec], writes=[c_ysv])
                S.op('act', lambda e: e.activation(out=yg[:, ct, NP:TT], in_=ysv[:], func=AF.Gelu_apprx_tanh), reads=[c_ysv], writes=[c_yg[ct][4]])
                for k in range(2):
                    for q in range(4):
                        S.op('pe', lambda e, k=k, q=q: e.transpose(PS(6)[0:NS, q * 128:(q + 1) * 128], hn[:, k, q, :], ident), reads=[c_hn, c_cst], writes=[c_ps[6]])
                    S.op('dve', lambda e, k=k: e.tensor_copy(out=st2[0:NS, k, :], in_=PS(6)[0:NS, :]), reads=[c_ps[6]], writes=[c_st2])
                S.dma('sp', O['s5_re_s'][j, :, ct * 512:(ct + 1) * 512], st2[0:NS, 0, :], reads=[c_st2])
                S.dma('sp', O['s5_im_s'][j, :, ct * 512:(ct + 1) * 512], st2[0:NS, 1, :], reads=[c_st2])
            for ct in range(8):
                do_ct(ct)
            for k, nm in enumerate(('s5_re_p', 's5_im_p')):
                S.op('pe', lambda e, k=k: e.transpose(PS(6)[0:32, k * 128:(k + 1) * 128], Hfin[:, k, :], ident), reads=[c_Hfin, c_cst], writes=[c_ps[6]])
            S.op('dve', lambda e: e.tensor_copy(out=stg[0:32, 0:256], in_=PS(6)[0:32, 0:256]), reads=[c_ps[6]], writes=[c_stg])
            S.dma('sp', O['s5_re_p'][j:j + 1, :].rearrange("o (s n) -> (o s) n", n=128), stg[0:32, 0:128], reads=[c_stg])
            S.dma('sp', O['s5_im_p'][j:j + 1, :].rearrange("o (s n) -> (o s) n", n=128), stg[0:32, 128:256], reads=[c_stg])
            if S5_STAGE < 6:
                S.barrier()
                return
            kq = 0
            for bi in range(4):
                c0 = bi * 512
                last = (bi == 3)
                for mt in range(8):
                    wq = kq % 2
                    vb = kq % 2
                    gb = 2 + (kq % 2)
                    kq += 1
                    wload(wgl[wq][:, :, 0:128], I['s5_w_glu'][j, :, mt * 128:(mt + 1) * 128], 8, c_wgl[wq])
                    wload(wgl[wq][:, :, 128:256], I['s5_w_glu'][j, :, D + mt * 128:D + (mt + 1) * 128], 8, c_wgl[wq])
                    segs = [(c0, 512, bi, vb, gb, 0)] + ([(NP, NS, 4, 4, 4, 32)] if last else [])
                    for (s0, sn, cb, vbb, gbb, go) in segs:
                        for kt in range(8):
                            S.op('pe', lambda e, kt=kt, wq=wq, s0=s0, sn=sn, vbb=vbb: e.matmul(PS(vbb)[:, 0:sn], lhsT=wgl[wq][:, kt, 0:128], rhs=yg[:, kt, s0:s0 + sn], start=(kt == 0), stop=(kt == 7)),
                                 reads=[c_wgl[wq], c_yg[kt][cb]], writes=[c_ps[vbb]])
                        for kt in range(8):
                            S.op('pe', lambda e, kt=kt, wq=wq, s0=s0, sn=sn, gbb=gbb, go=go: e.matmul(PS(gbb)[:, go:go + sn], lhsT=wgl[wq][:, kt, 128:256], rhs=yg[:, kt, s0:s0 + sn], start=(kt == 0), stop=(kt == 7)),
                                 reads=[c_wgl[wq], c_yg[kt][cb]], writes=[c_ps[gbb]])
                        S.op('act', lambda e, wq=wq, sn=sn, gbb=gbb, go=go: e.activation(out=sg[wq][:, 0:sn], in_=PS(gbb)[:, go:go + sn], func=AF.Sigmoid), reads=[c_ps[gbb]], writes=[c_sg[wq]])
                        S.op('dve', lambda e, wq=wq, sn=sn, vbb=vbb: e.tensor_tensor(out=sg[wq][:, 0:sn], in0=sg[wq][:, 0:sn], in1=PS(vbb)[:, 0:sn], op=ALU.mult), reads=[c_sg[wq], c_ps[vbb]], writes=[c_sg[wq]])
                        S.op('dve', lambda e, wq=wq, sn=sn, s0=s0, mt=mt: e.tensor_tensor(out=xres[:, mt, s0:s0 + sn], in0=xres[:, mt, s0:s0 + sn], in1=sg[wq][:, 0:sn], op=ALU.add),
                             reads=[c_sg[wq], c_x[mt][cb]], writes=[c_x[mt][cb]])
            S.barrier()


        TT_OFF = [0]

        def rwkv_layer(li):
            A0 = Arena()
            rmsnorm('norm_mix', li, A0)
            S.barrier()
            A = Arena()
            NB = 8
            BW = 256
            NW = BW + NS
            TTn = lambda e, out, a, b, op: e.tensor_tensor(out=out, in0=a, in1=b, op=op)
            mask_hi = cc('mask_hi')
            blk64 = cc('blk64')
            w1 = A.bf16(8 * 64).rearrange("p (k c) -> p k c", k=8); c_w1 = Cell()
            a1 = A.bf16(8 * 64).rearrange("p (k c) -> p k c", k=8); c_a1 = Cell()
            g1 = A.bf16(8 * 128).rearrange("p (k c) -> p k c", k=8); c_g1 = Cell()
            w2 = A.bf16(D); c_w2 = Cell()
            a2 = A.bf16(D); c_a2 = Cell()
            g2 = A.bf16(D); c_g2 = Cell()
            wload(w1[:], I['rw_w1'][:, :], 8, c_w1)
            wload(a1[:], I['rw_a1'][:, :], 8, c_a1)
            wload(g1[:], I['rw_g1'][:, :], 8, c_g1)
            S.dma('pool', w2[0:64, :], I['rw_w2'][:, :], writes=[c_w2])
            S.dma('pool', a2[0:64, :], I['rw_a2'][:, :], writes=[c_a2])
            S.dma('pool', g2[:, :], I['rw_g2'][:, :], writes=[c_g2])
            cmk = A.f32(BW); c_cmk = Cell()
            S.dma('sp', cmk[:], I['cmask'][:, 0:BW], writes=[c_cmk])
            sh0 = A.f32(8 * NS).rearrange("p (a b) -> p a b", a=8); c_sh0 = Cell()
            stg = arena[:, ARENA_W - D:ARENA_W]; c_stg = Cell()
            S.dma('sp', stg[0:NS, :], I['state_rwkv_shift'][:, :], writes=[c_stg])
            for ct in range(8):
                S.op('pe', lambda e, ct=ct: e.transpose(PS(7)[:, ct * NS:(ct + 1) * NS], stg[0:NS, ct * 128:(ct + 1) * 128], ident[0:NS, 0:NS]), reads=[c_stg, c_cst], writes=[c_ps[7]])
            S.op('dve', lambda e: e.tensor_copy(out=sh0[:].rearrange("p a b -> p (a b)"), in_=PS(7)[:, 0:8 * NS]), reads=[c_ps[7]], writes=[c_sh0])
            Mf = A.f32(8 * 64).rearrange("p (a i) -> p a i", a=8); c_Mf = cells(8)
            Mb = A.bf16(8 * 2 * 64).rearrange("p (a h i) -> p a h i", a=8, h=2); c_Mb = cells(8)
            S.op('pool', lambda e: e.memset(Mf[:], 0.0), writes=[c_Mf])
            S.op('pool', lambda e: e.memset(Mb[:], 0.0), writes=[c_Mb])
            SV = A.f32(8 * 7 * NS).rearrange("p (a v b) -> p a v b", a=8, v=7); c_SV = cells(8)
            onec = A.f32(1); c_one = Cell()
            gnc = A.f32(1)
            S.op('dve', lambda e: e.memset(onec[:], 1.0), writes=[c_one])
            S.op('dve', lambda e: e.memset(gnc[:], 64e-5), writes=[c_one])
            xx = A.bf16(8 * NW).rearrange("p (a n) -> p a n", a=8); c_xx = Cell()
            xs = [A.bf16(8 * NW).rearrange("p (a n) -> p a n", a=8) for _ in range(4)]; c_xs = cells(4)
            og = A.bf16(8 * NW).rearrange("p (a n) -> p a n", a=8); c_og = cells(8)
            l1 = [A.bf16(NW) for _ in range(3)]; c_l1 = cells(3)
            wr = [A.bf16(8 * 3 * 128).rearrange("p (k n c) -> p k n c", k=8, n=3) for _ in range(2)]; c_wr = cells(2, 3)
            wo = [A.bf16(8 * 128).rearrange("p (k c) -> p k c", k=8) for _ in range(2)]; c_wo = cells(2)
            TT_OFF[0] = A.off
            NT_ = 13
            Tt = [A.f32(NW) for _ in range(NT_)]; c_T = cells(NT_)
            KKm = [A.bf16(NW) for _ in range(2)]; c_KKm = cells(2)
            Rm = [A.bf16(NW) for _ in range(2)]; c_Rm = cells(2)
            Kb = A.bf16(NW); c_Kb = Cell()
            Bb = A.bf16(NW); c_Bb = Cell()
            KBe = A.f32(3 * BW).rearrange("p (v n) -> p v n", v=3); c_KBe = Cell()
            AT = A.bf16(4 * 2 * 5 * 64).rearrange("p (c h m t) -> p c h m t", c=4, h=2, m=5); c_AT = Cell()
            PQ = [A.bf16(4 * 2 * 2 * 64).rearrange("p (c h m t) -> p c h m t", c=4, h=2, m=2) for _ in range(2)]; c_PQ = cells(2)
            Y = A.bf16(4 * 2 * 64).rearrange("p (c h t) -> p c h t", c=4, h=2); c_Y = Cell()
            Tok = A.bf16(4 * 3 * 128).rearrange("p (c v n) -> p c v n", c=4, v=3); c_Tok = Cell()
            Wp = A.bf16(2 * 64).rearrange("p (h i) -> p h i", h=2); c_Wp = Cell()
            Us = A.bf16(2 * 64).rearrange("p (h i) -> p h i", h=2); c_Us = Cell()
            ot = A.f32(NW); c_ot = Cell()
            imp = cc('imp')
            mask5 = cc('mask5')

            def gn_gate(ct, ncol, o_ap, c_o, r_ap, k2_ap, v_ap, g_ap, c_in, out_ap, c_out, tA, tB, c_tA, c_tB):
                S.op('pe', lambda e: e.matmul(PS(6)[:, 0:ncol], lhsT=blk64, rhs=o_ap, start=True, stop=True), reads=[c_o, c_cst], writes=[c_ps[6]])
                S.op('dve', lambda e: e.scalar_tensor_tensor(out=tA, in0=PS(6)[:, 0:ncol], scalar=-1.0 / 64, in1=o_ap, op0=ALU.mult, op1=ALU.add), reads=[c_ps[6], c_o], writes=[c_tA])
                S.op('pool', lambda e: TTn(e, tB, tA, tA, ALU.mult), reads=[c_tA], writes=[c_tB])
                S.op('pe', lambda e: e.matmul(PS(6)[:, 0:ncol], lhsT=blk64, rhs=tB, start=True, stop=True), reads=[c_tB, c_cst], writes=[c_ps[6]])
                S.op('act', lambda e: e.activation(out=tB, in_=PS(6)[:, 0:ncol], func=AF.Ln, bias=gnc[:, 0:1], scale=1.0 / 64), reads=[c_ps[6], c_one], writes=[c_tB])
                S.op('act', lambda e: e.activation(out=tB, in_=tB, func=AF.Exp, scale=-0.5), reads=[c_tB], writes=[c_tB])
                S.op('dve', lambda e: TTn(e, tA, tA, tB, ALU.mult), reads=[c_tA, c_tB], writes=[c_tA])
                S.op('dve', lambda e: e.tensor_scalar(out=tA, in0=tA, scalar1=vcol('rw_ln_w', ct), scalar2=vcol('rw_ln_b', ct), op0=ALU.mult, op1=ALU.add), reads=[c_tA, c_vec], writes=[c_tA])
                S.op('dve', lambda e: e.scalar_tensor_tensor(out=tB, in0=r_ap, scalar=vcol('rw_r_k', ct), in1=k2_ap, op0=ALU.mult, op1=ALU.mult), reads=[c_in, c_vec, c_tB], writes=[c_tB])
                S.op('pe', lambda e: e.matmul(PS(6)[:, 0:ncol], lhsT=blk64, rhs=tB, start=True, stop=True), reads=[c_tB, c_cst], writes=[c_ps[6]])
                S.op('dve', lambda e: TTn(e, tB, v_ap, PS(6)[:, 0:ncol], ALU.mult), reads=[c_ps[6], c_in], writes=[c_tB])
                S.op('dve', lambda e: TTn(e, tA, tA, tB, ALU.add), reads=[c_tA, c_tB], writes=[c_tA])
                S.op('dve', lambda e: TTn(e, out_ap, tA, g_ap, ALU.mult), reads=[c_tA, c_in], writes=[c_out])

            kwr = [0]
            kwo = [0]

            prologue_done = set()

            def block_prologue(bi):
                prologue_done.add(bi)
                c0 = bi * BW
                last = (bi == NB - 1)
                nw = NW if last else BW
                if bi == 0:
                    S.op('dve', lambda e: TTn(e, xx[:, :, 1:BW], hb[:, :, 0:BW - 1], hb[:, :, 1:BW], ALU.subtract), reads=[c_h], writes=[c_xx])
                    S.op('dve', lambda e: e.tensor_scalar(out=xx[:, :, 0:1], in0=hb[:, :, 0:1], scalar1=-1.0, scalar2=None, op0=ALU.mult), reads=[c_h], writes=[c_xx])
                else:
                    S.op('dve', lambda e: TTn(e, xx[:, :, 0:BW], hb[:, :, c0 - 1:c0 + BW - 1], hb[:, :, c0:c0 + BW], ALU.subtract), reads=[c_h], writes=[c_xx])
                if last:
                    S.op('dve', lambda e: TTn(e, xx[:, :, BW:NW], sh0[:], hb[:, :, NP:TT], ALU.subtract), reads=[c_h, c_sh0], writes=[c_xx])

                def mk_xs(n, q):
                    for kt in range(8):
                        S.op('dve', lambda e, kt=kt: e.scalar_tensor_tensor(out=xs[q][:, kt, 0:nw], in0=xx[:, kt, 0:nw], scalar=vcol('rw_mu', kt, n), in1=hb[:, kt, c0:c0 + nw], op0=ALU.mult, op1=ALU.add),
                             reads=[c_xx, c_h, c_vec], writes=[c_xs[q]])
                for n, (wt_, c_wt_, m_, func, li_) in enumerate([(w1, c_w1, 64, AF.Tanh, 0), (a1, c_a1, 64, AF.Copy, 1), (g1, c_g1, 128, AF.Sigmoid, 2)]):
                    mk_xs(3 + n, 3)
                    for kt in range(8):
                        S.op('pe', lambda e, kt=kt, wt_=wt_, m_=m_: e.matmul(PS(7)[0:m_, 0:nw], lhsT=wt_[:, kt, :], rhs=xs[3][:, kt, 0:nw], start=(kt == 0), stop=(kt == 7)),
                             reads=[c_wt_, c_xs[3]], writes=[c_ps[7]])
                    S.op('act', lambda e, m_=m_, func=func, li_=li_: e.activation(out=l1[li_][0:m_, 0:nw], in_=PS(7)[0:m_, 0:nw], func=func), reads=[c_ps[7]], writes=[c_l1[li_]])
                for n in range(3):
                    mk_xs(n, n)

            def do_block(bi):
                c0 = bi * BW
                last = (bi == NB - 1)
                nw = NW if last else BW
                if bi not in prologue_done:
                    block_prologue(bi)
                for ct in range(8):
                    do_ct(bi, ct, c0, last, nw)
                for mt in range(8):
                    wq = kwo[0] % 2
                    kwo[0] += 1
                    wload(wo[wq][:], I['rw_w_o'][:, mt * 128:(mt + 1) * 128], 8, c_wo[wq])
                    segs = [(0, BW, c0, [c_x[mt][c0 // 512]])]
                    for (s0, sn_, x0, cx) in segs:
                        for kt in range(8):
                            S.op('pe', lambda e, kt=kt, wq=wq, s0=s0, sn_=sn_: e.matmul(PS(7)[:, 0:sn_], lhsT=wo[wq][:, kt, :], rhs=og[:, kt, s0:s0 + sn_], start=(kt == 0), stop=(kt == 7)),
                                 reads=[c_wo[wq], c_og[kt]], writes=[c_ps[7]])
                        S.op('dve', lambda e, mt=mt, sn_=sn_, x0=x0: TTn(e, xres[:, mt, x0:x0 + sn_], xres[:, mt, x0:x0 + sn_], PS(7)[:, 0:sn_], ALU.add), reads=[c_ps[7]] + cx, writes=cx)

            RS = [[Tt[0], Tt[2], Tt[5], Tt[7]], [A.f32(NW), A.f32(NW), A.f32(NW), A.f32(NW)]]
            c_RS = [[c_T[0], c_T[2], c_T[5], c_T[7]], cells(4)]
            proj_done = set()

            def emit_proj(bi, ct, nw):
                proj_done.add((bi, ct))
                s_ = ct % 2
                r_, v_, g_ = RS[s_][0], RS[s_][1], RS[s_][2]
                cr, cv, cg = c_RS[s_][0], c_RS[s_][1], c_RS[s_][2]
                k_, lw, a_ = Tt[1], Tt[3], Tt[4]
                ck, clw, ca = c_T[1], c_T[3], c_T[4]
                wq = kwr[0] % 2
                if kwr[0] == 0:
                    for n in range(3):
                        wload(wr[0][:, :, n, :], I['rw_w_rkv'][n, :, 0:128], 8, c_wr[0][n])
                kwr[0] += 1
                for n in range(3):
                    for kt in range(8):
                        S.op('pe', lambda e, n=n, kt=kt: e.matmul(PS(n)[:, 0:nw], lhsT=wr[wq][:, kt, n, :], rhs=xs[n][:, kt, 0:nw], start=(kt == 0), stop=(kt == 7)),
                             reads=[c_wr[wq][n], c_xs[n]], writes=[c_ps[n]])
                if not (bi == NB - 1 and ct == 7):
                    nct = (ct + 1) % 8
                    for n in range(3):
                        wload(wr[1 - wq][:, :, n, :], I['rw_w_rkv'][n, :, nct * 128:(nct + 1) * 128], 8, c_wr[1 - wq][n])
                S.op('pe', lambda e: e.matmul(PS(3)[:, 0:nw], lhsT=w2[0:64, ct * 128:(ct + 1) * 128], rhs=l1[0][0:64, 0:nw], start=True, stop=True), reads=[c_w2, c_l1[0]], writes=[c_ps[3]])
                S.op('pe', lambda e: e.matmul(PS(4)[:, 0:nw], lhsT=a2[0:64, ct * 128:(ct + 1) * 128], rhs=l1[1][0:64, 0:nw], start=True, stop=True), reads=[c_a2, c_l1[1]], writes=[c_ps[4]])
                S.op('pe', lambda e: e.matmul(PS(5)[:, 0:nw], lhsT=g2[:, ct * 128:(ct + 1) * 128], rhs=l1[2][:, 0:nw], start=True, stop=True), reads=[c_g2, c_l1[2]], writes=[c_ps[5]])
                W = slice(0, nw)
                S.op('act', lambda e: e.activation(out=r_[:, W], in_=PS(0)[:, W], func=AF.Copy), reads=[c_ps[0]], writes=[cr])
                S.op('act', lambda e: e.activation(out=k_[:, W], in_=PS(1)[:, W], func=AF.Copy), reads=[c_ps[1]], writes=[ck])
                S.op('act', lambda e: e.activation(out=v_[:, W], in_=PS(2)[:, W], func=AF.Copy), reads=[c_ps[2]], writes=[cv])
                S.op('act', lambda e: e.activation(out=g_[:, W], in_=PS(5)[:, W], func=AF.Copy), reads=[c_ps[5]], writes=[cg])
                S.op('act', lambda e: e.activation(out=KBe[:, 0, :], in_=PS(2)[:, 0:BW], func=AF.Copy), reads=[c_ps[2]], writes=[c_KBe])
                S.op('act', lambda e: e.activation(out=lw[:, W], in_=PS(3)[:, W], func=AF.Sigmoid, bias=vcol('rw_w0', ct), scale=1.0), reads=[c_ps[3], c_vec], writes=[clw])
                S.op('act', lambda e: e.activation(out=a_[:, W], in_=PS(4)[:, W], func=AF.Sigmoid, bias=vcol('rw_a0', ct), scale=1.0), reads=[c_ps[4], c_vec], writes=[ca])

            def do_ct(bi, ct, c0, last, nw):
                s_ = ct % 2
                r_, v_, g_, k2 = RS[s_]
                cr, cv, cg, ck2 = c_RS[s_]
                _, k_, _, lw, a_, _, kk, _, b_, cum, Wi, Wn, We = Tt
                _, ck, _, clw, ca, _, ckk, _, cb, ccum, cWi, cWn, cWe = c_T
                if (bi, ct) not in proj_done:
                    emit_proj(bi, ct, nw)
                W = slice(0, nw)
                CD = math.exp(-0.5)
                P_ = slice(0, BW)
                S.op('dve', lambda e: e.tensor_scalar(out=kk[:, W], in0=k_[:, W], scalar1=vcol('rw_k_k', ct), scalar2=None, op0=ALU.mult), reads=[ck, c_vec], writes=[ckk])
                S.op('pool', lambda e: TTn(e, b_[:, W], kk[:, W], kk[:, W], ALU.mult), reads=[ckk], writes=[cb])
                S.op('pe', lambda e: e.matmul(PS(6)[:, W], lhsT=blk64, rhs=b_[:, W], start=True, stop=True), reads=[cb, c_cst], writes=[c_ps[6]])
                S.op('dve', lambda e: e.tensor_tensor_scan(out=cum[:, P_], data0=cmk[:], data1=lw[:, P_], initial=0.0, op0=ALU.mult, op1=ALU.add), reads=[c_cmk, clw, ccum], writes=[ccum])
                S.op('dve', lambda e: e.tensor_scalar(out=k2[:, W], in0=a_[:, W], scalar1=-1.0, scalar2=vcol('rw_k_a', ct), op0=ALU.add, op1=ALU.mult), reads=[ca, c_vec], writes=[ck2])
                S.op('dve', lambda e: e.scalar_tensor_tensor(out=k2[:, W], in0=k2[:, W], scalar=1.0, in1=k_[:, W], op0=ALU.add, op1=ALU.mult), reads=[ck2, ck], writes=[ck2])
                S.op('pool', lambda e: TTn(e, We[:, P_], cum[:, P_], lw[:, P_], ALU.subtract), reads=[ccum, clw], writes=[cWe])
                S.op('dve', lambda e: e.tensor_scalar(out=b_[:, W], in0=PS(6)[:, W], scalar1=1e-24, scalar2=None, op0=ALU.max), reads=[c_ps[6]], writes=[cb])
                S.op('act', lambda e: e.activation(out=b_[:, W], in_=b_[:, W], func=AF.Ln), reads=[cb], writes=[cb])
                S.op('act', lambda e: e.activation(out=b_[:, W], in_=b_[:, W], func=AF.Exp, scale=-0.5), reads=[cb], writes=[cb])
                S.op('act', lambda e: e.activation(out=Wi[:, P_], in_=cum[:, P_], func=AF.Exp, scale=-CD), reads=[ccum], writes=[cWi])
                S.op('act', lambda e: e.activation(out=Wn[:, P_], in_=cum[:, P_], func=AF.Exp, scale=CD), reads=[ccum], writes=[cWn])
                S.op('act', lambda e: e.activation(out=We[:, P_], in_=We[:, P_], func=AF.Exp, scale=-CD), reads=[cWe], writes=[cWe])
                S.op('dve', lambda e: TTn(e, kk[:, W], kk[:, W], b_[:, W], ALU.mult), reads=[ckk, cb], writes=[ckk])
                S.op('pool', lambda e: TTn(e, b_[:, W], kk[:, W], a_[:, W], ALU.mult), reads=[ckk, ca], writes=[cb])
                if last:
                    for vi, (src, csrc) in enumerate([(kk, ckk), (lw, clw), (b_, cb), (k2, ck2), (r_, cr), (v_, cv), (g_, cg)]):
                        if vi == 1:
                            S.op('act', lambda e, src=src, vi=vi: e.activation(out=SV[:, ct, vi, :], in_=src[:, BW:NW], func=AF.Exp, scale=-CD), reads=[csrc], writes=[c_SV[ct]])
                        else:
                            S.op('pool', lambda e, src=src, vi=vi: e.tensor_copy(out=SV[:, ct, vi, :], in_=src[:, BW:NW]), reads=[csrc], writes=[c_SV[ct]])
                for h in range(2):
                    S.op('dve', lambda e, h=h: e.scalar_tensor_tensor(out=KKm[h][:, P_], in0=kk[:, P_], scalar=mask_hi[:, h:h + 1], in1=We[:, P_], op0=ALU.mult, op1=ALU.mult), reads=[ckk, cWe, c_cst], writes=[c_KKm[h]])
                    S.op('dve', lambda e, h=h: e.scalar_tensor_tensor(out=Rm[h][:, P_], in0=r_[:, P_], scalar=mask_hi[:, h:h + 1], in1=Wi[:, P_], op0=ALU.mult, op1=ALU.mult), reads=[cr, cWi, c_cst], writes=[c_Rm[h]])
                S.op('pool', lambda e: TTn(e, cum[:, P_], k2[:, P_], Wn[:, P_], ALU.mult), reads=[ck2, cWn, ccum], writes=[ccum])
                S.op('pool', lambda e: TTn(e, We[:, P_], b_[:, P_], Wn[:, P_], ALU.mult), reads=[cb, cWn, cWe, c_KKm], writes=[cWe])
                S.op('act', lambda e: e.activation(out=Kb[:, P_], in_=cum[:, P_], func=AF.Copy), reads=[ccum], writes=[c_Kb])
                S.op('act', lambda e: e.activation(out=Bb[:, P_], in_=We[:, P_], func=AF.Copy), reads=[cWe], writes=[c_Bb])
                wend = Wi[:, 63:BW:64]
                wend_b = wend.unsqueeze(2).to_broadcast([128, 4, 64])
                S.op('dve', lambda e: TTn(e, KBe[:, 1, :].rearrange("p (c n) -> p c n", c=4), cum[:, P_].rearrange("p (c n) -> p c n", c=4), wend_b, ALU.mult), reads=[ccum, cWi], writes=[c_KBe])
                S.op('dve', lambda e: e.scalar_tensor_tensor(out=KBe[:, 2, :].rearrange("p (c n) -> p c n", c=4), in0=We[:, P_].rearrange("p (c n) -> p c n", c=4), scalar=-1.0, in1=wend_b, op0=ALU.mult, op1=ALU.mult),
                     reads=[cWe, cWi], writes=[c_KBe])
                for c in range(4):
                    for vi in range(3):
                        bnk = (c * 3 + vi) // 4
                        o_ = ((c * 3 + vi) % 4) * 128
                        S.op('pe', lambda e, c=c, vi=vi, bnk=bnk, o_=o_: e.transpose(PS(bnk)[0:64, o_:o_ + 128], KBe[:, vi, c * 64:(c + 1) * 64], ident), reads=[c_KBe, c_cst], writes=[c_ps[bnk]])
                S.op('act', lambda e: e.activation(out=Tok[0:64].rearrange("p c v n -> p (c v n)"), in_=psum[0:64, 0:3, :].rearrange("p a n -> p (a n)"), func=AF.Copy), reads=[c_ps[0:3]], writes=[c_Tok])
                for c in range(4):
                    cs_ = slice(c * 64, (c + 1) * 64)
                    for h in range(2):
                        for m, (lh, rh, cl, crr) in enumerate([(Kb, KKm[h], c_Kb, c_KKm[h]), (Bb, KKm[h], c_Bb, c_KKm[h]), (KKm[h], Bb, c_KKm[h], c_Bb), (Kb, Rm[h], c_Kb, c_Rm[h]), (Bb, Rm[h], c_Bb, c_Rm[h])]):
                            idx = (c * 2 + h) * 5 + m
                            bnk = 3 + idx // 8
                            o_ = (idx % 8) * 64
                            S.op('pe', lambda e, lh=lh, rh=rh, cs_=cs_, bnk=bnk, o_=o_: e.matmul(PS(bnk)[0:64, o_:o_ + 64], lhsT=lh[:, cs_], rhs=rh[:, cs_], start=True, stop=True),
                                 reads=[cl, crr], writes=[c_ps[bnk]])
                S.op('dve', lambda e: TTn(e, AT[0:64].rearrange("p c h m t -> p (c h) (m t)"), psum[0:64, 3:8, :].rearrange("p a (u n) -> p (a u) n", u=8).rearrange("p (ch m) n -> p ch (m n)", m=5),
                                          mask5[0:64, :].unsqueeze(1).to_broadcast([64, 8, 320]), ALU.mult), reads=[c_ps[3:8], c_cst], writes=[c_AT])
                S.op('dve', lambda e: TTn(e, Y[0:64].rearrange("p c h t -> p (c h) t"), imp[0:64, :].unsqueeze(1).to_broadcast([64, 8, 64]), AT[0:64, :, :, 1, :].rearrange("p c h t -> p (c h) t"), ALU.subtract), reads=[c_AT, c_cst], writes=[c_Y])
                prevP = lambda c, h: AT[0:64, c, h, 1, :]
                prevQ = lambda c, h: AT[0:64, c, h, 2, :]
                c_prev = c_AT
                for lvl in range(1, 7):
                    pq = PQ[lvl % 2]
                    c_pq = c_PQ[lvl % 2]
                    for c in range(4):
                        for h in range(2):
                            o_ = (c * 2 + h) * 64
                            if lvl <= 4:
                                S.op('pe', lambda e, c=c, h=h, o_=o_, prevP=prevP, prevQ=prevQ: e.matmul(PS(3)[0:64, o_:o_ + 64], lhsT=prevQ(c, h), rhs=prevP(c, h), start=True, stop=True), reads=[c_prev], writes=[c_ps[3]])
                            if lvl <= 5:
                                S.op('pe', lambda e, c=c, h=h, o_=o_, prevP=prevP, prevQ=prevQ: e.matmul(PS(4)[0:64, o_:o_ + 64], lhsT=prevP(c, h), rhs=prevQ(c, h), start=True, stop=True), reads=[c_prev], writes=[c_ps[4]])
                            if lvl >= 2:
                                S.op('pe', lambda e, c=c, h=h, o_=o_, prevQ=prevQ: e.matmul(PS(5)[0:64, o_:o_ + 64], lhsT=prevQ(c, h), rhs=Y[0:64, c, h, :], start=True, stop=True), reads=[c_prev, c_Y], writes=[c_ps[5]])
                    if lvl <= 4:
                        S.op('act', lambda e, pq=pq: e.activation(out=pq[0:64, :, :, 0, :].rearrange("p c h t -> p (c h) t"), in_=PS(3)[0:64, :].rearrange("p (u t) -> p u t", u=8), func=AF.Copy), reads=[c_ps[3]], writes=[c_pq])
                    if lvl <= 5:
                        S.op('act', lambda e, pq=pq: e.activation(out=pq[0:64, :, :, 1, :].rearrange("p c h t -> p (c h) t"), in_=PS(4)[0:64, :].rearrange("p (u t) -> p u t", u=8), func=AF.Copy), reads=[c_ps[4]], writes=[c_pq])
                    if lvl >= 2:
                        S.op('dve', lambda e: TTn(e, Y[0:64].rearrange("p c h t -> p (c h t)"), Y[0:64].rearrange("p c h t -> p (c h t)"), PS(5)[0:64, :], ALU.add), reads=[c_ps[5], c_Y], writes=[c_Y])
                    prevP = (lambda pq: (lambda c, h: pq[0:64, c, h, 0, :]))(pq)
                    prevQ = (lambda pq: (lambda c, h: pq[0:64, c, h, 1, :]))(pq)
                    c_prev = c_pq
                for c in range(4):
                    cs_ = slice(c * 64, (c + 1) * 64)
                    for h in range(2):
                        S.op('pe', lambda e, h=h, cs_=cs_: e.matmul(PS(6)[0:64, h * 64:(h + 1) * 64], lhsT=KKm[h][:, cs_], rhs=Mb[:, ct, h, :], start=True, stop=False), reads=[c_KKm[h], c_Mb[ct]], writes=[c_ps[6]])
                        S.op('pe', lambda e, h=h, c=c: e.matmul(PS(6)[0:64, h * 64:(h + 1) * 64], lhsT=AT[0:64, c, h, 0, :], rhs=Tok[0:64, c, 0, h * 64:(h + 1) * 64], start=False, stop=True), reads=[c_AT, c_Tok], writes=[c_ps[6]])
                    S.op('act', lambda e: e.activation(out=Wp[0:64].rearrange("p h i -> p (h i)"), in_=PS(6)[0:64, 0:128], func=AF.Copy), reads=[c_ps[6]], writes=[c_Wp])
                    for h in range(2):
                        S.op('pe', lambda e, h=h, c=c: e.matmul(PS(7)[0:64, h * 64:(h + 1) * 64], lhsT=Y[0:64, c, h, :], rhs=Wp[0:64, h, :], start=True, stop=True), reads=[c_Y, c_Wp], writes=[c_ps[7]])
                    S.op('act', lambda e: e.activation(out=Us[0:64].rearrange("p h i -> p (h i)"), in_=PS(7)[0:64, 0:128], func=AF.Copy), reads=[c_ps[7]], writes=[c_Us])
                    for h in range(2):
                        p0 = 64 * h
                        S.op('pe', lambda e, h=h, cs_=cs_, p0=p0: e.matmul(PS(0)[p0:p0 + 64, cs_], lhsT=Mb[:, ct, h, :], rhs=Rm[h][:, cs_], start=True, stop=False, tile_position=(0, p0)), reads=[c_Mb[ct], c_Rm[h]], writes=[c_ps[0]])
                        S.op('pe', lambda e, h=h, c=c, cs_=cs_, p0=p0: e.matmul(PS(0)[p0:p0 + 64, cs_], lhsT=Tok[0:64, c, 0, h * 64:(h + 1) * 64], rhs=AT[0:64, c, h, 3, :], start=False, stop=False, tile_position=(0, p0)), reads=[c_Tok, c_AT], writes=[c_ps[0]])
                        S.op('pe', lambda e, h=h, c=c, cs_=cs_, p0=p0: e.matmul(PS(0)[p0:p0 + 64, cs_], lhsT=Us[0:64, h, :], rhs=AT[0:64, c, h, 4, :], start=False, stop=True, tile_position=(0, p0)), reads=[c_Us, c_AT], writes=[c_ps[0]])
                        S.op('pe', lambda e, h=h, c=c, p0=p0: e.matmul(PS(1)[p0:p0 + 64, 0:64], lhsT=Tok[0:64, c, 1, h * 64:(h + 1) * 64], rhs=Tok[0:64, c, 0, h * 64:(h + 1) * 64], start=True, stop=False, tile_position=(0, p0)), reads=[c_Tok], writes=[c_ps[1]])
                        S.op('pe', lambda e, h=h, c=c, p0=p0: e.matmul(PS(1)[p0:p0 + 64, 0:64], lhsT=Tok[0:64, c, 2, h * 64:(h + 1) * 64], rhs=Us[0:64, h, :], start=False, stop=True, tile_position=(0, p0)), reads=[c_Tok, c_Us], writes=[c_ps[1]])
                    S.op('dve', lambda e, c=c: e.scalar_tensor_tensor(out=Mf[:, ct, :], in0=Mf[:, ct, :], scalar=Wi[:, c * 64 + 63:c * 64 + 64], in1=PS(1)[:, 0:64], op0=ALU.mult, op1=ALU.add),
                         reads=[c_ps[1], c_Mf[ct], cWi], writes=[c_Mf[ct]])
                    S.op('dve', lambda e: TTn(e, Mb[:, ct, :, :], Mf[:, ct, :].unsqueeze(1).to_broadcast([128, 2, 64]), mask_hi.unsqueeze(2).to_broadcast([128, 2, 64]), ALU.mult),
                         reads=[c_Mf[ct], c_cst], writes=[c_Mb[ct]])
                S.op('act', lambda e: e.activation(out=ot[:, P_], in_=PS(0)[:, P_], func=AF.Copy), reads=[c_ps[0]], writes=[c_ot])
                if bi == 0:
                    S.op('dve', lambda e: TTn(e, cum[:, 0:2], r_[:, 0:2], k2[:, 0:2], ALU.mult), reads=[cr, ck2, ccum], writes=[ccum])
                    S.op('pe', lambda e: e.matmul(PS(6)[:, 0:2], lhsT=blk64, rhs=cum[:, 0:2], start=True, stop=True), reads=[ccum, c_cst], writes=[c_ps[6]])
                    S.op('dve', lambda e: TTn(e, ot[:, 0:1], v_[:, 0:1], PS(6)[:, 0:1], ALU.mult), reads=[c_ps[6], cv, c_ot], writes=[c_ot])
                if ct < 7:
                    emit_proj(bi, ct + 1, nw)
                if ct == 6 and bi + 1 < NB:
                    block_prologue(bi + 1)
                gn_gate(ct, BW, ot[:, P_], c_ot, r_[:, P_], k2[:, P_], v_[:, P_], g_[:, P_], [cr, ck2, cv, cg], og[:, ct, 0:BW], c_og[ct], cum[:, P_], We[:, P_], ccum, cWe)

            for bi in range(NB):
                do_block(bi)
            S.barrier()
            A2 = Arena()
            A2.off = TT_OFF[0]
            S0 = A2.f32(NS * 64).rearrange("p (b j) -> p b j", b=NS); c_S0 = Cell()
            S1 = A2.f32(NS * 64).rearrange("p (b j) -> p b j", b=NS); c_S1 = Cell()
            t1 = A2.f32(NS * 64).rearrange("p (b j) -> p b j", b=NS); c_t1 = Cell()
            rx = A2.f32(NS * 64).rearrange("p (b j) -> p b j", b=NS); c_rx = Cell()
            sa = A2.f32(NS); c_sa = Cell()
            os_ = A2.f32(NS); c_os = Cell()
            ogs = A2.f32(8 * NS).rearrange("p (a b) -> p a b", a=8); c_ogs = cells(8)
            ogsb = A2.bf16(8 * NS).rearrange("p (a b) -> p a b", a=8); c_ogsb = cells(8)
            tA = A2.f32(NS); c_tA = Cell()
            tB = A2.f32(NS); c_tB = Cell()
            i2 = cc('i2')

            def bvec(ct, vi, pb):
                S.op('dve', lambda e: TTn(e, rx[:], SV[:, ct, vi, :].unsqueeze(2).to_broadcast([128, NS, 64]), i2.unsqueeze(1).to_broadcast([128, NS, 64]), ALU.mult), reads=[c_SV[ct], c_cst], writes=[c_rx])
                for half in range(2):
                    S.op('pe', lambda e, half=half: e.matmul(PS(pb + half), lhsT=blk64, rhs=rx[:, half * 8:(half + 1) * 8, :].rearrange("p b j -> p (b j)"), start=True, stop=True), reads=[c_rx, c_cst], writes=[c_ps[pb + half]])
                return psum[:, pb:pb + 2, :].rearrange("p a (b j) -> p (a b) j", j=64), [c_ps[pb], c_ps[pb + 1]]

            for ct in range(8):
                S.dma('sp', S0[:], I['state_rwkv_wkv'][:, 2 * ct:2 * ct + 2, :, :].rearrange("b h i j -> (h i) b j"), writes=[c_S0])
                KKb, cK = bvec(ct, 0, 0)
                S.op('dve', lambda e, KKb=KKb: TTn(e, t1[:], S0[:], KKb, ALU.mult), reads=[c_S0] + cK, writes=[c_t1])
                S.op('dve', lambda e: e.tensor_reduce(out=sa[:], in_=t1[:], axis=AX.X, op=ALU.add), reads=[c_t1], writes=[c_sa])
                Wb, cW = bvec(ct, 1, 2)
                S.op('dve', lambda e, Wb=Wb: TTn(e, S1[:], S0[:], Wb, ALU.mult), reads=[c_S0] + cW, writes=[c_S1])
                Bq, cB = bvec(ct, 2, 0)
                S.op('dve', lambda e, Bq=Bq: TTn(e, t1[:], Bq, sa[:].unsqueeze(2).to_broadcast([128, NS, 64]), ALU.mult), reads=[c_sa] + cB, writes=[c_t1])
                S.op('dve', lambda e: TTn(e, S1[:], S1[:], t1[:], ALU.subtract), reads=[c_S1, c_t1], writes=[c_S1])
                K2q, cK2 = bvec(ct, 3, 2)
                S.op('dve', lambda e, K2q=K2q, ct=ct: TTn(e, t1[:], K2q, SV[:, ct, 5, :].unsqueeze(2).to_broadcast([128, NS, 64]), ALU.mult), reads=[c_SV[ct]] + cK2, writes=[c_t1])
                S.op('dve', lambda e: TTn(e, S1[:], S1[:], t1[:], ALU.add), reads=[c_S1, c_t1], writes=[c_S1])
                Rq, cRq = bvec(ct, 4, 0)
                S.op('dve', lambda e, Rq=Rq: TTn(e, t1[:], S1[:], Rq, ALU.mult), reads=[c_S1] + cRq, writes=[c_t1])
                S.op('dve', lambda e: e.tensor_reduce(out=os_[:], in_=t1[:], axis=AX.X, op=ALU.add), reads=[c_t1], writes=[c_os])
                S.dma('sp', O['wkv_s'][:, 2 * ct:2 * ct + 2, :, :].rearrange("b h i j -> (h i) b j"), S1[:], reads=[c_S1])
                gn_gate(ct, NS, os_[:], c_os, SV[:, ct, 4, :], SV[:, ct, 3, :], SV[:, ct, 5, :], SV[:, ct, 6, :], [c_SV[ct]], ogs[:, ct, :], c_ogs[ct], tA[:], tB[:], c_tA, c_tB)
                S.op('act', lambda e, ct=ct: e.activation(out=ogsb[:, ct, :], in_=ogs[:, ct, :], func=AF.Copy), reads=[c_ogs[ct]], writes=[c_ogsb[ct]])
            if dbg:
                for ct in range(8):
                    S.dma('sp', O['dbg'][8, ct * 128:(ct + 1) * 128, NP:TT], ogs[:, ct, :], reads=[c_ogs[ct]])
                    S.dma('sp', O['dbg'][8, ct * 128:(ct + 1) * 128, 0:NS], SV[:, ct, 4, :], reads=[c_SV[ct]])
                    S.dma('sp', O['dbg'][8, ct * 128:(ct + 1) * 128, NS:2 * NS], SV[:, ct, 6, :], reads=[c_SV[ct]])
                    S.dma('sp', O['dbg'][8, ct * 128:(ct + 1) * 128, 2 * NS:3 * NS], SV[:, ct, 3, :], reads=[c_SV[ct]])
                    S.dma('sp', O['dbg'][8, ct * 128:(ct + 1) * 128, 3 * NS:4 * NS], SV[:, ct, 5, :], reads=[c_SV[ct]])
            for mt in range(8):
                wq = kwo[0] % 2
                kwo[0] += 1
                wload(wo[wq][:], I['rw_w_o'][:, mt * 128:(mt + 1) * 128], 8, c_wo[wq])
                for kt in range(8):
                    S.op('pe', lambda e, kt=kt, wq=wq: e.matmul(PS(7)[:, 0:NS], lhsT=wo[wq][:, kt, :], rhs=ogsb[:, kt, :], start=(kt == 0), stop=(kt == 7)), reads=[c_wo[wq], c_ogsb[kt]], writes=[c_ps[7]])
                S.op('dve', lambda e, mt=mt: TTn(e, xres[:, mt, NP:TT], xres[:, mt, NP:TT], PS(7)[:, 0:NS], ALU.add), reads=[c_ps[7], c_x[mt][4]], writes=[c_x[mt][4]])
            shf = A2.f32(8 * (NS + 1)).rearrange("p (a b) -> p a b", a=8); c_shf = cells(8)
            S.op('dve', lambda e: e.tensor_copy(out=shf[:, :, 0:NS], in_=hb[:, :, NP:TT]), reads=[c_h], writes=[c_shf])
            S.op('dve', lambda e: e.tensor_copy(out=shf[:, :, NS:NS + 1], in_=hb[:, :, NP - 1:NP]), reads=[c_h], writes=[c_shf])
            shs = A2.f32(8 * NS).rearrange("p (a b) -> p a b", a=8); c_shs = cells(8)
            S.op('dve', lambda e: e.tensor_copy(out=shs[:], in_=shf[:, :, 0:NS]), reads=[c_shf], writes=[c_shs])
            fm_to_rows(shs, c_shs, O['shift_s'][:, :], stg, c_stg, 0)
            shp = A2.f32(8); c_shp = Cell()
            S.op('dve', lambda e: e.tensor_copy(out=shp[:], in_=shf[:, :, NS]), reads=[c_shf], writes=[c_shp])
            S.op('pe', lambda e: e.transpose(PS(0)[0:8, 0:128], shp[:, 0:8], ident), reads=[c_shp, c_cst], writes=[c_ps[0]])
            S.op('dve', lambda e: e.tensor_copy(out=stg[32:40, 0:128], in_=PS(0)[0:8, 0:128]), reads=[c_ps[0]], writes=[c_stg])
            S.dma('sp', O['shift_p'][:, :].rearrange("o (a p) -> (o a) p", p=128), stg[32:40, 0:128], reads=[c_stg])
            wst = A2.f32(16 * 64).rearrange("p (h j) -> p h j", h=16); c_wst = Cell()
            for ct in range(8):
                pb = 1 + (ct % 2)
                S.op('pe', lambda e, ct=ct, pb=pb: e.transpose(PS(pb)[0:64, 0:128], Mf[:, ct, :], ident), reads=[c_Mf[ct], c_cst], writes=[c_ps[pb]])
                S.op('dve', lambda e, ct=ct, pb=pb: e.tensor_copy(out=wst[0:64, 2 * ct:2 * ct + 2, :].rearrange("p h j -> p (h j)"), in_=PS(pb)[0:64, 0:128]), reads=[c_ps[pb]], writes=[c_wst])
            S.dma('sp', O['wkv_p'][:, :, :].rearrange("h i j -> i h j"), wst[0:64, :, :], reads=[c_wst])
            S.barrier()

        env = dict(locals())
        for li in range(DEPTH):
            kind = li % 3
            if mixers[li]:
                if kind == 0:
                    s5_layer(li, li // 3)
                elif kind == 1:
                    rwkv_layer(li)
                else:
                    lru_layer(li)
            dump(2 * li)
            if ffn:
                ffn_layer(li)
            dump(2 * li + 1)
        final_out()
        S.finish()
        S.emit()
    return nc


def s5_layer(env, li, j):
    raise NotImplementedError


def rwkv_layer(env, li):
    raise NotImplementedError


def lru_layer(env, li):
    raise NotImplementedError


OUT_ORDER = ['y_prompt', 'y_sample', 's5_re_p', 's5_re_s', 's5_im_p', 's5_im_s', 'wkv_p', 'wkv_s',
             'shift_p', 'shift_s', 'lru_h_p', 'lru_h_s', 'lru_conv_p', 'lru_conv_s', 'ffn_conv_p', 'ffn_conv_s']


def make_in_maps(inputs):
    f = lambda a: np.ascontiguousarray(np.asarray(a, dtype=np.float32))
    w = {}
    for name in ['norm_mix', 'norm_ffn', 's5_a_re', 's5_a_im', 's5_log_dt', 's5_b_re', 's5_b_im', 's5_c_re', 's5_c_im',
                 's5_d', 's5_w_glu', 'ffn_w_in', 'ffn_conv_w', 'ffn_w_out']:
        w[name] = f(inputs[name])
    w['norm_final'] = f(inputs['norm_final']).reshape(1, D)
    w['ffn_conv_b'] = f(inputs['ffn_conv_b']).reshape(4, 1, DFF)
    for name in ['rw_mu', 'rw_w_rkv', 'rw_w1', 'rw_w2', 'rw_a1', 'rw_a2', 'rw_g1', 'rw_g2', 'rw_w_o',
                 'lru_w_in', 'lru_conv_w', 'lru_w_rg', 'lru_w_ig', 'lru_w_out']:
        w[name] = f(inputs[name])[0]
    for name in ['rw_w0', 'rw_a0', 'rw_k_k', 'rw_k_a', 'rw_ln_w', 'rw_ln_b', 'lru_conv_b', 'lru_b_rg', 'lru_b_ig', 'lru_lambda']:
        w[name] = f(inputs[name]).reshape(1, D)
    w['rw_r_k'] = f(inputs['rw_r_k']).reshape(1, D)
    w['consts'] = CONSTS
    w['cmask'] = CMASK
    xp = f(inputs['x_prompt'])
    xs = f(inputs['x_sample']).reshape(128, D)
    maps = []
    for c in range(8):
        sl = slice(c * NS, (c + 1) * NS)
        m = dict(w)
        m['x_prompt'] = xp[c]
        m['x_sample'] = xs[sl]
        m['state_s5_re'] = f(inputs['state_s5_re'])[:, sl].reshape(2, NS, 4096)
        m['state_s5_im'] = f(inputs['state_s5_im'])[:, sl].reshape(2, NS, 4096)
        m['state_rwkv_wkv'] = f(inputs['state_rwkv_wkv'])[0, sl]
        m['state_rwkv_shift'] = f(inputs['state_rwkv_shift'])[0, sl]
        m['state_lru_h'] = f(inputs['state_lru_h'])[0, sl]
        m['state_lru_conv'] = f(inputs['state_lru_conv'])[0, sl]
        m['state_ffn_conv'] = f(inputs['state_ffn_conv'])[:, sl]
        maps.append({k: np.ascontiguousarray(v) for k, v in m.items()})
    return maps


def gather(results):
    cat = lambda name, axis: np.concatenate([r[name] for r in results], axis=axis)
    stk = lambda name, axis: np.stack([r[name] for r in results], axis=axis)
    out = {}
    out['y_prompt'] = stk('y_prompt', 0)
    out['y_sample'] = cat('y_sample', 0).reshape(128, 1, D)
    out['s5_re_p'] = stk('s5_re_p', 1).reshape(2, 8, 64, 64)
    out['s5_im_p'] = stk('s5_im_p', 1).reshape(2, 8, 64, 64)
    out['s5_re_s'] = cat('s5_re_s', 1).reshape(2, 128, 64, 64)
    out['s5_im_s'] = cat('s5_im_s', 1).reshape(2, 128, 64, 64)
    out['wkv_p'] = stk('wkv_p', 0)[None]
    out['wkv_s'] = cat('wkv_s', 0)[None]
    out['shift_p'] = cat('shift_p', 0)[None]
    out['shift_s'] = cat('shift_s', 0)[None]
    out['lru_h_p'] = cat('lru_h_p', 0)[None]
    out['lru_h_s'] = cat('lru_h_s', 0)[None]
    out['lru_conv_p'] = stk('lru_conv_p', 0)[None]
    out['lru_conv_s'] = cat('lru_conv_s', 0)[None]
    out['ffn_conv_p'] = stk('ffn_conv_p', 1)
    out['ffn_conv_s'] = cat('ffn_conv_s', 1)
    return tuple(np.ascontiguousarray(out[k], dtype=np.float32) for k in OUT_ORDER)


_NC_CACHE = {}


def kernel(**inputs):
    if 'nc' not in _NC_CACHE:
        _NC_CACHE['nc'] = build()
    nc = _NC_CACHE['nc']
    res = run_bass_kernel_spmd(nc, make_in_maps(inputs), core_ids=list(range(8)))
    return gather(res.results)
```

```python
import contextlib
import math
import numpy as np
import concourse.bass as bass
import concourse.mybir as mybir
from concourse.bass_utils import run_bass_kernel_spmd

F32 = mybir.dt.float32
BF16 = mybir.dt.bfloat16
I32 = mybir.dt.int32
AF = mybir.ActivationFunctionType
ALU = mybir.AluOpType
AX = mybir.AxisListType

ENGS = ['pe', 'act', 'dve', 'pool', 'sp']
N_DMA_SEMS = 32

NP = 2048
NS = 16
TT = NP + NS
D = 1024
DFF = 2816
NJ = 22
DEPTH = 4
EPS = 1e-6
S5_STAGE = 9


class Cell:
    __slots__ = ('w', 'r')

    def __init__(self):
        self.w = None
        self.r = {}


def cells(*shape):
    if len(shape) == 1:
        return [Cell() for _ in range(shape[0])]
    return [cells(*shape[1:]) for _ in range(shape[0])]


def flat(x):
    if isinstance(x, Cell):
        return [x]
    out = []
    for y in x:
        out.extend(flat(y))
    return out


class Sched:
    def __init__(self, nc):
        self.nc = nc
        self.ops = {e: [] for e in ENGS}
        self.cnt = {e: 0 for e in ENGS}
        self.seen = {e: {} for e in ENGS}
        self.dma_rr = 0
        self.dma_rr2 = 0
        for i in range(N_DMA_SEMS):
            self.cnt[('d', i)] = 0

    def _deps(self, reads, writes):
        deps = {}
        for c in reads:
            if c.w is not None:
                k, v = c.w
                if deps.get(k, 0) < v:
                    deps[k] = v
        for c in writes:
            if c.w is not None:
                k, v = c.w
                if deps.get(k, 0) < v:
                    deps[k] = v
            for k, v in c.r.items():
                if deps.get(k, 0) < v:
                    deps[k] = v
        return deps

    def _waits(self, eng, deps):
        waits = []
        seen = self.seen[eng]
        for k, v in deps.items():
            if k == eng and eng in ('pe', 'sp'):
                continue
            if seen.get(k, 0) < v:
                seen[k] = v
                waits.append((k, v))
        return waits

    def op(self, eng, fn, reads=(), writes=()):
        reads = flat(reads)
        writes = flat(writes)
        waits = self._waits(eng, self._deps(reads, writes))
        self.cnt[eng] += 1
        seq = self.cnt[eng]
        self.ops[eng].append((waits, fn, (eng, 1)))
        for c in reads:
            if c.r.get(eng, 0) < seq:
                c.r[eng] = seq
        for c in writes:
            c.w = (eng, seq)
            c.r = {}

    def dma(self, eng, out, in_, reads=(), writes=(), **kw):
        reads = flat(reads)
        writes = flat(writes)
        half = N_DMA_SEMS // 2
        if eng == 'pool':
            key = ('d', half + self.dma_rr2)
            self.dma_rr2 = (self.dma_rr2 + 1) % half
        else:
            key = ('d', self.dma_rr)
            self.dma_rr = (self.dma_rr + 1) % half
        deps = self._deps(reads, writes)
        prev = self.cnt[key]
        if prev > 0:
            deps[key] = max(deps.get(key, 0), prev)
        waits = self._waits(eng, deps)
        self.cnt[key] += 16
        val = self.cnt[key]

        def fn(e, out=out, in_=in_, kw=kw):
            return e.dma_start(out=out, in_=in_, **kw)
        self.ops[eng].append((waits, fn, (key, 16)))
        for c in reads:
            c.r[key] = val
        for c in writes:
            c.w = (key, val)
            c.r = {}

    def barrier(self, engines=('pe', 'act', 'dve', 'pool', 'sp')):
        snap = dict(self.cnt)
        for e in engines:
            deps = {k: v for k, v in snap.items() if v > 0 and k != e}
            waits = self._waits(e, deps)
            if waits:
                self.ops[e].append((waits, None, None))

    def finish(self):
        snap = dict(self.cnt)
        for e in ('sp',):
            deps = {k: v for k, v in snap.items() if v > 0 and k != e}
            waits = self._waits(e, deps)
            self.ops[e].append((waits, None, None))

    def emit(self):
        nc = self.nc
        sems = {}
        with contextlib.ExitStack() as st:
            for k in self.cnt:
                name = k if isinstance(k, str) else 'dma%d' % k[1]
                sems[k] = st.enter_context(nc.semaphore('s_' + name))
            block = st.enter_context(nc.Block())

            def run(engname):
                def body(e):
                    for waits, fn, inc in self.ops[engname]:
                        for k, v in waits:
                            e.wait_ge(sems[k], v)
                        if fn is not None:
                            fn(e).then_inc(sems[inc[0]], inc[1])
                return body
            block.tensor(run('pe'))
            block.scalar(run('act'))
            block.vector(run('dve'))
            block.gpsimd(run('pool'))
            block.sync(run('sp'))


VROWS = {}


def _vrow(name, n=1):
    VROWS[name] = (len_vrows[0], n)
    len_vrows[0] += n


len_vrows = [0]
_vrow('norm_mix', 4)
_vrow('norm_ffn', 4)
_vrow('norm_final', 1)
_vrow('s5_d', 2)
_vrow('rw_mu', 6)
_vrow('rw_w0')
_vrow('rw_a0')
_vrow('rw_k_k')
_vrow('rw_k_a')
_vrow('rw_r_k')
_vrow('rw_ln_w')
_vrow('rw_ln_b')
_vrow('lru_conv_w', 4)
_vrow('lru_conv_b')
_vrow('lru_b_rg')
_vrow('lru_b_ig')
_vrow('lru_lambda')
NV = len_vrows[0]

CST_COLS = {}


def make_consts():
    cols = []
    off = 0

    def add(name, arr):
        nonlocal off
        arr = np.asarray(arr, np.float32)
        CST_COLS[name] = (off, arr.shape[1])
        off += arr.shape[1]
        cols.append(arr)
    p = np.arange(128)
    add('ident', np.eye(128))
    add('ones', np.ones((128, 128)))
    add('blk64', (p[:, None] // 64 == p[None, :] // 64).astype(np.float32))
    add('mask_hi', (p[:, None] // 64 == np.arange(2)[None, :]).astype(np.float32))
    add('mask_c16', (((p[:, None] % 32) // 16) == np.arange(2)[None, :]).astype(np.float32))
    add('maskq', (p[:, None] // 32 == np.arange(4)[None, :]).astype(np.float32))
    add('i2', (p[:, None] % 64 == np.arange(64)[None, :]).astype(np.float32))
    s64 = np.arange(64)
    us = (s64[:, None] < s64[None, :]).astype(np.float32)
    ui = (s64[:, None] <= s64[None, :]).astype(np.float32)
    ls = (s64[:, None] > s64[None, :]).astype(np.float32)
    m5 = np.concatenate([us, us, ls, ui, -ui], 1)
    add('mask5', np.concatenate([m5, m5], 0))
    add('imp', np.concatenate([np.eye(64), np.eye(64)], 0))
    s = np.arange(64)
    up_strict = (s[:, None] < s[None, :]).astype(np.float32)
    up_incl = (s[:, None] <= s[None, :]).astype(np.float32)
    add('tri_s', np.concatenate([up_strict, up_strict], 0))
    add('tri_i', np.concatenate([up_incl, up_incl], 0))
    return np.concatenate(cols, 1)


CONSTS = make_consts()
CMASK = np.tile((np.arange(NP) % 64 != 0).astype(np.float32)[None, :], (128, 1))
NCST = CONSTS.shape[1]


def build(mixers=(True, True, True, True), ffn=True, dbg=False):
    nc = bass.Bass("TRN2", target_bir_lowering=False)
    S = Sched(nc)
    I = {}
    O = {}

    def din(name, shape, dt=F32):
        I[name] = nc.dram_tensor(name, list(shape), dt, kind="ExternalInput").ap()

    def dout(name, shape, dt=F32):
        O[name] = nc.dram_tensor(name, list(shape), dt, kind="ExternalOutput").ap()

    din('x_prompt', [NP, D])
    din('x_sample', [NS, D])
    din('state_s5_re', [2, NS, 4096])
    din('state_s5_im', [2, NS, 4096])
    din('state_rwkv_wkv', [NS, 16, 64, 64])
    din('state_rwkv_shift', [NS, D])
    din('state_lru_h', [NS, D])
    din('state_lru_conv', [NS, 3, D])
    din('state_ffn_conv', [4, NS, 2, DFF])
    din('norm_mix', [4, D]); din('norm_ffn', [4, D]); din('norm_final', [1, D])
    din('s5_a_re', [2, 64, 64]); din('s5_a_im', [2, 64, 64]); din('s5_log_dt', [2, 64])
    din('s5_b_re', [2, 64, 64, 16]); din('s5_b_im', [2, 64, 64, 16])
    din('s5_c_re', [2, 64, 16, 64]); din('s5_c_im', [2, 64, 16, 64])
    din('s5_d', [2, D]); din('s5_w_glu', [2, D, 2 * D])
    din('rw_mu', [6, D]); din('rw_w_rkv', [3, D, D]); din('rw_w0', [1, D]); din('rw_w1', [D, 64])
    din('rw_w2', [64, D]); din('rw_a0', [1, D]); din('rw_a1', [D, 64]); din('rw_a2', [64, D])
    din('rw_g1', [D, 128]); din('rw_g2', [128, D]); din('rw_k_k', [1, D]); din('rw_k_a', [1, D])
    din('rw_r_k', [1, D]); din('rw_ln_w', [1, D]); din('rw_ln_b', [1, D]); din('rw_w_o', [D, D])
    din('lru_w_in', [D, 2 * D]); din('lru_conv_w', [4, D]); din('lru_conv_b', [1, D])
    din('lru_w_rg', [4, 256, 256]); din('lru_b_rg', [1, D]); din('lru_w_ig', [4, 256, 256])
    din('lru_b_ig', [1, D]); din('lru_lambda', [1, D]); din('lru_w_out', [D, D])
    din('ffn_w_in', [4, D, 2 * DFF]); din('ffn_conv_w', [4, 3, DFF]); din('ffn_conv_b', [4, 1, DFF])
    din('ffn_w_out', [4, DFF, D])
    din('consts', [128, NCST])
    din('cmask', [128, NP])

    dout('y_prompt', [NP, D]); dout('y_sample', [NS, D])
    dout('s5_re_p', [2, 4096]); dout('s5_re_s', [2, NS, 4096])
    dout('s5_im_p', [2, 4096]); dout('s5_im_s', [2, NS, 4096])
    dout('wkv_p', [16, 64, 64]); dout('wkv_s', [NS, 16, 64, 64])
    dout('shift_p', [1, D]); dout('shift_s', [NS, D])
    dout('lru_h_p', [1, D]); dout('lru_h_s', [NS, D])
    dout('lru_conv_p', [3, D]); dout('lru_conv_s', [NS, 3, D])
    dout('ffn_conv_p', [4, 2, DFF]); dout('ffn_conv_s', [4, NS, 2, DFF])
    if dbg:
        dout('dbg', [9, D, TT])

    with contextlib.ExitStack() as st:
        def sb(name, shape, dt=F32):
            return st.enter_context(nc.sbuf_tensor(name, list(shape), dt))

        xres = sb('xres', [128, 8, TT]); c_x = cells(8, 5)
        hb = sb('hb', [128, 8, TT], BF16); c_h = cells(8, 5)
        cst = sb('cst', [128, NCST]); c_cst = Cell()
        identb = sb('identb', [128, 128], BF16)
        onesb = sb('onesb', [128, 128], BF16)
        blk64b = sb('blk64b', [128, 128], BF16)
        c_cb = Cell()
        vec = sb('vec', [128, 8, NV]); c_vec = Cell()
        epsc = sb('epsc', [128, 1]); c_eps = Cell()
        ARENA_W = 26800
        arena = sb('arena', [128, ARENA_W])
        psum = st.enter_context(nc.psum_tensor('psum', [128, 8, 512], F32))
        c_ps = cells(8)

        def PS(b):
            return psum[:, b, :]

        def cc(name):
            o, n = CST_COLS[name]
            return cst[:, o:o + n]

        class Arena:
            def __init__(self):
                self.off = 0

            def f32(self, n):
                a = arena[:, self.off:self.off + n]
                self.off += n
                assert self.off <= ARENA_W, self.off
                return a

            def bf16(self, n):
                n2 = (n + 1) // 2
                a = arena[:, self.off:self.off + n2].bitcast(BF16)
                self.off += n2
                assert self.off <= ARENA_W, self.off
                return a[:, 0:n]

        BLKS = [(0, 512), (512, 512), (1024, 512), (1536, 512), (NP, NS)]

        def vcol(name, ct, k=0):
            r0, _ = VROWS[name]
            return vec[:, ct, r0 + k:r0 + k + 1]

        S.dma('sp', cst[:], I['consts'][:, :], writes=[c_cst])
        o_id = CST_COLS['ident'][0]
        S.op('dve', lambda e: e.tensor_copy(out=identb[:], in_=cc('ident')), reads=[c_cst], writes=[c_cb])
        S.op('dve', lambda e: e.tensor_copy(out=onesb[:], in_=cc('ones')), reads=[c_cst], writes=[c_cb])
        S.op('dve', lambda e: e.tensor_copy(out=blk64b[:], in_=cc('blk64')), reads=[c_cst], writes=[c_cb])
        S.op('dve', lambda e: e.memset(epsc[:], EPS), writes=[c_eps])
        ident = cc('ident')

        A = Arena()
        vst = A.f32(D)
        c_vst = Cell()
        for name, (r0, n) in VROWS.items():
            S.dma('sp', vst[r0:r0 + n, :], I[name][:, :], writes=[c_vst])
        for ct in range(8):
            S.op('pe', lambda e, ct=ct: e.transpose(PS(0)[:, ct * NV:(ct + 1) * NV], vst[0:NV, ct * 128:(ct + 1) * 128], ident[0:NV, 0:NV]),
                 reads=[c_vst, c_cst], writes=[c_ps[0]])
        S.op('dve', lambda e: e.tensor_copy(out=vec[:].rearrange("p a b -> p (a b)"), in_=PS(0)[:, 0:8 * NV]), reads=[c_ps[0]], writes=[c_vec])

        xin = [A.f32(4 * D), A.f32(4 * D)]
        c_xin = cells(2)
        k = 0
        for tb in range(4):
            S.dma('sp', xin[tb % 2].rearrange("p (a c) -> p a c", a=4),
                  I['x_prompt'][tb * 512:(tb + 1) * 512, :].rearrange("(a p) c -> p a c", p=128), writes=[c_xin[tb % 2]])
            for ct in range(8):
                b = 1 + (k % 2)
                for a in range(4):
                    S.op('pe', lambda e, tb=tb, ct=ct, a=a, b=b: e.transpose(
                        PS(b)[:, a * 128:(a + 1) * 128], xin[tb % 2][:, a * D + ct * 128: a * D + (ct + 1) * 128], ident),
                        reads=[c_xin[tb % 2], c_cst], writes=[c_ps[b]])
                eng = 'act' if k % 2 == 0 else 'dve'
                if eng == 'act':
                    S.op('act', lambda e, tb=tb, ct=ct, b=b: e.activation(out=xres[:, ct, tb * 512:(tb + 1) * 512], in_=PS(b), func=AF.Copy),
                         reads=[c_ps[b]], writes=[c_x[ct][tb]])
                else:
                    S.op('dve', lambda e, tb=tb, ct=ct, b=b: e.tensor_copy(out=xres[:, ct, tb * 512:(tb + 1) * 512], in_=PS(b)),
                         reads=[c_ps[b]], writes=[c_x[ct][tb]])
                k += 1
        xsin = A.f32(D)
        c_xsin = Cell()
        S.dma('sp', xsin[0:NS, :], I['x_sample'][:, :], writes=[c_xsin])
        for ct in range(8):
            S.op('pe', lambda e, ct=ct: e.transpose(PS(3)[:, ct * NS:(ct + 1) * NS], xsin[0:NS, ct * 128:(ct + 1) * 128], ident[0:NS, 0:NS]),
                 reads=[c_xsin, c_cst], writes=[c_ps[3]])
        S.op('dve', lambda e: e.tensor_copy(out=xres[:, :, NP:TT], in_=PS(3)[:, 0:8 * NS].rearrange("p (a b) -> p a b", a=8)),
             reads=[c_ps[3]], writes=[c_x[ct][4] for ct in range(8)])
        S.barrier()

        def rmsnorm(gname, gk, A, block_cb=None):
            sq = [A.bf16(512), A.bf16(512)]
            c_sq = cells(2)
            rstd = [A.f32(512), A.f32(512)]
            c_rs = cells(2)
            k = 0
            for bi, (c0, n) in enumerate(BLKS):
                pb = 6 + (bi % 2)
                for ct in range(8):
                    q = k % 2
                    S.op('pool', lambda e, ct=ct, c0=c0, n=n, q=q: e.tensor_tensor(out=sq[q][:, 0:n], in0=xres[:, ct, c0:c0 + n], in1=xres[:, ct, c0:c0 + n], op=ALU.mult),
                         reads=[c_x[ct][bi]], writes=[c_sq[q]])
                    S.op('pe', lambda e, ct=ct, n=n, q=q, pb=pb: e.matmul(PS(pb)[:, 0:n], lhsT=onesb[:], rhs=sq[q][:, 0:n], start=(ct == 0), stop=(ct == 7)),
                         reads=[c_sq[q], c_cb], writes=[c_ps[pb]])
                    k += 1
                r = bi % 2
                S.op('act', lambda e, n=n, r=r, pb=pb: e.activation(out=rstd[r][:, 0:n], in_=PS(pb)[:, 0:n], func=AF.Ln, bias=epsc[:, 0:1], scale=1.0 / D),
                     reads=[c_ps[pb], c_eps], writes=[c_rs[r]])
                S.op('act', lambda e, n=n, r=r: e.activation(out=rstd[r][:, 0:n], in_=rstd[r][:, 0:n], func=AF.Exp, scale=-0.5),
                     reads=[c_rs[r]], writes=[c_rs[r]])
                if block_cb is not None:
                    block_cb(bi, c0, n, rstd[r], c_rs[r])
                    continue
                for ct in range(8):
                    S.op('dve', lambda e, ct=ct, c0=c0, n=n, r=r: e.scalar_tensor_tensor(
                        out=hb[:, ct, c0:c0 + n], in0=xres[:, ct, c0:c0 + n], scalar=vcol(gname, ct, gk), in1=rstd[r][:, 0:n], op0=ALU.mult, op1=ALU.mult),
                        reads=[c_x[ct][bi], c_rs[r], c_vec], writes=[c_h[ct][bi]])

        def wload(dst, src_rows_ap, kt, c_dst):
            S.dma('pool', dst, src_rows_ap.rearrange("(k p) c -> p k c", p=128), writes=[c_dst])

        def ffn_layer(li):
            A = Arena()
            rmsnorm('norm_ffn', li, A)
            act = A.bf16(NJ * 528).rearrange("p (j n) -> p j n", j=NJ)
            c_act = cells(NJ)
            NWB = 4
            win = [A.bf16(8 * 256).rearrange("p (k c) -> p k c", k=8) for _ in range(NWB)]
            c_win = cells(NWB, 2)
            wo = [A.bf16(NJ * 128).rearrange("p (j c) -> p j c", j=NJ) for _ in range(2)]
            c_wo = cells(2)
            NQ = 3
            G = [A.f32(514) for _ in range(NQ)]
            c_G = cells(NQ)
            acc = [A.f32(512) for _ in range(NQ)]
            c_acc = cells(NQ)
            halo = A.f32(2 * NJ).rearrange("p (s j) -> p s j", s=2)
            c_halo = cells(NJ)
            cw = A.f32(NJ * 4).rearrange("p (j k) -> p j k", j=NJ)
            c_cw = Cell()
            stS = A.f32(NJ * 32).rearrange("p (j b s) -> p j b s", j=NJ, b=NS)
            c_stS = Cell()
            gnew = A.f32(NJ * NS).rearrange("p (j b) -> p j b", j=NJ)
            c_gnew = cells(NJ)
            accs = [A.f32(NS) for _ in range(3)]
            c_accs = cells(3)
            stg = A.f32(DFF)
            c_stg = Cell()
            S.dma('sp', stg[0:3, :], I['ffn_conv_w'][li, :, :], writes=[c_stg])
            S.dma('sp', stg[3:4, :], I['ffn_conv_b'][li, :, :], writes=[c_stg])
            for j in range(NJ):
                S.op('pe', lambda e, j=j: e.transpose(PS(5)[:, j * 4:(j + 1) * 4], stg[0:4, j * 128:(j + 1) * 128], ident[0:4, 0:4]),
                     reads=[c_stg, c_cst], writes=[c_ps[5]])
            S.op('dve', lambda e: e.tensor_copy(out=cw[:].rearrange("p j k -> p (j k)"), in_=PS(5)[:, 0:NJ * 4]), reads=[c_ps[5]], writes=[c_cw])
            S.dma('sp', stg[0:32, :], I['state_ffn_conv'][li, :, :, :].rearrange("b s f -> (b s) f"), reads=[c_ps[5]], writes=[c_stg])
            for j0 in range(0, NJ, 16):
                j1 = min(NJ, j0 + 16)
                for j in range(j0, j1):
                    S.op('pe', lambda e, j=j, j0=j0: e.transpose(PS(5)[:, (j - j0) * 32:(j - j0 + 1) * 32], stg[0:32, j * 128:(j + 1) * 128], ident[0:32, 0:32]),
                         reads=[c_stg, c_cst], writes=[c_ps[5]])
                S.op('dve', lambda e, j0=j0, j1=j1: e.tensor_copy(out=stS[:, j0:j1, :, :].rearrange("p j b s -> p (j b s)"), in_=PS(5)[:, 0:(j1 - j0) * 32]),
                     reads=[c_ps[5]], writes=[c_stS])
            S.dma('sp', O['ffn_conv_s'][li, :, 0, :], I['state_ffn_conv'][li, :, 1, :])

            kw = 0
            ko = 0
            for bi in range(4):
                c0 = bi * 512
                last = (bi == 3)
                for j in range(NJ):
                    wb = kw % NWB
                    wload(win[wb][:, :, 0:128], I['ffn_w_in'][li, :, j * 128:(j + 1) * 128], 8, c_win[wb][0])
                    wload(win[wb][:, :, 128:256], I['ffn_w_in'][li, :, DFF + j * 128:DFF + (j + 1) * 128], 8, c_win[wb][1])
                    gb = kw % 3
                    ub = 3 + (kw % 3)
                    q = kw % NQ
                    kw += 1
                    for kt in range(8):
                        S.op('pe', lambda e, kt=kt, wb=wb, gb=gb, c0=c0: e.matmul(PS(gb), lhsT=win[wb][:, kt, 0:128], rhs=hb[:, kt, c0:c0 + 512], start=(kt == 0), stop=(kt == 7)),
                             reads=[c_win[wb][0], c_h[kt][bi]], writes=[c_ps[gb]])
                    for kt in range(8):
                        S.op('pe', lambda e, kt=kt, wb=wb, ub=ub, c0=c0: e.matmul(PS(ub), lhsT=win[wb][:, kt, 128:256], rhs=hb[:, kt, c0:c0 + 512], start=(kt == 0), stop=(kt == 7)),
                             reads=[c_win[wb][1], c_h[kt][bi]], writes=[c_ps[ub]])
                    if last:
                        for kt in range(8):
                            S.op('pe', lambda e, kt=kt, wb=wb: e.matmul(PS(7)[:, 0:NS], lhsT=win[wb][:, kt, 0:128], rhs=hb[:, kt, NP:TT], start=(kt == 0), stop=(kt == 7)),
                                 reads=[c_win[wb][0], c_h[kt][4]], writes=[c_ps[7]])
                        for kt in range(8):
                            S.op('pe', lambda e, kt=kt, wb=wb: e.matmul(PS(7)[:, 32:32 + NS], lhsT=win[wb][:, kt, 128:256], rhs=hb[:, kt, NP:TT], start=(kt == 0), stop=(kt == 7)),
                                 reads=[c_win[wb][1], c_h[kt][4]], writes=[c_ps[7]])
                    S.op('act', lambda e, q=q, gb=gb: e.activation(out=G[q][:, 2:514], in_=PS(gb), func=AF.Copy), reads=[c_ps[gb]], writes=[c_G[q]])
                    if bi == 0:
                        S.op('dve', lambda e, q=q: e.memset(G[q][:, 0:2], 0.0), writes=[c_G[q]])
                    else:
                        S.op('dve', lambda e, q=q, j=j: e.tensor_copy(out=G[q][:, 0:2], in_=halo[:, :, j]), reads=[c_halo[j]], writes=[c_G[q]])
                    S.op('act', lambda e, q=q, gb=gb, j=j: e.activation(out=acc[q][:], in_=PS(gb), func=AF.Identity, bias=cw[:, j, 3:4], scale=cw[:, j, 2:3]),
                         reads=[c_ps[gb], c_cw], writes=[c_acc[q]])
                    S.op('dve', lambda e, q=q, j=j: e.scalar_tensor_tensor(out=acc[q][:], in0=G[q][:, 1:513], scalar=cw[:, j, 1:2], in1=acc[q][:], op0=ALU.mult, op1=ALU.add),
                         reads=[c_G[q], c_acc[q], c_cw], writes=[c_acc[q]])
                    S.op('dve', lambda e, q=q, j=j: e.scalar_tensor_tensor(out=acc[q][:], in0=G[q][:, 0:512], scalar=cw[:, j, 0:1], in1=acc[q][:], op0=ALU.mult, op1=ALU.add),
                         reads=[c_G[q], c_acc[q], c_cw], writes=[c_acc[q]])
                    S.op('dve', lambda e, q=q, j=j: e.tensor_copy(out=halo[:, :, j], in_=G[q][:, 512:514]), reads=[c_G[q]], writes=[c_halo[j]])
                    S.op('act', lambda e, q=q: e.activation(out=acc[q][:], in_=acc[q][:], func=AF.Silu), reads=[c_acc[q]], writes=[c_acc[q]])
                    S.op('dve', lambda e, q=q, ub=ub, j=j: e.tensor_tensor(out=act[:, j, 0:512], in0=acc[q][:], in1=PS(ub), op=ALU.mult),
                         reads=[c_acc[q], c_ps[ub]], writes=[c_act[j]])
                    if last:
                        S.op('dve', lambda e, q=q, j=j: e.tensor_scalar(out=accs[q][:], in0=PS(7)[:, 0:NS], scalar1=cw[:, j, 2:3], scalar2=cw[:, j, 3:4], op0=ALU.mult, op1=ALU.add),
                             reads=[c_ps[7], c_cw], writes=[c_accs[q]])
                        S.op('dve', lambda e, q=q, j=j: e.scalar_tensor_tensor(out=accs[q][:], in0=stS[:, j, :, 1], scalar=cw[:, j, 1:2], in1=accs[q][:], op0=ALU.mult, op1=ALU.add),
                             reads=[c_stS, c_accs[q], c_cw], writes=[c_accs[q]])
                        S.op('dve', lambda e, q=q, j=j: e.scalar_tensor_tensor(out=accs[q][:], in0=stS[:, j, :, 0], scalar=cw[:, j, 0:1], in1=accs[q][:], op0=ALU.mult, op1=ALU.add),
                             reads=[c_stS, c_accs[q], c_cw], writes=[c_accs[q]])
                        S.op('act', lambda e, q=q: e.activation(out=accs[q][:], in_=accs[q][:], func=AF.Silu), reads=[c_accs[q]], writes=[c_accs[q]])
                        S.op('act', lambda e, j=j: e.activation(out=gnew[:, j, :], in_=PS(7)[:, 0:NS], func=AF.Copy), reads=[c_ps[7]], writes=[c_gnew[j]])
                        S.op('dve', lambda e, q=q, j=j: e.tensor_tensor(out=act[:, j, 512:528], in0=accs[q][:], in1=PS(7)[:, 32:32 + NS], op=ALU.mult),
                             reads=[c_accs[q], c_ps[7]], writes=[c_act[j]])
                for mt in range(8):
                    wq = ko % 2
                    ob = 6 + (ko % 2)
                    ko += 1
                    wload(wo[wq][:], I['ffn_w_out'][li, :, mt * 128:(mt + 1) * 128], NJ, c_wo[wq])
                    for j in range(NJ):
                        S.op('pe', lambda e, j=j, wq=wq, ob=ob: e.matmul(PS(ob), lhsT=wo[wq][:, j, :], rhs=act[:, j, 0:512], start=(j == 0), stop=(j == NJ - 1)),
                             reads=[c_wo[wq], c_act[j]], writes=[c_ps[ob]])
                    S.op('dve', lambda e, mt=mt, ob=ob, c0=c0: e.tensor_tensor(out=xres[:, mt, c0:c0 + 512], in0=xres[:, mt, c0:c0 + 512], in1=PS(ob), op=ALU.add),
                         reads=[c_ps[ob], c_x[mt][bi]], writes=[c_x[mt][bi]])
                    if last:
                        for j in range(NJ):
                            S.op('pe', lambda e, j=j, wq=wq: e.matmul(PS(5)[:, 0:NS], lhsT=wo[wq][:, j, :], rhs=act[:, j, 512:528], start=(j == 0), stop=(j == NJ - 1)),
                                 reads=[c_wo[wq], c_act[j]], writes=[c_ps[5]])
                        S.op('dve', lambda e, mt=mt: e.tensor_tensor(out=xres[:, mt, NP:TT], in0=xres[:, mt, NP:TT], in1=PS(5)[:, 0:NS], op=ALU.add),
                             reads=[c_ps[5], c_x[mt][4]], writes=[c_x[mt][4]])
            S.op('pe', lambda e: e.transpose(PS(0)[0:2 * NJ, 0:128], halo[:].rearrange("p s j -> p (s j)"), ident), reads=[c_halo, c_cst], writes=[c_ps[0]])
            S.op('dve', lambda e: e.tensor_copy(out=stg[0:2 * NJ, 0:128], in_=PS(0)[0:2 * NJ, 0:128]), reads=[c_ps[0]], writes=[c_stg])
            S.dma('sp', O['ffn_conv_p'][li, :, :].rearrange("s (j p) -> (s j) p", p=128), stg[0:2 * NJ, 0:128], reads=[c_stg])
            stg2 = stg
            c_stg2 = c_stg
            for j0 in range(0, NJ, 4):
                j1 = min(NJ, j0 + 4)
                for j in range(j0, j1):
                    S.op('pe', lambda e, j=j, j0=j0: e.transpose(PS(1)[0:NS, (j - j0) * 128:(j - j0 + 1) * 128], gnew[:, j, :], ident), reads=[c_gnew[j], c_cst], writes=[c_ps[1]])
                S.op('dve', lambda e, j0=j0, j1=j1: e.tensor_copy(out=stg2[0:NS, j0 * 128:j1 * 128], in_=PS(1)[0:NS, 0:(j1 - j0) * 128]), reads=[c_ps[1]], writes=[c_stg2])
            S.dma('sp', O['ffn_conv_s'][li, :, 1, :], stg2[0:NS, :], reads=[c_stg2])
            S.barrier()

        def final_out():
            A = Arena()
            yf = [A.f32(8 * 512).rearrange("p (a t) -> p a t", a=8) for _ in range(2)]
            c_y = cells(2)
            ost = [A.f32(D), A.f32(D)]
            c_ost = cells(2)
            kk = [0]

            def cb(bi, c0, n, rs, c_r):
                yq = bi % 2
                for ct in range(8):
                    S.op('dve', lambda e, ct=ct, c0=c0, n=n, yq=yq: e.scalar_tensor_tensor(
                        out=yf[yq][:, ct, 0:n], in0=xres[:, ct, c0:c0 + n], scalar=vcol('norm_final', ct, 0), in1=rs[:, 0:n], op0=ALU.mult, op1=ALU.mult),
                        reads=[c_x[ct][bi], c_r, c_vec], writes=[c_y[yq]])
                for t4 in range((n + 127) // 128):
                    rows = min(128, n - t4 * 128)
                    q = kk[0] % 2
                    for half in range(2):
                        b = kk[0] % 2
                        kk[0] += 1
                        for a in range(4):
                            ct = half * 4 + a
                            S.op('pe', lambda e, t4=t4, ct=ct, a=a, b=b, yq=yq, rows=rows: e.transpose(PS(b)[0:rows, a * 128:(a + 1) * 128], yf[yq][:, ct, t4 * 128:t4 * 128 + rows], ident),
                                 reads=[c_y[yq], c_cst], writes=[c_ps[b]])
                        if half == 0:
                            S.op('act', lambda e, q=q, b=b, rows=rows: e.activation(out=ost[q][0:rows, 0:512], in_=PS(b)[0:rows, :], func=AF.Copy), reads=[c_ps[b]], writes=[c_ost[q]])
                        else:
                            S.op('dve', lambda e, q=q, b=b, rows=rows: e.tensor_copy(out=ost[q][0:rows, 512:1024], in_=PS(b)[0:rows, :]), reads=[c_ps[b]], writes=[c_ost[q]])
                    if bi < 4:
                        t0 = c0 + t4 * 128
                        S.dma('sp', O['y_prompt'][t0:t0 + 128, :], ost[q][:], reads=[c_ost[q]])
                    else:
                        S.dma('sp', O['y_sample'][:, :], ost[q][0:NS, :], reads=[c_ost[q]])
            rmsnorm('norm_final', 0, A, block_cb=cb)

        def dump(k):
            if dbg:
                for ct in range(8):
                    S.dma('sp', O['dbg'][k, ct * 128:(ct + 1) * 128, :], xres[:, ct, :], reads=[c_x[ct]])


        def lru_layer(li):
            A = Arena()
            rmsnorm('norm_mix', li, A)
            hg = A.bf16(8 * TT).rearrange("p (a t) -> p a t", a=8)
            c_hg = cells(8, 5)
            wt = [A.bf16(8 * 256).rearrange("p (k c) -> p k c", k=8) for _ in range(2)]
            c_wt = cells(2, 2)
            wg2 = [A.bf16(2 * 2 * 256).rearrange("p (g k c) -> p g k c", g=2, k=2) for _ in range(2)]
            c_wg2 = cells(2)
            Ub = [A.f32(3 + 512) for _ in range(2)]
            c_Ub = cells(2)
            uc = [A.f32(528) for _ in range(2)]
            c_uc = cells(2)
            ucb = [A.bf16(528) for _ in range(2)]
            c_ucb = cells(2)
            NT = 6
            tb = [A.f32(528) for _ in range(NT)]
            c_tb = cells(NT)
            uh = A.f32(24).rearrange("p (k a) -> p k a", k=3)
            c_uh = cells(8)
            hprev = A.f32(8)
            c_hp = cells(8)
            lc = A.f32(16).rearrange("p (a k) -> p a k", a=8)
            c_lc = Cell()
            cs = A.f32(8 * 48).rearrange("p (a b k) -> p a b k", a=8, b=NS)
            c_cs = Cell()
            h0 = A.f32(8 * NS).rearrange("p (a b) -> p a b", a=8)
            c_h0 = Cell()
            hs = A.f32(8 * NS).rearrange("p (a b) -> p a b", a=8)
            c_hs = cells(8)
            us = A.f32(8 * NS).rearrange("p (a b) -> p a b", a=8)
            c_us = cells(8)
            stg = A.f32(D)
            c_stg = Cell()
            onec = A.f32(1)
            c_one = Cell()
            S.op('dve', lambda e: e.memset(onec[:], 1.0), writes=[c_one])
            for ct in range(8):
                S.op('act', lambda e, ct=ct: e.activation(out=lc[:, ct, 0:1], in_=vcol('lru_lambda', ct), func=AF.Exp, scale=-1.0), reads=[c_vec], writes=[c_lc])
            S.op('act', lambda e: e.activation(out=lc[:, :, 0:1], in_=lc[:, :, 0:1], func=AF.Ln, bias=onec[:, 0:1], scale=1.0), reads=[c_lc, c_one], writes=[c_lc])
            S.op('dve', lambda e: e.tensor_scalar(out=lc[:, :, 1:2], in0=lc[:, :, 0:1], scalar1=-16.0, scalar2=None, op0=ALU.mult), reads=[c_lc], writes=[c_lc])
            S.op('dve', lambda e: e.tensor_scalar(out=lc[:, :, 0:1], in0=lc[:, :, 0:1], scalar1=-8.0, scalar2=None, op0=ALU.mult), reads=[c_lc], writes=[c_lc])
            S.dma('sp', stg[0:48, :], I['state_lru_conv'][:, :, :].rearrange("b k c -> (b k) c"), writes=[c_stg])
            for ct in range(8):
                S.op('pe', lambda e, ct=ct: e.transpose(PS(7)[:, ct * 48:(ct + 1) * 48], stg[0:48, ct * 128:(ct + 1) * 128], ident[0:48, 0:48]), reads=[c_stg, c_cst], writes=[c_ps[7]])
            S.op('dve', lambda e: e.tensor_copy(out=cs[:].rearrange("p a b k -> p (a b k)"), in_=PS(7)[:, 0:384]), reads=[c_ps[7]], writes=[c_cs])
            S.dma('sp', stg[0:NS, :], I['state_lru_h'][:, :], reads=[c_ps[7]], writes=[c_stg])
            for ct in range(8):
                S.op('pe', lambda e, ct=ct: e.transpose(PS(7)[:, ct * NS:(ct + 1) * NS], stg[0:NS, ct * 128:(ct + 1) * 128], ident[0:NS, 0:NS]), reads=[c_stg, c_cst], writes=[c_ps[7]])
            S.op('dve', lambda e: e.tensor_copy(out=h0[:].rearrange("p a b -> p (a b)"), in_=PS(7)[:, 0:8 * NS]), reads=[c_ps[7]], writes=[c_h0])
            S.dma('sp', O['lru_conv_s'][:, 0:2, :], I['state_lru_conv'][:, 1:3, :])

            kq = 0
            kt_ = 0
            for n in range(4):
                g2 = n % 2
                wload(wg2[g2][:, 0, :, :], I['lru_w_rg'][n, :, :], 2, c_wg2[g2])
                wload(wg2[g2][:, 1, :, :], I['lru_w_ig'][n, :, :], 2, c_wg2[g2])
                for bi in range(4):
                    c0 = bi * 512
                    last = (bi == 3)
                    nn = 528 if last else 512
                    for c2 in range(2):
                        ct = 2 * n + c2
                        wq = kq % 2
                        kq += 1
                        wload(wt[wq][:, :, 0:128], I['lru_w_in'][:, D + ct * 128:D + (ct + 1) * 128], 8, c_wt[wq][0])
                        wload(wt[wq][:, :, 128:256], I['lru_w_in'][:, ct * 128:(ct + 1) * 128], 8, c_wt[wq][1])
                        ub = c2
                        gb = 2 + c2
                        for kt in range(8):
                            S.op('pe', lambda e, kt=kt, wq=wq, ub=ub, c0=c0: e.matmul(PS(ub), lhsT=wt[wq][:, kt, 0:128], rhs=hb[:, kt, c0:c0 + 512], start=(kt == 0), stop=(kt == 7)),
                                 reads=[c_wt[wq][0], c_h[kt][bi]], writes=[c_ps[ub]])
                        for kt in range(8):
                            S.op('pe', lambda e, kt=kt, wq=wq, gb=gb, c0=c0: e.matmul(PS(gb), lhsT=wt[wq][:, kt, 128:256], rhs=hb[:, kt, c0:c0 + 512], start=(kt == 0), stop=(kt == 7)),
                                 reads=[c_wt[wq][1], c_h[kt][bi]], writes=[c_ps[gb]])
                        if last:
                            sb_ = 4
                            for kt in range(8):
                                S.op('pe', lambda e, kt=kt, wq=wq, c2=c2: e.matmul(PS(4)[:, c2 * 64:c2 * 64 + NS], lhsT=wt[wq][:, kt, 0:128], rhs=hb[:, kt, NP:TT], start=(kt == 0), stop=(kt == 7)),
                                     reads=[c_wt[wq][0], c_h[kt][4]], writes=[c_ps[4]])
                            for kt in range(8):
                                S.op('pe', lambda e, kt=kt, wq=wq, c2=c2: e.matmul(PS(4)[:, c2 * 64 + 32:c2 * 64 + 32 + NS], lhsT=wt[wq][:, kt, 128:256], rhs=hb[:, kt, NP:TT], start=(kt == 0), stop=(kt == 7)),
                                     reads=[c_wt[wq][1], c_h[kt][4]], writes=[c_ps[4]])
                        S.op('act', lambda e, c2=c2, ub=ub: e.activation(out=Ub[c2][:, 3:515], in_=PS(ub), func=AF.Copy), reads=[c_ps[ub]], writes=[c_Ub[c2]])
                        if bi == 0:
                            S.op('dve', lambda e, c2=c2: e.memset(Ub[c2][:, 0:3], 0.0), writes=[c_Ub[c2]])
                        else:
                            S.op('dve', lambda e, c2=c2, ct=ct: e.tensor_copy(out=Ub[c2][:, 0:3], in_=uh[:, :, ct]), reads=[c_uh[ct]], writes=[c_Ub[c2]])
                        S.op('dve', lambda e, c2=c2, ub=ub, ct=ct: e.tensor_scalar(out=uc[c2][:, 0:512], in0=PS(ub), scalar1=vcol('lru_conv_w', ct, 3), scalar2=vcol('lru_conv_b', ct), op0=ALU.mult, op1=ALU.add),
                             reads=[c_ps[ub], c_vec], writes=[c_uc[c2]])
                        for k in range(3):
                            S.op('dve', lambda e, c2=c2, ct=ct, k=k: e.scalar_tensor_tensor(out=uc[c2][:, 0:512], in0=Ub[c2][:, k:k + 512], scalar=vcol('lru_conv_w', ct, k), in1=uc[c2][:, 0:512], op0=ALU.mult, op1=ALU.add),
                                 reads=[c_Ub[c2], c_uc[c2], c_vec], writes=[c_uc[c2]])
                        S.op('dve', lambda e, c2=c2, ct=ct: e.tensor_copy(out=uh[:, :, ct], in_=Ub[c2][:, 512:515]), reads=[c_Ub[c2]], writes=[c_uh[ct]])
                        if last:
                            o4 = c2 * 64
                            S.op('dve', lambda e, c2=c2, ct=ct, o4=o4: e.tensor_scalar(out=uc[c2][:, 512:528], in0=PS(4)[:, o4:o4 + NS], scalar1=vcol('lru_conv_w', ct, 3), scalar2=vcol('lru_conv_b', ct), op0=ALU.mult, op1=ALU.add),
                                 reads=[c_ps[4], c_vec], writes=[c_uc[c2]])
                            for k in range(3):
                                S.op('dve', lambda e, c2=c2, ct=ct, k=k: e.scalar_tensor_tensor(out=uc[c2][:, 512:528], in0=cs[:, ct, :, k], scalar=vcol('lru_conv_w', ct, k), in1=uc[c2][:, 512:528], op0=ALU.mult, op1=ALU.add),
                                     reads=[c_cs, c_uc[c2], c_vec], writes=[c_uc[c2]])
                            S.op('act', lambda e, ct=ct, o4=o4: e.activation(out=us[:, ct, :], in_=PS(4)[:, o4:o4 + NS], func=AF.Copy), reads=[c_ps[4]], writes=[c_us[ct]])
                        S.op('act', lambda e, c2=c2, nn=nn: e.activation(out=ucb[c2][:, 0:nn], in_=uc[c2][:, 0:nn], func=AF.Copy), reads=[c_uc[c2]], writes=[c_ucb[c2]])
                    for c2 in range(2):
                        ct = 2 * n + c2
                        rb = 5
                        ib = 6
                        T = [tb[(kt_ + i) % NT] for i in range(3)]
                        cT = [c_tb[(kt_ + i) % NT] for i in range(3)]
                        kt_ += 3
                        segs = [(0, 512)] + ([(512, NS)] if last else [])
                        for (s0, sn) in segs:
                            for k2 in range(2):
                                S.op('pe', lambda e, k2=k2, c2=c2, g2=g2, s0=s0, sn=sn: e.matmul(PS(5)[:, 0:sn], lhsT=wg2[g2][:, 0, k2, c2 * 128:(c2 + 1) * 128], rhs=ucb[k2][:, s0:s0 + sn], start=(k2 == 0), stop=(k2 == 1)),
                                     reads=[c_wg2[g2], c_ucb[k2]], writes=[c_ps[5]])
                            for k2 in range(2):
                                S.op('pe', lambda e, k2=k2, c2=c2, g2=g2, s0=s0, sn=sn: e.matmul(PS(6)[:, 0:sn], lhsT=wg2[g2][:, 1, k2, c2 * 128:(c2 + 1) * 128], rhs=ucb[k2][:, s0:s0 + sn], start=(k2 == 0), stop=(k2 == 1)),
                                     reads=[c_wg2[g2], c_ucb[k2]], writes=[c_ps[6]])
                            S.op('act', lambda e, ct=ct, s0=s0, sn=sn, T=T: e.activation(out=T[0][:, s0:s0 + sn], in_=PS(5)[:, 0:sn], func=AF.Sigmoid, bias=vcol('lru_b_rg', ct), scale=1.0), reads=[c_ps[5], c_vec], writes=[cT[0]])
                            S.op('act', lambda e, ct=ct, s0=s0, sn=sn, T=T: e.activation(out=T[1][:, s0:s0 + sn], in_=PS(6)[:, 0:sn], func=AF.Sigmoid, bias=vcol('lru_b_ig', ct), scale=1.0), reads=[c_ps[6], c_vec], writes=[cT[1]])
                        S.op('act', lambda e, ct=ct, nn=nn, T=T: e.activation(out=T[2][:, 0:nn], in_=T[0][:, 0:nn], func=AF.Exp, scale=lc[:, ct, 1:2]), reads=[cT[0], c_lc], writes=[cT[2]])
                        S.op('act', lambda e, ct=ct, nn=nn, T=T: e.activation(out=T[0][:, 0:nn], in_=T[0][:, 0:nn], func=AF.Exp, scale=lc[:, ct, 0:1]), reads=[cT[0], c_lc], writes=[cT[0]])
                        S.op('act', lambda e, nn=nn, T=T: e.activation(out=T[2][:, 0:nn], in_=T[2][:, 0:nn], func=AF.Sqrt, bias=onec[:, 0:1], scale=-1.0), reads=[cT[2], c_one], writes=[cT[2]])
                        S.op('dve', lambda e, nn=nn, T=T: e.tensor_tensor(out=T[1][:, 0:nn], in0=T[1][:, 0:nn], in1=T[2][:, 0:nn], op=ALU.mult), reads=[cT[1], cT[2]], writes=[cT[1]])
                        S.op('dve', lambda e, nn=nn, T=T, c2=c2: e.tensor_tensor(out=T[1][:, 0:nn], in0=T[1][:, 0:nn], in1=uc[c2][:, 0:nn], op=ALU.mult), reads=[cT[1], c_uc[c2]], writes=[cT[1]])
                        if bi == 0:
                            S.op('dve', lambda e, T=T: e.tensor_tensor_scan(out=T[2][:, 0:512], data0=T[0][:, 0:512], data1=T[1][:, 0:512], initial=0.0, op0=ALU.mult, op1=ALU.add),
                                 reads=[cT[0], cT[1]], writes=[cT[2]])
                        else:
                            S.op('dve', lambda e, T=T, ct=ct: e.tensor_tensor_scan(out=T[2][:, 0:512], data0=T[0][:, 0:512], data1=T[1][:, 0:512], initial=hprev[:, ct:ct + 1], op0=ALU.mult, op1=ALU.add),
                                 reads=[cT[0], cT[1], c_hp[ct]], writes=[cT[2]])
                        S.op('dve', lambda e, T=T, ct=ct: e.tensor_copy(out=hprev[:, ct:ct + 1], in_=T[2][:, 511:512]), reads=[cT[2]], writes=[c_hp[ct]])
                        if last:
                            S.op('dve', lambda e, T=T, ct=ct: e.tensor_tensor(out=T[2][:, 512:528], in0=T[0][:, 512:528], in1=h0[:, ct, :], op=ALU.mult), reads=[cT[0], c_h0], writes=[cT[2]])
                            S.op('dve', lambda e, T=T: e.tensor_tensor(out=T[2][:, 512:528], in0=T[2][:, 512:528], in1=T[1][:, 512:528], op=ALU.add), reads=[cT[1], cT[2]], writes=[cT[2]])
                            S.op('act', lambda e, T=T, ct=ct: e.activation(out=hs[:, ct, :], in_=T[2][:, 512:528], func=AF.Copy), reads=[cT[2]], writes=[c_hs[ct]])
                        gb = 2 + c2
                        S.op('act', lambda e, T=T, gb=gb: e.activation(out=T[0][:, 0:512], in_=PS(gb), func=AF.Gelu_apprx_tanh), reads=[c_ps[gb]], writes=[cT[0]])
                        if last:
                            o4 = c2 * 64 + 32
                            S.op('act', lambda e, T=T, o4=o4: e.activation(out=T[0][:, 512:528], in_=PS(4)[:, o4:o4 + NS], func=AF.Gelu_apprx_tanh), reads=[c_ps[4]], writes=[cT[0]])
                        S.op('dve', lambda e, T=T, ct=ct, c0=c0: e.tensor_tensor(out=hg[:, ct, c0:c0 + 512], in0=T[2][:, 0:512], in1=T[0][:, 0:512], op=ALU.mult), reads=[cT[0], cT[2]], writes=[c_hg[ct][bi]])
                        if last:
                            S.op('dve', lambda e, T=T, ct=ct: e.tensor_tensor(out=hg[:, ct, NP:TT], in0=T[2][:, 512:528], in1=T[0][:, 512:528], op=ALU.mult), reads=[cT[0], cT[2]], writes=[c_hg[ct][4]])
            proj_residual(hg, c_hg, I['lru_w_out'], wt, c_wt)
            S.op('pe', lambda e: e.transpose(PS(0)[0:8, 0:128], hprev[:, 0:8], ident), reads=[c_hp, c_cst], writes=[c_ps[0]])
            S.op('dve', lambda e: e.tensor_copy(out=stg[0:8, 0:128], in_=PS(0)[0:8, 0:128]), reads=[c_ps[0]], writes=[c_stg])
            S.dma('sp', O['lru_h_p'][:, :].rearrange("o (a p) -> (o a) p", p=128), stg[0:8, 0:128], reads=[c_stg])
            S.op('pe', lambda e: e.transpose(PS(1)[0:24, 0:128], uh[:].rearrange("p k a -> p (k a)"), ident), reads=[c_uh, c_cst], writes=[c_ps[1]])
            S.op('dve', lambda e: e.tensor_copy(out=stg[32:56, 0:128], in_=PS(1)[0:24, 0:128]), reads=[c_ps[1]], writes=[c_stg])
            S.dma('sp', O['lru_conv_p'][:, :].rearrange("k (a p) -> (k a) p", p=128), stg[32:56, 0:128], reads=[c_stg])
            fm_to_rows(hs, c_hs, O['lru_h_s'][:, :], stg, c_stg, 64)
            fm_to_rows(us, c_us, O['lru_conv_s'][:, 2, :], stg, c_stg, 96)
            S.barrier()

        def fm_to_rows(src, c_src, dst_ap, stg, c_stg, prow):
            for half in range(2):
                pb = 2 + half
                for a in range(4):
                    ct = half * 4 + a
                    S.op('pe', lambda e, ct=ct, a=a, pb=pb: e.transpose(PS(pb)[0:NS, a * 128:(a + 1) * 128], src[:, ct, :], ident), reads=[c_src[ct], c_cst], writes=[c_ps[pb]])
                S.op('dve', lambda e, half=half, pb=pb: e.tensor_copy(out=stg[prow:prow + NS, half * 512:(half + 1) * 512], in_=PS(pb)[0:NS, :]), reads=[c_ps[pb]], writes=[c_stg])
            S.dma('sp', dst_ap, stg[prow:prow + NS, :], reads=[c_stg])

        def proj_residual(src, c_src, w_ap, wbuf, c_wbuf):
            ko = 0
            for bi in range(4):
                c0 = bi * 512
                last = (bi == 3)
                for mt in range(8):
                    wq = ko % 2
                    ob = 5 + (ko % 2)
                    ko += 1
                    wload(wbuf[wq][:, :, 0:128], w_ap[:, mt * 128:(mt + 1) * 128], 8, c_wbuf[wq])
                    for kt in range(8):
                        S.op('pe', lambda e, kt=kt, wq=wq, ob=ob, c0=c0: e.matmul(PS(ob), lhsT=wbuf[wq][:, kt, 0:128], rhs=src[:, kt, c0:c0 + 512], start=(kt == 0), stop=(kt == 7)),
                             reads=[c_wbuf[wq], c_src[kt][bi]], writes=[c_ps[ob]])
                    S.op('dve', lambda e, mt=mt, ob=ob, c0=c0: e.tensor_tensor(out=xres[:, mt, c0:c0 + 512], in0=xres[:, mt, c0:c0 + 512], in1=PS(ob), op=ALU.add),
                         reads=[c_ps[ob], c_x[mt][bi]], writes=[c_x[mt][bi]])
                    if last:
                        for kt in range(8):
                            S.op('pe', lambda e, kt=kt, wq=wq: e.matmul(PS(7)[:, 0:NS], lhsT=wbuf[wq][:, kt, 0:128], rhs=src[:, kt, NP:TT], start=(kt == 0), stop=(kt == 7)),
                                 reads=[c_wbuf[wq], c_src[kt][4]], writes=[c_ps[7]])
                        S.op('dve', lambda e, mt=mt: e.tensor_tensor(out=xres[:, mt, NP:TT], in0=xres[:, mt, NP:TT], in1=PS(7)[:, 0:NS], op=ALU.add),
                             reads=[c_ps[7], c_x[mt][4]], writes=[c_x[mt][4]])


        def s5_layer(li, j):
            A = Arena()
            R = A.f32(4096)
            c_R = cells(4)
            Rr = [R[:, i * 1024:(i + 1) * 1024] for i in range(4)]
            A0 = Arena()
            A0.off = 0
            rmsnorm('norm_mix', li, A0)
            S.barrier()
            yg = A.bf16(8 * TT).rearrange("p (a t) -> p a t", a=8)
            c_yg = cells(8, 5)
            stg = A.f32(512)
            c_stg = Cell()
            prm = A.f32(96).rearrange("p (k s) -> p k s", k=3)
            c_prm = Cell()
            NSM = 22
            sm = A.f32(NSM * 32).rearrange("p (k s) -> p k s", k=NSM)
            c_sm = Cell()
            Apw = A.f32(9 * 2 * 32).rearrange("p (d k s) -> p d k s", d=9, k=2)
            cks = A.f32(8 * 2 * 32).rearrange("p (l k s) -> p l k s", l=8, k=2)
            Apn = A.f32(9 * 32).rearrange("p (d s) -> p d s", d=9)
            Bst = R[:, 2048:3072].rearrange("p (k s c) -> p k s c", k=2, s=32)
            c_Bst = c_R[2]
            c_Bbar = Cell()
            Bbar = A.f32(2 * 512).rearrange("p (k s c) -> p k s c", k=2, s=32)
            Cn = [A.f32(2 * 64).rearrange("p (k q) -> p k q", k=2) for _ in range(2)]
            c_Cn = cells(2)
            SmT = A.f32(2 * 2 * 32 * 16).rearrange("p (t k s n) -> p t k s n", t=2, k=2, s=32)
            c_SmT = Cell()
            Hfin = A.f32(64).rearrange("p (k s) -> p k s", k=2)
            c_Hfin = Cell()
            CCe = A.f32(256).rearrange("p (k c) -> p k c", k=2)
            c_CCe = Cell()
            CCt = A.f32(256).rearrange("p (k q c) -> p k q c", k=2, q=4)
            c_CCt = Cell()
            CA = A.bf16(4 * 9 * 2 * 32).rearrange("p (q d k c) -> p q d k c", q=4, d=9, k=2)
            c_CA = Cell()
            Wz = A.f32(2 * 128).rearrange("p (k q g c) -> p k q g c", k=2, q=4, g=2)
            c_Wz = cells(2)
            W1 = A.bf16(8 * 2 * 128).rearrange("p (s k c) -> p s k c", s=8, k=2)
            c_W1 = Cell()
            ZBb = A.bf16(2 * 128).rearrange("p (k c) -> p k c", k=2)
            c_ZBb = Cell()
            KTf = A.bf16(8 * 128).rearrange("p (d q c) -> p d q c", d=8, q=4)
            c_KTf = Cell()
            E = A.f32(2 * 1024).rearrange("p (k q n) -> p k q n", k=2, q=4)
            c_E = Cell()
            Et = R[:, 2048:4096].rearrange("p (i q n) -> p i q n", i=4, q=4)
            c_Et = [c_R[2], c_R[3]]
            Hb = A.bf16(2 * 4 * 258).rearrange("p (k q n) -> p k q n", k=2, q=4)
            c_Hb = Cell()
            h0 = A.f32(2 * 64).rearrange("p (k q b) -> p k q b", k=2, q=4)
            c_h0 = Cell()
            hn = A.f32(2 * 64).rearrange("p (k q b) -> p k q b", k=2, q=4)
            c_hn = Cell()
            hnb = A.bf16(2 * 64).rearrange("p (k q b) -> p k q b", k=2, q=4)
            c_hnb = Cell()
            ts_ = A.f32(4 * 64).rearrange("p (i q b) -> p i q b", i=4, q=4)
            c_ts = Cell()
            ysv = A.f32(NS)
            c_ysv = Cell()
            st2 = A.f32(2 * 512).rearrange("p (k c) -> p k c", k=2)
            c_st2 = Cell()
            wgl = [R[:, i * 1024:(i + 1) * 1024].bitcast(BF16).rearrange("p (k c) -> p k c", k=8) for i in range(2)]
            c_wgl = [c_R[0], c_R[1]]
            sg = [R[:, 2048:2576], R[:, 3072:3600]]
            c_sg = [c_R[2], c_R[3]]

            SMI = [0]

            def smn():
                SMI[0] += 1
                assert SMI[0] <= NSM
                return sm[:, SMI[0] - 1, :]

            def dv(fn, eng='dve'):
                S.op(eng, fn, reads=[c_sm, c_prm], writes=[c_sm])

            S.dma('sp', stg[0:32, 0:128], I['s5_a_re'][j, :, :].rearrange("(s g) p -> s (g p)", g=2), writes=[c_stg])
            S.dma('sp', stg[0:32, 128:256], I['s5_a_im'][j, :, :].rearrange("(s g) p -> s (g p)", g=2), writes=[c_stg])
            S.dma('sp', stg[0:32, 256:258], I['s5_log_dt'][j:j + 1, :].rearrange("o (s g) -> (o s) g", g=2), writes=[c_stg])
            S.op('dve', lambda e: e.tensor_copy(out=stg[0:32, 384:512].rearrange("p (g n) -> p g n", g=2), in_=stg[0:32, 256:258].unsqueeze(2).to_broadcast([32, 2, 64])), reads=[c_stg], writes=[c_stg])
            for k, c0 in enumerate((0, 128, 384)):
                S.op('pe', lambda e, k=k, c0=c0: e.transpose(PS(7)[:, k * 32:(k + 1) * 32], stg[0:32, c0:c0 + 128], ident[0:32, 0:32]), reads=[c_stg, c_cst], writes=[c_ps[7]])
            S.op('dve', lambda e: e.tensor_copy(out=prm[:].rearrange("p k s -> p (k s)"), in_=PS(7)[:, 0:96]), reads=[c_ps[7]], writes=[c_prm])
            are, aim, ldt = prm[:, 0, :], prm[:, 1, :], prm[:, 2, :]
            dtb, lre, lrd, ang, mag, y_, nf, f_, m_, sinv, cosv, abr, abi, den, t1_, t2_, cre, cim, m8, rm8 = [smn() for _ in range(20)]
            TS = lambda e, out, in0, s1, op0, s2=None, op1=None: e.tensor_scalar(out=out, in0=in0, scalar1=s1, scalar2=s2, op0=op0, **({'op1': op1} if op1 is not None else {}))
            TTn = lambda e, out, a, b, op: e.tensor_tensor(out=out, in0=a, in1=b, op=op)
            dv(lambda e: e.activation(out=dtb, in_=ldt, func=AF.Exp), 'act')
            dv(lambda e: TS(e, lre, are, -1e-4, ALU.min))
            dv(lambda e: TTn(e, lrd, lre, dtb, ALU.mult))
            dv(lambda e: TTn(e, ang, aim, dtb, ALU.mult))
            dv(lambda e: e.activation(out=mag, in_=lrd, func=AF.Exp), 'act')
            dv(lambda e: e.activation(out=m8, in_=lrd, func=AF.Exp, scale=8.0), 'act')
            dv(lambda e: e.memset(m_, math.pi / 2))
            dv(lambda e: e.activation(out=sinv, in_=ang, func=AF.Sin, scale=1.0 / 8), 'act')
            dv(lambda e: e.activation(out=cosv, in_=ang, func=AF.Sin, bias=m_[:, 0:1], scale=-1.0 / 8), 'act')
            for _ in range(3):
                dv(lambda e: TTn(e, y_, sinv, cosv, ALU.mult))
                dv(lambda e: TTn(e, nf, cosv, cosv, ALU.mult))
                dv(lambda e: TTn(e, f_, sinv, sinv, ALU.mult))
                dv(lambda e: TTn(e, cosv, nf, f_, ALU.subtract))
                dv(lambda e: TS(e, sinv, y_, 2.0, ALU.mult))
            dv(lambda e: TTn(e, abr, mag, cosv, ALU.mult))
            dv(lambda e: TTn(e, abi, mag, sinv, ALU.mult))
            dv(lambda e: TTn(e, den, lre, lre, ALU.mult))
            dv(lambda e: TTn(e, t1_, aim, aim, ALU.mult))
            dv(lambda e: TTn(e, den, den, t1_, ALU.add))
            dv(lambda e: e.reciprocal(out=den, in_=den))
            dv(lambda e: TS(e, t1_, abr, -1.0, ALU.add))
            dv(lambda e: TTn(e, cre, t1_, lre, ALU.mult))
            dv(lambda e: TTn(e, t2_, abi, aim, ALU.mult))
            dv(lambda e: TTn(e, cre, cre, t2_, ALU.add))
            dv(lambda e: TTn(e, cre, cre, den, ALU.mult))
            dv(lambda e: TTn(e, cim, abi, lre, ALU.mult))
            dv(lambda e: TTn(e, t2_, t1_, aim, ALU.mult))
            dv(lambda e: TTn(e, cim, cim, t2_, ALU.subtract))
            dv(lambda e: TTn(e, cim, cim, den, ALU.mult))
            dv(lambda e: e.memset(Apw[:, 0, 0, :], 1.0))
            dv(lambda e: e.memset(Apw[:, 0, 1, :], 0.0))
            for d in range(8):
                pr, pi = Apw[:, d, 0, :], Apw[:, d, 1, :]
                qr, qi = Apw[:, d + 1, 0, :], Apw[:, d + 1, 1, :]
                dv(lambda e, pr=pr, qr=qr: TTn(e, qr, pr, abr, ALU.mult))
                dv(lambda e, pi=pi: TTn(e, t2_, pi, abi, ALU.mult))
                dv(lambda e, qr=qr: TTn(e, qr, qr, t2_, ALU.subtract))
                dv(lambda e, pr=pr, qi=qi: TTn(e, qi, pr, abi, ALU.mult))
                dv(lambda e, pi=pi: TTn(e, t2_, pi, abr, ALU.mult))
                dv(lambda e, qi=qi: TTn(e, qi, qi, t2_, ALU.add))
            dv(lambda e: TS(e, Apn[:], Apw[:, :, 1, :], -1.0, ALU.mult))
            dv(lambda e: e.reciprocal(out=rm8, in_=m8))
            dv(lambda e: TTn(e, cks[:, 0, 0, :], Apw[:, 8, 0, :], rm8, ALU.mult))
            dv(lambda e: TTn(e, cks[:, 0, 1, :], Apw[:, 8, 1, :], rm8, ALU.mult))
            dv(lambda e: TS(e, cks[:, 0, 1, :], cks[:, 0, 1, :], -1.0, ALU.mult))
            for l in range(7):
                c0_, s0_ = cks[:, l, 0, :], cks[:, l, 1, :]
                c1_, s1_ = cks[:, l + 1, 0, :], cks[:, l + 1, 1, :]
                dv(lambda e, c0_=c0_, c1_=c1_: TTn(e, c1_, c0_, c0_, ALU.mult))
                dv(lambda e, s0_=s0_: TTn(e, t2_, s0_, s0_, ALU.mult))
                dv(lambda e, c1_=c1_: TTn(e, c1_, c1_, t2_, ALU.subtract))
                dv(lambda e, c0_=c0_, s0_=s0_, s1_=s1_: TTn(e, s1_, c0_, s0_, ALU.mult))
                dv(lambda e, s1_=s1_: TS(e, s1_, s1_, 2.0, ALU.mult))
            for k, nm in enumerate(('s5_b_re', 's5_b_im')):
                S.dma('sp', Bst[:, k, :, :], I[nm][j, :, :, :].rearrange("(s g) p c -> (g p) s c", g=2), writes=[c_Bst])
            bc = lambda x: x.unsqueeze(2).to_broadcast([128, 32, 16])
            Bt0 = R[:, 0:512].rearrange("p (s c) -> p s c", s=32)
            Bt1 = R[:, 512:1024].rearrange("p (s c) -> p s c", s=32)
            S.op('dve', lambda e: TTn(e, Bt0, Bst[:, 0, :, :], bc(cre), ALU.mult), reads=[c_Bst, c_sm], writes=[c_R[0]])
            S.op('dve', lambda e: TTn(e, Bt1, Bst[:, 1, :, :], bc(cim), ALU.mult), reads=[c_Bst, c_sm], writes=[c_R[0]])
            S.op('dve', lambda e: TTn(e, Bbar[:, 0, :, :], Bt0, Bt1, ALU.subtract), reads=[c_R[0]], writes=[c_Bbar])
            S.op('dve', lambda e: TTn(e, Bt0, Bst[:, 1, :, :], bc(cre), ALU.mult), reads=[c_Bst, c_sm], writes=[c_R[0]])
            S.op('dve', lambda e: TTn(e, Bt1, Bst[:, 0, :, :], bc(cim), ALU.mult), reads=[c_Bst, c_sm], writes=[c_R[0]])
            S.op('dve', lambda e: TTn(e, Bbar[:, 1, :, :], Bt0, Bt1, ALU.add), reads=[c_R[0]], writes=[c_Bbar])
            def load_C(ct):
                for k, nm in enumerate(('s5_c_re', 's5_c_im')):
                    S.dma('sp', Cn[ct % 2][:, k, :], I[nm][j, 8 * ct:8 * ct + 8, :, :].rearrange("g c p -> (g c) p"), writes=[c_Cn[ct % 2]])
            load_C(0)
            S.op('pool', lambda e: e.memset(SmT[:, :, 0, :, 0:1], 1.0), writes=[c_SmT])
            S.op('pool', lambda e: e.memset(SmT[:, :, 1, :, 0:1], 0.0), writes=[c_SmT])
            stt = R[:, 2048:4096].rearrange("p (i s n) -> p i s n", i=4, s=32)
            for t in range(2):
                for l in range(4):
                    n_ = 1 << l
                    ckl = cks[:, 4 * t + l, 0, :].unsqueeze(2).to_broadcast([128, 32, n_])
                    skl = cks[:, 4 * t + l, 1, :].unsqueeze(2).to_broadcast([128, 32, n_])
                    lo = slice(0, n_)
                    hi = slice(n_, 2 * n_)
                    S.op('pool', lambda e, t=t, ckl=ckl, lo=lo, n_=n_: TTn(e, stt[:, 0, :, 0:n_], SmT[:, t, 0, :, lo], ckl, ALU.mult), reads=[c_SmT, c_sm], writes=[c_R[2], c_R[3]])
                    S.op('pool', lambda e, t=t, skl=skl, lo=lo, n_=n_: TTn(e, stt[:, 1, :, 0:n_], SmT[:, t, 1, :, lo], skl, ALU.mult), reads=[c_SmT, c_sm], writes=[c_R[2], c_R[3]])
                    S.op('pool', lambda e, t=t, skl=skl, lo=lo, n_=n_: TTn(e, stt[:, 2, :, 0:n_], SmT[:, t, 0, :, lo], skl, ALU.mult), reads=[c_SmT, c_sm], writes=[c_R[2], c_R[3]])
                    S.op('pool', lambda e, t=t, ckl=ckl, lo=lo, n_=n_: TTn(e, stt[:, 3, :, 0:n_], SmT[:, t, 1, :, lo], ckl, ALU.mult), reads=[c_SmT, c_sm], writes=[c_R[2], c_R[3]])
                    S.op('pool', lambda e, t=t, hi=hi, n_=n_: TTn(e, SmT[:, t, 0, :, hi], stt[:, 0, :, 0:n_], stt[:, 1, :, 0:n_], ALU.subtract), reads=[c_R[2], c_R[3]], writes=[c_SmT])
                    S.op('pool', lambda e, t=t, hi=hi, n_=n_: TTn(e, SmT[:, t, 1, :, hi], stt[:, 2, :, 0:n_], stt[:, 3, :, 0:n_], ALU.add), reads=[c_R[2], c_R[3]], writes=[c_SmT])
            mask_hi = cc('mask_hi')
            mask_c16 = cc('mask_c16')
            maskq = cc('maskq')
            Dname = 's5_d'

            def do_ct(ct):
                if S5_STAGE < 2:
                    return
                Asl = lambda d, k: Apw[:, d, k, 4 * ct:4 * ct + 4]
                for k in range(2):
                    S.op('pool', lambda e, k=k: e.tensor_tensor(out=CCe[:, k, :].rearrange("p (g n) -> p g n", g=2), in0=Cn[ct % 2][:, k, :].unsqueeze(1).to_broadcast([128, 2, 64]),
                                                                in1=mask_c16.unsqueeze(2).to_broadcast([128, 2, 64]), op=ALU.mult), reads=[c_Cn[ct % 2], c_cst], writes=[c_CCe])
                    S.op('pe', lambda e, k=k: e.transpose(PS(6)[:, k * 128:(k + 1) * 128], CCe[:, k, :], ident), reads=[c_CCe, c_cst], writes=[c_ps[6]])
                S.op('act', lambda e: e.activation(out=CCt[:].rearrange("p k q c -> p (k q c)"), in_=PS(6)[:, 0:256], func=AF.Copy), reads=[c_ps[6]], writes=[c_CCt])
                if ct < 7:
                    load_C(ct + 1)
                for (d0, d1) in ((0, 5), (5, 9)):
                    nd = d1 - d0
                    t0 = R[:, 2048:2048 + 128 * nd].rearrange("p (q d c) -> p q d c", q=4, d=nd)
                    t1 = R[:, 3072:3072 + 128 * nd].rearrange("p (q d c) -> p q d c", q=4, d=nd)
                    Cb = lambda k, nd=nd: CCt[:, k, :, :].unsqueeze(2).to_broadcast([128, 4, nd, 32])
                    Ab = lambda k, nd=nd, d0=d0, d1=d1: Apw[:, d0:d1, k, 4 * ct:4 * ct + 4].rearrange("p d q -> p q d").unsqueeze(3).to_broadcast([128, 4, nd, 32])
                    S.op('pool', lambda e, t0=t0, Cb=Cb, Ab=Ab: TTn(e, t0, Cb(0), Ab(0), ALU.mult), reads=[c_CCt, c_sm], writes=[c_R[2]])
                    S.op('pool', lambda e, t1=t1, Cb=Cb, Ab=Ab: TTn(e, t1, Cb(1), Ab(1), ALU.mult), reads=[c_CCt, c_sm], writes=[c_R[3]])
                    S.op('dve', lambda e, t0=t0, t1=t1, d0=d0, d1=d1: TTn(e, CA[:, :, d0:d1, 0, :], t0, t1, ALU.subtract), reads=[c_R[2], c_R[3]], writes=[c_CA])
                    An = lambda nd=nd, d0=d0, d1=d1: Apn[:, d0:d1, 4 * ct:4 * ct + 4].rearrange("p d q -> p q d").unsqueeze(3).to_broadcast([128, 4, nd, 32])
                    S.op('pool', lambda e, t0=t0, Cb=Cb, An=An: TTn(e, t0, Cb(0), An(), ALU.mult), reads=[c_CCt, c_sm], writes=[c_R[2]])
                    S.op('pool', lambda e, t1=t1, Cb=Cb, Ab=Ab: TTn(e, t1, Cb(1), Ab(0), ALU.mult), reads=[c_CCt, c_sm], writes=[c_R[3]])
                    S.op('dve', lambda e, t0=t0, t1=t1, d0=d0, d1=d1: TTn(e, CA[:, :, d0:d1, 1, :], t0, t1, ALU.subtract), reads=[c_R[2], c_R[3]], writes=[c_CA])
                u_ = [R[:, 2048 + 512 * i:2048 + 512 * (i + 1)].rearrange("p (d q c) -> p d q c", d=8, q=4) for i in range(4)]
                Bb_ = lambda k: Bbar[:, k, 4 * ct:4 * ct + 4, :].unsqueeze(1).to_broadcast([128, 8, 4, 16])
                Ad_ = lambda k: Apw[:, 0:8, k, 4 * ct:4 * ct + 4].unsqueeze(3).to_broadcast([128, 8, 4, 16])
                S.op('pool', lambda e: TTn(e, u_[0], Bb_(0), Ad_(0), ALU.mult), reads=[c_Bbar, c_sm], writes=[c_R[2]])
                S.op('pool', lambda e: TTn(e, u_[1], Bb_(1), Ad_(1), ALU.mult), reads=[c_Bbar, c_sm], writes=[c_R[2]])
                S.op('pool', lambda e: TTn(e, u_[2], Bb_(0), Ad_(1), ALU.mult), reads=[c_Bbar, c_sm], writes=[c_R[3]])
                S.op('pool', lambda e: TTn(e, u_[3], Bb_(1), Ad_(0), ALU.mult), reads=[c_Bbar, c_sm], writes=[c_R[3]])
                S.op('dve', lambda e: TTn(e, u_[0], u_[0], u_[1], ALU.subtract), reads=[c_R[2]], writes=[c_R[2]])
                S.op('dve', lambda e: TTn(e, u_[2], u_[2], u_[3], ALU.add), reads=[c_R[3]], writes=[c_R[3]])
                for s_ in range(8):
                    d = 7 - s_
                    for k in range(2):
                        S.op('dve', lambda e, k=k, d=d: e.tensor_tensor(out=Wz[:, k, :, :, :], in0=u_[2 * k][:, d, :, :].unsqueeze(2).to_broadcast([128, 4, 2, 16]),
                                                                  in1=mask_hi.unsqueeze(1).unsqueeze(3).to_broadcast([128, 4, 2, 16]), op=ALU.mult),
                             reads=[c_R[2 + k], c_cst], writes=[c_Wz[k]])
                        S.op('pe', lambda e, k=k: e.transpose(PS(6)[:, 256 + k * 128:256 + (k + 1) * 128], Wz[:, k, :, :, :].rearrange("p q g c -> p (q g c)"), ident),
                             reads=[c_Wz[k], c_cst], writes=[c_ps[6]])
                        if s_ == 7:
                            S.op('act', lambda e, k=k: e.activation(out=ZBb[:, k, :], in_=Wz[:, k, :, :, :].rearrange("p q g c -> p (q g c)"), func=AF.Copy), reads=[c_Wz[k]], writes=[c_ZBb])
                    S.op('act', lambda e, s_=s_: e.activation(out=W1[:, s_, :, :].rearrange("p k c -> p (k c)"), in_=PS(6)[:, 256:512], func=AF.Copy), reads=[c_ps[6]], writes=[c_W1])
                for q in range(4):
                    for k in range(2):
                        S.op('pe', lambda e, q=q, k=k: e.matmul(PS(7)[32 * q:32 * q + 32, 0:288].rearrange("p (d c) -> p d c", d=9), lhsT=ZBb[:, k, 32 * q:32 * q + 32], rhs=CA[:, q, :, k, :], start=(k == 0), stop=(k == 1), tile_position=(0, 32 * q)),
                             reads=[c_ZBb, c_CA], writes=[c_ps[7]])
                S.op('dve', lambda e: e.tensor_tensor(out=KTf[:], in0=PS(7)[:, 0:256].rearrange("p (d c) -> p d c", d=8).unsqueeze(2).to_broadcast([128, 8, 4, 32]),
                                                      in1=maskq.unsqueeze(1).unsqueeze(3).to_broadcast([128, 8, 4, 32]), op=ALU.mult), reads=[c_ps[7], c_cst], writes=[c_KTf])
                if S5_STAGE < 3:
                    return
                for q in range(4):
                    for k in range(2):
                        bnk = 2 * k + q // 2
                        o_ = (q % 2) * 256
                        for s_ in range(8):
                            S.op('pe', lambda e, q=q, k=k, s_=s_, bnk=bnk, o_=o_: e.matmul(PS(bnk)[:, o_:o_ + 256], lhsT=W1[32 * q:32 * q + 32, s_, k, :], rhs=hb[32 * q:32 * q + 32, ct, s_:NP:8], start=(s_ == 0), stop=(s_ == 7), tile_position=(32 * q, 0)),
                                 reads=[c_W1, c_h[ct][0:4]], writes=[c_ps[bnk]])
                Xre = psum[:, 0:2, :].rearrange("p a (h n) -> p (a h) n", h=2)
                Xim = psum[:, 2:4, :].rearrange("p a (h n) -> p (a h) n", h=2)
                Ea = lambda k: SmT[:, 1, k, 4 * ct:4 * ct + 4, :].unsqueeze(3).to_broadcast([128, 4, 16, 16])
                Eb = lambda k: SmT[:, 0, k, 4 * ct:4 * ct + 4, :].unsqueeze(2).to_broadcast([128, 4, 16, 16])
                e0 = R[:, 2048:3072].rearrange("p (q a b) -> p q a b", q=4, a=16)
                e1 = R[:, 3072:4096].rearrange("p (q a b) -> p q a b", q=4, a=16)
                Ev = lambda k: E[:, k, :, :].rearrange("p q (a b) -> p q a b", a=16)
                S.op('pool', lambda e: TTn(e, e0, Ea(0), Eb(0), ALU.mult), reads=[c_SmT], writes=[c_R[2]])
                S.op('pool', lambda e: TTn(e, e1, Ea(1), Eb(1), ALU.mult), reads=[c_SmT], writes=[c_R[3]])
                S.op('dve', lambda e: TTn(e, Ev(0), e0, e1, ALU.subtract), reads=[c_R[2], c_R[3]], writes=[c_E])
                S.op('pool', lambda e: TTn(e, e0, Ea(0), Eb(1), ALU.mult), reads=[c_SmT], writes=[c_R[2]])
                S.op('pool', lambda e: TTn(e, e1, Ea(1), Eb(0), ALU.mult), reads=[c_SmT], writes=[c_R[3]])
                S.op('dve', lambda e: TTn(e, Ev(1), e0, e1, ALU.add), reads=[c_R[2], c_R[3]], writes=[c_E])
                Er = E[:, 0, :, :]
                Ei = E[:, 1, :, :]
                R4 = [r.rearrange("p (q n) -> p q n", q=4) for r in Rr]
                S.op('dve', lambda e: TTn(e, R4[0], Xre, Er, ALU.mult), reads=[c_ps[0], c_ps[1], c_E], writes=[c_R[0]])
                S.op('dve', lambda e: TTn(e, R4[1], Xim, Ei, ALU.mult), reads=[c_ps[2], c_ps[3], c_E], writes=[c_R[1]])
                S.op('dve', lambda e: TTn(e, R4[0], R4[0], R4[1], ALU.subtract), reads=[c_R[0], c_R[1]], writes=[c_R[0]])
                S.op('dve', lambda e: TTn(e, R4[1], Xre, Ei, ALU.mult), reads=[c_ps[0], c_ps[1], c_E], writes=[c_R[1]])
                S.op('dve', lambda e: TTn(e, R4[2], Xim, Er, ALU.mult), reads=[c_ps[2], c_ps[3], c_E], writes=[c_R[2]])
                S.op('dve', lambda e: TTn(e, R4[1], R4[1], R4[2], ALU.add), reads=[c_R[1], c_R[2]], writes=[c_R[1]])
                for q in range(4):
                    st_i = 4 * ct + q
                    S.op('dve', lambda e, q=q, st_i=st_i: e.tensor_tensor_scan(out=R4[2][:, q, :], data0=m8[:, st_i:st_i + 1].to_broadcast([128, 256]), data1=R4[0][:, q, :], initial=0.0, op0=ALU.mult, op1=ALU.add),
                         reads=[c_R[0], c_sm], writes=[c_R[2]])
                    S.op('dve', lambda e, q=q, st_i=st_i: e.tensor_tensor_scan(out=R4[3][:, q, :], data0=m8[:, st_i:st_i + 1].to_broadcast([128, 256]), data1=R4[1][:, q, :], initial=0.0, op0=ALU.mult, op1=ALU.add),
                         reads=[c_R[1], c_sm], writes=[c_R[3]])
                S.op('pool', lambda e: TTn(e, R4[0], R4[2], Er, ALU.mult), reads=[c_R[2], c_E], writes=[c_R[0]])
                S.op('pool', lambda e: TTn(e, R4[1], R4[3], Ei, ALU.mult), reads=[c_R[3], c_E], writes=[c_R[1]])
                S.op('dve', lambda e: TTn(e, R4[0], R4[0], R4[1], ALU.add), reads=[c_R[0], c_R[1]], writes=[c_R[0]])
                S.op('pool', lambda e: TTn(e, R4[1], R4[3], Er, ALU.mult), reads=[c_R[3], c_E], writes=[c_R[1]])
                S.op('dve', lambda e: TTn(e, R4[2], R4[2], Ei, ALU.mult), reads=[c_R[2], c_E], writes=[c_R[2]])
                S.op('dve', lambda e: TTn(e, R4[1], R4[1], R4[2], ALU.subtract), reads=[c_R[1], c_R[2]], writes=[c_R[1]])
                S.op('act', lambda e: e.activation(out=Hb[:, 0, :, 1:257], in_=R4[0], func=AF.Copy), reads=[c_R[0]], writes=[c_Hb])
                S.op('act', lambda e: e.activation(out=Hb[:, 1, :, 1:257], in_=R4[1], func=AF.Copy), reads=[c_R[1]], writes=[c_Hb])
                S.op('pool', lambda e: e.memset(Hb[:, :, :, 0:1], 0.0), writes=[c_Hb])
                S.op('act', lambda e: e.activation(out=Hfin[:, 0, 4 * ct:4 * ct + 4], in_=R4[0][:, :, 255], func=AF.Copy), reads=[c_R[0]], writes=[c_Hfin])
                S.op('act', lambda e: e.activation(out=Hfin[:, 1, 4 * ct:4 * ct + 4], in_=R4[1][:, :, 255], func=AF.Copy), reads=[c_R[1]], writes=[c_Hfin])
                if S5_STAGE < 4:
                    return
                yv = R[:, 0:2048]
                for tau in range(8):
                    yb = 4 + (tau % 2)
                    for s_ in range(tau + 1):
                        S.op('pe', lambda e, tau=tau, s_=s_, yb=yb: e.matmul(PS(yb)[:, 0:256], lhsT=KTf[:, tau - s_, :, :].rearrange("p q c -> p (q c)"), rhs=hb[:, ct, s_:NP:8], start=(s_ == 0), stop=False),
                             reads=[c_KTf, c_h[ct][0:4]], writes=[c_ps[yb]])
                    for q in range(4):
                        for k in range(2):
                            S.op('pe', lambda e, tau=tau, q=q, k=k, yb=yb: e.matmul(PS(yb)[32 * q:32 * q + 32, 0:256], lhsT=CA[:, q, tau + 1, k, :], rhs=Hb[:, k, q, 0:256], start=False, stop=(k == 1), tile_position=(0, 32 * q)),
                                 reads=[c_CA, c_Hb], writes=[c_ps[yb]])
                    S.op('dve', lambda e, tau=tau, yb=yb: e.scalar_tensor_tensor(out=yv[:, tau:NP:8], in0=hb[:, ct, tau:NP:8], scalar=vcol(Dname, ct, j), in1=PS(yb)[:, 0:256], op0=ALU.mult, op1=ALU.add),
                         reads=[c_ps[yb], c_h[ct][0:4], c_vec, c_R[2], c_R[3]], writes=[c_R[0], c_R[1]])
                for bi in range(4):
                    S.op('act', lambda e, bi=bi: e.activation(out=yg[:, ct, bi * 512:(bi + 1) * 512], in_=yv[:, bi * 512:(bi + 1) * 512], func=AF.Gelu_apprx_tanh), reads=[c_R[0], c_R[1]], writes=[c_yg[ct][bi]])
                if S5_STAGE < 5:
                    return
                for k, nm in enumerate(('state_s5_re', 'state_s5_im')):
                    S.dma('sp', st2[0:NS, k, :], I[nm][j, :, ct * 512:(ct + 1) * 512], writes=[c_st2])
                for k in range(2):
                    for q in range(4):
                        S.op('pe', lambda e, k=k, q=q: e.transpose(PS(6)[:, (k * 4 + q) * NS:(k * 4 + q + 1) * NS], st2[0:NS, k, q * 128:(q + 1) * 128], ident[0:NS, 0:NS]), reads=[c_st2, c_cst], writes=[c_ps[6]])
                S.op('act', lambda e: e.activation(out=h0[:].rearrange("p k q b -> p (k q b)"), in_=PS(6)[:, 0:128], func=AF.Copy), reads=[c_ps[6]], writes=[c_h0])
                for q in range(4):
                    for k in range(2):
                        S.op('pe', lambda e, q=q, k=k: e.matmul(PS(4 + q)[:, 480 + k * NS:480 + (k + 1) * NS], lhsT=W1[32 * q:32 * q + 32, 7, k, :], rhs=hb[32 * q:32 * q + 32, ct, NP:TT], start=True, stop=True, tile_position=(32 * q, 0)),
                             reads=[c_W1, c_h[ct][4]], writes=[c_ps[4 + q]])
                xs_ = lambda k: psum[:, 4:8, 480 + k * NS:480 + (k + 1) * NS]
                bb = lambda x: x.unsqueeze(2).to_broadcast([128, 4, NS])
                S.op('dve', lambda e: TTn(e, ts_[:, 0, :, :], h0[:, 0, :, :], bb(Asl(1, 0)), ALU.mult), reads=[c_h0, c_sm], writes=[c_ts])
                S.op('dve', lambda e: TTn(e, ts_[:, 1, :, :], h0[:, 1, :, :], bb(Asl(1, 1)), ALU.mult), reads=[c_h0, c_sm], writes=[c_ts])
                S.op('dve', lambda e: TTn(e, ts_[:, 0, :, :], ts_[:, 0, :, :], ts_[:, 1, :, :], ALU.subtract), reads=[c_ts], writes=[c_ts])
                S.op('dve', lambda e: TTn(e, hn[:, 0, :, :], ts_[:, 0, :, :], xs_(0), ALU.add), reads=[c_ts, c_ps[4:8]], writes=[c_hn])
                S.op('dve', lambda e: TTn(e, ts_[:, 2, :, :], h0[:, 1, :, :], bb(Asl(1, 0)), ALU.mult), reads=[c_h0, c_sm], writes=[c_ts])
                S.op('dve', lambda e: TTn(e, ts_[:, 3, :, :], h0[:, 0, :, :], bb(Asl(1, 1)), ALU.mult), reads=[c_h0, c_sm], writes=[c_ts])
                S.op('dve', lambda e: TTn(e, ts_[:, 2, :, :], ts_[:, 2, :, :], ts_[:, 3, :, :], ALU.add), reads=[c_ts], writes=[c_ts])
                S.op('dve', lambda e: TTn(e, hn[:, 1, :, :], ts_[:, 2, :, :], xs_(1), ALU.add), reads=[c_ts, c_ps[4:8]], writes=[c_hn])
                S.op('act', lambda e: e.activation(out=hnb[:].rearrange("p k q b -> p (k q b)"), in_=hn[:].rearrange("p k q b -> p (k q b)"), func=AF.Copy), reads=[c_hn], writes=[c_hnb])
                for q in range(4):
                    for k in range(2):
                        S.op('pe', lambda e, q=q, k=k: e.matmul(PS(7)[32 * q:32 * q + 32, 448:448 + NS], lhsT=CA[:, q, 0, k, :], rhs=hnb[:, k, q, :], start=(k == 0), stop=(k == 1), tile_position=(0, 32 * q)),
                             reads=[c_CA, c_hnb], writes=[c_ps[7]])
                S.op('dve', lambda e: e.scalar_tensor_tensor(out=ysv[:], in0=hb[:, ct, NP:TT], scalar=vcol(Dname, ct, j), in1=PS(7)[:, 448:448 + NS], op0=ALU.mult, op1=ALU.add),
                     reads=[c_ps[7], c_h[ct][4], c_vec], writes=[c_ysv])
                S.op('act', lambda e: e.activation(out=yg[:, ct, NP:TT], in_=ysv[:], func=AF.Gelu_apprx_tanh), reads=[c_ysv], writes=[c_yg[ct][4]])
                for k in range(2):
                    for q in range(4):
                        S.op('pe', lambda e, k=k, q=q: e.transpose(PS(6)[0:NS, q * 128:(q + 1) * 128], hn[:, k, q, :], ident), reads=[c_hn, c_cst], writes=[c_ps[6]])
                    S.op('dve', lambda e, k=k: e.tensor_copy(out=st2[0:NS, k, :], in_=PS(6)[0:NS, :]), reads=[c_ps[6]], writes=[c_st2])
                S.dma('sp', O['s5_re_s'][j, :, ct * 512:(ct + 1) * 512], st2[0:NS, 0, :], reads=[c_st2])
                S.dma('sp', O['s5_im_s'][j, :, ct * 512:(ct + 1) * 512], st2[0:NS, 1, :], reads=[c_st2])
            for ct in range(8):
                do_ct(ct)
            for k, nm in enumerate(('s5_re_p', 's5_im_p')):
                S.op('pe', lambda e, k=k: e.transpose(PS(6)[0:32, k * 128:(k + 1) * 128], Hfin[:, k, :], ident), reads=[c_Hfin, c_cst], writes=[c_ps[6]])
            S.op('dve', lambda e: e.tensor_copy(out=stg[0:32, 0:256], in_=PS(6)[0:32, 0:256]), reads=[c_ps[6]], writes=[c_stg])
            S.dma('sp', O['s5_re_p'][j:j + 1, :].rearrange("o (s n) -> (o s) n", n=128), stg[0:32, 0:128], reads=[c_stg])
            S.dma('sp', O['s5_im_p'][j:j + 1, :].rearrange("o (s n) -> (o s) n", n=128), stg[0:32, 128:256], reads=[c_stg])
            if S5_STAGE < 6:
                S.barrier()
                return
            kq = 0
            for bi in range(4):
                c0 = bi * 512
                last = (bi == 3)
                for mt in range(8):
                    wq = kq % 2
                    vb = kq % 2
                    gb = 2 + (kq % 2)
                    kq += 1
                    wload(wgl[wq][:, :, 0:128], I['s5_w_glu'][j, :, mt * 128:(mt + 1) * 128], 8, c_wgl[wq])
                    wload(wgl[wq][:, :, 128:256], I['s5_w_glu'][j, :, D + mt * 128:D + (mt + 1) * 128], 8, c_wgl[wq])
                    segs = [(c0, 512, bi, vb, gb, 0)] + ([(NP, NS, 4, 4, 4, 32)] if last else [])
                    for (s0, sn, cb, vbb, gbb, go) in segs:
                        for kt in range(8):
                            S.op('pe', lambda e, kt=kt, wq=wq, s0=s0, sn=sn, vbb=vbb: e.matmul(PS(vbb)[:, 0:sn], lhsT=wgl[wq][:, kt, 0:128], rhs=yg[:, kt, s0:s0 + sn], start=(kt == 0), stop=(kt == 7)),
                                 reads=[c_wgl[wq], c_yg[kt][cb]], writes=[c_ps[vbb]])
                        for kt in range(8):
                            S.op('pe', lambda e, kt=kt, wq=wq, s0=s0, sn=sn, gbb=gbb, go=go: e.matmul(PS(gbb)[:, go:go + sn], lhsT=wgl[wq][:, kt, 128:256], rhs=yg[:, kt, s0:s0 + sn], start=(kt == 0), stop=(kt == 7)),
                                 reads=[c_wgl[wq], c_yg[kt][cb]], writes=[c_ps[gbb]])
                        S.op('act', lambda e, wq=wq, sn=sn, gbb=gbb, go=go: e.activation(out=sg[wq][:, 0:sn], in_=PS(gbb)[:, go:go + sn], func=AF.Sigmoid), reads=[c_ps[gbb]], writes=[c_sg[wq]])
                        S.op('dve', lambda e, wq=wq, sn=sn, vbb=vbb: e.tensor_tensor(out=sg[wq][:, 0:sn], in0=sg[wq][:, 0:sn], in1=PS(vbb)[:, 0:sn], op=ALU.mult), reads=[c_sg[wq], c_ps[vbb]], writes=[c_sg[wq]])
                        S.op('dve', lambda e, wq=wq, sn=sn, s0=s0, mt=mt: e.tensor_tensor(out=xres[:, mt, s0:s0 + sn], in0=xres[:, mt, s0:s0 + sn], in1=sg[wq][:, 0:sn], op=ALU.add),
                             reads=[c_sg[wq], c_x[mt][cb]], writes=[c_x[mt][cb]])
            S.barrier()


        TT_OFF = [0]

        def rwkv_layer(li):
            A0 = Arena()
            rmsnorm('norm_mix', li, A0)
            S.barrier()
            A = Arena()
            NB = 8
            BW = 256
            NW = BW + NS
            TTn = lambda e, out, a, b, op: e.tensor_tensor(out=out, in0=a, in1=b, op=op)
            mask_hi = cc('mask_hi')
            blk64 = cc('blk64')
            w1 = A.bf16(8 * 64).rearrange("p (k c) -> p k c", k=8); c_w1 = Cell()
            a1 = A.bf16(8 * 64).rearrange("p (k c) -> p k c", k=8); c_a1 = Cell()
            g1 = A.bf16(8 * 128).rearrange("p (k c) -> p k c", k=8); c_g1 = Cell()
            w2 = A.bf16(D); c_w2 = Cell()
            a2 = A.bf16(D); c_a2 = Cell()
            g2 = A.bf16(D); c_g2 = Cell()
            wload(w1[:], I['rw_w1'][:, :], 8, c_w1)
            wload(a1[:], I['rw_a1'][:, :], 8, c_a1)
            wload(g1[:], I['rw_g1'][:, :], 8, c_g1)
            S.dma('pool', w2[0:64, :], I['rw_w2'][:, :], writes=[c_w2])
            S.dma('pool', a2[0:64, :], I['rw_a2'][:, :], writes=[c_a2])
            S.dma('pool', g2[:, :], I['rw_g2'][:, :], writes=[c_g2])
            cmk = A.f32(BW); c_cmk = Cell()
            S.dma('sp', cmk[:], I['cmask'][:, 0:BW], writes=[c_cmk])
            sh0 = A.f32(8 * NS).rearrange("p (a b) -> p a b", a=8); c_sh0 = Cell()
            stg = arena[:, ARENA_W - D:ARENA_W]; c_stg = Cell()
            S.dma('sp', stg[0:NS, :], I['state_rwkv_shift'][:, :], writes=[c_stg])
            for ct in range(8):
                S.op('pe', lambda e, ct=ct: e.transpose(PS(7)[:, ct * NS:(ct + 1) * NS], stg[0:NS, ct * 128:(ct + 1) * 128], ident[0:NS, 0:NS]), reads=[c_stg, c_cst], writes=[c_ps[7]])
            S.op('dve', lambda e: e.tensor_copy(out=sh0[:].rearrange("p a b -> p (a b)"), in_=PS(7)[:, 0:8 * NS]), reads=[c_ps[7]], writes=[c_sh0])
            Mf = A.f32(8 * 64).rearrange("p (a i) -> p a i", a=8); c_Mf = cells(8)
            Mb = A.bf16(8 * 2 * 64).rearrange("p (a h i) -> p a h i", a=8, h=2); c_Mb = cells(8)
            S.op('pool', lambda e: e.memset(Mf[:], 0.0), writes=[c_Mf])
            S.op('pool', lambda e: e.memset(Mb[:], 0.0), writes=[c_Mb])
            SV = A.f32(8 * 7 * NS).rearrange("p (a v b) -> p a v b", a=8, v=7); c_SV = cells(8)
            onec = A.f32(1); c_one = Cell()
            gnc = A.f32(1)
            S.op('dve', lambda e: e.memset(onec[:], 1.0), writes=[c_one])
            S.op('dve', lambda e: e.memset(gnc[:], 64e-5), writes=[c_one])
            xx = A.bf16(8 * NW).rearrange("p (a n) -> p a n", a=8); c_xx = Cell()
            xs = [A.bf16(8 * NW).rearrange("p (a n) -> p a n", a=8) for _ in range(4)]; c_xs = cells(4)
            og = A.bf16(8 * NW).rearrange("p (a n) -> p a n", a=8); c_og = cells(8)
            l1 = [A.bf16(NW) for _ in range(3)]; c_l1 = cells(3)
            wr = [A.bf16(8 * 3 * 128).rearrange("p (k n c) -> p k n c", k=8, n=3) for _ in range(2)]; c_wr = cells(2, 3)
            wo = [A.bf16(8 * 128).rearrange("p (k c) -> p k c", k=8) for _ in range(2)]; c_wo = cells(2)
            TT_OFF[0] = A.off
            NT_ = 13
            Tt = [A.f32(NW) for _ in range(NT_)]; c_T = cells(NT_)
            KKm = [A.bf16(NW) for _ in range(2)]; c_KKm = cells(2)
            Rm = [A.bf16(NW) for _ in range(2)]; c_Rm = cells(2)
            Kb = A.bf16(NW); c_Kb = Cell()
            Bb = A.bf16(NW); c_Bb = Cell()
            KBe = A.f32(3 * BW).rearrange("p (v n) -> p v n", v=3); c_KBe = Cell()
            AT = A.bf16(4 * 2 * 5 * 64).rearrange("p (c h m t) -> p c h m t", c=4, h=2, m=5); c_AT = Cell()
            PQ = [A.bf16(4 * 2 * 2 * 64).rearrange("p (c h m t) -> p c h m t", c=4, h=2, m=2) for _ in range(2)]; c_PQ = cells(2)
            Y = A.bf16(4 * 2 * 64).rearrange("p (c h t) -> p c h t", c=4, h=2); c_Y = Cell()
            Tok = A.bf16(4 * 3 * 128).rearrange("p (c v n) -> p c v n", c=4, v=3); c_Tok = Cell()
            Wp = A.bf16(2 * 64).rearrange("p (h i) -> p h i", h=2); c_Wp = Cell()
            Us = A.bf16(2 * 64).rearrange("p (h i) -> p h i", h=2); c_Us = Cell()
            ot = A.f32(NW); c_ot = Cell()
            imp = cc('imp')
            mask5 = cc('mask5')

            def gn_gate(ct, ncol, o_ap, c_o, r_ap, k2_ap, v_ap, g_ap, c_in, out_ap, c_out, tA, tB, c_tA, c_tB):
                S.op('pe', lambda e: e.matmul(PS(6)[:, 0:ncol], lhsT=blk64, rhs=o_ap, start=True, stop=True), reads=[c_o, c_cst], writes=[c_ps[6]])
                S.op('dve', lambda e: e.scalar_tensor_tensor(out=tA, in0=PS(6)[:, 0:ncol], scalar=-1.0 / 64, in1=o_ap, op0=ALU.mult, op1=ALU.add), reads=[c_ps[6], c_o], writes=[c_tA])
                S.op('pool', lambda e: TTn(e, tB, tA, tA, ALU.mult), reads=[c_tA], writes=[c_tB])
                S.op('pe', lambda e: e.matmul(PS(6)[:, 0:ncol], lhsT=blk64, rhs=tB, start=True, stop=True), reads=[c_tB, c_cst], writes=[c_ps[6]])
                S.op('act', lambda e: e.activation(out=tB, in_=PS(6)[:, 0:ncol], func=AF.Ln, bias=gnc[:, 0:1], scale=1.0 / 64), reads=[c_ps[6], c_one], writes=[c_tB])
                S.op('act', lambda e: e.activation(out=tB, in_=tB, func=AF.Exp, scale=-0.5), reads=[c_tB], writes=[c_tB])
                S.op('dve', lambda e: TTn(e, tA, tA, tB, ALU.mult), reads=[c_tA, c_tB], writes=[c_tA])
                S.op('dve', lambda e: e.tensor_scalar(out=tA, in0=tA, scalar1=vcol('rw_ln_w', ct), scalar2=vcol('rw_ln_b', ct), op0=ALU.mult, op1=ALU.add), reads=[c_tA, c_vec], writes=[c_tA])
                S.op('dve', lambda e: e.scalar_tensor_tensor(out=tB, in0=r_ap, scalar=vcol('rw_r_k', ct), in1=k2_ap, op0=ALU.mult, op1=ALU.mult), reads=[c_in, c_vec, c_tB], writes=[c_tB])
                S.op('pe', lambda e: e.matmul(PS(6)[:, 0:ncol], lhsT=blk64, rhs=tB, start=True, stop=True), reads=[c_tB, c_cst], writes=[c_ps[6]])
                S.op('dve', lambda e: TTn(e, tB, v_ap, PS(6)[:, 0:ncol], ALU.mult), reads=[c_ps[6], c_in], writes=[c_tB])
                S.op('dve', lambda e: TTn(e, tA, tA, tB, ALU.add), reads=[c_tA, c_tB], writes=[c_tA])
                S.op('dve', lambda e: TTn(e, out_ap, tA, g_ap, ALU.mult), reads=[c_tA, c_in], writes=[c_out])

            kwr = [0]
            kwo = [0]

            prologue_done = set()

            def block_prologue(bi):
                prologue_done.add(bi)
                c0 = bi * BW
                last = (bi == NB - 1)
                nw = NW if last else BW
                if bi == 0:
                    S.op('dve', lambda e: TTn(e, xx[:, :, 1:BW], hb[:, :, 0:BW - 1], hb[:, :, 1:BW], ALU.subtract), reads=[c_h], writes=[c_xx])
                    S.op('dve', lambda e: e.tensor_scalar(out=xx[:, :, 0:1], in0=hb[:, :, 0:1], scalar1=-1.0, scalar2=None, op0=ALU.mult), reads=[c_h], writes=[c_xx])
                else:
                    S.op('dve', lambda e: TTn(e, xx[:, :, 0:BW], hb[:, :, c0 - 1:c0 + BW - 1], hb[:, :, c0:c0 + BW], ALU.subtract), reads=[c_h], writes=[c_xx])
                if last:
                    S.op('dve', lambda e: TTn(e, xx[:, :, BW:NW], sh0[:], hb[:, :, NP:TT], ALU.subtract), reads=[c_h, c_sh0], writes=[c_xx])

                def mk_xs(n, q):
                    for kt in range(8):
                        S.op('dve', lambda e, kt=kt: e.scalar_tensor_tensor(out=xs[q][:, kt, 0:nw], in0=xx[:, kt, 0:nw], scalar=vcol('rw_mu', kt, n), in1=hb[:, kt, c0:c0 + nw], op0=ALU.mult, op1=ALU.add),
                             reads=[c_xx, c_h, c_vec], writes=[c_xs[q]])
                for n, (wt_, c_wt_, m_, func, li_) in enumerate([(w1, c_w1, 64, AF.Tanh, 0), (a1, c_a1, 64, AF.Copy, 1), (g1, c_g1, 128, AF.Sigmoid, 2)]):
                    mk_xs(3 + n, 3)
                    for kt in range(8):
                        S.op('pe', lambda e, kt=kt, wt_=wt_, m_=m_: e.matmul(PS(7)[0:m_, 0:nw], lhsT=wt_[:, kt, :], rhs=xs[3][:, kt, 0:nw], start=(kt == 0), stop=(kt == 7)),
                             reads=[c_wt_, c_xs[3]], writes=[c_ps[7]])
                    S.op('act', lambda e, m_=m_, func=func, li_=li_: e.activation(out=l1[li_][0:m_, 0:nw], in_=PS(7)[0:m_, 0:nw], func=func), reads=[c_ps[7]], writes=[c_l1[li_]])
                for n in range(3):
                    mk_xs(n, n)

            def do_block(bi):
                c0 = bi * BW
                last = (bi == NB - 1)
                nw = NW if last else BW
                if bi not in prologue_done:
                    block_prologue(bi)
                for ct in range(8):
                    do_ct(bi, ct, c0, last, nw)
                for mt in range(8):
                    wq = kwo[0] % 2
                    kwo[0] += 1
                    wload(wo[wq][:], I['rw_w_o'][:, mt * 128:(mt + 1) * 128], 8, c_wo[wq])
                    segs = [(0, BW, c0, [c_x[mt][c0 // 512]])]
                    for (s0, sn_, x0, cx) in segs:
                        for kt in range(8):
                            S.op('pe', lambda e, kt=kt, wq=wq, s0=s0, sn_=sn_: e.matmul(PS(7)[:, 0:sn_], lhsT=wo[wq][:, kt, :], rhs=og[:, kt, s0:s0 + sn_], start=(kt == 0), stop=(kt == 7)),
                                 reads=[c_wo[wq], c_og[kt]], writes=[c_ps[7]])
                        S.op('dve', lambda e, mt=mt, sn_=sn_, x0=x0: TTn(e, xres[:, mt, x0:x0 + sn_], xres[:, mt, x0:x0 + sn_], PS(7)[:, 0:sn_], ALU.add), reads=[c_ps[7]] + cx, writes=cx)

            RS = [[Tt[0], Tt[2], Tt[5], Tt[7]], [A.f32(NW), A.f32(NW), A.f32(NW), A.f32(NW)]]
            c_RS = [[c_T[0], c_T[2], c_T[5], c_T[7]], cells(4)]
            proj_done = set()

            def emit_proj(bi, ct, nw):
                proj_done.add((bi, ct))
                s_ = ct % 2
                r_, v_, g_ = RS[s_][0], RS[s_][1], RS[s_][2]
                cr, cv, cg = c_RS[s_][0], c_RS[s_][1], c_RS[s_][2]
                k_, lw, a_ = Tt[1], Tt[3], Tt[4]
                ck, clw, ca = c_T[1], c_T[3], c_T[4]
                wq = kwr[0] % 2
                if kwr[0] == 0:
                    for n in range(3):
                        wload(wr[0][:, :, n, :], I['rw_w_rkv'][n, :, 0:128], 8, c_wr[0][n])
                kwr[0] += 1
                for n in range(3):
                    for kt in range(8):
                        S.op('pe', lambda e, n=n, kt=kt: e.matmul(PS(n)[:, 0:nw], lhsT=wr[wq][:, kt, n, :], rhs=xs[n][:, kt, 0:nw], start=(kt == 0), stop=(kt == 7)),
                             reads=[c_wr[wq][n], c_xs[n]], writes=[c_ps[n]])
                if not (bi == NB - 1 and ct == 7):
                    nct = (ct + 1) % 8
                    for n in range(3):
                        wload(wr[1 - wq][:, :, n, :], I['rw_w_rkv'][n, :, nct * 128:(nct + 1) * 128], 8, c_wr[1 - wq][n])
                S.op('pe', lambda e: e.matmul(PS(3)[:, 0:nw], lhsT=w2[0:64, ct * 128:(ct + 1) * 128], rhs=l1[0][0:64, 0:nw], start=True, stop=True), reads=[c_w2, c_l1[0]], writes=[c_ps[3]])
                S.op('pe', lambda e: e.matmul(PS(4)[:, 0:nw], lhsT=a2[0:64, ct * 128:(ct + 1) * 128], rhs=l1[1][0:64, 0:nw], start=True, stop=True), reads=[c_a2, c_l1[1]], writes=[c_ps[4]])
                S.op('pe', lambda e: e.matmul(PS(5)[:, 0:nw], lhsT=g2[:, ct * 128:(ct + 1) * 128], rhs=l1[2][:, 0:nw], start=True, stop=True), reads=[c_g2, c_l1[2]], writes=[c_ps[5]])
                W = slice(0, nw)
                S.op('act', lambda e: e.activation(out=r_[:, W], in_=PS(0)[:, W], func=AF.Copy), reads=[c_ps[0]], writes=[cr])
                S.op('act', lambda e: e.activation(out=k_[:, W], in_=PS(1)[:, W], func=AF.Copy), reads=[c_ps[1]], writes=[ck])
                S.op('act', lambda e: e.activation(out=v_[:, W], in_=PS(2)[:, W], func=AF.Copy), reads=[c_ps[2]], writes=[cv])
                S.op('act', lambda e: e.activation(out=g_[:, W], in_=PS(5)[:, W], func=AF.Copy), reads=[c_ps[5]], writes=[cg])
                S.op('act', lambda e: e.activation(out=KBe[:, 0, :], in_=PS(2)[:, 0:BW], func=AF.Copy), reads=[c_ps[2]], writes=[c_KBe])
                S.op('act', lambda e: e.activation(out=lw[:, W], in_=PS(3)[:, W], func=AF.Sigmoid, bias=vcol('rw_w0', ct), scale=1.0), reads=[c_ps[3], c_vec], writes=[clw])
                S.op('act', lambda e: e.activation(out=a_[:, W], in_=PS(4)[:, W], func=AF.Sigmoid, bias=vcol('rw_a0', ct), scale=1.0), reads=[c_ps[4], c_vec], writes=[ca])

            def do_ct(bi, ct, c0, last, nw):
                s_ = ct % 2
                r_, v_, g_, k2 = RS[s_]
                cr, cv, cg, ck2 = c_RS[s_]
                _, k_, _, lw, a_, _, kk, _, b_, cum, Wi, Wn, We = Tt
                _, ck, _, clw, ca, _, ckk, _, cb, ccum, cWi, cWn, cWe = c_T
                if (bi, ct) not in proj_done:
                    emit_proj(bi, ct, nw)
                W = slice(0, nw)
                CD = math.exp(-0.5)
                P_ = slice(0, BW)
                S.op('dve', lambda e: e.tensor_scalar(out=kk[:, W], in0=k_[:, W], scalar1=vcol('rw_k_k', ct), scalar2=None, op0=ALU.mult), reads=[ck, c_vec], writes=[ckk])
                S.op('pool', lambda e: TTn(e, b_[:, W], kk[:, W], kk[:, W], ALU.mult), reads=[ckk], writes=[cb])
                S.op('pe', lambda e: e.matmul(PS(6)[:, W], lhsT=blk64, rhs=b_[:, W], start=True, stop=True), reads=[cb, c_cst], writes=[c_ps[6]])
                S.op('dve', lambda e: e.tensor_tensor_scan(out=cum[:, P_], data0=cmk[:], data1=lw[:, P_], initial=0.0, op0=ALU.mult, op1=ALU.add), reads=[c_cmk, clw, ccum], writes=[ccum])
                S.op('dve', lambda e: e.tensor_scalar(out=k2[:, W], in0=a_[:, W], scalar1=-1.0, scalar2=vcol('rw_k_a', ct), op0=ALU.add, op1=ALU.mult), reads=[ca, c_vec], writes=[ck2])
                S.op('dve', lambda e: e.scalar_tensor_tensor(out=k2[:, W], in0=k2[:, W], scalar=1.0, in1=k_[:, W], op0=ALU.add, op1=ALU.mult), reads=[ck2, ck], writes=[ck2])
                S.op('pool', lambda e: TTn(e, We[:, P_], cum[:, P_], lw[:, P_], ALU.subtract), reads=[ccum, clw], writes=[cWe])
                S.op('dve', lambda e: e.tensor_scalar(out=b_[:, W], in0=PS(6)[:, W], scalar1=1e-24, scalar2=None, op0=ALU.max), reads=[c_ps[6]], writes=[cb])
                S.op('act', lambda e: e.activation(out=b_[:, W], in_=b_[:, W], func=AF.Ln), reads=[cb], writes=[cb])
                S.op('act', lambda e: e.activation(out=b_[:, W], in_=b_[:, W], func=AF.Exp, scale=-0.5), reads=[cb], writes=[cb])
                S.op('act', lambda e: e.activation(out=Wi[:, P_], in_=cum[:, P_], func=AF.Exp, scale=-CD), reads=[ccum], writes=[cWi])
                S.op('act', lambda e: e.activation(out=Wn[:, P_], in_=cum[:, P_], func=AF.Exp, scale=CD), reads=[ccum], writes=[cWn])
                S.op('act', lambda e: e.activation(out=We[:, P_], in_=We[:, P_], func=AF.Exp, scale=-CD), reads=[cWe], writes=[cWe])
                S.op('dve', lambda e: TTn(e, kk[:, W], kk[:, W], b_[:, W], ALU.mult), reads=[ckk, cb], writes=[ckk])
                S.op('pool', lambda e: TTn(e, b_[:, W], kk[:, W], a_[:, W], ALU.mult), reads=[ckk, ca], writes=[cb])
                if last:
                    for vi, (src, csrc) in enumerate([(kk, ckk), (lw, clw), (b_, cb), (k2, ck2), (r_, cr), (v_, cv), (g_, cg)]):
                        if vi == 1:
                            S.op('act', lambda e, src=src, vi=vi: e.activation(out=SV[:, ct, vi, :], in_=src[:, BW:NW], func=AF.Exp, scale=-CD), reads=[csrc], writes=[c_SV[ct]])
                        else:
                            S.op('pool', lambda e, src=src, vi=vi: e.tensor_copy(out=SV[:, ct, vi, :], in_=src[:, BW:NW]), reads=[csrc], writes=[c_SV[ct]])
                for h in range(2):
                    S.op('dve', lambda e, h=h: e.scalar_tensor_tensor(out=KKm[h][:, P_], in0=kk[:, P_], scalar=mask_hi[:, h:h + 1], in1=We[:, P_], op0=ALU.mult, op1=ALU.mult), reads=[ckk, cWe, c_cst], writes=[c_KKm[h]])
                    S.op('dve', lambda e, h=h: e.scalar_tensor_tensor(out=Rm[h][:, P_], in0=r_[:, P_], scalar=mask_hi[:, h:h + 1], in1=Wi[:, P_], op0=ALU.mult, op1=ALU.mult), reads=[cr, cWi, c_cst], writes=[c_Rm[h]])
                S.op('pool', lambda e: TTn(e, cum[:, P_], k2[:, P_], Wn[:, P_], ALU.mult), reads=[ck2, cWn, ccum], writes=[ccum])
                S.op('pool', lambda e: TTn(e, We[:, P_], b_[:, P_], Wn[:, P_], ALU.mult), reads=[cb, cWn, cWe, c_KKm], writes=[cWe])
                S.op('act', lambda e: e.activation(out=Kb[:, P_], in_=cum[:, P_], func=AF.Copy), reads=[ccum], writes=[c_Kb])
                S.op('act', lambda e: e.activation(out=Bb[:, P_], in_=We[:, P_], func=AF.Copy), reads=[cWe], writes=[c_Bb])
                wend = Wi[:, 63:BW:64]
                wend_b = wend.unsqueeze(2).to_broadcast([128, 4, 64])
                S.op('dve', lambda e: TTn(e, KBe[:, 1, :].rearrange("p (c n) -> p c n", c=4), cum[:, P_].rearrange("p (c n) -> p c n", c=4), wend_b, ALU.mult), reads=[ccum, cWi], writes=[c_KBe])
                S.op('dve', lambda e: e.scalar_tensor_tensor(out=KBe[:, 2, :].rearrange("p (c n) -> p c n", c=4), in0=We[:, P_].rearrange("p (c n) -> p c n", c=4), scalar=-1.0, in1=wend_b, op0=ALU.mult, op1=ALU.mult),
                     reads=[cWe, cWi], writes=[c_KBe])
                for c in range(4):
                    for vi in range(3):
                        bnk = (c * 3 + vi) // 4
                        o_ = ((c * 3 + vi) % 4) * 128
                        S.op('pe', lambda e, c=c, vi=vi, bnk=bnk, o_=o_: e.transpose(PS(bnk)[0:64, o_:o_ + 128], KBe[:, vi, c * 64:(c + 1) * 64], ident), reads=[c_KBe, c_cst], writes=[c_ps[bnk]])
                S.op('act', lambda e: e.activation(out=Tok[0:64].rearrange("p c v n -> p (c v n)"), in_=psum[0:64, 0:3, :].rearrange("p a n -> p (a n)"), func=AF.Copy), reads=[c_ps[0:3]], writes=[c_Tok])
                for c in range(4):
                    cs_ = slice(c * 64, (c + 1) * 64)
                    for h in range(2):
                        for m, (lh, rh, cl, crr) in enumerate([(Kb, KKm[h], c_Kb, c_KKm[h]), (Bb, KKm[h], c_Bb, c_KKm[h]), (KKm[h], Bb, c_KKm[h], c_Bb), (Kb, Rm[h], c_Kb, c_Rm[h]), (Bb, Rm[h], c_Bb, c_Rm[h])]):
                            idx = (c * 2 + h) * 5 + m
                            bnk = 3 + idx // 8
                            o_ = (idx % 8) * 64
                            S.op('pe', lambda e, lh=lh, rh=rh, cs_=cs_, bnk=bnk, o_=o_: e.matmul(PS(bnk)[0:64, o_:o_ + 64], lhsT=lh[:, cs_], rhs=rh[:, cs_], start=True, stop=True),
                                 reads=[cl, crr], writes=[c_ps[bnk]])
                S.op('dve', lambda e: TTn(e, AT[0:64].rearrange("p c h m t -> p (c h) (m t)"), psum[0:64, 3:8, :].rearrange("p a (u n) -> p (a u) n", u=8).rearrange("p (ch m) n -> p ch (m n)", m=5),
                                          mask5[0:64, :].unsqueeze(1).to_broadcast([64, 8, 320]), ALU.mult), reads=[c_ps[3:8], c_cst], writes=[c_AT])
                S.op('dve', lambda e: TTn(e, Y[0:64].rearrange("p c h t -> p (c h) t"), imp[0:64, :].unsqueeze(1).to_broadcast([64, 8, 64]), AT[0:64, :, :, 1, :].rearrange("p c h t -> p (c h) t"), ALU.subtract), reads=[c_AT, c_cst], writes=[c_Y])
                prevP = lambda c, h: AT[0:64, c, h, 1, :]
                prevQ = lambda c, h: AT[0:64, c, h, 2, :]
                c_prev = c_AT
                for lvl in range(1, 7):
                    pq = PQ[lvl % 2]
                    c_pq = c_PQ[lvl % 2]
                    for c in range(4):
                        for h in range(2):
                            o_ = (c * 2 + h) * 64
                            if lvl <= 4:
                                S.op('pe', lambda e, c=c, h=h, o_=o_, prevP=prevP, prevQ=prevQ: e.matmul(PS(3)[0:64, o_:o_ + 64], lhsT=prevQ(c, h), rhs=prevP(c, h), start=True, stop=True), reads=[c_prev], writes=[c_ps[3]])
                            if lvl <= 5:
                                S.op('pe', lambda e, c=c, h=h, o_=o_, prevP=prevP, prevQ=prevQ: e.matmul(PS(4)[0:64, o_:o_ + 64], lhsT=prevP(c, h), rhs=prevQ(c, h), start=True, stop=True), reads=[c_prev], writes=[c_ps[4]])
                            if lvl >= 2:
                                S.op('pe', lambda e, c=c, h=h, o_=o_, prevQ=prevQ: e.matmul(PS(5)[0:64, o_:o_ + 64], lhsT=prevQ(c, h), rhs=Y[0:64, c, h, :], start=True, stop=True), reads=[c_prev, c_Y], writes=[c_ps[5]])
                    if lvl <= 4:
                        S.op('act', lambda e, pq=pq: e.activation(out=pq[0:64, :, :, 0, :].rearrange("p c h t -> p (c h) t"), in_=PS(3)[0:64, :].rearrange("p (u t) -> p u t", u=8), func=AF.Copy), reads=[c_ps[3]], writes=[c_pq])
                    if lvl <= 5:
                        S.op('act', lambda e, pq=pq: e.activation(out=pq[0:64, :, :, 1, :].rearrange("p c h t -> p (c h) t"), in_=PS(4)[0:64, :].rearrange("p (u t) -> p u t", u=8), func=AF.Copy), reads=[c_ps[4]], writes=[c_pq])
                    if lvl >= 2:
                        S.op('dve', lambda e: TTn(e, Y[0:64].rearrange("p c h t -> p (c h t)"), Y[0:64].rearrange("p c h t -> p (c h t)"), PS(5)[0:64, :], ALU.add), reads=[c_ps[5], c_Y], writes=[c_Y])
                    prevP = (lambda pq: (lambda c, h: pq[0:64, c, h, 0, :]))(pq)
                    prevQ = (lambda pq: (lambda c, h: pq[0:64, c, h, 1, :]))(pq)
                    c_prev = c_pq
                for c in range(4):
                    cs_ = slice(c * 64, (c + 1) * 64)
                    for h in range(2):
                        S.op('pe', lambda e, h=h, cs_=cs_: e.matmul(PS(6)[0:64, h * 64:(h + 1) * 64], lhsT=KKm[h][:, cs_], rhs=Mb[:, ct, h, :], start=True, stop=False), reads=[c_KKm[h], c_Mb[ct]], writes=[c_ps[6]])
                        S.op('pe', lambda e, h=h, c=c: e.matmul(PS(6)[0:64, h * 64:(h + 1) * 64], lhsT=AT[0:64, c, h, 0, :], rhs=Tok[0:64, c, 0, h * 64:(h + 1) * 64], start=False, stop=True), reads=[c_AT, c_Tok], writes=[c_ps[6]])
                    S.op('act', lambda e: e.activation(out=Wp[0:64].rearrange("p h i -> p (h i)"), in_=PS(6)[0:64, 0:128], func=AF.Copy), reads=[c_ps[6]], writes=[c_Wp])
                    for h in range(2):
                        S.op('pe', lambda e, h=h, c=c: e.matmul(PS(7)[0:64, h * 64:(h + 1) * 64], lhsT=Y[0:64, c, h, :], rhs=Wp[0:64, h, :], start=True, stop=True), reads=[c_Y, c_Wp], writes=[c_ps[7]])
                    S.op('act', lambda e: e.activation(out=Us[0:64].rearrange("p h i -> p (h i)"), in_=PS(7)[0:64, 0:128], func=AF.Copy), reads=[c_ps[7]], writes=[c_Us])
                    for h in range(2):
                        p0 = 64 * h
                        S.op('pe', lambda e, h=h, cs_=cs_, p0=p0: e.matmul(PS(0)[p0:p0 + 64, cs_], lhsT=Mb[:, ct, h, :], rhs=Rm[h][:, cs_], start=True, stop=False, tile_position=(0, p0)), reads=[c_Mb[ct], c_Rm[h]], writes=[c_ps[0]])
                        S.op('pe', lambda e, h=h, c=c, cs_=cs_, p0=p0: e.matmul(PS(0)[p0:p0 + 64, cs_], lhsT=Tok[0:64, c, 0, h * 64:(h + 1) * 64], rhs=AT[0:64, c, h, 3, :], start=False, stop=False, tile_position=(0, p0)), reads=[c_Tok, c_AT], writes=[c_ps[0]])
                        S.op('pe', lambda e, h=h, c=c, cs_=cs_, p0=p0: e.matmul(PS(0)[p0:p0 + 64, cs_], lhsT=Us[0:64, h, :], rhs=AT[0:64, c, h, 4, :], start=False, stop=True, tile_position=(0, p0)), reads=[c_Us, c_AT], writes=[c_ps[0]])
                        S.op('pe', lambda e, h=h, c=c, p0=p0: e.matmul(PS(1)[p0:p0 + 64, 0:64], lhsT=Tok[0:64, c, 1, h * 64:(h + 1) * 64], rhs=Tok[0:64, c, 0, h * 64:(h + 1) * 64], start=True, stop=False, tile_position=(0, p0)), reads=[c_Tok], writes=[c_ps[1]])
                        S.op('pe', lambda e, h=h, c=c, p0=p0: e.matmul(PS(1)[p0:p0 + 64, 0:64], lhsT=Tok[0:64, c, 2, h * 64:(h + 1) * 64], rhs=Us[0:64, h, :], start=False, stop=True, tile_position=(0, p0)), reads=[c_Tok, c_Us], writes=[c_ps[1]])
                    S.op('dve', lambda e, c=c: e.scalar_tensor_tensor(out=Mf[:, ct, :], in0=Mf[:, ct, :], scalar=Wi[:, c * 64 + 63:c * 64 + 64], in1=PS(1)[:, 0:64], op0=ALU.mult, op1=ALU.add),
                         reads=[c_ps[1], c_Mf[ct], cWi], writes=[c_Mf[ct]])
                    S.op('dve', lambda e: TTn(e, Mb[:, ct, :, :], Mf[:, ct, :].unsqueeze(1).to_broadcast([128, 2, 64]), mask_hi.unsqueeze(2).to_broadcast([128, 2, 64]), ALU.mult),
                         reads=[c_Mf[ct], c_cst], writes=[c_Mb[ct]])
                S.op('act', lambda e: e.activation(out=ot[:, P_], in_=PS(0)[:, P_], func=AF.Copy), reads=[c_ps[0]], writes=[c_ot])
                if bi == 0:
                    S.op('dve', lambda e: TTn(e, cum[:, 0:2], r_[:, 0:2], k2[:, 0:2], ALU.mult), reads=[cr, ck2, ccum], writes=[ccum])
                    S.op('pe', lambda e: e.matmul(PS(6)[:, 0:2], lhsT=blk64, rhs=cum[:, 0:2], start=True, stop=True), reads=[ccum, c_cst], writes=[c_ps[6]])
                    S.op('dve', lambda e: TTn(e, ot[:, 0:1], v_[:, 0:1], PS(6)[:, 0:1], ALU.mult), reads=[c_ps[6], cv, c_ot], writes=[c_ot])
                if ct < 7:
                    emit_proj(bi, ct + 1, nw)
                if ct == 6 and bi + 1 < NB:
                    block_prologue(bi + 1)
                gn_gate(ct, BW, ot[:, P_], c_ot, r_[:, P_], k2[:, P_], v_[:, P_], g_[:, P_], [cr, ck2, cv, cg], og[:, ct, 0:BW], c_og[ct], cum[:, P_], We[:, P_], ccum, cWe)

            for bi in range(NB):
                do_block(bi)
            S.barrier()
            A2 = Arena()
            A2.off = TT_OFF[0]
            S0 = A2.f32(NS * 64).rearrange("p (b j) -> p b j", b=NS); c_S0 = Cell()
            S1 = A2.f32(NS * 64).rearrange("p (b j) -> p b j", b=NS); c_S1 = Cell()
            t1 = A2.f32(NS * 64).rearrange("p (b j) -> p b j", b=NS); c_t1 = Cell()
            rx = A2.f32(NS * 64).rearrange("p (b j) -> p b j", b=NS); c_rx = Cell()
            sa = A2.f32(NS); c_sa = Cell()
            os_ = A2.f32(NS); c_os = Cell()
            ogs = A2.f32(8 * NS).rearrange("p (a b) -> p a b", a=8); c_ogs = cells(8)
            ogsb = A2.bf16(8 * NS).rearrange("p (a b) -> p a b", a=8); c_ogsb = cells(8)
            tA = A2.f32(NS); c_tA = Cell()
            tB = A2.f32(NS); c_tB = Cell()
            i2 = cc('i2')

            def bvec(ct, vi, pb):
                S.op('dve', lambda e: TTn(e, rx[:], SV[:, ct, vi, :].unsqueeze(2).to_broadcast([128, NS, 64]), i2.unsqueeze(1).to_broadcast([128, NS, 64]), ALU.mult), reads=[c_SV[ct], c_cst], writes=[c_rx])
                for half in range(2):
                    S.op('pe', lambda e, half=half: e.matmul(PS(pb + half), lhsT=blk64, rhs=rx[:, half * 8:(half + 1) * 8, :].rearrange("p b j -> p (b j)"), start=True, stop=True), reads=[c_rx, c_cst], writes=[c_ps[pb + half]])
                return psum[:, pb:pb + 2, :].rearrange("p a (b j) -> p (a b) j", j=64), [c_ps[pb], c_ps[pb + 1]]

            for ct in range(8):
                S.dma('sp', S0[:], I['state_rwkv_wkv'][:, 2 * ct:2 * ct + 2, :, :].rearrange("b h i j -> (h i) b j"), writes=[c_S0])
                KKb, cK = bvec(ct, 0, 0)
                S.op('dve', lambda e, KKb=KKb: TTn(e, t1[:], S0[:], KKb, ALU.mult), reads=[c_S0] + cK, writes=[c_t1])
                S.op('dve', lambda e: e.tensor_reduce(out=sa[:], in_=t1[:], axis=AX.X, op=ALU.add), reads=[c_t1], writes=[c_sa])
                Wb, cW = bvec(ct, 1, 2)
                S.op('dve', lambda e, Wb=Wb: TTn(e, S1[:], S0[:], Wb, ALU.mult), reads=[c_S0] + cW, writes=[c_S1])
                Bq, cB = bvec(ct, 2, 0)
                S.op('dve', lambda e, Bq=Bq: TTn(e, t1[:], Bq, sa[:].unsqueeze(2).to_broadcast([128, NS, 64]), ALU.mult), reads=[c_sa] + cB, writes=[c_t1])
                S.op('dve', lambda e: TTn(e, S1[:], S1[:], t1[:], ALU.subtract), reads=[c_S1, c_t1], writes=[c_S1])
                K2q, cK2 = bvec(ct, 3, 2)
                S.op('dve', lambda e, K2q=K2q, ct=ct: TTn(e, t1[:], K2q, SV[:, ct, 5, :].unsqueeze(2).to_broadcast([128, NS, 64]), ALU.mult), reads=[c_SV[ct]] + cK2, writes=[c_t1])
                S.op('dve', lambda e: TTn(e, S1[:], S1[:], t1[:], ALU.add), reads=[c_S1, c_t1], writes=[c_S1])
                Rq, cRq = bvec(ct, 4, 0)
                S.op('dve', lambda e, Rq=Rq: TTn(e, t1[:], S1[:], Rq, ALU.mult), reads=[c_S1] + cRq, writes=[c_t1])
                S.op('dve', lambda e: e.tensor_reduce(out=os_[:], in_=t1[:], axis=AX.X, op=ALU.add), reads=[c_t1], writes=[c_os])
                S.dma('sp', O['wkv_s'][:, 2 * ct:2 * ct + 2, :, :].rearrange("b h i j -> (h i) b j"), S1[:], reads=[c_S1])
                gn_gate(ct, NS, os_[:], c_os, SV[:, ct, 4, :], SV[:, ct, 3, :], SV[:, ct, 5, :], SV[:, ct, 6, :], [c_SV[ct]], ogs[:, ct, :], c_ogs[ct], tA[:], tB[:], c_tA, c_tB)
                S.op('act', lambda e, ct=ct: e.activation(out=ogsb[:, ct, :], in_=ogs[:, ct, :], func=AF.Copy), reads=[c_ogs[ct]], writes=[c_ogsb[ct]])
            if dbg:
                for ct in range(8):
                    S.dma('sp', O['dbg'][8, ct * 128:(ct + 1) * 128, NP:TT], ogs[:, ct, :], reads=[c_ogs[ct]])
                    S.dma('sp', O['dbg'][8, ct * 128:(ct + 1) * 128, 0:NS], SV[:, ct, 4, :], reads=[c_SV[ct]])
                    S.dma('sp', O['dbg'][8, ct * 128:(ct + 1) * 128, NS:2 * NS], SV[:, ct, 6, :], reads=[c_SV[ct]])
                    S.dma('sp', O['dbg'][8, ct * 128:(ct + 1) * 128, 2 * NS:3 * NS], SV[:, ct, 3, :], reads=[c_SV[ct]])
                    S.dma('sp', O['dbg'][8, ct * 128:(ct + 1) * 128, 3 * NS:4 * NS], SV[:, ct, 5, :], reads=[c_SV[ct]])
            for mt in range(8):
                wq = kwo[0] % 2
                kwo[0] += 1
                wload(wo[wq][:], I['rw_w_o'][:, mt * 128:(mt + 1) * 128], 8, c_wo[wq])
                for kt in range(8):
                    S.op('pe', lambda e, kt=kt, wq=wq: e.matmul(PS(7)[:, 0:NS], lhsT=wo[wq][:, kt, :], rhs=ogsb[:, kt, :], start=(kt == 0), stop=(kt == 7)), reads=[c_wo[wq], c_ogsb[kt]], writes=[c_ps[7]])
                S.op('dve', lambda e, mt=mt: TTn(e, xres[:, mt, NP:TT], xres[:, mt, NP:TT], PS(7)[:, 0:NS], ALU.add), reads=[c_ps[7], c_x[mt][4]], writes=[c_x[mt][4]])
            shf = A2.f32(8 * (NS + 1)).rearrange("p (a b) -> p a b", a=8); c_shf = cells(8)
            S.op('dve', lambda e: e.tensor_copy(out=shf[:, :, 0:NS], in_=hb[:, :, NP:TT]), reads=[c_h], writes=[c_shf])
            S.op('dve', lambda e: e.tensor_copy(out=shf[:, :, NS:NS + 1], in_=hb[:, :, NP - 1:NP]), reads=[c_h], writes=[c_shf])
            shs = A2.f32(8 * NS).rearrange("p (a b) -> p a b", a=8); c_shs = cells(8)
            S.op('dve', lambda e: e.tensor_copy(out=shs[:], in_=shf[:, :, 0:NS]), reads=[c_shf], writes=[c_shs])
            fm_to_rows(shs, c_shs, O['shift_s'][:, :], stg, c_stg, 0)
            shp = A2.f32(8); c_shp = Cell()
            S.op('dve', lambda e: e.tensor_copy(out=shp[:], in_=shf[:, :, NS]), reads=[c_shf], writes=[c_shp])
            S.op('pe', lambda e: e.transpose(PS(0)[0:8, 0:128], shp[:, 0:8], ident), reads=[c_shp, c_cst], writes=[c_ps[0]])
            S.op('dve', lambda e: e.tensor_copy(out=stg[32:40, 0:128], in_=PS(0)[0:8, 0:128]), reads=[c_ps[0]], writes=[c_stg])
            S.dma('sp', O['shift_p'][:, :].rearrange("o (a p) -> (o a) p", p=128), stg[32:40, 0:128], reads=[c_stg])
            wst = A2.f32(16 * 64).rearrange("p (h j) -> p h j", h=16); c_wst = Cell()
            for ct in range(8):
                pb = 1 + (ct % 2)
                S.op('pe', lambda e, ct=ct, pb=pb: e.transpose(PS(pb)[0:64, 0:128], Mf[:, ct, :], ident), reads=[c_Mf[ct], c_cst], writes=[c_ps[pb]])
                S.op('dve', lambda e, ct=ct, pb=pb: e.tensor_copy(out=wst[0:64, 2 * ct:2 * ct + 2, :].rearrange("p h j -> p (h j)"), in_=PS(pb)[0:64, 0:128]), reads=[c_ps[pb]], writes=[c_wst])
            S.dma('sp', O['wkv_p'][:, :, :].rearrange("h i j -> i h j"), wst[0:64, :, :], reads=[c_wst])
            S.barrier()

        env = dict(locals())
        for li in range(DEPTH):
            kind = li % 3
            if mixers[li]:
                if kind == 0:
                    s5_layer(li, li // 3)
                elif kind == 1:
                    rwkv_layer(li)
                else:
                    lru_layer(li)
            dump(2 * li)
            if ffn:
                ffn_layer(li)
            dump(2 * li + 1)
        final_out()
        S.finish()
        S.emit()
    return nc


def s5_layer(env, li, j):
    raise NotImplementedError


def rwkv_layer(env, li):
    raise NotImplementedError


def lru_layer(env, li):
    raise NotImplementedError


OUT_ORDER = ['y_prompt', 'y_sample', 's5_re_p', 's5_re_s', 's5_im_p', 's5_im_s', 'wkv_p', 'wkv_s',
             'shift_p', 'shift_s', 'lru_h_p', 'lru_h_s', 'lru_conv_p', 'lru_conv_s', 'ffn_conv_p', 'ffn_conv_s']


def make_in_maps(inputs):
    f = lambda a: np.ascontiguousarray(np.asarray(a, dtype=np.float32))
    w = {}
    for name in ['norm_mix', 'norm_ffn', 's5_a_re', 's5_a_im', 's5_log_dt', 's5_b_re', 's5_b_im', 's5_c_re', 's5_c_im',
                 's5_d', 's5_w_glu', 'ffn_w_in', 'ffn_conv_w', 'ffn_w_out']:
        w[name] = f(inputs[name])
    w['norm_final'] = f(inputs['norm_final']).reshape(1, D)
    w['ffn_conv_b'] = f(inputs['ffn_conv_b']).reshape(4, 1, DFF)
    for name in ['rw_mu', 'rw_w_rkv', 'rw_w1', 'rw_w2', 'rw_a1', 'rw_a2', 'rw_g1', 'rw_g2', 'rw_w_o',
                 'lru_w_in', 'lru_conv_w', 'lru_w_rg', 'lru_w_ig', 'lru_w_out']:
        w[name] = f(inputs[name])[0]
    for name in ['rw_w0', 'rw_a0', 'rw_k_k', 'rw_k_a', 'rw_ln_w', 'rw_ln_b', 'lru_conv_b', 'lru_b_rg', 'lru_b_ig', 'lru_lambda']:
        w[name] = f(inputs[name]).reshape(1, D)
    w['rw_r_k'] = f(inputs['rw_r_k']).reshape(1, D)
    w['consts'] = CONSTS
    w['cmask'] = CMASK
    xp = f(inputs['x_prompt'])
    xs = f(inputs['x_sample']).reshape(128, D)
    maps = []
    for c in range(8):
        sl = slice(c * NS, (c + 1) * NS)
        m = dict(w)
        m['x_prompt'] = xp[c]
        m['x_sample'] = xs[sl]
        m['state_s5_re'] = f(inputs['state_s5_re'])[:, sl].reshape(2, NS, 4096)
        m['state_s5_im'] = f(inputs['state_s5_im'])[:, sl].reshape(2, NS, 4096)
        m['state_rwkv_wkv'] = f(inputs['state_rwkv_wkv'])[0, sl]
        m['state_rwkv_shift'] = f(inputs['state_rwkv_shift'])[0, sl]
        m['state_lru_h'] = f(inputs['state_lru_h'])[0, sl]
        m['state_lru_conv'] = f(inputs['state_lru_conv'])[0, sl]
        m['state_ffn_conv'] = f(inputs['state_ffn_conv'])[:, sl]
        maps.append({k: np.ascontiguousarray(v) for k, v in m.items()})
    return maps


def gather(results):
    cat = lambda name, axis: np.concatenate([r[name] for r in results], axis=axis)
    stk = lambda name, axis: np.stack([r[name] for r in results], axis=axis)
    out = {}
    out['y_prompt'] = stk('y_prompt', 0)
    out['y_sample'] = cat('y_sample', 0).reshape(128, 1, D)
    out['s5_re_p'] = stk('s5_re_p', 1).reshape(2, 8, 64, 64)
    out['s5_im_p'] = stk('s5_im_p', 1).reshape(2, 8, 64, 64)
    out['s5_re_s'] = cat('s5_re_s', 1).reshape(2, 128, 64, 64)
    out['s5_im_s'] = cat('s5_im_s', 1).reshape(2, 128, 64, 64)
    out['wkv_p'] = stk('wkv_p', 0)[None]
    out['wkv_s'] = cat('wkv_s', 0)[None]
    out['shift_p'] = cat('shift_p', 0)[None]
    out['shift_s'] = cat('shift_s', 0)[None]
    out['lru_h_p'] = cat('lru_h_p', 0)[None]
    out['lru_h_s'] = cat('lru_h_s', 0)[None]
    out['lru_conv_p'] = stk('lru_conv_p', 0)[None]
    out['lru_conv_s'] = cat('lru_conv_s', 0)[None]
    out['ffn_conv_p'] = stk('ffn_conv_p', 1)
    out['ffn_conv_s'] = cat('ffn_conv_s', 1)
    return tuple(np.ascontiguousarray(out[k], dtype=np.float32) for k in OUT_ORDER)


_NC_CACHE = {}


def kernel(**inputs):
    if 'nc' not in _NC_CACHE:
        _NC_CACHE['nc'] = build()
    nc = _NC_CACHE['nc']
    res = run_bass_kernel_spmd(nc, make_in_maps(inputs), core_ids=list(range(8)))
    return gather(res.results)
```
